# Optimizing a Trainium2 kernel written in Bass

```python
import math
import jax, jax.numpy as jnp
from jax import lax
import numpy as np

D_MODEL = 1024
BATCH = 8
SEQ = 4096
DEPTH = 2

GRID_W = 64
CTX_LEN = 256
D_FF = 2816
N_MOD = 9
EPS = 1e-6
NEG = -1e30

SSM_WIDTH = 384
SSM_GROUP = 16
SSM_GROUPS = SSM_WIDTH // SSM_GROUP
SSM_STATE = 64
DT_MIN = 1e-3
DT_MAX = 1e-1

HEAD_DIM = 64
GQA_HEADS = 8
GQA_KV_HEADS = 2
GQA_WIDTH = GQA_HEADS * HEAD_DIM
GQA_KV_WIDTH = GQA_KV_HEADS * HEAD_DIM
WINDOW = 128
Q_BLOCK = 128
ROPE_THETA = 10000.0

NA_HEADS = 8
NA_WIDTH = NA_HEADS * HEAD_DIM
NA_ROWS_MAX = 8
NA_COLS = 16

N_BRANCH = 3
IN_SPLITS = (SSM_WIDTH, GQA_WIDTH, GQA_KV_WIDTH, GQA_KV_WIDTH, NA_WIDTH, NA_WIDTH, NA_WIDTH, N_BRANCH * D_MODEL)
N_IN = SSM_WIDTH + GQA_WIDTH + 2 * GQA_KV_WIDTH + 3 * NA_WIDTH + N_BRANCH * D_MODEL

kernel_name = "hybrid_s5_swa_natten_prefix_dit_block"


def rmsnorm(x, g):
    xf = x.astype(jnp.float32)
    y = xf * lax.rsqrt(jnp.mean(xf * xf, axis=-1, keepdims=True) + EPS)
    return (y * g.astype(jnp.float32)).astype(x.dtype)


def modulate(h, shift, scale):
    return h * (1.0 + scale) + shift


def swiglu(h, wg, wu, wd):
    return (jax.nn.silu(h @ wg) * (h @ wu)) @ wd


def split_in(z):
    offs = np.cumsum(IN_SPLITS)[:-1]
    return jnp.split(z, [int(o) for o in offs], axis=-1)


def to_heads(t, n):
    return t.reshape(t.shape[:2] + (n, HEAD_DIM))


def joint_softmax(parts, sink=None):
    sizes = [p.shape[-1] for p in parts]
    cols = list(parts)
    if sink is not None:
        cols.append(jnp.broadcast_to(sink, parts[0].shape[:-1] + (1,)))
    probs = jax.nn.softmax(jnp.concatenate(cols, axis=-1), axis=-1)
    offs = np.cumsum(sizes)
    return [probs[..., int(o) - s:int(o)] for o, s in zip(offs, sizes)]


def axial_rope_tables(n_tokens):
    t = jnp.arange(n_tokens)
    pos = jnp.stack([t // GRID_W, t % GRID_W], axis=-1).astype(jnp.float32)
    half = HEAD_DIM // 2
    inv = ROPE_THETA ** (-jnp.arange(0, half, 2, dtype=jnp.float32) / half)
    ang = pos[:, :, None] * inv
    return jnp.cos(ang), jnp.sin(ang)


def apply_axial_rope(x, cos, sin):
    b_, l_, h_, _ = x.shape
    xs = x.reshape(b_, l_, h_, 2, 2, HEAD_DIM // 4)
    x1, x2 = xs[..., 0, :], xs[..., 1, :]
    cs, sn = cos[None, :, None], sin[None, :, None]
    out = jnp.stack([x1 * cs - x2 * sn, x1 * sn + x2 * cs], axis=-2)
    return out.reshape(x.shape).astype(x.dtype)


def s5_discretize(a_re, a_im, log_dt, b_re, b_im):
    f32 = jnp.float32
    lam = lax.complex(a_re.astype(f32), a_im.astype(f32))
    dt = jnp.exp(log_dt.astype(f32))[:, None]
    lam_bar = jnp.exp(lam * dt)
    b = lax.complex(b_re.astype(f32), b_im.astype(f32))
    b_bar = ((lam_bar - 1.0) / lam)[..., None] * b
    return lam_bar, b_bar


def s5_scan(u, lam_bar, b_bar, s0):
    bu = lax.complex(jnp.einsum('blgi,gpi->blgp', u, b_bar.real),
                     jnp.einsum('blgi,gpi->blgp', u, b_bar.imag))
    if s0 is not None:
        bu = bu.at[:, 0].add(lam_bar * s0)
    a = jnp.broadcast_to(lam_bar, (1, u.shape[1]) + lam_bar.shape)

    def combine(left, right):
        a_l, b_l = left
        a_r, b_r = right
        return a_l * a_r, a_r * b_l + b_r

    _, states = lax.associative_scan(combine, (a, bu), axis=1)
    return states


def s5_readout(states, c_re, c_im):
    return (jnp.einsum('blgp,gip->blgi', states.real, c_re)
            - jnp.einsum('blgp,gip->blgi', states.imag, c_im))


def s5_mixer(u, uc, p, ctx_out):
    f32 = jnp.float32
    b_, l_, _ = u.shape
    n_c = uc.shape[1]
    ul = u.astype(f32).reshape(b_, l_, SSM_GROUPS, SSM_GROUP)
    ucg = uc.astype(f32).reshape(b_, n_c, SSM_GROUPS, SSM_GROUP)
    d = p['ssm_d'].astype(f32).reshape(SSM_GROUPS, SSM_GROUP)
    y = d * ul
    yc = d * ucg if ctx_out else None
    for direction in range(2):
        lam_bar, b_bar = s5_discretize(p['ssm_a_re'][direction], p['ssm_a_im'][direction],
                                       p['ssm_log_dt'][direction], p['ssm_b_re'][direction], p['ssm_b_im'][direction])
        c_re = p['ssm_c_re'][direction].astype(f32)
        c_im = p['ssm_c_im'][direction].astype(f32)
        rev = (lambda t: jnp.flip(t, axis=1)) if direction == 1 else (lambda t: t)
        st_c = s5_scan(rev(ucg), lam_bar, b_bar, None)
        st_l = s5_scan(rev(ul), lam_bar, b_bar, st_c[:, -1])
        y = y + rev(s5_readout(st_l, c_re, c_im))
        if ctx_out:
            yc = yc + rev(s5_readout(st_c, c_re, c_im))
    w_glu = p['ssm_w_glu']

    def glu(t):
        t = jax.nn.gelu(t.reshape(t.shape[:2] + (SSM_WIDTH,)))
        return (t * jax.nn.sigmoid(t @ w_glu)).astype(u.dtype)

    return glu(y), (glu(yc) if ctx_out else None)


def window_gqa_latent(q, k, v, kc, vc, sink):
    b_, l_, h_, dh = q.shape
    grp = h_ // GQA_KV_HEADS
    nb = l_ // Q_BLOCK
    span = Q_BLOCK + 2 * WINDOW
    scale = dh ** -0.5
    pad = ((0, 0), (WINDOW, WINDOW), (0, 0), (0, 0))
    kp, vp = jnp.pad(k, pad), jnp.pad(v, pad)
    qb = q.reshape(b_, nb, Q_BLOCK, GQA_KV_HEADS, grp, dh).transpose(1, 0, 2, 3, 4, 5)
    sink_l = sink.astype(jnp.float32).reshape(GQA_KV_HEADS, grp)[None, :, :, None, None]

    def block(args):
        i, qi = args
        start = i * Q_BLOCK
        kb = lax.dynamic_slice_in_dim(kp, start, span, axis=1)
        vb = lax.dynamic_slice_in_dim(vp, start, span, axis=1)
        qpos = start + jnp.arange(Q_BLOCK)
        kpos = start - WINDOW + jnp.arange(span)
        valid = ((jnp.abs(qpos[:, None] - kpos[None, :]) <= WINDOW)
                 & (kpos >= 0)[None, :] & (kpos < l_)[None, :])
        s_loc = jnp.einsum('bqkgd,bskd->bkgqs', qi, kb, preferred_element_type=jnp.float32) * scale
        s_loc = jnp.where(valid, s_loc, NEG)
        s_ctx = jnp.einsum('bqkgd,bckd->bkgqc', qi, kc, preferred_element_type=jnp.float32) * scale
        p_loc, p_ctx = joint_softmax([s_loc, s_ctx], sink_l)
        return (jnp.einsum('bkgqs,bskd->bqkgd', p_loc.astype(v.dtype), vb)
                + jnp.einsum('bkgqc,bckd->bqkgd', p_ctx.astype(v.dtype), vc))

    out = lax.map(block, (jnp.arange(nb), qb))
    return out.transpose(1, 0, 2, 3, 4, 5).reshape(b_, l_, h_ * dh)


def neighborhood_attn_latent(q, k, v, kc, vc, rpb):
    b_, l_, h_, dh = q.shape
    rows = l_ // GRID_W
    kh = min(NA_ROWS_MAX, rows)
    kw = NA_COLS
    scale = dh ** -0.5
    qg = q.reshape(b_, rows, GRID_W, h_, dh).transpose(1, 0, 2, 3, 4)
    kg = k.reshape(b_, rows, GRID_W, h_, dh)
    vg = v.reshape(b_, rows, GRID_W, h_, dh)
    cols = np.arange(GRID_W)
    col_start = np.clip(cols - kw // 2, 0, GRID_W - kw)
    col_idx = col_start[:, None] + np.arange(kw)[None, :]
    col_bias_idx = col_idx - cols[:, None] + (kw - 1)
    rpb_c = rpb.astype(jnp.float32)[:, :, col_bias_idx]

    def row_block(args):
        r, qr = args
        rs = jnp.clip(r - kh // 2, 0, rows - kh)
        kband = lax.dynamic_slice_in_dim(kg, rs, kh, axis=1)
        vband = lax.dynamic_slice_in_dim(vg, rs, kh, axis=1)
        k_nb = kband[:, :, col_idx]
        v_nb = vband[:, :, col_idx]
        row_bias_idx = rs + jnp.arange(kh) - r + (NA_ROWS_MAX - 1)
        bias = rpb_c[:, row_bias_idx].transpose(0, 2, 1, 3)
        s_loc = jnp.einsum('bchd,bicjhd->bhcij', qr, k_nb, preferred_element_type=jnp.float32) * scale + bias[None]
        s_loc = s_loc.reshape(b_, h_, GRID_W, kh * kw)
        s_ctx = jnp.einsum('bchd,bkhd->bhck', qr, kc, preferred_element_type=jnp.float32) * scale
        p_loc, p_ctx = joint_softmax([s_loc, s_ctx])
        p_loc = p_loc.reshape(b_, h_, GRID_W, kh, kw)
        return (jnp.einsum('bhcij,bicjhd->bchd', p_loc.astype(v.dtype), v_nb)
                + jnp.einsum('bhck,bkhd->bchd', p_ctx.astype(v.dtype), vc))

    out = lax.map(row_block, (jnp.arange(rows), qg))
    return out.transpose(1, 0, 2, 3, 4).reshape(b_, l_, h_ * dh)


def context_self_attn(q, k, v, sink):
    b_, n_c, h_, dh = q.shape
    hkv = k.shape[2]
    grp = h_ // hkv
    qg = q.reshape(b_, n_c, hkv, grp, dh)
    s = jnp.einsum('bqkgd,bckd->bkgqc', qg, k, preferred_element_type=jnp.float32) * dh ** -0.5
    sk = None if sink is None else sink.astype(jnp.float32).reshape(hkv, grp)[None, :, :, None, None]
    (pr,) = joint_softmax([s], sk)
    o = jnp.einsum('bkgqc,bckd->bqkgd', pr.astype(v.dtype), v)
    return o.reshape(b_, n_c, h_ * dh)


def merge_branches(y_ssm, y_gqa, y_na, gates, p):
    g_s, g_a, g_n = jnp.split(gates, N_BRANCH, axis=-1)
    m = (jax.nn.sigmoid(g_s) * (y_ssm @ p['w_p_ssm'])
         + jax.nn.sigmoid(g_a) * (y_gqa @ p['w_p_gqa'])
         + jax.nn.sigmoid(g_n) * (y_na @ p['w_p_na']))
    return m @ p['w_out']


def ffn_half(h_stream, m, gain, w):
    h = modulate(rmsnorm(h_stream, gain), m[0], m[1])
    return h_stream + 0.5 * m[2] * swiglu(h, *w)


def trunk_layer(x, xc, p, cos, sin, ctx_out):
    ml = jnp.split(p['mod_lat'], N_MOD, axis=-1)
    mc = jnp.split(p['mod_ctx'], N_MOD, axis=-1)
    g = p['norm_g']
    x = ffn_half(x, ml[0:3], g[0], p['ffn1'])
    xc = ffn_half(xc, mc[0:3], g[0], p['ffn1'])
    h = modulate(rmsnorm(x, g[1]), ml[3], ml[4])
    hc = modulate(rmsnorm(xc, g[1]), mc[3], mc[4])
    u, gq, gk, gv, nq, nk, nv, gates = split_in(h @ p['w_in'])
    uc, gqc, gkc, gvc, nqc, nkc, nvc, gates_c = split_in(hc @ p['w_in'])
    y_ssm, y_ssm_c = s5_mixer(u, uc, p, ctx_out)
    kc_a, vc_a = to_heads(gkc, GQA_KV_HEADS), to_heads(gvc, GQA_KV_HEADS)
    y_gqa = window_gqa_latent(apply_axial_rope(to_heads(gq, GQA_HEADS), cos, sin),
                              apply_axial_rope(to_heads(gk, GQA_KV_HEADS), cos, sin),
                              to_heads(gv, GQA_KV_HEADS), kc_a, vc_a, p['gqa_sink'])
    kc_n, vc_n = to_heads(nkc, NA_HEADS), to_heads(nvc, NA_HEADS)
    y_na = neighborhood_attn_latent(to_heads(nq, NA_HEADS), to_heads(nk, NA_HEADS), to_heads(nv, NA_HEADS),
                                    kc_n, vc_n, p['na_rpb'])
    x = x + ml[5] * merge_branches(y_ssm, y_gqa, y_na, gates, p)
    if ctx_out:
        y_gqa_c = context_self_attn(to_heads(gqc, GQA_HEADS), kc_a, vc_a, p['gqa_sink'])
        y_na_c = context_self_attn(to_heads(nqc, NA_HEADS), kc_n, vc_n, None)
        xc = xc + mc[5] * merge_branches(y_ssm_c, y_gqa_c, y_na_c, gates_c, p)
    x = ffn_half(x, ml[6:9], g[2], p['ffn2'])
    if ctx_out:
        xc = ffn_half(xc, mc[6:9], g[2], p['ffn2'])
    return x, xc


def setup_inputs(seed: int = 0) -> dict:
    key = jax.random.key(seed)
    ks = iter(jax.random.split(key, 40))
    f32 = jnp.float32

    def nrm(shape, s):
        return jax.random.normal(next(ks), shape, f32) * s

    D, F, G, P, I = D_MODEL, D_FF, SSM_GROUPS, SSM_STATE, SSM_GROUP
    return {
        'x': nrm((BATCH, SEQ, D), 1.0),
        'c': nrm((BATCH, D), 1.0),
        'ctx': nrm((BATCH, CTX_LEN, D), 1.0),
        'c_ctx': nrm((D,), 1.0),
        'w_ada': nrm((DEPTH, D, N_MOD * D), 0.5 * D ** -0.5),
        'b_ada': nrm((DEPTH, N_MOD * D), 0.02),
        'norm_g': 1.0 + nrm((DEPTH, 3, D), 0.02),
        'ffn1_wg': nrm((DEPTH, D, F), D ** -0.5),
        'ffn1_wu': nrm((DEPTH, D, F), D ** -0.5),
        'ffn1_wd': nrm((DEPTH, F, D), F ** -0.5),
        'ffn2_wg': nrm((DEPTH, D, F), D ** -0.5),
        'ffn2_wu': nrm((DEPTH, D, F), D ** -0.5),
        'ffn2_wd': nrm((DEPTH, F, D), F ** -0.5),
        'w_in': nrm((DEPTH, D, N_IN), D ** -0.5),
        'ssm_a_re': -0.5 + nrm((DEPTH, 2, G, P), 0.01),
        'ssm_a_im': math.pi * jnp.arange(P, dtype=f32) + nrm((DEPTH, 2, G, P), 0.01),
        'ssm_log_dt': jax.random.uniform(next(ks), (DEPTH, 2, G), f32, math.log(DT_MIN), math.log(DT_MAX)),
        'ssm_b_re': nrm((DEPTH, 2, G, P, I), (2 * I) ** -0.5),
        'ssm_b_im': nrm((DEPTH, 2, G, P, I), (2 * I) ** -0.5),
        'ssm_c_re': nrm((DEPTH, 2, G, I, P), P ** -0.5),
        'ssm_c_im': nrm((DEPTH, 2, G, I, P), P ** -0.5),
        'ssm_d': nrm((DEPTH, SSM_WIDTH), 1.0),
        'ssm_w_glu': nrm((DEPTH, SSM_WIDTH, SSM_WIDTH), SSM_WIDTH ** -0.5),
        'gqa_sink': nrm((DEPTH, GQA_HEADS), 0.5),
        'na_rpb': nrm((DEPTH, NA_HEADS, 2 * NA_ROWS_MAX - 1, 2 * NA_COLS - 1), 0.1),
        'w_p_ssm': nrm((DEPTH, SSM_WIDTH, D), SSM_WIDTH ** -0.5),
        'w_p_gqa': nrm((DEPTH, GQA_WIDTH, D), GQA_WIDTH ** -0.5),
        'w_p_na': nrm((DEPTH, NA_WIDTH, D), NA_WIDTH ** -0.5),
        'w_out': nrm((DEPTH, D, D), D ** -0.5),
        'final_g': 1.0 + nrm((D,), 0.02),
    }


def reference(x, c, ctx, c_ctx, w_ada, b_ada, norm_g, ffn1_wg, ffn1_wu, ffn1_wd, ffn2_wg, ffn2_wu, ffn2_wd,
              w_in, ssm_a_re, ssm_a_im, ssm_log_dt, ssm_b_re, ssm_b_im, ssm_c_re, ssm_c_im, ssm_d, ssm_w_glu,
              gqa_sink, na_rpb, w_p_ssm, w_p_gqa, w_p_na, w_out, final_g):
    cos, sin = axial_rope_tables(x.shape[1])
    xc = ctx
    s_lat = jax.nn.silu(c)
    s_ctx = jax.nn.silu(c_ctx)
    for l in range(DEPTH):
        p = {
            'mod_lat': (s_lat @ w_ada[l] + b_ada[l])[:, None, :],
            'mod_ctx': (s_ctx @ w_ada[l] + b_ada[l])[None, None, :],
            'norm_g': norm_g[l],
            'ffn1': (ffn1_wg[l], ffn1_wu[l], ffn1_wd[l]),
            'ffn2': (ffn2_wg[l], ffn2_wu[l], ffn2_wd[l]),
            'w_in': w_in[l],
            'ssm_a_re': ssm_a_re[l], 'ssm_a_im': ssm_a_im[l], 'ssm_log_dt': ssm_log_dt[l],
            'ssm_b_re': ssm_b_re[l], 'ssm_b_im': ssm_b_im[l],
            'ssm_c_re': ssm_c_re[l], 'ssm_c_im': ssm_c_im[l],
            'ssm_d': ssm_d[l], 'ssm_w_glu': ssm_w_glu[l],
            'gqa_sink': gqa_sink[l], 'na_rpb': na_rpb[l],
            'w_p_ssm': w_p_ssm[l], 'w_p_gqa': w_p_gqa[l], 'w_p_na': w_p_na[l], 'w_out': w_out[l],
        }
        x, xc = trunk_layer(x, xc, p, cos, sin, l < DEPTH - 1)
    return rmsnorm(x, final_g)
```

```python
import contextlib, math, os
import numpy as np
import ml_dtypes
import concourse.bass as bass
import concourse.mybir as mybir
from concourse.bass_utils import run_bass_kernel_spmd

F32 = mybir.dt.float32; BF16 = mybir.dt.bfloat16; I32 = mybir.dt.int32
AF = mybir.ActivationFunctionType; ALU = mybir.AluOpType

D = 1024; T = 4096; NCX = 256; TT = T + NCX; FF = 2816; KC = 8; FC = 22; DEPTH = 2
NEG = -30000.0
SAME_SYNC = True
PI = math.pi


class Sem:
    def __init__(s, h, name): s.h = h; s.total = 0; s.name = name


class Eng:
    def __init__(s, name, obj, sem): s.name = name; s.obj = obj; s.sem = sem; s.known = {}


class Buf:
    def __init__(s, name, t=None, dsem=None, qsem=None):
        s.name = name; s.t = t; s.w = []; s.r = []; s.pre = []; s.dsem = dsem; s.qsem = qsem

    def __getitem__(s, k): return s.t[k]


class Trk:
    def __init__(self, nc):
        self.nc = nc
        self.es = contextlib.ExitStack()
        self.sems = []
        self.E = {}
        for n, o in (('pe', nc.tensor), ('act', nc.scalar), ('dve', nc.vector), ('pool', nc.gpsimd), ('sp', nc.sync)):
            self.E[n] = Eng(n, o, self.new_sem('e_' + n) if n != 'sp' else None)
        self.dpool = []; self.dnext = 0; self.uid = 0; self.qpool = []; self.qnext = 0

    def new_sem(self, name):
        s = Sem(self.es.enter_context(self.nc.semaphore(name)), name); self.sems.append(s); return s

    def dsem(self):
        if self.dnext >= len(self.dpool): self.dpool.append(self.new_sem('d%d' % len(self.dpool)))
        s = self.dpool[self.dnext]; self.dnext += 1; return s

    def qsem(self):
        if self.qnext >= len(self.qpool): self.qpool.append(self.new_sem('q%d' % len(self.qpool)))
        s = self.qpool[self.qnext]; self.qnext += 1; return s

    def _wait(self, eng, evs):
        need = {}
        for (sem, val, src) in evs:
            if src == eng.name and (src == 'pe' or not SAME_SYNC): continue
            if eng.known.get(sem, 0) >= val: continue
            need[sem] = max(need.get(sem, 0), val)
        for sem, val in need.items():
            eng.obj.wait_ge(sem.h, val); eng.known[sem] = val

    def _pre(self, eng, reads, writes, acc):
        evs = []
        for b in reads: evs += b.w
        for b in writes:
            b.pre = b.w + b.r; evs += b.pre
        for b in acc: evs += b.pre
        self._wait(eng, evs)

    def _post(self, ev, reads, writes, acc):
        for b in reads: b.r.append(ev)
        for b in writes: b.w = [ev]; b.r = []
        for b in acc: b.w.append(ev)

    def group(self, en, fns, reads=(), writes=(), acc=()):
        eng = self.E[en]
        self._pre(eng, reads, writes, acc)
        ins = None
        for f in fns: ins = f()
        eng.sem.total += 1
        ins.then_inc(eng.sem.h, 1)
        self._post((eng.sem, eng.sem.total, en), reads, writes, acc)

    def op(self, en, fn, reads=(), writes=(), acc=()):
        self.group(en, [fn], reads, writes, acc)

    def dma(self, q, out, in_, reads=(), writes=(), acc=(), sem=None):
        eng = self.E[q]
        self._pre(eng, reads, writes, acc)
        if sem is None:
            for b in list(writes) + list(acc) + list(reads):
                if b.dsem is not None: sem = (b.qsem if q == 'pool' else b.dsem); break
        ins = eng.obj.dma_start(out=out, in_=in_)
        sem.total += 16
        ins.then_inc(sem.h, 16)
        self._post((sem, sem.total, 'dma'), reads, writes, acc)

    def barrier(self):
        for eng in self.E.values():
            for s in self.sems:
                if s.total > 0 and eng.known.get(s, 0) < s.total:
                    eng.obj.wait_ge(s.h, s.total); eng.known[s] = s.total


def tile_w(W, kp=128):
    K, M = W.shape
    return np.ascontiguousarray(W.reshape(K // kp, kp, M // 128, 128).transpose(2, 1, 0, 3))


IN_OFF = dict(u=0, gq=384, gk=896, gv=1024, nq=1152, nk=1664, nv=2176, gates=2688)
ROT_PERM = np.concatenate([np.arange(16, 32), np.arange(0, 16), np.arange(48, 64), np.arange(32, 48)])
FM_CHUNKS = ([('u', i) for i in range(3)] + [('gq', i) for i in range(4)] + [('gq2', i) for i in range(4)]
             + [('gk', 0), ('gk2', 0)] + [('nq', i) for i in range(4)] + [('nk', i) for i in range(4)]
             + [('gates', i) for i in range(24)])


def win_fm_cols():
    cols = []
    for kind, i in FM_CHUNKS:
        if kind == 'u': c = IN_OFF['u'] + i * 128 + np.arange(128)
        elif kind == 'gq': c = IN_OFF['gq'] + i * 128 + np.arange(128)
        elif kind == 'gq2': c = IN_OFF['gq'] + i * 128 + np.concatenate([ROT_PERM, 64 + ROT_PERM])
        elif kind == 'gk': c = IN_OFF['gk'] + np.arange(128)
        elif kind == 'gk2': c = IN_OFF['gk'] + np.concatenate([ROT_PERM, 64 + ROT_PERM])
        elif kind == 'nq': c = IN_OFF['nq'] + i * 128 + np.arange(128)
        elif kind == 'nk': c = IN_OFF['nk'] + i * 128 + np.arange(128)
        else: c = IN_OFF['gates'] + i * 128 + np.arange(128)
        cols.append(c)
    return np.concatenate(cols)


def rope_tables():
    t = np.arange(T)
    pos = np.stack([t // 64, t % 64], 0).astype(np.float32)
    inv = (10000.0 ** (-np.arange(0, 32, 2, dtype=np.float32) / 32)).astype(np.float32)
    C = np.zeros((64, T), np.float32); S = np.zeros((64, T), np.float32)
    for ax in range(2):
        ang = (pos[ax][None, :] * inv[:, None]).astype(np.float32)
        for half in range(2):
            sl = slice(ax * 32 + half * 16, ax * 32 + half * 16 + 16)
            C[sl] = np.cos(ang)
            S[sl] = -np.sin(ang) if half == 0 else np.sin(ang)
    C2 = np.concatenate([C, C], 0); S2 = np.concatenate([S, S], 0)
    return np.stack([C2 * 0.125, S2 * 0.125, C2, S2], 0).astype(np.float32)


def na_bias_tables(rpb):
    L = rpb.shape[0]
    kc = np.arange(64)[:, None]; qc = np.arange(64)[None, :]
    cs = np.clip(qc - 8, 0, 48)
    valid = (kc >= cs) & (kc < cs + 16)
    idx = np.clip(kc - qc + 15, 0, 30)
    tab = np.where(valid[None, None, None], rpb[:, :, :, idx], np.float32(NEG)).astype(np.float32)
    negt = np.full((L, 8, 64, 64), NEG, np.float32)
    VT = np.zeros((L, 8, 128, 14, 64), np.float32)
    for d in range(-7, 7):
        VT[:, :, 0:64, d + 7] = tab[:, :, d + 7]
        VT[:, :, 64:128, d + 7] = tab[:, :, d + 8]
    OD = np.zeros((L, 8, 128, 5, 64), np.float32)
    for k, d in enumerate((-5, -3, -1, 1, 3)):
        OD[:, :, 0:64, k] = negt if d == -5 else tab[:, :, d + 7]
        OD[:, :, 64:128, k] = negt if d == 3 else tab[:, :, d + 8]
    CB = np.zeros((L, 8, 128, 5, 2, 64), np.float32)
    for k in range(5):
        CB[:, :, :, k, 0, :] = VT[:, :, :, 3 + 2 * k, :] if k < 4 else np.float32(NEG)
        CB[:, :, :, k, 1, :] = OD[:, :, :, k, :]
    return VT, OD, CB


def host_consts():
    c = {}
    c['ident'] = np.eye(128, dtype=np.float32)
    c['rope'] = rope_tables()
    k = np.arange(128)[:, None]; q = np.arange(128)[None, :]
    c['gmask'] = np.stack([np.where(k >= q, 0.0, NEG), np.where(k <= q, 0.0, NEG)], 1).astype(np.float32)
    p = np.arange(128)
    m2 = np.zeros((128, 4, 8, 16), np.float32)
    mc = np.zeros((128, 4, 2, 64), np.float32)
    for qq in range(4):
        for pp in range(128):
            m2[pp, qq, 2 * qq + (pp >= 64), :] = 1.0
            for half in range(2):
                if pp // 16 == 2 * qq + half: mc[pp, qq, half, :] = 1.0
    c['mask2'] = m2; c['maskc'] = mc
    io = np.zeros((128, 2, 64), np.float32)
    io[:, 0, :] = np.arange(1, 65)[None, :]; io[:, 1, :] = np.arange(64, 0, -1)[None, :]
    c['iota64'] = io
    c['iota9'] = np.broadcast_to(np.arange(9, dtype=np.float32)[None, :], (128, 9)).copy()
    return c


def build(dbg=(), stop_after=None, only=None, ext_in=()):
    nc = bass.Bass("TRN2", target_bir_lowering=False)
    tr = Trk(nc)
    dbg = set(dbg)

    def din(name, shape, dt=F32):
        return nc.dram_tensor(name, list(shape), dt, kind="ExternalInput").ap()

    def dscr(name, shape, dt):
        if name in ext_in: return nc.dram_tensor(name, list(shape), dt, kind="ExternalInput").ap()
        if name in dbg: return nc.dram_tensor(name, list(shape), dt, kind="ExternalOutput").ap()
        return nc.dram_tensor(name, list(shape), dt).ap()

    x_in = din('x', [T, D]); ctx_in = din('ctx', [NCX, D]); svec_in = din('svec', [128, KC, 2])
    out_d = nc.dram_tensor('out', [T, D], F32, kind="ExternalOutput").ap()
    WSH = dict(wgu1=[FC, 128, 2, KC, 128], wd1=[KC, 128, FC, 128], wgu2=[FC, 128, 2, KC, 128], wd2=[KC, 128, FC, 128],
               winfm=[45, 128, KC, 128], wintm=[128, KC, 640], wada=[72, 128, KC, 128], wglu=[3, 128, 3, 128],
               wps=[8, 128, 3, 128], wpg=[8, 64, 8, 128], wpn=[8, 64, 8, 128], wout=[8, 128, KC, 128])
    WORDER = ['wada', 'wgu1', 'wd1', 'winfm', 'wintm', 'wglu', 'wps', 'wpg', 'wpn', 'wout', 'wgu2', 'wd2']
    w_f = {k: din(k, [DEPTH] + v) for k, v in WSH.items()}
    w_b = {(k, l): dscr('%s_b%d' % (k, l), v, BF16) for k, v in WSH.items() for l in range(DEPTH)}
    w_buf = {(k, l): Buf('wb_%s%d' % (k, l)) for k in WSH for l in range(DEPTH)}
    bada_in = din('bada', [DEPTH, 128, 72]); normg_in = din('normg', [DEPTH, 128, 3, KC]); fing_in = din('fing', [128, KC])
    sa_in = din('ssm_a', [DEPTH, 2, 128, 3, 12])
    sb_in = din('ssm_b', [DEPTH, 2, 2, 128, 12, 16]); sc_in = din('ssm_c', [DEPTH, 2, 2, 128, 3, 64])
    sd_in = din('ssm_d', [DEPTH, 128, 3]); sink_in = din('sink', [DEPTH, 64, 8])
    navt_in = din('navt', [DEPTH, 8, 128, 14, 64]); naod_in = din('naod', [DEPTH, 8, 128, 5, 64]); nacb_in = din('nacb', [DEPTH, 8, 128, 5, 128])
    ident_in = din('ident', [128, 128]); rope_in = din('rope', [4, 128, T]); gmask_in = din('gmask', [128, 2, 128])
    mask2_in = din('mask2', [128, 4, 8, 16]); maskc_in = din('maskc', [128, 4, 2, 64]); iota64_in = din('iota64', [128, 2, 64]); iota9_in = din('iota9', [128, 9])

    XT = dscr('XT', [D, TT], F32)
    U = dscr('U', [384, TT], F32); TG = dscr('TG', [384, TT], F32)
    QG = dscr('QG', [8, 64, T], BF16); QGC = dscr('QGC', [8, 64, NCX], BF16)
    KG = dscr('KG', [2, 64, T], BF16); KGC = dscr('KGC', [2, 64, NCX], BF16)
    QN = dscr('QN', [8, 64, T], BF16); QNC = dscr('QNC', [8, 64, NCX], BF16)
    KN = dscr('KN', [8, 64, T], BF16); KNC = dscr('KNC', [8, 64, NCX], BF16)
    VG = dscr('VG', [TT, 128], BF16); VN = dscr('VN', [TT, 512], BF16)
    SG = dscr('SG', [3072, TT], BF16)
    YG = dscr('YG', [8, 64, T], BF16); YGC = dscr('YGC', [8, 64, NCX], BF16)
    YN = dscr('YN', [8, 64, T], BF16); YNC = dscr('YNC', [8, 64, NCX], BF16)

    def uname(n):
        tr.uid += 1; return '%s_%d' % (n, tr.uid)

    class Phase:
        def __init__(s, reset=True):
            s.es = contextlib.ExitStack()
            if reset: tr.dnext = 0; tr.qnext = 0

        def sb(s, name, shape, dt=F32, dma=False):
            t = s.es.enter_context(nc.sbuf_tensor(uname(name), list(shape), dt))
            return Buf(name, t, tr.dsem() if dma else None, tr.qsem() if dma else None)

        def ps(s, name, shape=(128, 512), dt=F32):
            t = s.es.enter_context(nc.psum_tensor(uname(name), list(shape), dt))
            return Buf(name, t)

        def close(s):
            tr.barrier(); s.es.close()

    V = nc.vector; A = nc.scalar; G = nc.gpsimd; PE = nc.tensor

    G1 = ['wada', 'wgu1', 'wd1', 'winfm', 'wintm']
    G2 = ['wglu', 'wps', 'wpg', 'wpn', 'wout', 'wgu2', 'wd2']

    def emit_casts(l, keys):
        if only is not None: return
        for k in keys:
            sem = tr.new_sem('c_%s%d' % (k, l))
            n = int(np.prod(WSH[k]))
            src = w_f[k][l]; dst = w_b[(k, l)]
            names = 'abcde'[:len(WSH[k])]
            pat = ' '.join(names)
            fs = src.rearrange('%s -> (%s)' % (pat, pat)).rearrange('(p n) -> p n', p=128)
            fd = dst.rearrange('%s -> (%s)' % (pat, pat)).rearrange('(p n) -> p n', p=128)
            cols = n // 128
            npieces = max(1, -(-cols // 16384))
            step = -(-cols // npieces)
            first = True
            for c0 in range(0, cols, step):
                c1 = min(cols, c0 + step)
                tr.dma('pool', fd[:, c0:c1], fs[:, c0:c1], writes=[w_buf[(k, l)]] if first else (),
                       acc=() if first else [w_buf[(k, l)]], sem=sem)
                first = False

    emit_casts(0, G1)

    gp = Phase()
    ident = gp.sb('ident', [128, 128], F32, dma=True)
    tr.dma('sp', ident[:], ident_in[:, :], writes=[ident])
    ones_b = gp.sb('ones_b', [128, 128], BF16)
    tr.op('dve', lambda: V.memset(ones_b[:], 1.0), writes=[ones_b])
    svec = gp.sb('svec', [128, KC, 2], F32, dma=True)
    tr.dma('sp', svec[:], svec_in[:, :, :], writes=[svec])
    svb = gp.sb('svb', [128, KC, 2], BF16)
    tr.op('act', lambda: A.activation(out=svb[:], in_=svec[:], func=AF.Silu), reads=[svec], writes=[svb])
    fing = gp.sb('fing', [128, KC], F32, dma=True)
    tr.dma('sp', fing[:], fing_in[:, :], writes=[fing])
    modA = gp.sb('modA', [128, 3, KC, 2]); modB = gp.sb('modB', [128, 3, KC, 2]); modG = gp.sb('modG', [128, 3, KC, 2])

    def compute_mod(l):
        ph = Phase()
        bada = ph.sb('bada', [128, 72], F32, dma=True); tr.dma('sp', bada[:], bada_in[l], writes=[bada])
        normg = ph.sb('normg', [128, 3, KC], F32, dma=True); tr.dma('sp', normg[:], normg_in[l], writes=[normg])
        wr = [ph.sb('wada%d' % i, [128, 8, KC, 128], BF16, dma=True) for i in range(2)]
        mps = ph.ps('mps', [128, 72, 2])
        mod = ph.sb('mod', [128, 72, 2])
        for jg in range(9):
            wbuf = wr[jg % 2]
            tr.dma('sp', wbuf[:], w_b[('wada', l)][jg * 8:(jg + 1) * 8].rearrange('j p k c -> p j k c'),
                   reads=[w_buf[('wada', l)]], writes=[wbuf])
            fns = []
            for jj in range(8):
                j = jg * 8 + jj
                for kc in range(KC):
                    fns.append(lambda j=j, jj=jj, kc=kc: PE.matmul(mps[:, j, :], lhsT=wbuf[:, jj, kc, :], rhs=svb[:, kc, :],
                                                                  start=(kc == 0), stop=(kc == KC - 1)))
            tr.group('pe', fns, reads=[wbuf, svb], writes=[mps] if jg == 0 else (), acc=() if jg == 0 else [mps])
        tr.op('dve', lambda: V.tensor_tensor(out=mod[:], in0=mps[:], in1=bada[:].unsqueeze(2).to_broadcast([128, 72, 2]), op=ALU.add),
              reads=[mps, bada], writes=[mod])
        for i in range(3):
            sh = mod[:, 8 * (3 * i):8 * (3 * i) + 8, :]; sc = mod[:, 8 * (3 * i + 1):8 * (3 * i + 1) + 8, :]
            gt = mod[:, 8 * (3 * i + 2):8 * (3 * i + 2) + 8, :]
            gb = normg[:, i, :].unsqueeze(2).to_broadcast([128, KC, 2])
            fns = [lambda sc=sc, i=i, gb=gb: V.scalar_tensor_tensor(out=modA[:, i], in0=sc, scalar=1.0, in1=gb, op0=ALU.add, op1=ALU.mult),
                   lambda sh=sh, i=i: V.tensor_copy(out=modB[:, i], in_=sh),
                   lambda gt=gt, i=i: V.tensor_scalar(out=modG[:, i], in0=gt, scalar1=(1.0 if i == 1 else 0.5), scalar2=None, op0=ALU.mult)]
            tr.group('dve', fns, reads=[mod, normg], writes=[modA, modB, modG] if i == 0 else (), acc=() if i == 0 else [modA, modB, modG])
        ph.close()

    def rms_mod(ph, xT, hT, N, site, col, ps_ss, tmpbig, sq, rstd):
        tr.op('act', lambda: A.activation(out=sq[:, :, :N], in_=xT[:, :, :N], func=AF.Square), reads=[xT], writes=[sq])
        tr.group('pe', [lambda kc=kc: PE.matmul(ps_ss[:, :N], lhsT=ones_b[:], rhs=sq[:, kc, :N], start=(kc == 0), stop=(kc == KC - 1))
                        for kc in range(KC)], reads=[sq, ones_b], writes=[ps_ss])
        tr.op('act', lambda: A.activation(out=rstd[:, :N], in_=ps_ss[:, :N], func=AF.Sqrt, scale=1.0 / D, bias=epsb[:, 0:1]),
              reads=[ps_ss, epsb], writes=[rstd])
        tr.op('dve', lambda: V.reciprocal(out=rstd[:, :N], in_=rstd[:, :N]), reads=[rstd], writes=[rstd])
        tr.op('dve', lambda: V.tensor_tensor(out=tmpbig[:, :, :N], in0=xT[:, :, :N],
                                             in1=rstd[:, :N].unsqueeze(1).to_broadcast([128, KC, N]), op=ALU.mult),
              reads=[xT, rstd], writes=[tmpbig])
        tr.group('act', [lambda kc=kc: A.activation(out=hT[:, kc, :N], in_=tmpbig[:, kc, :N], func=AF.Identity,
                                                    scale=modA[:, site, kc, col:col + 1], bias=modB[:, site, kc, col:col + 1])
                         for kc in range(KC)], reads=[tmpbig, modA, modB], writes=[hT])

    def ffn(ph, l, which, xT, hT, aT, N, col, site, R):
        wgu_k, wd_k = ('wgu1', 'wd1') if which == 1 else ('wgu2', 'wd2')
        rms_mod(ph, xT, hT, N, site, col, R['ps_m'][0], R['tmpbig'], R['sq'], R['rstd'])
        for f in range(FC):
            wb = R['wgu'][R['i_wgu'] % 3]; R['i_wgu'] += 1
            tr.dma('sp', wb[:], w_b[(wgu_k, l)][f], reads=[w_buf[(wgu_k, l)]], writes=[wb])
            pg = R['ps_g'][f % 2]; pu = R['ps_u'][f % 2]
            tr.group('pe', [lambda kc=kc: PE.matmul(pg[:, :N], lhsT=wb[:, 0, kc, :], rhs=hT[:, kc, :N], start=(kc == 0), stop=(kc == KC - 1))
                            for kc in range(KC)], reads=[wb, hT], writes=[pg])
            tr.group('pe', [lambda kc=kc: PE.matmul(pu[:, :N], lhsT=wb[:, 1, kc, :], rhs=hT[:, kc, :N], start=(kc == 0), stop=(kc == KC - 1))
                            for kc in range(KC)], reads=[wb, hT], writes=[pu])
            sg = R['sgt'][f % 2]
            tr.op('act', lambda: A.activation(out=sg[:, :N], in_=pg[:, :N], func=AF.Silu), reads=[pg], writes=[sg])
            tr.op('dve', lambda: V.tensor_tensor(out=aT[:, f, :N], in0=sg[:, :N], in1=pu[:, :N], op=ALU.mult),
                  reads=[sg, pu], writes=[aT] if f == 0 else (), acc=() if f == 0 else [aT])
        for m in range(KC):
            wb = R['wd'][R['i_wd'] % 2]; R['i_wd'] += 1
            tr.dma('sp', wb[:], w_b[(wd_k, l)][m], reads=[w_buf[(wd_k, l)]], writes=[wb])
            pd = R['ps_m'][m % 2]
            tr.group('pe', [lambda f=f: PE.matmul(pd[:, :N], lhsT=wb[:, f, :], rhs=aT[:, f, :N], start=(f == 0), stop=(f == FC - 1))
                            for f in range(FC)], reads=[wb, aT], writes=[pd])
            tr.op('dve', lambda: V.scalar_tensor_tensor(out=xT[:, m, :N], in0=pd[:, :N], scalar=modG[:, site, m, col:col + 1],
                                                        in1=xT[:, m, :N], op0=ALU.mult, op1=ALU.add),
                  reads=[pd, modG, xT], acc=[xT])

    def ffn_bufs(ph):
        R = {}
        R['wgu'] = [ph.sb('wgu%d' % i, [128, 2, KC, 128], BF16, dma=True) for i in range(3)]
        R['wd'] = [ph.sb('wd%d' % i, [128, FC, 128], BF16, dma=True) for i in range(2)]
        R['i_wgu'] = 0; R['i_wd'] = 0
        R['ps_g'] = [ph.ps('psg%d' % i) for i in range(2)]
        R['ps_u'] = [ph.ps('psu%d' % i) for i in range(2)]
        R['ps_m'] = [ph.ps('psm%d' % i) for i in range(2)]
        R['sgt'] = [ph.sb('sgt%d' % i, [128, 512]) for i in range(2)]
        R['tmpbig'] = ph.sb('tmpbig', [128, KC, 512]); R['sq'] = ph.sb('sq', [128, KC, 512], BF16)
        R['rstd'] = ph.sb('rstd', [128, 512])
        return R

    tiles = [(0, NCX, 1)] + [(NCX + i * 512, 512, 0) for i in range(8)]

    def phase_A(l):
        ph = Phase()
        R = ffn_bufs(ph)
        xTs = [ph.sb('xT%d' % i, [128, KC, 512], F32, dma=True) for i in range(2)]
        hT = ph.sb('hT', [128, KC, 512], BF16); aT = ph.sb('aT', [128, FC, 512], BF16)
        win = [ph.sb('win%d' % i, [128, KC, 128], BF16, dma=True) for i in range(3)]
        wtm = ph.sb('wtm', [128, KC, 640], BF16, dma=True)
        tr.dma('sp', wtm[:], w_b[('wintm', l)], reads=[w_buf[('wintm', l)]], writes=[wtm])
        rp = [ph.sb('rope%d' % i, [128, 4, 512], F32, dma=True) for i in range(2)]
        stf = [ph.sb('stf%d' % i, [128, 512], F32, dma=True) for i in range(2)]
        stb = [ph.sb('stb%d' % i, [128, 640], BF16, dma=True) for i in range(3)]
        t1 = ph.sb('t1', [128, 512]); t2 = ph.sb('t2', [128, 512])
        xin = [ph.sb('xin%d' % i, [128, D], F32, dma=True) for i in range(2)] if l == 0 else None
        ps_q = ph.ps('psq'); ps_q2 = ph.ps('psq2')
        cnt = dict(win=0, stf=0, stb=0, xin=0)

        def load_x(ti):
            c0, N, col = tiles[ti]; xT = xTs[ti % 2]
            if l > 0:
                tr.dma('sp', xT[:, :, :N], XT[:, c0:c0 + N].rearrange('(k p) t -> p k t', p=128), writes=[xT])
            else:
                for ts in range(N // 128):
                    xb = xin[cnt['xin'] % 2]; cnt['xin'] += 1
                    src = ctx_in[ts * 128:(ts + 1) * 128, :] if ti == 0 else x_in[c0 - NCX + ts * 128:c0 - NCX + (ts + 1) * 128, :]
                    tr.dma('sp', xb[:], src, writes=[xb])
                    for hf in range(2):
                        pt = R['ps_g'][hf]
                        tr.group('pe', [lambda k=k, hf=hf: PE.transpose(pt[:, k * 128:(k + 1) * 128], xb[:, (hf * 4 + k) * 128:(hf * 4 + k + 1) * 128], ident[:])
                                        for k in range(4)], reads=[xb, ident], writes=[pt])
                        tr.op('act', lambda hf=hf, pt=pt, ts=ts: A.activation(out=xT[:, hf * 4:hf * 4 + 4, ts * 128:(ts + 1) * 128],
                                                                               in_=pt[:].rearrange('p (k t) -> p k t', k=4), func=AF.Copy),
                              reads=[pt], writes=[xT] if (ts == 0 and hf == 0) else (), acc=() if (ts == 0 and hf == 0) else [xT])

        load_x(0)
        for ti in range(9):
            c0, N, col = tiles[ti]; xT = xTs[ti % 2]; lat = ti > 0
            if lat:
                rpb = rp[ti % 2]
                tr.dma('sp', rpb[:], rope_in[:, :, c0 - NCX:c0 - NCX + 512].rearrange('a p t -> p a t'), writes=[rpb])
            ffn(ph, l, 1, xT, hT, aT, N, col, 0, R)
            if ti + 1 < 9: load_x(ti + 1)
            rms_mod(ph, xT, hT, N, 1, col, R['ps_m'][0], R['tmpbig'], R['sq'], R['rstd'])
            tr.dma('pool', XT[:, c0:c0 + N].rearrange('(k p) t -> p k t', p=128), xT[:, :, :N], reads=[xT])
            ci = 0
            while ci < len(FM_CHUNKS):
                kind, i = FM_CHUNKS[ci]

                def wmm(ci, pbuf):
                    wb = win[cnt['win'] % 3]; cnt['win'] += 1
                    tr.dma('sp', wb[:], w_b[('winfm', l)][ci], reads=[w_buf[('winfm', l)]], writes=[wb])
                    tr.group('pe', [lambda kc=kc: PE.matmul(pbuf[:, :N], lhsT=wb[:, kc, :], rhs=hT[:, kc, :N], start=(kc == 0), stop=(kc == KC - 1))
                                    for kc in range(KC)], reads=[wb, hT], writes=[pbuf])

                if kind in ('gq', 'gk') and lat:
                    ci2 = ci + (4 if kind == 'gq' else 1)
                    wmm(ci, ps_q); wmm(ci2, ps_q2)
                    o = 0 if kind == 'gq' else 2
                    st = stb[cnt['stb'] % 3]; cnt['stb'] += 1
                    tr.op('dve', lambda: V.tensor_tensor(out=t1[:], in0=ps_q[:], in1=rpb[:, o, :], op=ALU.mult), reads=[ps_q, rpb], writes=[t1])
                    tr.op('dve', lambda: V.tensor_tensor(out=t2[:], in0=ps_q2[:], in1=rpb[:, o + 1, :], op=ALU.mult), reads=[ps_q2, rpb], writes=[t2])
                    tr.op('pool', lambda: G.tensor_tensor(out=st[:, :512], in0=t1[:], in1=t2[:], op=ALU.add), reads=[t1, t2], writes=[st])
                    dst = QG[2 * i:2 * i + 2] if kind == 'gq' else KG[0:2]
                    tr.dma('pool', dst.rearrange('h d t -> (h d) t')[:, c0 - NCX:c0 - NCX + 512], st[:, :512], reads=[st])
                elif kind in ('gq2', 'gk2'):
                    pass
                else:
                    pb = R['ps_m'][ci % 2]
                    wmm(ci, pb)
                    if kind == 'u':
                        st = stf[cnt['stf'] % 2]; cnt['stf'] += 1
                        tr.op('act', lambda: A.activation(out=st[:, :N], in_=pb[:, :N], func=AF.Copy), reads=[pb], writes=[st])
                        tr.dma('pool', U[i * 128:(i + 1) * 128, c0:c0 + N], st[:, :N], reads=[st])
                    else:
                        st = stb[cnt['stb'] % 3]; cnt['stb'] += 1
                        if kind == 'gates':
                            tr.op('act', lambda: A.activation(out=st[:, :N], in_=pb[:, :N], func=AF.Sigmoid), reads=[pb], writes=[st])
                            tr.dma('pool', SG[i * 128:(i + 1) * 128, c0:c0 + N], st[:, :N], reads=[st])
                        else:
                            sc = 0.125 if kind in ('gq', 'nq') else 1.0
                            tr.op('act', lambda: A.activation(out=st[:, :N], in_=pb[:, :N], func=AF.Copy, scale=sc), reads=[pb], writes=[st])
                            if lat:
                                dst = {'nq': QN, 'nk': KN}[kind][2 * i:2 * i + 2].rearrange('h d t -> (h d) t')[:, c0 - NCX:c0 - NCX + 512]
                            else:
                                dd = {'gq': QGC, 'gk': KGC, 'nq': QNC, 'nk': KNC}[kind]
                                dst = (dd[2 * i:2 * i + 2] if kind != 'gk' else dd[0:2]).rearrange('h d t -> (h d) t')
                            tr.dma('pool', dst, st[:, :N], reads=[st])
                ci += 1
            for ts in range(N // 128):
                pv = R['ps_g'][ts % 2]; pv2 = R['ps_u'][ts % 2]
                tr.group('pe', [lambda kc=kc: PE.matmul(pv[:, :128], lhsT=hT[:, kc, ts * 128:(ts + 1) * 128], rhs=wtm[:, kc, 0:128],
                                                        start=(kc == 0), stop=(kc == KC - 1)) for kc in range(KC)], reads=[hT, wtm], writes=[pv])
                tr.group('pe', [lambda kc=kc: PE.matmul(pv2[:, :512], lhsT=hT[:, kc, ts * 128:(ts + 1) * 128], rhs=wtm[:, kc, 128:640],
                                                        start=(kc == 0), stop=(kc == KC - 1)) for kc in range(KC)], reads=[hT, wtm], writes=[pv2])
                st = stb[cnt['stb'] % 3]; cnt['stb'] += 1
                tr.op('act', lambda: A.activation(out=st[:, 0:128], in_=pv[:, 0:128], func=AF.Copy), reads=[pv], writes=[st])
                tr.op('dve', lambda: V.tensor_copy(out=st[:, 128:640], in_=pv2[:, :512]), reads=[pv2], acc=[st])
                r0 = c0 + ts * 128
                tr.dma('pool', VG[r0:r0 + 128, :], st[:, 0:128], reads=[st])
                tr.dma('pool', VN[r0:r0 + 128, :], st[:, 128:640], reads=[st])
        ph.close()

    def sin_rr(src, srcb, dst, dstb, shift, tA, tAb, tI, tIb):
        if shift:
            tr.op('dve', lambda: V.tensor_scalar(out=tA, in0=src, scalar1=shift, scalar2=None, op0=ALU.add), reads=[srcb], writes=[tAb])
            x, xb = tA, tAb
        else:
            x, xb = src, srcb
        tr.op('dve', lambda: V.tensor_scalar(out=tI, in0=x, scalar1=1.0 / (2 * PI), scalar2=None, op0=ALU.mult), reads=[xb], writes=[tIb])
        tr.op('dve', lambda: V.tensor_copy(out=dst, in_=tI), reads=[tIb], writes=[dstb])
        tr.op('dve', lambda: V.scalar_tensor_tensor(out=dst, in0=dst, scalar=-2 * PI, in1=x, op0=ALU.mult, op1=ALU.add), reads=[dstb, xb], writes=[dstb])
        tr.op('dve', lambda: V.tensor_scalar(out=dst, in0=dst, scalar1=-PI, scalar2=PI, op0=ALU.max, op1=ALU.min), reads=[dstb], writes=[dstb])
        tr.op('act', lambda: A.activation(out=dst, in_=dst, func=AF.Sin), reads=[dstb], writes=[dstb])

    def phase_B(l):
        ph = Phase()
        io64 = ph.sb('io64', [128, 2, 64], F32, dma=True); tr.dma('sp', io64[:], iota64_in[:, :, :], writes=[io64])
        io9 = ph.sb('io9', [128, 9], F32, dma=True); tr.dma('sp', io9[:], iota9_in[:, :], writes=[io9])
        m2 = ph.sb('mask2', [128, 4, 8, 16], F32, dma=True); tr.dma('sp', m2[:], mask2_in[:, :, :, :], writes=[m2])
        mcm = ph.sb('maskc', [128, 4, 2, 64], F32, dma=True); tr.dma('sp', mcm[:], maskc_in[:, :, :, :], writes=[mcm])
        sdt = ph.sb('sd', [128, 3], F32, dma=True); tr.dma('sp', sdt[:], sd_in[l], writes=[sdt])
        identb = ph.sb('identb', [128, 128], BF16)
        tr.op('act', lambda: A.activation(out=identb[:], in_=ident[:], func=AF.Copy), reads=[ident], writes=[identb])
        pst = ph.ps('pst', [128, 128])
        prm = {}
        BP = int(os.environ.get('BPREP', '99'))
        if BP == 1: ph.close(); return
        pers = {}
        for d in range(2):
            pers[d] = dict(w=ph.sb('w%d' % d, [128, 16, 12]), bb=ph.sb('bb%d' % d, [128, 2, 12, 16]), RK=ph.sb('rk%d' % d, [128, 12, 9]),
                           LRk=ph.sb('lrk%d' % d, [128, 12, 9]), LIk=ph.sb('lik%d' % d, [128, 12, 9]),
                           C2=ph.sb('c2_%d' % d, [128, 12, 2, 64]), S2=ph.sb('s2_%d' % d, [128, 12, 2, 64]),
                           Bstg=ph.sb('bstg%d' % d, [128, 12, 2, 128]), Cf=ph.sb('cf%d' % d, [128, 12, 2, 128]),
                           cpad=ph.sb('cpad%d' % d, [128, 12, 2, 128], BF16))
        pp = Phase(reset=False)
        A64 = pp.sb('a64', [128, 12, 64]); TA64 = pp.sb('ta64', [128, 12, 64]); TI64 = pp.sb('ti64', [128, 12, 64], I32)
        C64 = pp.sb('c64', [128, 12, 64]); S64 = pp.sb('s64', [128, 12, 64])
        for d in range(2):
            PD = pers[d]
            sa = pp.sb('sa%d' % d, [128, 3, 12], F32, dma=True); tr.dma('sp', sa[:], sa_in[l, d], writes=[sa])
            sbr = pp.sb('sbr%d' % d, [128, 12, 16], F32, dma=True); tr.dma('sp', sbr[:], sb_in[l, d, 0], writes=[sbr])
            sbi = pp.sb('sbi%d' % d, [128, 12, 16], F32, dma=True); tr.dma('sp', sbi[:], sb_in[l, d, 1], writes=[sbi])
            scr_ = pp.sb('scr%d' % d, [128, 3, 64], F32, dma=True); tr.dma('sp', scr_[:], sc_in[l, d, 0], writes=[scr_])
            sci = pp.sb('sci%d' % d, [128, 3, 64], F32, dma=True); tr.dma('sp', sci[:], sc_in[l, d, 1], writes=[sci])
            w = PD['w']
            ki = pp.sb('ki%d' % d, [128, 12], I32)
            are = sa[:, 0, :]; aim = sa[:, 1, :]; ldt = sa[:, 2, :]
            DT, ARDT, RR, TH, KF, THR, CX, CM, CO, SI, LR, LI = [w[:, i, :] for i in range(12)]
            N1, N2, DEN, T3 = [w[:, 12 + i, :] for i in range(4)]
            tr.op('act', lambda: A.activation(out=DT, in_=ldt, func=AF.Exp), reads=[sa], writes=[w])
            tr.op('dve', lambda: V.tensor_tensor(out=ARDT, in0=are, in1=DT, op=ALU.mult), reads=[w, sa], acc=[w])
            tr.op('act', lambda: A.activation(out=RR, in_=ARDT, func=AF.Exp), reads=[w], acc=[w])
            tr.op('dve', lambda: V.tensor_tensor(out=TH, in0=aim, in1=DT, op=ALU.mult), reads=[w, sa], acc=[w])
            tr.op('dve', lambda: V.tensor_scalar(out=ki[:], in0=TH, scalar1=1.0 / (2 * PI), scalar2=None, op0=ALU.mult), reads=[w], writes=[ki])
            tr.op('dve', lambda: V.tensor_copy(out=KF, in_=ki[:]), reads=[ki], acc=[w])
            tr.op('dve', lambda: V.scalar_tensor_tensor(out=THR, in0=KF, scalar=-2 * PI, in1=TH, op0=ALU.mult, op1=ALU.add), reads=[w], acc=[w])
            tr.op('dve', lambda: V.tensor_scalar(out=THR, in0=THR, scalar1=-PI, scalar2=PI, op0=ALU.max, op1=ALU.min), reads=[w], acc=[w])
            tr.op('dve', lambda: V.tensor_scalar(out=CX, in0=THR, scalar1=PI / 2, scalar2=None, op0=ALU.add), reads=[w], acc=[w])
            tr.op('dve', lambda: V.tensor_scalar(out=CM, in0=CX, scalar1=PI, scalar2=-2 * PI, op0=ALU.is_gt, op1=ALU.mult), reads=[w], acc=[w])
            tr.op('dve', lambda: V.tensor_tensor(out=CX, in0=CX, in1=CM, op=ALU.add), reads=[w], acc=[w])
            tr.op('dve', lambda: V.tensor_scalar(out=CX, in0=CX, scalar1=-PI, scalar2=PI, op0=ALU.max, op1=ALU.min), reads=[w], acc=[w])
            tr.op('act', lambda: A.activation(out=CO, in_=CX, func=AF.Sin), reads=[w], acc=[w])
            tr.op('act', lambda: A.activation(out=SI, in_=THR, func=AF.Sin), reads=[w], acc=[w])
            tr.op('dve', lambda: V.tensor_tensor(out=LR, in0=RR, in1=CO, op=ALU.mult), reads=[w], acc=[w])
            tr.op('dve', lambda: V.tensor_scalar(out=LR, in0=LR, scalar1=-1.0, scalar2=None, op0=ALU.add), reads=[w], acc=[w])
            tr.op('dve', lambda: V.tensor_tensor(out=LI, in0=RR, in1=SI, op=ALU.mult), reads=[w], acc=[w])
            tr.op('dve', lambda: V.tensor_tensor(out=N1, in0=LR, in1=are, op=ALU.mult), reads=[w, sa], acc=[w])
            tr.op('dve', lambda: V.tensor_tensor(out=T3, in0=LI, in1=aim, op=ALU.mult), reads=[w, sa], acc=[w])
            tr.op('dve', lambda: V.tensor_tensor(out=N1, in0=N1, in1=T3, op=ALU.add), reads=[w], acc=[w])
            tr.op('dve', lambda: V.tensor_tensor(out=N2, in0=LI, in1=are, op=ALU.mult), reads=[w, sa], acc=[w])
            tr.op('dve', lambda: V.tensor_tensor(out=T3, in0=LR, in1=aim, op=ALU.mult), reads=[w, sa], acc=[w])
            tr.op('dve', lambda: V.tensor_tensor(out=N2, in0=N2, in1=T3, op=ALU.subtract), reads=[w], acc=[w])
            tr.op('dve', lambda: V.tensor_tensor(out=DEN, in0=are, in1=are, op=ALU.mult), reads=[w, sa], acc=[w])
            tr.op('dve', lambda: V.tensor_tensor(out=T3, in0=aim, in1=aim, op=ALU.mult), reads=[w, sa], acc=[w])
            tr.op('dve', lambda: V.tensor_tensor(out=DEN, in0=DEN, in1=T3, op=ALU.add), reads=[w], acc=[w])
            tr.op('dve', lambda: V.reciprocal(out=DEN, in_=DEN), reads=[w], acc=[w])
            tr.op('dve', lambda: V.tensor_tensor(out=N1, in0=N1, in1=DEN, op=ALU.mult), reads=[w], acc=[w])
            tr.op('dve', lambda: V.tensor_tensor(out=N2, in0=N2, in1=DEN, op=ALU.mult), reads=[w], acc=[w])
            bb = PD['bb']; tb = pp.sb('tb%d' % d, [128, 2, 12, 16])
            cre = N1.unsqueeze(2).to_broadcast([128, 12, 16]); cim = N2.unsqueeze(2).to_broadcast([128, 12, 16])
            tr.op('dve', lambda: V.tensor_tensor(out=bb[:, 0], in0=sbr[:], in1=cre, op=ALU.mult), reads=[w, sbr], writes=[bb])
            tr.op('dve', lambda: V.tensor_tensor(out=tb[:, 0], in0=sbi[:], in1=cim, op=ALU.mult), reads=[w, sbi], writes=[tb])
            tr.op('dve', lambda: V.tensor_tensor(out=bb[:, 0], in0=bb[:, 0], in1=tb[:, 0], op=ALU.subtract), reads=[bb, tb], acc=[bb])
            tr.op('dve', lambda: V.tensor_tensor(out=bb[:, 1], in0=sbi[:], in1=cre, op=ALU.mult), reads=[w, sbi], acc=[bb])
            tr.op('dve', lambda: V.tensor_tensor(out=tb[:, 1], in0=sbr[:], in1=cim, op=ALU.mult), reads=[w, sbr], acc=[tb])
            tr.op('dve', lambda: V.tensor_tensor(out=bb[:, 1], in0=bb[:, 1], in1=tb[:, 1], op=ALU.add), reads=[bb, tb], acc=[bb])
            if BP == 2: pp.close(); ph.close(); return
            ANG = pp.sb('ang9_%d' % d, [128, 12, 9]); TA9 = pp.sb('ta9_%d' % d, [128, 12, 9]); TI9 = pp.sb('ti9_%d' % d, [128, 12, 9], I32)
            COk = pp.sb('cok%d' % d, [128, 12, 9]); SIk = pp.sb('sik%d' % d, [128, 12, 9]); RK = PD['RK']
            LRk = PD['LRk']; LIk = PD['LIk']
            i9b = io9[:].unsqueeze(1).to_broadcast([128, 12, 9])
            tr.op('dve', lambda: V.tensor_tensor(out=ANG[:], in0=THR.unsqueeze(2).to_broadcast([128, 12, 9]), in1=i9b, op=ALU.mult), reads=[w, io9], writes=[ANG])
            sin_rr(ANG[:], ANG, SIk[:], SIk, 0.0, TA9[:], TA9, TI9[:], TI9)
            sin_rr(ANG[:], ANG, COk[:], COk, PI / 2, TA9[:], TA9, TI9[:], TI9)
            tr.op('dve', lambda: V.tensor_tensor(out=RK[:], in0=ARDT.unsqueeze(2).to_broadcast([128, 12, 9]), in1=i9b, op=ALU.mult), reads=[w, io9], writes=[RK])
            tr.op('act', lambda: A.activation(out=RK[:], in_=RK[:], func=AF.Exp), reads=[RK], writes=[RK])
            tr.op('dve', lambda: V.tensor_tensor(out=LRk[:], in0=RK[:], in1=COk[:], op=ALU.mult), reads=[RK, COk], writes=[LRk])
            tr.op('dve', lambda: V.tensor_tensor(out=LIk[:], in0=RK[:], in1=SIk[:], op=ALU.mult), reads=[RK, SIk], writes=[LIk])
            if BP == 3: pp.close(); ph.close(); return
            TH8 = pp.sb('th8_%d' % d, [128, 12]); TA8 = pp.sb('ta8_%d' % d, [128, 12]); TI8 = pp.sb('ti8_%d' % d, [128, 12], I32)
            tr.op('dve', lambda: V.tensor_scalar(out=TA8[:], in0=THR, scalar1=8.0, scalar2=None, op0=ALU.mult), reads=[w], writes=[TA8])
            tr.op('dve', lambda: V.tensor_scalar(out=TI8[:], in0=TA8[:], scalar1=1.0 / (2 * PI), scalar2=None, op0=ALU.mult), reads=[TA8], writes=[TI8])
            tr.op('dve', lambda: V.tensor_copy(out=TH8[:], in_=TI8[:]), reads=[TI8], writes=[TH8])
            tr.op('dve', lambda: V.scalar_tensor_tensor(out=TH8[:], in0=TH8[:], scalar=-2 * PI, in1=TA8[:], op0=ALU.mult, op1=ALU.add), reads=[TH8, TA8], writes=[TH8])
            tr.op('dve', lambda: V.tensor_tensor(out=A64[:], in0=TH8[:].unsqueeze(2).to_broadcast([128, 12, 64]),
                                                 in1=io64[:, d, :].unsqueeze(1).to_broadcast([128, 12, 64]), op=ALU.mult), reads=[TH8, io64], writes=[A64])
            sin_rr(A64[:], A64, S64[:], S64, 0.0, TA64[:], TA64, TI64[:], TI64)
            sin_rr(A64[:], A64, C64[:], C64, PI / 2, TA64[:], TA64, TI64[:], TI64)
            if BP == 4: pp.close(); ph.close(); return
            C2 = PD['C2']; S2 = PD['S2']
            tr.group('act', [lambda: A.activation(out=C2[:, :, 0, :], in_=C64[:], func=AF.Copy),
                             lambda: A.activation(out=C2[:, :, 1, :], in_=C64[:], func=AF.Copy)], reads=[C64], writes=[C2])
            tr.group('act', [lambda: A.activation(out=S2[:, :, 0, :], in_=S64[:], func=AF.Copy),
                             lambda: A.activation(out=S2[:, :, 1, :], in_=S64[:], func=AF.Copy, scale=-1.0)], reads=[S64], writes=[S2])
            if BP == 5: pp.close(); ph.close(); return
            Bstg = PD['Bstg']; Cf = PD['Cf']
            cpad = PD['cpad']
            stg = [pp.sb('stg%d_%d' % (d, i), [128, 128]) for i in range(2)]
            n = 0
            for s in range(12):
                for ri in range(2):
                    tr.op('dve', lambda: V.tensor_tensor(out=Bstg[:, s, ri, :].rearrange('p (a b) -> p a b', a=8),
                                                         in0=bb[:, ri, s, :].unsqueeze(1).to_broadcast([128, 8, 16]),
                                                         in1=m2[:, s % 4], op=ALU.mult), reads=[bb, m2],
                          writes=[Bstg] if (s == 0 and ri == 0) else (), acc=() if (s == 0 and ri == 0) else [Bstg])
                    sg_ = stg[n % 2]; n += 1
                    csrc = (scr_ if ri == 0 else sci)
                    tr.op('dve', lambda: V.tensor_tensor(out=sg_[:].rearrange('p (a b) -> p a b', a=2),
                                                         in0=csrc[:, s // 4, :].unsqueeze(1).to_broadcast([128, 2, 64]),
                                                         in1=mcm[:, s % 4], op=ALU.mult), reads=[csrc, mcm], writes=[sg_])
                    tr.op('pe', lambda: PE.transpose(pst[:], sg_[:], ident[:]), reads=[sg_, ident], writes=[pst])
                    first = (s == 0 and ri == 0)
                    tr.op('act', lambda: A.activation(out=Cf[:, s, ri, :], in_=pst[:], func=AF.Copy), reads=[pst],
                          writes=[Cf] if first else (), acc=() if first else [Cf])
                    tr.op('act', lambda: A.activation(out=cpad[:, s, ri, :], in_=pst[:], func=AF.Copy, scale=(1.0 if ri == 0 else -1.0)),
                          reads=[pst], writes=[cpad] if first else (), acc=() if first else [cpad])
            prm[d] = dict(w=w, LRk=LRk, LIk=LIk, RK=RK, C2=C2, S2=S2, Bstg=Bstg, Cf=Cf, cpad=cpad)

        pp.close()
        BCUT = int(os.environ.get('BCUT', '99'))
        if BCUT == 0: ph.close(); return
        yacc = ph.sb('yacc', [128, TT], F32, dma=True); ub = ph.sb('ub', [128, TT], BF16)
        XH = ph.sb('XH', [128, 4, 2, 8, 128], BF16)
        XT = ph.sb('XTt', [128, 4, 2, 8, 128], BF16)
        Kb = ph.sb('Kb', [128, 8, 128], BF16)
        tq = [ph.sb('tq%d' % i, [128, 8, 128]) for i in range(2)]
        pxt = ph.ps('pxt', [128, 8, 128], BF16)
        pk = [ph.ps('pk%d' % i, [128, 4, 128]) for i in range(2)]
        pv = [ph.ps('pv%d' % i, [128, 4, 2, 64]) for i in range(2)]
        po = ph.ps('po', [128, 8, 64])
        t1 = ph.sb('bt1', [128, 4, 2, 64]); t2 = ph.sb('bt2', [128, 4, 2, 64]); Wt = ph.sb('bW', [128, 4, 2, 64]); Zt = ph.sb('bZ', [128, 4, 2, 64])
        Sf = [ph.sb('bSf%d' % i, [128, 4, 2, 64]) for i in range(2)]
        Sp = [ph.sb('bSp%d' % i, [128, 4, 2, 64], BF16) for i in range(2)]
        segs = [(0, 32)] + [(NCX + i * 512, 64) for i in range(8)]
        for j3 in range(3):
            tr.dma('sp', yacc[:], U[j3 * 128:(j3 + 1) * 128, :], writes=[yacc])
            tr.op('act', lambda: A.activation(out=ub[:], in_=yacc[:], func=AF.Copy), reads=[yacc], writes=[ub])
            tr.op('dve', lambda: V.tensor_scalar(out=yacc[:], in0=yacc[:], scalar1=sdt[:, j3:j3 + 1], scalar2=None, op0=ALU.mult), reads=[yacc, sdt], writes=[yacc])
            for d in range(2):
                P = prm[d]

                def scaled(dstbuf, src, k0, neg_im):
                    for s4 in range(4):
                        s = 4 * j3 + s4
                        sre = src[:, s, 0, :].unsqueeze(1).to_broadcast([128, 8, 128]); sim = src[:, s, 1, :].unsqueeze(1).to_broadcast([128, 8, 128])
                        lr = P['LRk'][:, s, k0:k0 + 8].unsqueeze(2).to_broadcast([128, 8, 128])
                        li = P['LIk'][:, s, k0:k0 + 8].unsqueeze(2).to_broadcast([128, 8, 128])
                        first = (s4 == 0)
                        tr.op('dve', lambda: V.tensor_tensor(out=tq[0][:], in0=sre, in1=lr, op=ALU.mult), reads=[src_b, P['LRk']], writes=[tq[0]])
                        tr.op('dve', lambda: V.tensor_tensor(out=tq[1][:], in0=sim, in1=li, op=ALU.mult), reads=[src_b, P['LIk']], writes=[tq[1]])
                        tr.op('dve', lambda: V.tensor_tensor(out=dstbuf[:, s4, 0], in0=tq[0][:], in1=tq[1][:], op=ALU.subtract), reads=[tq[0], tq[1]],
                              writes=[dstbuf] if first else (), acc=() if first else [dstbuf])
                        tr.op('dve', lambda: V.tensor_tensor(out=tq[0][:], in0=sre, in1=li, op=ALU.mult), reads=[src_b, P['LIk']], writes=[tq[0]])
                        tr.op('dve', lambda: V.tensor_tensor(out=tq[1][:], in0=sim, in1=lr, op=ALU.mult), reads=[src_b, P['LRk']], writes=[tq[1]])
                        if neg_im:
                            tr.op('dve', lambda: V.scalar_tensor_tensor(out=dstbuf[:, s4, 1], in0=tq[0][:], scalar=-1.0, in1=tq[1][:], op0=ALU.mult, op1=ALU.subtract),
                                  reads=[tq[0], tq[1]], acc=[dstbuf])
                        else:
                            tr.op('dve', lambda: V.tensor_tensor(out=dstbuf[:, s4, 1], in0=tq[0][:], in1=tq[1][:], op=ALU.add), reads=[tq[0], tq[1]], acc=[dstbuf])

                src_b = P['Bstg']
                scaled(XH, P['Bstg'], 0, False)
                if BCUT == 1: ph.close(); return
                for half in range(2):
                    pkk = pk[half]
                    fns = []
                    for tt in range(4):
                        tau = half * 4 + tt
                        k = 0
                        for s4 in range(4):
                            for ri in range(2):
                                fns.append(lambda tt=tt, tau=tau, s4=s4, ri=ri, k=k: PE.matmul(pkk[:, tt, :], lhsT=XH[:, s4, ri, tau, :], rhs=P['cpad'][:, 4 * j3 + s4, ri, :],
                                                                                                 start=(k == 0), stop=(k == 7)))
                                k += 1
                    tr.group('pe', fns, reads=[XH, P['cpad']], writes=[pkk])
                    tr.op('act', lambda: A.activation(out=Kb[:, half * 4:half * 4 + 4, :], in_=pkk[:], func=AF.Copy), reads=[pkk],
                          writes=[Kb] if half == 0 else (), acc=() if half == 0 else [Kb])
                if BCUT == 2: ph.close(); return
                for s4 in range(4):
                    for ri in range(2):
                        tr.group('pe', [lambda tau=tau: PE.transpose(pxt[:, tau, :], XH[:, s4, ri, tau, :], identb[:]) for tau in range(8)],
                                 reads=[XH, identb], writes=[pxt])
                        first = (s4 == 0 and ri == 0)
                        tr.op('act' if ri == 0 else 'dve',
                              (lambda: A.activation(out=XT[:, s4, ri], in_=pxt[:], func=AF.Copy)) if ri == 0 else (lambda: V.tensor_copy(out=XT[:, s4, ri], in_=pxt[:])),
                              reads=[pxt], writes=[XT] if first else (), acc=() if first else [XT])
                if BCUT == 3: ph.close(); return
                src_b = P['Cf']
                scaled(XH, P['Cf'], 1, True)
                order = list(range(9)) if d == 0 else [0] + list(range(8, 0, -1))
                prev = None

                def vmm(oi):
                    t0, nC = segs[order[oi]]
                    pvv = pv[oi % 2]
                    fns = []
                    for s4 in range(4):
                        for ri in range(2):
                            for j in range(8):
                                tau = (7 - j) if d == 0 else j
                                fns.append(lambda s4=s4, ri=ri, j=j, tau=tau: PE.matmul(pvv[:, s4, ri, :nC], lhsT=XT[:, s4, ri, tau, :],
                                                                                         rhs=ub[:, t0 + j:t0 + j + 8 * (nC - 1) + 1:8], start=(j == 0), stop=(j == 7)))
                    tr.group('pe', fns, reads=[XT, ub], writes=[pvv])

                if BCUT == 4: ph.close(); return
                vmm(0)
                for oi in range(9):
                    t0, nC = segs[order[oi]]
                    par = oi % 2
                    pvv = pv[par]
                    s0 = 4 * j3
                    if d == 0: csl = slice(0, nC)
                    else: csl = slice(64 - nC, 64)
                    C2v = P['C2'][:, s0:s0 + 4, :, csl]; S2v = P['S2'][:, s0:s0 + 4, :, csl]
                    tr.op('dve', lambda: V.tensor_tensor(out=t1[:, :, :, :nC], in0=pvv[:, :, :, :nC], in1=C2v, op=ALU.mult), reads=[pvv, P['C2']], writes=[t1])
                    tr.op('dve', lambda: V.tensor_tensor(out=t2[:, :, :, :nC], in0=pvv[:, :, ::-1, :nC], in1=S2v, op=ALU.mult), reads=[pvv, P['S2']], writes=[t2])
                    tr.op('dve', lambda: V.tensor_tensor(out=Wt[:, :, :, :nC], in0=t1[:, :, :, :nC], in1=t2[:, :, :, :nC], op=ALU.add), reads=[t1, t2], writes=[Wt])
                    if BCUT == 5: ph.close(); return
                    if oi + 1 < 9: vmm(oi + 1)
                    fns = []
                    for s4 in range(4):
                        rr = P['RK'][:, s0 + s4, 8:9].to_broadcast([128, nC])
                        for ri in range(2):
                            if prev is None: ini = 0.0
                            else:
                                pcol = (prev[1] - 1) if d == 0 else 0
                                ini = Sf[1 - par][:, s4, ri, pcol:pcol + 1]
                            if d == 0:
                                fns.append(lambda s4=s4, ri=ri, rr=rr, ini=ini: V.tensor_tensor_scan(out=Zt[:, s4, ri, :nC], data0=rr, data1=Wt[:, s4, ri, :nC],
                                                                                                     initial=ini, op0=ALU.mult, op1=ALU.add))
                            else:
                                fns.append(lambda s4=s4, ri=ri, rr=rr, ini=ini: V.tensor_tensor_scan(out=Zt[:, s4, ri, :nC][:, ::-1], data0=rr, data1=Wt[:, s4, ri, :nC][:, ::-1],
                                                                                                     initial=ini, op0=ALU.mult, op1=ALU.add))
                    tr.group('dve', fns, reads=[Wt, P['RK']] + ([Sf[1 - par]] if prev is not None else []), writes=[Zt])
                    if BCUT == 6: ph.close(); return
                    tr.op('dve', lambda: V.tensor_tensor(out=t1[:, :, :, :nC], in0=Zt[:, :, :, :nC], in1=C2v, op=ALU.mult), reads=[Zt, P['C2']], writes=[t1])
                    tr.op('dve', lambda: V.tensor_tensor(out=t2[:, :, :, :nC], in0=Zt[:, :, ::-1, :nC], in1=S2v, op=ALU.mult), reads=[Zt, P['S2']], writes=[t2])
                    tr.op('dve', lambda: V.tensor_tensor(out=Sf[par][:, :, :, :nC], in0=t1[:, :, :, :nC], in1=t2[:, :, :, :nC], op=ALU.subtract), reads=[t1, t2], writes=[Sf[par]])
                    if BCUT == 7: ph.close(); return
                    spv = Sp[par]
                    if d == 0:
                        f1 = lambda: A.activation(out=spv[:, :, :, 1:nC], in_=Sf[par][:, :, :, 0:nC - 1], func=AF.Copy)
                        cdst = spv[:, :, :, 0:1]
                    else:
                        f1 = lambda: A.activation(out=spv[:, :, :, 0:nC - 1], in_=Sf[par][:, :, :, 1:nC], func=AF.Copy)
                        cdst = spv[:, :, :, nC - 1:nC]
                    if prev is None:
                        f2 = lambda: A.activation(out=cdst, in_=Sf[par][:, :, :, 0:1], func=AF.Copy, scale=0.0)
                        rdl = [Sf[par]]
                    else:
                        pcol = (prev[1] - 1) if d == 0 else 0
                        f2 = lambda: A.activation(out=cdst, in_=Sf[1 - par][:, :, :, pcol:pcol + 1], func=AF.Copy)
                        rdl = [Sf[par], Sf[1 - par]]
                    tr.group('act', [f1, f2], reads=rdl, writes=[spv])
                    if BCUT == 8: ph.close(); return
                    fns = []
                    for j in range(8):
                        ntap = (j + 1) if d == 0 else (8 - j)
                        hk = j if d == 0 else (7 - j)
                        tot = ntap + 8; k = 0
                        for tau in range(ntap):
                            off = (j - tau) if d == 0 else (j + tau)
                            fns.append(lambda j=j, tau=tau, off=off, k=k, tot=tot: PE.matmul(po[:, j, :nC], lhsT=Kb[:, tau, :], rhs=ub[:, t0 + off:t0 + off + 8 * (nC - 1) + 1:8],
                                                                                              start=(k == 0), stop=(k == tot - 1)))
                            k += 1
                        for s4 in range(4):
                            for ri in range(2):
                                fns.append(lambda j=j, s4=s4, ri=ri, hk=hk, k=k, tot=tot: PE.matmul(po[:, j, :nC], lhsT=XH[:, s4, ri, hk, :], rhs=spv[:, s4, ri, :nC],
                                                                                                      start=(k == 0), stop=(k == tot - 1)))
                                k += 1
                    tr.group('pe', fns, reads=[Kb, ub, XH, spv], writes=[po])
                    if BCUT == 9: ph.close(); return
                    yv = yacc[:, t0:t0 + 8 * nC].rearrange('p (c j) -> p j c', j=8)
                    tr.op('dve', lambda: V.tensor_tensor(out=yv, in0=yv, in1=po[:, :, :nC], op=ALU.add), reads=[po, yacc], acc=[yacc])
                    prev = (oi, nC)
            for g0 in range(0, TT, 512):
                n = min(512, TT - g0); yv = yacc[:, g0:g0 + n]
                a0b = tq[0]; a1b = tq[1]
                a0 = tq[0][:].rearrange('p a b -> p (a b)'); a1 = tq[1][:].rearrange('p a b -> p (a b)')
                tr.op('act', lambda: A.activation(out=a0[:, :n], in_=yv, func=AF.Square), reads=[yacc], writes=[a0b])
                tr.op('dve', lambda: V.tensor_scalar(out=a0[:, :n], in0=a0[:, :n], scalar1=0.044715, scalar2=1.0, op0=ALU.mult, op1=ALU.add), reads=[a0b], writes=[a0b])
                tr.op('dve', lambda: V.tensor_tensor(out=a1[:, :n], in0=a0[:, :n], in1=yv, op=ALU.mult), reads=[a0b, yacc], writes=[a1b])
                tr.op('act', lambda: A.activation(out=a1[:, 512:512 + n], in_=a1[:, :n], func=AF.Sigmoid, scale=2.0 * math.sqrt(2.0 / PI)), reads=[a1b], writes=[a1b])
                st = stg_o[(g0 // 512) % 2]
                tr.op('dve', lambda: V.tensor_tensor(out=st[:, :n], in0=a1[:, 512:512 + n], in1=yv, op=ALU.mult), reads=[a1b, yacc], writes=[st])
                tr.dma('pool', TG[j3 * 128:(j3 + 1) * 128, g0:g0 + n], st[:, :n], reads=[st])
        ph.close()

    gm = None
    stg_o = None

    def phase_C(l, ctx_out):
        nonlocal gm
        ph = Phase()
        gm = ph.sb('gm', [128, 2, 128], F32, dma=True); tr.dma('sp', gm[:], gmask_in[:, :, :], writes=[gm])
        snk = ph.sb('snk', [64, 8], F32, dma=True); tr.dma('sp', snk[:], sink_in[l], writes=[snk])
        esk = ph.sb('esk', [64, 8])
        tr.op('act', lambda: A.activation(out=esk[:], in_=snk[:], func=AF.Exp), reads=[snk], writes=[esk])
        vsb = ph.sb('vsb', [128, 34, 128], BF16, dma=True)
        tr.dma('sp', vsb[:], VG.rearrange('(c p) d -> p c d', p=128), writes=[vsb])
        kT = [ph.sb('kT%d' % g, [64, TT], BF16, dma=True) for g in range(2)]
        for g in range(2):
            tr.dma('sp', kT[g][:, 0:NCX], KGC[g], writes=[kT[g]])
            tr.dma('sp', kT[g][:, NCX:TT], KG[g], acc=[kT[g]])
        qb = [ph.sb('qb%d' % i, [64, 4, 128], BF16, dma=True) for i in range(3)]
        ps_s = [ph.ps('pss%d' % i) for i in range(4)]
        ps_o = [ph.ps('pso%d' % i) for i in range(2)]; ps_d = [ph.ps('psd%d' % i) for i in range(2)]
        pT = [ph.sb('pT%d' % i, [128, 512], BF16) for i in range(4)]
        tmpm = [ph.sb('tmpm%d' % i, [128, 512]) for i in range(2)]; den = ph.sb('den', [64, 512])
        ob = [ph.sb('ob%d' % i, [64, 4, 128], BF16, dma=True) for i in range(2)]
        units = []
        if ctx_out:
            for g in range(2):
                for bi in range(2): units.append(('c', g, bi))
        for g in range(2):
            for bi in range(32): units.append(('l', g, bi))
        items = []
        uinfo = []
        for ui, (kind, g, bi) in enumerate(units):
            keys = [(kT[g][:, c * 128:(c + 1) * 128], None, vsb[:, c, g * 64:(g + 1) * 64]) for c in range(2)]
            if kind == 'l':
                for dlt in (-1, 0, 1):
                    kb = bi + dlt
                    if kb < 0 or kb > 31: continue
                    m = None if dlt == 0 else gm[:, (0 if dlt == -1 else 1), :].unsqueeze(1).to_broadcast([128, 4, 128])
                    keys.append((kT[g][:, NCX + kb * 128:NCX + (kb + 1) * 128], m, vsb[:, 2 + kb, g * 64:(g + 1) * 64]))
            for ki, (k_ap, m_ap, v_ap) in enumerate(keys): items.append((ui, ki, len(keys), k_ap, m_ap, v_ap))
        cnt = dict(n=0, m=0)
        st = {}

        def stage1(ii):
            ui, ki, nk, k_ap, m_ap, v_ap = items[ii]
            kind, g, bi = units[ui]
            q = qb[ui % 3]
            if ki == 0:
                srcq = (QGC if kind == 'c' else QG)[4 * g:4 * g + 4, :, bi * 128:(bi + 1) * 128].rearrange('h d t -> d h t')
                tr.dma('sp', q[:], srcq, writes=[q])
            pss = ps_s[cnt['n'] % 4]; pt = pT[cnt['n'] % 4]; cnt['n'] += 1
            tr.op('pe', lambda: PE.matmul(pss[:, :], lhsT=k_ap, rhs=q[:], start=True, stop=True), reads=[q, kT[g]], writes=[pss])
            if m_ap is not None:
                tm = tmpm[cnt['m'] % 2]; cnt['m'] += 1
                tr.op('dve', lambda: V.tensor_tensor(out=tm[:].rearrange('p (h t) -> p h t', h=4), in0=pss[:].rearrange('p (h t) -> p h t', h=4),
                                                     in1=m_ap, op=ALU.add), reads=[pss, gm], writes=[tm])
                tr.op('act', lambda: A.activation(out=pt[:], in_=tm[:], func=AF.Exp), reads=[tm], writes=[pt])
            else:
                tr.op('act', lambda: A.activation(out=pt[:], in_=pss[:], func=AF.Exp), reads=[pss], writes=[pt])
            st[ii] = pt

        def stage2(ii):
            ui, ki, nk, k_ap, m_ap, v_ap = items[ii]
            kind, g, bi = units[ui]
            pt = st.pop(ii)
            pso = ps_o[ui % 2]; psd = ps_d[ui % 2]; o = ob[ui % 2]
            tr.op('pe', lambda: PE.matmul(pso[0:64, :], lhsT=v_ap, rhs=pt[:], start=(ki == 0), stop=(ki == nk - 1)),
                  reads=[pt, vsb], writes=[pso] if ki == 0 else (), acc=() if ki == 0 else [pso])
            tr.op('pe', lambda: PE.matmul(psd[0:64, :], lhsT=ones_b[:, 0:64], rhs=pt[:], start=(ki == 0), stop=(ki == nk - 1)),
                  reads=[pt, ones_b], writes=[psd] if ki == 0 else (), acc=() if ki == 0 else [psd])
            if ki == nk - 1:
                tr.op('dve', lambda: V.tensor_tensor(out=den[:].rearrange('p (h t) -> p h t', h=4), in0=psd[0:64, :].rearrange('p (h t) -> p h t', h=4),
                                                     in1=esk[:, 4 * g:4 * g + 4].unsqueeze(2).to_broadcast([64, 4, 128]), op=ALU.add),
                      reads=[psd, esk], writes=[den])
                tr.op('dve', lambda: V.reciprocal(out=den[:], in_=den[:]), reads=[den], writes=[den])
                tr.op('dve', lambda: V.tensor_tensor(out=o[:].rearrange('p h t -> p (h t)'), in0=pso[0:64, :], in1=den[:], op=ALU.mult),
                      reads=[pso, den], writes=[o])
                dst = (YGC if kind == 'c' else YG)[4 * g:4 * g + 4, :, bi * 128:(bi + 1) * 128].rearrange('h d t -> d h t')
                tr.dma('pool', dst, o[:], reads=[o])

        KD = 2
        for ii in range(len(items) + KD):
            if ii < len(items): stage1(ii)
            if ii - KD >= 0: stage2(ii - KD)
        ph.close()

    def attn_unit3(q, keys, ps_s, pso, psd, pT, tmpm, qbufs, kbufs, vbufs, epi):
        nk = len(keys)
        for ki, (k_ap, m_ap, v_ap) in enumerate(keys):
            pss = ps_s[ki % len(ps_s)]
            tr.op('pe', lambda: PE.matmul(pss[:, :], lhsT=k_ap, rhs=q[:], start=True, stop=True), reads=qbufs + kbufs, writes=[pss])
            pt = pT[ki % len(pT)]
            if m_ap is not None:
                tr.op('dve', lambda: V.tensor_tensor(out=tmpm[:].rearrange('p (h t) -> p h t', h=4), in0=pss[:].rearrange('p (h t) -> p h t', h=4),
                                                     in1=m_ap, op=ALU.add), reads=[pss, gm], writes=[tmpm])
                tr.op('act', lambda: A.activation(out=pt[:], in_=tmpm[:], func=AF.Exp), reads=[tmpm], writes=[pt])
            else:
                tr.op('act', lambda: A.activation(out=pt[:], in_=pss[:], func=AF.Exp), reads=[pss], writes=[pt])
            tr.op('pe', lambda: PE.matmul(pso[0:64, :], lhsT=v_ap, rhs=pt[:], start=(ki == 0), stop=(ki == nk - 1)),
                  reads=[pt] + vbufs, writes=[pso] if ki == 0 else (), acc=() if ki == 0 else [pso])
            tr.op('pe', lambda: PE.matmul(psd[0:64, :], lhsT=ones_b[:, 0:64], rhs=pt[:], start=(ki == 0), stop=(ki == nk - 1)),
                  reads=[pt, ones_b], writes=[psd] if ki == 0 else (), acc=() if ki == 0 else [psd])
        epi()

    def phase_D(l, ctx_out):
        ph = Phase()
        kT = [ph.sb('nkT%d' % i, [64, TT], BF16, dma=True) for i in range(2)]
        qT = [ph.sb('nqT%d' % i, [64, TT], BF16, dma=True) for i in range(2)]
        vs = [ph.sb('nvs%d' % i, [128, 34, 64], BF16, dma=True) for i in range(2)]
        vt = [ph.sb('nvt%d' % i, [128, 14, 64], F32, dma=True) for i in range(2)]
        od = [ph.sb('nod%d' % i, [128, 5, 64], F32, dma=True) for i in range(2)]
        cbt = [ph.sb('ncb%d' % i, [128, 5, 128], F32, dma=True) for i in range(2)]
        yb = [ph.sb('nyb%d' % i, [64, TT], BF16, dma=True) for i in range(2)]
        RD = 2
        ps_L = [ph.ps('npl%d' % i) for i in range(RD)]
        ps_X = [ph.ps('npx%d' % i) for i in range(RD)]
        ps_od = [ph.ps('npo%d' % i) for i in range(RD)]
        tmp = [ph.sb('ntmp%d' % i, [128, 640]) for i in range(RD)]
        pT = [ph.sb('npT%d' % i, [128, 640], BF16) for i in range(RD)]
        pC = [ph.sb('npC%d' % i, [128, 256], BF16) for i in range(RD)]
        rd = [ph.sb('nrd%d' % i, [64, 256]) for i in range(RD)]
        n = 0
        for h in range(8):
            k_ = kT[h % 2]; q_ = qT[h % 2]; v_ = vs[h % 2]; vt_ = vt[h % 2]; od_ = od[h % 2]; cb_ = cbt[h % 2]; y_ = yb[h % 2]
            tr.dma('sp', k_[:, 0:NCX], KNC[h], writes=[k_]); tr.dma('sp', k_[:, NCX:TT], KN[h], acc=[k_])
            tr.dma('sp', q_[:, 0:NCX], QNC[h], writes=[q_]); tr.dma('sp', q_[:, NCX:TT], QN[h], acc=[q_])
            tr.dma('sp', v_[:], VN[:, h * 64:(h + 1) * 64].rearrange('(c p) d -> p c d', p=128), writes=[v_])
            tr.dma('sp', vt_[:], navt_in[l, h], writes=[vt_]); tr.dma('sp', od_[:], naod_in[l, h], writes=[od_])
            tr.dma('sp', cb_[:], nacb_in[l, h], writes=[cb_])
            units = ([('c', 0), ('c', 1)] if ctx_out else []) + [('l', r) for r in range(4)] + [('p', r) for r in range(4, 60, 2)] + [('l', r) for r in range(60, 64)]
            for ui, (kind, r) in enumerate(units):
                psl = ps_L[n % RD]; psx = ps_X[n % RD]; pob = ps_od[n % RD]
                tm = tmp[n % RD]; pt = pT[n % RD]; pc = pC[n % RD]; rdn = rd[n % RD]; n += 1
                p0 = 0
                if kind == 'c':
                    Nq = 128; qap = q_[:, r * 128:(r + 1) * 128]; npair = 0; oc0 = r * 128
                elif kind == 'p':
                    Nq = 128; qap = q_[:, NCX + r * 64:NCX + (r + 2) * 64]; oc0 = NCX + r * 64
                    p0 = (r - 4) // 2; npair = 5
                else:
                    Nq = 64; qap = q_[:, NCX + r * 64:NCX + (r + 1) * 64]; oc0 = NCX + r * 64
                    rs = min(max(r - 4, 0), 56)
                    p0 = rs // 2; npair = 4; i0_ = rs - r + 7
                    bias = vt_[:, i0_:i0_ + 7:2, :]
                fns = [lambda c=c: PE.matmul(psx[:, 128 + c * Nq:128 + (c + 1) * Nq], lhsT=k_[:, c * 128:(c + 1) * 128], rhs=qap, start=True, stop=True) for c in range(2)]
                if npair == 5:
                    fns.append(lambda: PE.matmul(psx[:, 0:Nq], lhsT=k_[:, NCX + (p0 + 4) * 128:NCX + (p0 + 5) * 128], rhs=qap, start=True, stop=True))
                tr.group('pe', fns, reads=[k_, q_], writes=[psx])
                if npair:
                    tr.group('pe', [lambda k=k: PE.matmul(psl[:, k * Nq:(k + 1) * Nq], lhsT=k_[:, NCX + (p0 + k) * 128:NCX + (p0 + k + 1) * 128], rhs=qap,
                                                          start=True, stop=True) for k in range(4)], reads=[k_, q_], writes=[psl])
                tr.op('act', lambda: A.activation(out=pc[:, :2 * Nq], in_=psx[:, 128:128 + 2 * Nq], func=AF.Exp), reads=[psx], writes=[pc])
                if kind == 'l':
                    tr.op('dve', lambda: V.tensor_tensor(out=tm[:, :256].rearrange('p (a b) -> p a b', a=4),
                                                         in0=psl[:, :256].rearrange('p (a b) -> p a b', a=4), in1=bias, op=ALU.add),
                          reads=[psl, vt_], writes=[tm])
                    tr.op('act', lambda: A.activation(out=pt[:, :256], in_=tm[:, :256], func=AF.Exp), reads=[tm], writes=[pt])
                elif kind == 'p':
                    tr.op('dve', lambda: V.tensor_tensor(out=tm[:, 0:512], in0=psl[:, 0:512], in1=cb_[:, 0:4, :].rearrange('p a b -> p (a b)'), op=ALU.add),
                          reads=[psl, cb_], writes=[tm])
                    tr.op('dve', lambda: V.tensor_tensor(out=tm[:, 512:640], in0=psx[:, 0:128], in1=cb_[:, 4, :], op=ALU.add),
                          reads=[psx, cb_], acc=[tm])
                    tr.op('act', lambda: A.activation(out=pt[:, :640], in_=tm[:, :640], func=AF.Exp), reads=[tm], writes=[pt])
                mm = [(v_[:, c, :], pc[:, c * Nq:(c + 1) * Nq]) for c in range(2)]
                mm += [(v_[:, 2 + p0 + k, :], pt[:, k * Nq:(k + 1) * Nq]) for k in range(npair)]
                tr.group('pe', [lambda i=i, a=a, b=b: PE.matmul(pob[0:64, 0:Nq], lhsT=a, rhs=b, start=(i == 0), stop=(i == len(mm) - 1))
                                for i, (a, b) in enumerate(mm)], reads=[v_, pc, pt], writes=[pob])
                tr.group('pe', [lambda i=i, b=b: PE.matmul(pob[0:64, 128:128 + Nq], lhsT=ones_b[:, 0:64], rhs=b, start=(i == 0), stop=(i == len(mm) - 1))
                                for i, (a, b) in enumerate(mm)], reads=[ones_b, pc, pt], acc=[pob])
                tr.op('dve', lambda: V.reciprocal(out=rdn[:, :Nq], in_=pob[0:64, 128:128 + Nq]), reads=[pob], writes=[rdn])
                tr.op('dve', lambda: V.tensor_tensor(out=y_[:, oc0:oc0 + Nq], in0=pob[0:64, 0:Nq], in1=rdn[:, :Nq], op=ALU.mult),
                      reads=[pob, rdn], writes=[y_] if ui == 0 else (), acc=() if ui == 0 else [y_])
            if ctx_out: tr.dma('pool', YNC[h], y_[:, 0:NCX], reads=[y_])
            tr.dma('pool', YN[h], y_[:, NCX:TT], reads=[y_])
        ph.close()

    pcbig = None

    def phase_E(l, ctx_out, last):
        ph = Phase()
        R = ffn_bufs(ph)
        xTs = [ph.sb('xT%d' % i, [128, KC, 512], F32, dma=True) for i in range(2)]
        hT = ph.sb('hT', [128, KC, 512], BF16); aT = ph.sb('aT', [128, FC, 512], BF16)
        tg = [ph.sb('tg%d' % i, [128, 3, 512], F32, dma=True) for i in range(2)]
        tgb = ph.sb('tgb', [128, 3, 512], BF16); ys = ph.sb('ys', [128, 3, 512], BF16)
        yg = [ph.sb('yg%d' % i, [64, 8, 512], BF16, dma=True) for i in range(2)]
        yn = [ph.sb('yn%d' % i, [64, 8, 512], BF16, dma=True) for i in range(2)]
        sgr = [ph.sb('sg%d' % i, [128, 3, 512], BF16, dma=True) for i in range(2)]
        wpgr = [ph.sb('wpg%d' % i, [64, 8, 128], BF16, dma=True) for i in range(2)]
        wpnr = [ph.sb('wpn%d' % i, [64, 8, 128], BF16, dma=True) for i in range(2)]
        wglu = ph.sb('wglu', [128, 3, 3, 128], BF16, dma=True)
        tr.dma('sp', wglu[:], w_b[('wglu', l)].rearrange('m p k c -> p m k c'), reads=[w_buf[('wglu', l)]], writes=[wglu])
        wps = ph.sb('wps', [128, 8, 3, 128], BF16, dma=True)
        tr.dma('sp', wps[:], w_b[('wps', l)].rearrange('m p k c -> p m k c'), reads=[w_buf[('wps', l)]], writes=[wps])
        wo = [ph.sb('wo%d' % i, [128, KC, 128], BF16, dma=True) for i in range(2)]
        acc = ph.sb('acc', [128, 512]); t2 = ph.sb('t2e', [128, 512])
        ost = [ph.sb('ost%d' % i, [128, D], F32, dma=True) for i in range(1)] if last else None
        tl = tiles if ctx_out else tiles[1:]
        cnt = dict(wo=0, ost=0)

        def loads(idx):
            c0, N, col = tl[idx]; b = idx % 2
            tr.dma('sp', xTs[b][:, :, :N], XT[:, c0:c0 + N].rearrange('(k p) t -> p k t', p=128), writes=[xTs[b]])
            tr.dma('sp', tg[b][:, :, :N], TG[:, c0:c0 + N].rearrange('(k p) t -> p k t', p=128), writes=[tg[b]])
            if col == 1:
                tr.dma('sp', yg[b][:, :, :N], YGC.rearrange('h d t -> d h t'), writes=[yg[b]])
                tr.dma('sp', yn[b][:, :, :N], YNC.rearrange('h d t -> d h t'), writes=[yn[b]])
            else:
                tr.dma('sp', yg[b][:, :, :N], YG[:, :, c0 - NCX:c0 - NCX + N].rearrange('h d t -> d h t'), writes=[yg[b]])
                tr.dma('sp', yn[b][:, :, :N], YN[:, :, c0 - NCX:c0 - NCX + N].rearrange('h d t -> d h t'), writes=[yn[b]])

        mcount = 0
        loads(0)
        for idx in range(len(tl)):
            c0, N, col = tl[idx]; b = idx % 2
            xT = xTs[b]; tg_ = tg[b]; yg_ = yg[b]; yn_ = yn[b]
            if idx + 1 < len(tl): loads(idx + 1)
            tr.op('act', lambda: A.activation(out=tgb[:, :, :N], in_=tg_[:, :, :N], func=AF.Copy), reads=[tg_], writes=[tgb])
            for m in range(3):
                pg = R['ps_g'][m % 2]
                tr.group('pe', [lambda kc=kc: PE.matmul(pg[:, :N], lhsT=wglu[:, m, kc, :], rhs=tgb[:, kc, :N], start=(kc == 0), stop=(kc == 2))
                                for kc in range(3)], reads=[wglu, tgb], writes=[pg])
                tr.op('act', lambda: A.activation(out=acc[:, :N], in_=pg[:, :N], func=AF.Sigmoid), reads=[pg], writes=[acc])
                tr.op('dve', lambda: V.tensor_tensor(out=ys[:, m, :N], in0=acc[:, :N], in1=tg_[:, m, :N], op=ALU.mult), reads=[acc, tg_],
                      writes=[ys] if m == 0 else (), acc=() if m == 0 else [ys])
            for m in range(KC):
                p1 = R['ps_g'][m % 2]; p2 = R['ps_u'][m % 2]; p3 = R['ps_m'][m % 2]
                sg_ = sgr[mcount % 2]; wpg = wpgr[mcount % 2]; wpn = wpnr[mcount % 2]; mcount += 1
                tr.dma('sp', sg_[:, :, :N], SG[:, c0:c0 + N].rearrange('(b m p) t -> m p b t', b=3, p=128)[m], writes=[sg_])
                tr.dma('sp', wpg[:], w_b[('wpg', l)][m], reads=[w_buf[('wpg', l)]], writes=[wpg])
                tr.dma('sp', wpn[:], w_b[('wpn', l)][m], reads=[w_buf[('wpn', l)]], writes=[wpn])
                tr.group('pe', [lambda kc=kc: PE.matmul(p1[:, :N], lhsT=wps[:, m, kc, :], rhs=ys[:, kc, :N], start=(kc == 0), stop=(kc == 2))
                                for kc in range(3)], reads=[wps, ys], writes=[p1])
                tr.group('pe', [lambda h=h: PE.matmul(p2[:, :N], lhsT=wpg[:, h, :], rhs=yg_[:, h, :N], start=(h == 0), stop=(h == 7))
                                for h in range(8)], reads=[wpg, yg_], writes=[p2])
                tr.group('pe', [lambda h=h: PE.matmul(p3[:, :N], lhsT=wpn[:, h, :], rhs=yn_[:, h, :N], start=(h == 0), stop=(h == 7))
                                for h in range(8)], reads=[wpn, yn_], writes=[p3])
                tr.op('dve', lambda: V.tensor_tensor(out=acc[:, :N], in0=p1[:, :N], in1=sg_[:, 0, :N], op=ALU.mult), reads=[p1, sg_], writes=[acc])
                tr.op('dve', lambda: V.tensor_tensor(out=t2[:, :N], in0=p2[:, :N], in1=sg_[:, 1, :N], op=ALU.mult), reads=[p2, sg_], writes=[t2])
                tr.op('pool', lambda: G.tensor_tensor(out=acc[:, :N], in0=acc[:, :N], in1=t2[:, :N], op=ALU.add), reads=[acc, t2], writes=[acc])
                tr.op('dve', lambda: V.tensor_tensor(out=t2[:, :N], in0=p3[:, :N], in1=sg_[:, 2, :N], op=ALU.mult), reads=[p3, sg_], writes=[t2])
                tr.op('pool', lambda: G.tensor_tensor(out=hT[:, m, :N], in0=acc[:, :N], in1=t2[:, :N], op=ALU.add), reads=[acc, t2],
                      writes=[hT] if m == 0 else (), acc=() if m == 0 else [hT])
            for m in range(KC):
                wb = wo[cnt['wo'] % 2]; cnt['wo'] += 1
                tr.dma('sp', wb[:], w_b[('wout', l)][m], reads=[w_buf[('wout', l)]], writes=[wb])
                pd = R['ps_m'][m % 2]
                tr.group('pe', [lambda kc=kc: PE.matmul(pd[:, :N], lhsT=wb[:, kc, :], rhs=hT[:, kc, :N], start=(kc == 0), stop=(kc == KC - 1))
                                for kc in range(KC)], reads=[wb, hT], writes=[pd])
                tr.op('dve', lambda: V.scalar_tensor_tensor(out=xT[:, m, :N], in0=pd[:, :N], scalar=modG[:, 1, m, col:col + 1],
                                                            in1=xT[:, m, :N], op0=ALU.mult, op1=ALU.add), reads=[pd, modG, xT], acc=[xT])
            ffn(ph, l, 2, xT, hT, aT, N, col, 2, R)
            if not last:
                tr.dma('pool', XT[:, c0:c0 + N].rearrange('(k p) t -> p k t', p=128), xT[:, :, :N], reads=[xT])
            else:
                sq = R['sq']; rstd = R['rstd']; tb = R['tmpbig']; pss = R['ps_m'][0]
                tr.op('act', lambda: A.activation(out=sq[:, :, :N], in_=xT[:, :, :N], func=AF.Square), reads=[xT], writes=[sq])
                tr.group('pe', [lambda kc=kc: PE.matmul(pss[:, :N], lhsT=ones_b[:], rhs=sq[:, kc, :N], start=(kc == 0), stop=(kc == KC - 1))
                                for kc in range(KC)], reads=[sq, ones_b], writes=[pss])
                tr.op('act', lambda: A.activation(out=rstd[:, :N], in_=pss[:, :N], func=AF.Sqrt, scale=1.0 / D, bias=epsb[:, 0:1]), reads=[pss, epsb], writes=[rstd])
                tr.op('dve', lambda: V.reciprocal(out=rstd[:, :N], in_=rstd[:, :N]), reads=[rstd], writes=[rstd])
                tr.op('dve', lambda: V.tensor_tensor(out=tb[:, :, :N], in0=xT[:, :, :N], in1=rstd[:, :N].unsqueeze(1).to_broadcast([128, KC, N]), op=ALU.mult),
                      reads=[xT, rstd], writes=[tb])
                tr.group('act', [lambda kc=kc: A.activation(out=tb[:, kc, :N], in_=tb[:, kc, :N], func=AF.Identity, scale=fing[:, kc:kc + 1])
                                 for kc in range(KC)], reads=[tb, fing], writes=[tb])
                for ts in range(N // 128):
                    o_ = ost[0]; cnt['ost'] += 1
                    for hf in range(2):
                        pt = R['ps_g'][hf]
                        tr.group('pe', [lambda k=k: PE.transpose(pt[:, k * 128:(k + 1) * 128], tb[:, hf * 4 + k, ts * 128:(ts + 1) * 128], ident[:])
                                        for k in range(4)], reads=[tb, ident], writes=[pt])
                        tr.op('act' if hf == 0 else 'dve',
                              (lambda: A.activation(out=o_[:, 0:512], in_=pt[:], func=AF.Copy)) if hf == 0 else (lambda: V.tensor_copy(out=o_[:, 512:1024], in_=pt[:])),
                              reads=[pt], writes=[o_] if hf == 0 else (), acc=() if hf == 0 else [o_])
                    r0 = c0 - NCX + ts * 128
                    tr.dma('pool', out_d[r0:r0 + 128, :], o_[:], reads=[o_])
        ph.close()

    epsb = gp.sb('epsb', [128, 1]); tr.op('dve', lambda: V.memset(epsb[:], 1e-6), writes=[epsb])
    tr.barrier()

    def run():
        nonlocal stg_o, pcbig
        stg_o = [gp.sb('stgo%d' % i, [128, 512], F32, dma=True) for i in range(2)]
        pcbig = gp.sb('pcbig', [128, 256], BF16)
        if only is not None:
            {'B': phase_B}[only[0]](only[1]); return
        for l in range(DEPTH):
            ctx_out = l < DEPTH - 1
            compute_mod(l)
            if stop_after == ('mod', l): return
            phase_A(l)
            if stop_after == ('A', l): return
            emit_casts(l, G2)
            phase_B(l)
            if stop_after == ('B', l): return
            if l + 1 < DEPTH: emit_casts(l + 1, G1)
            phase_C(l, ctx_out)
            if stop_after == ('C', l): return
            phase_D(l, ctx_out)
            if stop_after == ('D', l): return
            phase_E(l, ctx_out, l == DEPTH - 1)
            if stop_after == ('E', l): return

    run()
    tr.barrier()
    gp.es.close()
    tr.es.close()
    return nc


def prep_shared(inp):
    sh = {}
    L = DEPTH
    sh['wgu1'] = np.stack([np.stack([tile_w(inp['ffn1_wg'][l]), tile_w(inp['ffn1_wu'][l])], 2) for l in range(L)])
    sh['wd1'] = np.stack([tile_w(inp['ffn1_wd'][l]) for l in range(L)])
    sh['wgu2'] = np.stack([np.stack([tile_w(inp['ffn2_wg'][l]), tile_w(inp['ffn2_wu'][l])], 2) for l in range(L)])
    sh['wd2'] = np.stack([tile_w(inp['ffn2_wd'][l]) for l in range(L)])
    cols = win_fm_cols()
    sh['winfm'] = np.stack([tile_w(inp['w_in'][l][:, cols]) for l in range(L)])
    tmc = np.concatenate([IN_OFF['gv'] + np.arange(128), IN_OFF['nv'] + np.arange(512)])
    sh['wintm'] = np.stack([np.ascontiguousarray(inp['w_in'][l][:, tmc].reshape(KC, 128, 640).transpose(1, 0, 2)) for l in range(L)])
    sh['wada'] = np.stack([tile_w(inp['w_ada'][l]) for l in range(L)])
    sh['wglu'] = np.stack([tile_w(inp['ssm_w_glu'][l]) for l in range(L)])
    sh['wps'] = np.stack([tile_w(inp['w_p_ssm'][l]) for l in range(L)])
    sh['wpg'] = np.stack([tile_w(inp['w_p_gqa'][l], 64) for l in range(L)])
    sh['wpn'] = np.stack([tile_w(inp['w_p_na'][l], 64) for l in range(L)])
    sh['wout'] = np.stack([tile_w(inp['w_out'][l]) for l in range(L)])
    sh['bada'] = np.ascontiguousarray(inp['b_ada'].reshape(L, 72, 128).transpose(0, 2, 1))
    sh['normg'] = np.ascontiguousarray(inp['norm_g'].reshape(L, 3, KC, 128).transpose(0, 3, 1, 2))
    sh['fing'] = np.ascontiguousarray(inp['final_g'].reshape(KC, 128).T)

    def st(a):
        return a.reshape(L, 2, 12, 2, 64).transpose(0, 1, 3, 4, 2).reshape(L, 2, 128, 12)
    ldt = np.broadcast_to(inp['ssm_log_dt'][:, :, :, None], (L, 2, 24, 64))
    sh['ssm_a'] = np.ascontiguousarray(np.stack([st(inp['ssm_a_re']), st(inp['ssm_a_im']), st(ldt)], 3))

    def sbt(a):
        return a.reshape(L, 2, 12, 2, 64, 16).transpose(0, 1, 3, 4, 2, 5).reshape(L, 2, 128, 12, 16)
    sh['ssm_b'] = np.ascontiguousarray(np.stack([sbt(inp['ssm_b_re']), sbt(inp['ssm_b_im'])], 2))

    def sct(a):
        return a.reshape(L, 2, 3, 8, 16, 64).transpose(0, 1, 3, 4, 2, 5).reshape(L, 2, 128, 3, 64)
    sh['ssm_c'] = np.ascontiguousarray(np.stack([sct(inp['ssm_c_re']), sct(inp['ssm_c_im'])], 2))
    sh['ssm_d'] = np.ascontiguousarray(inp['ssm_d'].reshape(L, 3, 128).transpose(0, 2, 1))
    sh['sink'] = np.ascontiguousarray(np.broadcast_to(inp['gqa_sink'][:, None, :], (L, 64, 8)))
    vt, od, cb = na_bias_tables(inp['na_rpb'])
    sh['navt'] = vt; sh['naod'] = od; sh['nacb'] = cb.reshape(L, 8, 128, 5, 128)
    sh.update(host_consts())
    return {k: np.ascontiguousarray(v, dtype=np.float32) for k, v in sh.items()}


def core_inputs(inp, b, sh):
    m = dict(sh)
    m['x'] = np.ascontiguousarray(inp['x'][b]); m['ctx'] = np.ascontiguousarray(inp['ctx'][b])
    sv = np.stack([inp['c'][b].reshape(KC, 128).T, inp['c_ctx'].reshape(KC, 128).T], 2)
    m['svec'] = np.ascontiguousarray(sv, dtype=np.float32)
    return m


def kernel(**inputs):
    inp = {k: np.asarray(v) for k, v in inputs.items()}
    sh = prep_shared(inp)
    nc = build()
    in_maps = [core_inputs(inp, b, sh) for b in range(8)]
    res = run_bass_kernel_spmd(nc, in_maps, core_ids=list(range(8)))
    return np.stack([np.asarray(r['out'], dtype=np.float32) for r in res.results], 0)
```

```python
import contextlib, math, os
import numpy as np
import ml_dtypes
import concourse.bass as bass
import concourse.mybir as mybir
from concourse.bass_utils import run_bass_kernel_spmd

F32 = mybir.dt.float32; BF16 = mybir.dt.bfloat16; I32 = mybir.dt.int32
AF = mybir.ActivationFunctionType; ALU = mybir.AluOpType

D = 1024; T = 4096; NCX = 256; TT = T + NCX; FF = 2816; KC = 8; FC = 22; DEPTH = 2
NEG = -30000.0
SAME_SYNC = True
PI = math.pi


class Sem:
    def __init__(s, h, name): s.h = h; s.total = 0; s.name = name


class Eng:
    def __init__(s, name, obj, sem): s.name = name; s.obj = obj; s.sem = sem; s.known = {}


class Buf:
    def __init__(s, name, t=None, dsem=None, qsem=None):
        s.name = name; s.t = t; s.w = []; s.r = []; s.pre = []; s.dsem = dsem; s.qsem = qsem

    def __getitem__(s, k): return s.t[k]


class Trk:
    def __init__(self, nc):
        self.nc = nc
        self.es = contextlib.ExitStack()
        self.sems = []
        self.E = {}
        for n, o in (('pe', nc.tensor), ('act', nc.scalar), ('dve', nc.vector), ('pool', nc.gpsimd), ('sp', nc.sync)):
            self.E[n] = Eng(n, o, self.new_sem('e_' + n) if n != 'sp' else None)
        self.dpool = []; self.dnext = 0; self.uid = 0; self.qpool = []; self.qnext = 0

    def new_sem(self, name):
        s = Sem(self.es.enter_context(self.nc.semaphore(name)), name); self.sems.append(s); return s

    def dsem(self):
        if self.dnext >= len(self.dpool): self.dpool.append(self.new_sem('d%d' % len(self.dpool)))
        s = self.dpool[self.dnext]; self.dnext += 1; return s

    def qsem(self):
        if self.qnext >= len(self.qpool): self.qpool.append(self.new_sem('q%d' % len(self.qpool)))
        s = self.qpool[self.qnext]; self.qnext += 1; return s

    def _wait(self, eng, evs):
        need = {}
        for (sem, val, src) in evs:
            if src == eng.name and (src == 'pe' or not SAME_SYNC): continue
            if eng.known.get(sem, 0) >= val: continue
            need[sem] = max(need.get(sem, 0), val)
        for sem, val in need.items():
            eng.obj.wait_ge(sem.h, val); eng.known[sem] = val

    def _pre(self, eng, reads, writes, acc):
        evs = []
        for b in reads: evs += b.w
        for b in writes:
            b.pre = b.w + b.r; evs += b.pre
        for b in acc: evs += b.pre
        self._wait(eng, evs)

    def _post(self, ev, reads, writes, acc):
        for b in reads: b.r.append(ev)
        for b in writes: b.w = [ev]; b.r = []
        for b in acc: b.w.append(ev)

    def group(self, en, fns, reads=(), writes=(), acc=()):
        eng = self.E[en]
        self._pre(eng, reads, writes, acc)
        ins = None
        for f in fns: ins = f()
        eng.sem.total += 1
        ins.then_inc(eng.sem.h, 1)
        self._post((eng.sem, eng.sem.total, en), reads, writes, acc)

    def op(self, en, fn, reads=(), writes=(), acc=()):
        self.group(en, [fn], reads, writes, acc)

    def dma(self, q, out, in_, reads=(), writes=(), acc=(), sem=None):
        eng = self.E[q]
        self._pre(eng, reads, writes, acc)
        if sem is None:
            for b in list(writes) + list(acc) + list(reads):
                if b.dsem is not None: sem = (b.qsem if q == 'pool' else b.dsem); break
        ins = eng.obj.dma_start(out=out, in_=in_)
        sem.total += 16
        ins.then_inc(sem.h, 16)
        self._post((sem, sem.total, 'dma'), reads, writes, acc)

    def barrier(self):
        for eng in self.E.values():
            for s in self.sems:
                if s.total > 0 and eng.known.get(s, 0) < s.total:
                    eng.obj.wait_ge(s.h, s.total); eng.known[s] = s.total


def tile_w(W, kp=128):
    K, M = W.shape
    return np.ascontiguousarray(W.reshape(K // kp, kp, M // 128, 128).transpose(2, 1, 0, 3))


IN_OFF = dict(u=0, gq=384, gk=896, gv=1024, nq=1152, nk=1664, nv=2176, gates=2688)
ROT_PERM = np.concatenate([np.arange(16, 32), np.arange(0, 16), np.arange(48, 64), np.arange(32, 48)])
FM_CHUNKS = ([('u', i) for i in range(3)] + [('gq', i) for i in range(4)] + [('gq2', i) for i in range(4)]
             + [('gk', 0), ('gk2', 0)] + [('nq', i) for i in range(4)] + [('nk', i) for i in range(4)]
             + [('gates', i) for i in range(24)])


def win_fm_cols():
    cols = []
    for kind, i in FM_CHUNKS:
        if kind == 'u': c = IN_OFF['u'] + i * 128 + np.arange(128)
        elif kind == 'gq': c = IN_OFF['gq'] + i * 128 + np.arange(128)
        elif kind == 'gq2': c = IN_OFF['gq'] + i * 128 + np.concatenate([ROT_PERM, 64 + ROT_PERM])
        elif kind == 'gk': c = IN_OFF['gk'] + np.arange(128)
        elif kind == 'gk2': c = IN_OFF['gk'] + np.concatenate([ROT_PERM, 64 + ROT_PERM])
        elif kind == 'nq': c = IN_OFF['nq'] + i * 128 + np.arange(128)
        elif kind == 'nk': c = IN_OFF['nk'] + i * 128 + np.arange(128)
        else: c = IN_OFF['gates'] + i * 128 + np.arange(128)
        cols.append(c)
    return np.concatenate(cols)


def rope_tables():
    t = np.arange(T)
    pos = np.stack([t // 64, t % 64], 0).astype(np.float32)
    inv = (10000.0 ** (-np.arange(0, 32, 2, dtype=np.float32) / 32)).astype(np.float32)
    C = np.zeros((64, T), np.float32); S = np.zeros((64, T), np.float32)
    for ax in range(2):
        ang = (pos[ax][None, :] * inv[:, None]).astype(np.float32)
        for half in range(2):
            sl = slice(ax * 32 + half * 16, ax * 32 + half * 16 + 16)
            C[sl] = np.cos(ang)
            S[sl] = -np.sin(ang) if half == 0 else np.sin(ang)
    C2 = np.concatenate([C, C], 0); S2 = np.concatenate([S, S], 0)
    return np.stack([C2 * 0.125, S2 * 0.125, C2, S2], 0).astype(np.float32)


def na_bias_tables(rpb):
    L = rpb.shape[0]
    kc = np.arange(64)[:, None]; qc = np.arange(64)[None, :]
    cs = np.clip(qc - 8, 0, 48)
    valid = (kc >= cs) & (kc < cs + 16)
    idx = np.clip(kc - qc + 15, 0, 30)
    tab = np.where(valid[None, None, None], rpb[:, :, :, idx], np.float32(NEG)).astype(np.float32)
    negt = np.full((L, 8, 64, 64), NEG, np.float32)
    VT = np.zeros((L, 8, 128, 14, 64), np.float32)
    for d in range(-7, 7):
        VT[:, :, 0:64, d + 7] = tab[:, :, d + 7]
        VT[:, :, 64:128, d + 7] = tab[:, :, d + 8]
    OD = np.zeros((L, 8, 128, 5, 64), np.float32)
    for k, d in enumerate((-5, -3, -1, 1, 3)):
        OD[:, :, 0:64, k] = negt if d == -5 else tab[:, :, d + 7]
        OD[:, :, 64:128, k] = negt if d == 3 else tab[:, :, d + 8]
    CB = np.zeros((L, 8, 128, 5, 2, 64), np.float32)
    for k in range(5):
        CB[:, :, :, k, 0, :] = VT[:, :, :, 3 + 2 * k, :] if k < 4 else np.float32(NEG)
        CB[:, :, :, k, 1, :] = OD[:, :, :, k, :]
    return VT, OD, CB


def host_consts():
    c = {}
    c['ident'] = np.eye(128, dtype=np.float32)
    c['rope'] = rope_tables()
    k = np.arange(128)[:, None]; q = np.arange(128)[None, :]
    c['gmask'] = np.stack([np.where(k >= q, 0.0, NEG), np.where(k <= q, 0.0, NEG)], 1).astype(np.float32)
    p = np.arange(128)
    m2 = np.zeros((128, 4, 8, 16), np.float32)
    mc = np.zeros((128, 4, 2, 64), np.float32)
    for qq in range(4):
        for pp in range(128):
            m2[pp, qq, 2 * qq + (pp >= 64), :] = 1.0
            for half in range(2):
                if pp // 16 == 2 * qq + half: mc[pp, qq, half, :] = 1.0
    c['mask2'] = m2; c['maskc'] = mc
    io = np.zeros((128, 2, 64), np.float32)
    io[:, 0, :] = np.arange(1, 65)[None, :]; io[:, 1, :] = np.arange(64, 0, -1)[None, :]
    c['iota64'] = io
    c['iota9'] = np.broadcast_to(np.arange(9, dtype=np.float32)[None, :], (128, 9)).copy()
    return c


def build(dbg=(), stop_after=None, only=None, ext_in=()):
    nc = bass.Bass("TRN2", target_bir_lowering=False)
    tr = Trk(nc)
    dbg = set(dbg)

    def din(name, shape, dt=F32):
        return nc.dram_tensor(name, list(shape), dt, kind="ExternalInput").ap()

    def dscr(name, shape, dt):
        if name in ext_in: return nc.dram_tensor(name, list(shape), dt, kind="ExternalInput").ap()
        if name in dbg: return nc.dram_tensor(name, list(shape), dt, kind="ExternalOutput").ap()
        return nc.dram_tensor(name, list(shape), dt).ap()

    x_in = din('x', [T, D]); ctx_in = din('ctx', [NCX, D]); svec_in = din('svec', [128, KC, 2])
    out_d = nc.dram_tensor('out', [T, D], F32, kind="ExternalOutput").ap()
    WSH = dict(wgu1=[FC, 128, 2, KC, 128], wd1=[KC, 128, FC, 128], wgu2=[FC, 128, 2, KC, 128], wd2=[KC, 128, FC, 128],
               winfm=[45, 128, KC, 128], wintm=[128, KC, 640], wada=[72, 128, KC, 128], wglu=[3, 128, 3, 128],
               wps=[8, 128, 3, 128], wpg=[8, 64, 8, 128], wpn=[8, 64, 8, 128], wout=[8, 128, KC, 128])
    WORDER = ['wada', 'wgu1', 'wd1', 'winfm', 'wintm', 'wglu', 'wps', 'wpg', 'wpn', 'wout', 'wgu2', 'wd2']
    w_f = {k: din(k, [DEPTH] + v) for k, v in WSH.items()}
    w_b = {(k, l): dscr('%s_b%d' % (k, l), v, BF16) for k, v in WSH.items() for l in range(DEPTH)}
    w_buf = {(k, l): Buf('wb_%s%d' % (k, l)) for k in WSH for l in range(DEPTH)}
    bada_in = din('bada', [DEPTH, 128, 72]); normg_in = din('normg', [DEPTH, 128, 3, KC]); fing_in = din('fing', [128, KC])
    sa_in = din('ssm_a', [DEPTH, 2, 128, 3, 12])
    sb_in = din('ssm_b', [DEPTH, 2, 2, 128, 12, 16]); sc_in = din('ssm_c', [DEPTH, 2, 2, 128, 3, 64])
    sd_in = din('ssm_d', [DEPTH, 128, 3]); sink_in = din('sink', [DEPTH, 64, 8])
    navt_in = din('navt', [DEPTH, 8, 128, 14, 64]); naod_in = din('naod', [DEPTH, 8, 128, 5, 64]); nacb_in = din('nacb', [DEPTH, 8, 128, 5, 128])
    ident_in = din('ident', [128, 128]); rope_in = din('rope', [4, 128, T]); gmask_in = din('gmask', [128, 2, 128])
    mask2_in = din('mask2', [128, 4, 8, 16]); maskc_in = din('maskc', [128, 4, 2, 64]); iota64_in = din('iota64', [128, 2, 64]); iota9_in = din('iota9', [128, 9])

    XT = dscr('XT', [D, TT], F32)
    U = dscr('U', [384, TT], F32); TG = dscr('TG', [384, TT], F32)
    QG = dscr('QG', [8, 64, T], BF16); QGC = dscr('QGC', [8, 64, NCX], BF16)
    KG = dscr('KG', [2, 64, T], BF16); KGC = dscr('KGC', [2, 64, NCX], BF16)
    QN = dscr('QN', [8, 64, T], BF16); QNC = dscr('QNC', [8, 64, NCX], BF16)
    KN = dscr('KN', [8, 64, T], BF16); KNC = dscr('KNC', [8, 64, NCX], BF16)
    VG = dscr('VG', [TT, 128], BF16); VN = dscr('VN', [TT, 512], BF16)
    SG = dscr('SG', [3072, TT], BF16)
    YG = dscr('YG', [8, 64, T], BF16); YGC = dscr('YGC', [8, 64, NCX], BF16)
    YN = dscr('YN', [8, 64, T], BF16); YNC = dscr('YNC', [8, 64, NCX], BF16)

    def uname(n):
        tr.uid += 1; return '%s_%d' % (n, tr.uid)

    class Phase:
        def __init__(s, reset=True):
            s.es = contextlib.ExitStack()
            if reset: tr.dnext = 0; tr.qnext = 0

        def sb(s, name, shape, dt=F32, dma=False):
            t = s.es.enter_context(nc.sbuf_tensor(uname(name), list(shape), dt))
            return Buf(name, t, tr.dsem() if dma else None, tr.qsem() if dma else None)

        def ps(s, name, shape=(128, 512), dt=F32):
            t = s.es.enter_context(nc.psum_tensor(uname(name), list(shape), dt))
            return Buf(name, t)

        def close(s):
            tr.barrier(); s.es.close()

    V = nc.vector; A = nc.scalar; G = nc.gpsimd; PE = nc.tensor

    G1 = ['wada', 'wgu1', 'wd1', 'winfm', 'wintm']
    G2 = ['wglu', 'wps', 'wpg', 'wpn', 'wout', 'wgu2', 'wd2']

    def emit_casts(l, keys):
        if only is not None: return
        for k in keys:
            sem = tr.new_sem('c_%s%d' % (k, l))
            n = int(np.prod(WSH[k]))
            src = w_f[k][l]; dst = w_b[(k, l)]
            names = 'abcde'[:len(WSH[k])]
            pat = ' '.join(names)
            fs = src.rearrange('%s -> (%s)' % (pat, pat)).rearrange('(p n) -> p n', p=128)
            fd = dst.rearrange('%s -> (%s)' % (pat, pat)).rearrange('(p n) -> p n', p=128)
            cols = n // 128
            npieces = max(1, -(-cols // 16384))
            step = -(-cols // npieces)
            first = True
            for c0 in range(0, cols, step):
                c1 = min(cols, c0 + step)
                tr.dma('pool', fd[:, c0:c1], fs[:, c0:c1], writes=[w_buf[(k, l)]] if first else (),
                       acc=() if first else [w_buf[(k, l)]], sem=sem)
                first = False

    emit_casts(0, G1)

    gp = Phase()
    ident = gp.sb('ident', [128, 128], F32, dma=True)
    tr.dma('sp', ident[:], ident_in[:, :], writes=[ident])
    ones_b = gp.sb('ones_b', [128, 128], BF16)
    tr.op('dve', lambda: V.memset(ones_b[:], 1.0), writes=[ones_b])
    svec = gp.sb('svec', [128, KC, 2], F32, dma=True)
    tr.dma('sp', svec[:], svec_in[:, :, :], writes=[svec])
    svb = gp.sb('svb', [128, KC, 2], BF16)
    tr.op('act', lambda: A.activation(out=svb[:], in_=svec[:], func=AF.Silu), reads=[svec], writes=[svb])
    fing = gp.sb('fing', [128, KC], F32, dma=True)
    tr.dma('sp', fing[:], fing_in[:, :], writes=[fing])
    modA = gp.sb('modA', [128, 3, KC, 2]); modB = gp.sb('modB', [128, 3, KC, 2]); modG = gp.sb('modG', [128, 3, KC, 2])

    def compute_mod(l):
        ph = Phase()
        bada = ph.sb('bada', [128, 72], F32, dma=True); tr.dma('sp', bada[:], bada_in[l], writes=[bada])
        normg = ph.sb('normg', [128, 3, KC], F32, dma=True); tr.dma('sp', normg[:], normg_in[l], writes=[normg])
        wr = [ph.sb('wada%d' % i, [128, 8, KC, 128], BF16, dma=True) for i in range(2)]
        mps = ph.ps('mps', [128, 72, 2])
        mod = ph.sb('mod', [128, 72, 2])
        for jg in range(9):
            wbuf = wr[jg % 2]
            tr.dma('sp', wbuf[:], w_b[('wada', l)][jg * 8:(jg + 1) * 8].rearrange('j p k c -> p j k c'),
                   reads=[w_buf[('wada', l)]], writes=[wbuf])
            fns = []
            for jj in range(8):
                j = jg * 8 + jj
                for kc in range(KC):
                    fns.append(lambda j=j, jj=jj, kc=kc: PE.matmul(mps[:, j, :], lhsT=wbuf[:, jj, kc, :], rhs=svb[:, kc, :],
                                                                  start=(kc == 0), stop=(kc == KC - 1)))
            tr.group('pe', fns, reads=[wbuf, svb], writes=[mps] if jg == 0 else (), acc=() if jg == 0 else [mps])
        tr.op('dve', lambda: V.tensor_tensor(out=mod[:], in0=mps[:], in1=bada[:].unsqueeze(2).to_broadcast([128, 72, 2]), op=ALU.add),
              reads=[mps, bada], writes=[mod])
        for i in range(3):
            sh = mod[:, 8 * (3 * i):8 * (3 * i) + 8, :]; sc = mod[:, 8 * (3 * i + 1):8 * (3 * i + 1) + 8, :]
            gt = mod[:, 8 * (3 * i + 2):8 * (3 * i + 2) + 8, :]
            gb = normg[:, i, :].unsqueeze(2).to_broadcast([128, KC, 2])
            fns = [lambda sc=sc, i=i, gb=gb: V.scalar_tensor_tensor(out=modA[:, i], in0=sc, scalar=1.0, in1=gb, op0=ALU.add, op1=ALU.mult),
                   lambda sh=sh, i=i: V.tensor_copy(out=modB[:, i], in_=sh),
                   lambda gt=gt, i=i: V.tensor_scalar(out=modG[:, i], in0=gt, scalar1=(1.0 if i == 1 else 0.5), scalar2=None, op0=ALU.mult)]
            tr.group('dve', fns, reads=[mod, normg], writes=[modA, modB, modG] if i == 0 else (), acc=() if i == 0 else [modA, modB, modG])
        ph.close()

    def rms_mod(ph, xT, hT, N, site, col, ps_ss, tmpbig, sq, rstd):
        tr.op('act', lambda: A.activation(out=sq[:, :, :N], in_=xT[:, :, :N], func=AF.Square), reads=[xT], writes=[sq])
        tr.group('pe', [lambda kc=kc: PE.matmul(ps_ss[:, :N], lhsT=ones_b[:], rhs=sq[:, kc, :N], start=(kc == 0), stop=(kc == KC - 1))
                        for kc in range(KC)], reads=[sq, ones_b], writes=[ps_ss])
        tr.op('act', lambda: A.activation(out=rstd[:, :N], in_=ps_ss[:, :N], func=AF.Sqrt, scale=1.0 / D, bias=epsb[:, 0:1]),
              reads=[ps_ss, epsb], writes=[rstd])
        tr.op('dve', lambda: V.reciprocal(out=rstd[:, :N], in_=rstd[:, :N]), reads=[rstd], writes=[rstd])
        tr.op('dve', lambda: V.tensor_tensor(out=tmpbig[:, :, :N], in0=xT[:, :, :N],
                                             in1=rstd[:, :N].unsqueeze(1).to_broadcast([128, KC, N]), op=ALU.mult),
              reads=[xT, rstd], writes=[tmpbig])
        tr.group('act', [lambda kc=kc: A.activation(out=hT[:, kc, :N], in_=tmpbig[:, kc, :N], func=AF.Identity,
                                                    scale=modA[:, site, kc, col:col + 1], bias=modB[:, site, kc, col:col + 1])
                         for kc in range(KC)], reads=[tmpbig, modA, modB], writes=[hT])

    def ffn(ph, l, which, xT, hT, aT, N, col, site, R):
        wgu_k, wd_k = ('wgu1', 'wd1') if which == 1 else ('wgu2', 'wd2')
        rms_mod(ph, xT, hT, N, site, col, R['ps_m'][0], R['tmpbig'], R['sq'], R['rstd'])
        for f in range(FC):
            wb = R['wgu'][R['i_wgu'] % 3]; R['i_wgu'] += 1
            tr.dma('sp', wb[:], w_b[(wgu_k, l)][f], reads=[w_buf[(wgu_k, l)]], writes=[wb])
            pg = R['ps_g'][f % 2]; pu = R['ps_u'][f % 2]
            tr.group('pe', [lambda kc=kc: PE.matmul(pg[:, :N], lhsT=wb[:, 0, kc, :], rhs=hT[:, kc, :N], start=(kc == 0), stop=(kc == KC - 1))
                            for kc in range(KC)], reads=[wb, hT], writes=[pg])
            tr.group('pe', [lambda kc=kc: PE.matmul(pu[:, :N], lhsT=wb[:, 1, kc, :], rhs=hT[:, kc, :N], start=(kc == 0), stop=(kc == KC - 1))
                            for kc in range(KC)], reads=[wb, hT], writes=[pu])
            sg = R['sgt'][f % 2]
            tr.op('act', lambda: A.activation(out=sg[:, :N], in_=pg[:, :N], func=AF.Silu), reads=[pg], writes=[sg])
            tr.op('dve', lambda: V.tensor_tensor(out=aT[:, f, :N], in0=sg[:, :N], in1=pu[:, :N], op=ALU.mult),
                  reads=[sg, pu], writes=[aT] if f == 0 else (), acc=() if f == 0 else [aT])
        for m in range(KC):
            wb = R['wd'][R['i_wd'] % 2]; R['i_wd'] += 1
            tr.dma('sp', wb[:], w_b[(wd_k, l)][m], reads=[w_buf[(wd_k, l)]], writes=[wb])
            pd = R['ps_m'][m % 2]
            tr.group('pe', [lambda f=f: PE.matmul(pd[:, :N], lhsT=wb[:, f, :], rhs=aT[:, f, :N], start=(f == 0), stop=(f == FC - 1))
                            for f in range(FC)], reads=[wb, aT], writes=[pd])
            tr.op('dve', lambda: V.scalar_tensor_tensor(out=xT[:, m, :N], in0=pd[:, :N], scalar=modG[:, site, m, col:col + 1],
                                                        in1=xT[:, m, :N], op0=ALU.mult, op1=ALU.add),
                  reads=[pd, modG, xT], acc=[xT])

    def ffn_bufs(ph):
        R = {}
        R['wgu'] = [ph.sb('wgu%d' % i, [128, 2, KC, 128], BF16, dma=True) for i in range(3)]
        R['wd'] = [ph.sb('wd%d' % i, [128, FC, 128], BF16, dma=True) for i in range(2)]
        R['i_wgu'] = 0; R['i_wd'] = 0
        R['ps_g'] = [ph.ps('psg%d' % i) for i in range(2)]
        R['ps_u'] = [ph.ps('psu%d' % i) for i in range(2)]
        R['ps_m'] = [ph.ps('psm%d' % i) for i in range(2)]
        R['sgt'] = [ph.sb('sgt%d' % i, [128, 512]) for i in range(2)]
        R['tmpbig'] = ph.sb('tmpbig', [128, KC, 512]); R['sq'] = ph.sb('sq', [128, KC, 512], BF16)
        R['rstd'] = ph.sb('rstd', [128, 512])
        return R

    tiles = [(0, NCX, 1)] + [(NCX + i * 512, 512, 0) for i in range(8)]

    def phase_A(l):
        ph = Phase()
        R = ffn_bufs(ph)
        xTs = [ph.sb('xT%d' % i, [128, KC, 512], F32, dma=True) for i in range(2)]
        hT = ph.sb('hT', [128, KC, 512], BF16); aT = ph.sb('aT', [128, FC, 512], BF16)
        win = [ph.sb('win%d' % i, [128, KC, 128], BF16, dma=True) for i in range(3)]
        wtm = ph.sb('wtm', [128, KC, 640], BF16, dma=True)
        tr.dma('sp', wtm[:], w_b[('wintm', l)], reads=[w_buf[('wintm', l)]], writes=[wtm])
        rp = [ph.sb('rope%d' % i, [128, 4, 512], F32, dma=True) for i in range(2)]
        stf = [ph.sb('stf%d' % i, [128, 512], F32, dma=True) for i in range(2)]
        stb = [ph.sb('stb%d' % i, [128, 640], BF16, dma=True) for i in range(3)]
        t1 = ph.sb('t1', [128, 512]); t2 = ph.sb('t2', [128, 512])
        xin = [ph.sb('xin%d' % i, [128, D], F32, dma=True) for i in range(2)] if l == 0 else None
        ps_q = ph.ps('psq'); ps_q2 = ph.ps('psq2')
        cnt = dict(win=0, stf=0, stb=0, xin=0)

        def load_x(ti):
            c0, N, col = tiles[ti]; xT = xTs[ti % 2]
            if l > 0:
                tr.dma('sp', xT[:, :, :N], XT[:, c0:c0 + N].rearrange('(k p) t -> p k t', p=128), writes=[xT])
            else:
                for ts in range(N // 128):
                    xb = xin[cnt['xin'] % 2]; cnt['xin'] += 1
                    src = ctx_in[ts * 128:(ts + 1) * 128, :] if ti == 0 else x_in[c0 - NCX + ts * 128:c0 - NCX + (ts + 1) * 128, :]
                    tr.dma('sp', xb[:], src, writes=[xb])
                    for hf in range(2):
                        pt = R['ps_g'][hf]
                        tr.group('pe', [lambda k=k, hf=hf: PE.transpose(pt[:, k * 128:(k + 1) * 128], xb[:, (hf * 4 + k) * 128:(hf * 4 + k + 1) * 128], ident[:])
                                        for k in range(4)], reads=[xb, ident], writes=[pt])
                        tr.op('act', lambda hf=hf, pt=pt, ts=ts: A.activation(out=xT[:, hf * 4:hf * 4 + 4, ts * 128:(ts + 1) * 128],
                                                                               in_=pt[:].rearrange('p (k t) -> p k t', k=4), func=AF.Copy),
                              reads=[pt], writes=[xT] if (ts == 0 and hf == 0) else (), acc=() if (ts == 0 and hf == 0) else [xT])

        load_x(0)
        for ti in range(9):
            c0, N, col = tiles[ti]; xT = xTs[ti % 2]; lat = ti > 0
            if lat:
                rpb = rp[ti % 2]
                tr.dma('sp', rpb[:], rope_in[:, :, c0 - NCX:c0 - NCX + 512].rearrange('a p t -> p a t'), writes=[rpb])
            ffn(ph, l, 1, xT, hT, aT, N, col, 0, R)
            if ti + 1 < 9: load_x(ti + 1)
            rms_mod(ph, xT, hT, N, 1, col, R['ps_m'][0], R['tmpbig'], R['sq'], R['rstd'])
            tr.dma('pool', XT[:, c0:c0 + N].rearrange('(k p) t -> p k t', p=128), xT[:, :, :N], reads=[xT])
            ci = 0
            while ci < len(FM_CHUNKS):
                kind, i = FM_CHUNKS[ci]

                def wmm(ci, pbuf):
                    wb = win[cnt['win'] % 3]; cnt['win'] += 1
                    tr.dma('sp', wb[:], w_b[('winfm', l)][ci], reads=[w_buf[('winfm', l)]], writes=[wb])
                    tr.group('pe', [lambda kc=kc: PE.matmul(pbuf[:, :N], lhsT=wb[:, kc, :], rhs=hT[:, kc, :N], start=(kc == 0), stop=(kc == KC - 1))
                                    for kc in range(KC)], reads=[wb, hT], writes=[pbuf])

                if kind in ('gq', 'gk') and lat:
                    ci2 = ci + (4 if kind == 'gq' else 1)
                    wmm(ci, ps_q); wmm(ci2, ps_q2)
                    o = 0 if kind == 'gq' else 2
                    st = stb[cnt['stb'] % 3]; cnt['stb'] += 1
                    tr.op('dve', lambda: V.tensor_tensor(out=t1[:], in0=ps_q[:], in1=rpb[:, o, :], op=ALU.mult), reads=[ps_q, rpb], writes=[t1])
                    tr.op('dve', lambda: V.tensor_tensor(out=t2[:], in0=ps_q2[:], in1=rpb[:, o + 1, :], op=ALU.mult), reads=[ps_q2, rpb], writes=[t2])
                    tr.op('pool', lambda: G.tensor_tensor(out=st[:, :512], in0=t1[:], in1=t2[:], op=ALU.add), reads=[t1, t2], writes=[st])
                    dst = QG[2 * i:2 * i + 2] if kind == 'gq' else KG[0:2]
                    tr.dma('pool', dst.rearrange('h d t -> (h d) t')[:, c0 - NCX:c0 - NCX + 512], st[:, :512], reads=[st])
                elif kind in ('gq2', 'gk2'):
                    pass
                else:
                    pb = R['ps_m'][ci % 2]
                    wmm(ci, pb)
                    if kind == 'u':
                        st = stf[cnt['stf'] % 2]; cnt['stf'] += 1
                        tr.op('act', lambda: A.activation(out=st[:, :N], in_=pb[:, :N], func=AF.Copy), reads=[pb], writes=[st])
                        tr.dma('pool', U[i * 128:(i + 1) * 128, c0:c0 + N], st[:, :N], reads=[st])
                    else:
                        st = stb[cnt['stb'] % 3]; cnt['stb'] += 1
                        if kind == 'gates':
                            tr.op('act', lambda: A.activation(out=st[:, :N], in_=pb[:, :N], func=AF.Sigmoid), reads=[pb], writes=[st])
                            tr.dma('pool', SG[i * 128:(i + 1) * 128, c0:c0 + N], st[:, :N], reads=[st])
                        else:
                            sc = 0.125 if kind in ('gq', 'nq') else 1.0
                            tr.op('act', lambda: A.activation(out=st[:, :N], in_=pb[:, :N], func=AF.Copy, scale=sc), reads=[pb], writes=[st])
                            if lat:
                                dst = {'nq': QN, 'nk': KN}[kind][2 * i:2 * i + 2].rearrange('h d t -> (h d) t')[:, c0 - NCX:c0 - NCX + 512]
                            else:
                                dd = {'gq': QGC, 'gk': KGC, 'nq': QNC, 'nk': KNC}[kind]
                                dst = (dd[2 * i:2 * i + 2] if kind != 'gk' else dd[0:2]).rearrange('h d t -> (h d) t')
                            tr.dma('pool', dst, st[:, :N], reads=[st])
                ci += 1
            for ts in range(N // 128):
                pv = R['ps_g'][ts % 2]; pv2 = R['ps_u'][ts % 2]
                tr.group('pe', [lambda kc=kc: PE.matmul(pv[:, :128], lhsT=hT[:, kc, ts * 128:(ts + 1) * 128], rhs=wtm[:, kc, 0:128],
                                                        start=(kc == 0), stop=(kc == KC - 1)) for kc in range(KC)], reads=[hT, wtm], writes=[pv])
                tr.group('pe', [lambda kc=kc: PE.matmul(pv2[:, :512], lhsT=hT[:, kc, ts * 128:(ts + 1) * 128], rhs=wtm[:, kc, 128:640],
                                                        start=(kc == 0), stop=(kc == KC - 1)) for kc in range(KC)], reads=[hT, wtm], writes=[pv2])
                st = stb[cnt['stb'] % 3]; cnt['stb'] += 1
                tr.op('act', lambda: A.activation(out=st[:, 0:128], in_=pv[:, 0:128], func=AF.Copy), reads=[pv], writes=[st])
                tr.op('dve', lambda: V.tensor_copy(out=st[:, 128:640], in_=pv2[:, :512]), reads=[pv2], acc=[st])
                r0 = c0 + ts * 128
                tr.dma('pool', VG[r0:r0 + 128, :], st[:, 0:128], reads=[st])
                tr.dma('pool', VN[r0:r0 + 128, :], st[:, 128:640], reads=[st])
        ph.close()

    def sin_rr(src, srcb, dst, dstb, shift, tA, tAb, tI, tIb):
        if shift:
            tr.op('dve', lambda: V.tensor_scalar(out=tA, in0=src, scalar1=shift, scalar2=None, op0=ALU.add), reads=[srcb], writes=[tAb])
            x, xb = tA, tAb
        else:
            x, xb = src, srcb
        tr.op('dve', lambda: V.tensor_scalar(out=tI, in0=x, scalar1=1.0 / (2 * PI), scalar2=None, op0=ALU.mult), reads=[xb], writes=[tIb])
        tr.op('dve', lambda: V.tensor_copy(out=dst, in_=tI), reads=[tIb], writes=[dstb])
        tr.op('dve', lambda: V.scalar_tensor_tensor(out=dst, in0=dst, scalar=-2 * PI, in1=x, op0=ALU.mult, op1=ALU.add), reads=[dstb, xb], writes=[dstb])
        tr.op('dve', lambda: V.tensor_scalar(out=dst, in0=dst, scalar1=-PI, scalar2=PI, op0=ALU.max, op1=ALU.min), reads=[dstb], writes=[dstb])
        tr.op('act', lambda: A.activation(out=dst, in_=dst, func=AF.Sin), reads=[dstb], writes=[dstb])

    def phase_B(l):
        ph = Phase()
        io64 = ph.sb('io64', [128, 2, 64], F32, dma=True); tr.dma('sp', io64[:], iota64_in[:, :, :], writes=[io64])
        io9 = ph.sb('io9', [128, 9], F32, dma=True); tr.dma('sp', io9[:], iota9_in[:, :], writes=[io9])
        m2 = ph.sb('mask2', [128, 4, 8, 16], F32, dma=True); tr.dma('sp', m2[:], mask2_in[:, :, :, :], writes=[m2])
        mcm = ph.sb('maskc', [128, 4, 2, 64], F32, dma=True); tr.dma('sp', mcm[:], maskc_in[:, :, :, :], writes=[mcm])
        sdt = ph.sb('sd', [128, 3], F32, dma=True); tr.dma('sp', sdt[:], sd_in[l], writes=[sdt])
        identb = ph.sb('identb', [128, 128], BF16)
        tr.op('act', lambda: A.activation(out=identb[:], in_=ident[:], func=AF.Copy), reads=[ident], writes=[identb])
        pst = ph.ps('pst', [128, 128])
        prm = {}
        BP = int(os.environ.get('BPREP', '99'))
        if BP == 1: ph.close(); return
        pers = {}
        for d in range(2):
            pers[d] = dict(w=ph.sb('w%d' % d, [128, 16, 12]), bb=ph.sb('bb%d' % d, [128, 2, 12, 16]), RK=ph.sb('rk%d' % d, [128, 12, 9]),
                           LRk=ph.sb('lrk%d' % d, [128, 12, 9]), LIk=ph.sb('lik%d' % d, [128, 12, 9]),
                           C2=ph.sb('c2_%d' % d, [128, 12, 2, 64]), S2=ph.sb('s2_%d' % d, [128, 12, 2, 64]),
                           scr=ph.sb('scr%d' % d, [128, 3, 64], F32, dma=True), sci=ph.sb('sci%d' % d, [128, 3, 64], F32, dma=True))
        pp = Phase(reset=False)
        A64 = pp.sb('a64', [128, 12, 64]); TA64 = pp.sb('ta64', [128, 12, 64]); TI64 = pp.sb('ti64', [128, 12, 64], I32)
        C64 = pp.sb('c64', [128, 12, 64]); S64 = pp.sb('s64', [128, 12, 64])
        for d in range(2):
            PD = pers[d]
            sa = pp.sb('sa%d' % d, [128, 3, 12], F32, dma=True); tr.dma('sp', sa[:], sa_in[l, d], writes=[sa])
            sbr = pp.sb('sbr%d' % d, [128, 12, 16], F32, dma=True); tr.dma('sp', sbr[:], sb_in[l, d, 0], writes=[sbr])
            sbi = pp.sb('sbi%d' % d, [128, 12, 16], F32, dma=True); tr.dma('sp', sbi[:], sb_in[l, d, 1], writes=[sbi])
            scr_ = PD['scr']; tr.dma('sp', scr_[:], sc_in[l, d, 0], writes=[scr_])
            sci = PD['sci']; tr.dma('sp', sci[:], sc_in[l, d, 1], writes=[sci])
            w = PD['w']
            ki = pp.sb('ki%d' % d, [128, 12], I32)
            are = sa[:, 0, :]; aim = sa[:, 1, :]; ldt = sa[:, 2, :]
            DT, ARDT, RR, TH, KF, THR, CX, CM, CO, SI, LR, LI = [w[:, i, :] for i in range(12)]
            N1, N2, DEN, T3 = [w[:, 12 + i, :] for i in range(4)]
            tr.op('act', lambda: A.activation(out=DT, in_=ldt, func=AF.Exp), reads=[sa], writes=[w])
            tr.op('dve', lambda: V.tensor_tensor(out=ARDT, in0=are, in1=DT, op=ALU.mult), reads=[w, sa], acc=[w])
            tr.op('act', lambda: A.activation(out=RR, in_=ARDT, func=AF.Exp), reads=[w], acc=[w])
            tr.op('dve', lambda: V.tensor_tensor(out=TH, in0=aim, in1=DT, op=ALU.mult), reads=[w, sa], acc=[w])
            tr.op('dve', lambda: V.tensor_scalar(out=ki[:], in0=TH, scalar1=1.0 / (2 * PI), scalar2=None, op0=ALU.mult), reads=[w], writes=[ki])
            tr.op('dve', lambda: V.tensor_copy(out=KF, in_=ki[:]), reads=[ki], acc=[w])
            tr.op('dve', lambda: V.scalar_tensor_tensor(out=THR, in0=KF, scalar=-2 * PI, in1=TH, op0=ALU.mult, op1=ALU.add), reads=[w], acc=[w])
            tr.op('dve', lambda: V.tensor_scalar(out=THR, in0=THR, scalar1=-PI, scalar2=PI, op0=ALU.max, op1=ALU.min), reads=[w], acc=[w])
            tr.op('dve', lambda: V.tensor_scalar(out=CX, in0=THR, scalar1=PI / 2, scalar2=None, op0=ALU.add), reads=[w], acc=[w])
            tr.op('dve', lambda: V.tensor_scalar(out=CM, in0=CX, scalar1=PI, scalar2=-2 * PI, op0=ALU.is_gt, op1=ALU.mult), reads=[w], acc=[w])
            tr.op('dve', lambda: V.tensor_tensor(out=CX, in0=CX, in1=CM, op=ALU.add), reads=[w], acc=[w])
            tr.op('dve', lambda: V.tensor_scalar(out=CX, in0=CX, scalar1=-PI, scalar2=PI, op0=ALU.max, op1=ALU.min), reads=[w], acc=[w])
            tr.op('act', lambda: A.activation(out=CO, in_=CX, func=AF.Sin), reads=[w], acc=[w])
            tr.op('act', lambda: A.activation(out=SI, in_=THR, func=AF.Sin), reads=[w], acc=[w])
            tr.op('dve', lambda: V.tensor_tensor(out=LR, in0=RR, in1=CO, op=ALU.mult), reads=[w], acc=[w])
            tr.op('dve', lambda: V.tensor_scalar(out=LR, in0=LR, scalar1=-1.0, scalar2=None, op0=ALU.add), reads=[w], acc=[w])
            tr.op('dve', lambda: V.tensor_tensor(out=LI, in0=RR, in1=SI, op=ALU.mult), reads=[w], acc=[w])
            tr.op('dve', lambda: V.tensor_tensor(out=N1, in0=LR, in1=are, op=ALU.mult), reads=[w, sa], acc=[w])
            tr.op('dve', lambda: V.tensor_tensor(out=T3, in0=LI, in1=aim, op=ALU.mult), reads=[w, sa], acc=[w])
            tr.op('dve', lambda: V.tensor_tensor(out=N1, in0=N1, in1=T3, op=ALU.add), reads=[w], acc=[w])
            tr.op('dve', lambda: V.tensor_tensor(out=N2, in0=LI, in1=are, op=ALU.mult), reads=[w, sa], acc=[w])
            tr.op('dve', lambda: V.tensor_tensor(out=T3, in0=LR, in1=aim, op=ALU.mult), reads=[w, sa], acc=[w])
            tr.op('dve', lambda: V.tensor_tensor(out=N2, in0=N2, in1=T3, op=ALU.subtract), reads=[w], acc=[w])
            tr.op('dve', lambda: V.tensor_tensor(out=DEN, in0=are, in1=are, op=ALU.mult), reads=[w, sa], acc=[w])
            tr.op('dve', lambda: V.tensor_tensor(out=T3, in0=aim, in1=aim, op=ALU.mult), reads=[w, sa], acc=[w])
            tr.op('dve', lambda: V.tensor_tensor(out=DEN, in0=DEN, in1=T3, op=ALU.add), reads=[w], acc=[w])
            tr.op('dve', lambda: V.reciprocal(out=DEN, in_=DEN), reads=[w], acc=[w])
            tr.op('dve', lambda: V.tensor_tensor(out=N1, in0=N1, in1=DEN, op=ALU.mult), reads=[w], acc=[w])
            tr.op('dve', lambda: V.tensor_tensor(out=N2, in0=N2, in1=DEN, op=ALU.mult), reads=[w], acc=[w])
            bb = PD['bb']; tb = pp.sb('tb%d' % d, [128, 2, 12, 16])
            cre = N1.unsqueeze(2).to_broadcast([128, 12, 16]); cim = N2.unsqueeze(2).to_broadcast([128, 12, 16])
            tr.op('dve', lambda: V.tensor_tensor(out=bb[:, 0], in0=sbr[:], in1=cre, op=ALU.mult), reads=[w, sbr], writes=[bb])
            tr.op('dve', lambda: V.tensor_tensor(out=tb[:, 0], in0=sbi[:], in1=cim, op=ALU.mult), reads=[w, sbi], writes=[tb])
            tr.op('dve', lambda: V.tensor_tensor(out=bb[:, 0], in0=bb[:, 0], in1=tb[:, 0], op=ALU.subtract), reads=[bb, tb], acc=[bb])
            tr.op('dve', lambda: V.tensor_tensor(out=bb[:, 1], in0=sbi[:], in1=cre, op=ALU.mult), reads=[w, sbi], acc=[bb])
            tr.op('dve', lambda: V.tensor_tensor(out=tb[:, 1], in0=sbr[:], in1=cim, op=ALU.mult), reads=[w, sbr], acc=[tb])
            tr.op('dve', lambda: V.tensor_tensor(out=bb[:, 1], in0=bb[:, 1], in1=tb[:, 1], op=ALU.add), reads=[bb, tb], acc=[bb])
            if BP == 2: pp.close(); ph.close(); return
            ANG = pp.sb('ang9_%d' % d, [128, 12, 9]); TA9 = pp.sb('ta9_%d' % d, [128, 12, 9]); TI9 = pp.sb('ti9_%d' % d, [128, 12, 9], I32)
            COk = pp.sb('cok%d' % d, [128, 12, 9]); SIk = pp.sb('sik%d' % d, [128, 12, 9]); RK = PD['RK']
            LRk = PD['LRk']; LIk = PD['LIk']
            i9b = io9[:].unsqueeze(1).to_broadcast([128, 12, 9])
            tr.op('dve', lambda: V.tensor_tensor(out=ANG[:], in0=THR.unsqueeze(2).to_broadcast([128, 12, 9]), in1=i9b, op=ALU.mult), reads=[w, io9], writes=[ANG])
            sin_rr(ANG[:], ANG, SIk[:], SIk, 0.0, TA9[:], TA9, TI9[:], TI9)
            sin_rr(ANG[:], ANG, COk[:], COk, PI / 2, TA9[:], TA9, TI9[:], TI9)
            tr.op('dve', lambda: V.tensor_tensor(out=RK[:], in0=ARDT.unsqueeze(2).to_broadcast([128, 12, 9]), in1=i9b, op=ALU.mult), reads=[w, io9], writes=[RK])
            tr.op('act', lambda: A.activation(out=RK[:], in_=RK[:], func=AF.Exp), reads=[RK], writes=[RK])
            tr.op('dve', lambda: V.tensor_tensor(out=LRk[:], in0=RK[:], in1=COk[:], op=ALU.mult), reads=[RK, COk], writes=[LRk])
            tr.op('dve', lambda: V.tensor_tensor(out=LIk[:], in0=RK[:], in1=SIk[:], op=ALU.mult), reads=[RK, SIk], writes=[LIk])
            if BP == 3: pp.close(); ph.close(); return
            TH8 = pp.sb('th8_%d' % d, [128, 12]); TA8 = pp.sb('ta8_%d' % d, [128, 12]); TI8 = pp.sb('ti8_%d' % d, [128, 12], I32)
            tr.op('dve', lambda: V.tensor_scalar(out=TA8[:], in0=THR, scalar1=8.0, scalar2=None, op0=ALU.mult), reads=[w], writes=[TA8])
            tr.op('dve', lambda: V.tensor_scalar(out=TI8[:], in0=TA8[:], scalar1=1.0 / (2 * PI), scalar2=None, op0=ALU.mult), reads=[TA8], writes=[TI8])
            tr.op('dve', lambda: V.tensor_copy(out=TH8[:], in_=TI8[:]), reads=[TI8], writes=[TH8])
            tr.op('dve', lambda: V.scalar_tensor_tensor(out=TH8[:], in0=TH8[:], scalar=-2 * PI, in1=TA8[:], op0=ALU.mult, op1=ALU.add), reads=[TH8, TA8], writes=[TH8])
            tr.op('dve', lambda: V.tensor_tensor(out=A64[:], in0=TH8[:].unsqueeze(2).to_broadcast([128, 12, 64]),
                                                 in1=io64[:, d, :].unsqueeze(1).to_broadcast([128, 12, 64]), op=ALU.mult), reads=[TH8, io64], writes=[A64])
            sin_rr(A64[:], A64, S64[:], S64, 0.0, TA64[:], TA64, TI64[:], TI64)
            sin_rr(A64[:], A64, C64[:], C64, PI / 2, TA64[:], TA64, TI64[:], TI64)
            if BP == 4: pp.close(); ph.close(); return
            C2 = PD['C2']; S2 = PD['S2']
            tr.group('act', [lambda: A.activation(out=C2[:, :, 0, :], in_=C64[:], func=AF.Copy),
                             lambda: A.activation(out=C2[:, :, 1, :], in_=C64[:], func=AF.Copy)], reads=[C64], writes=[C2])
            tr.group('act', [lambda: A.activation(out=S2[:, :, 0, :], in_=S64[:], func=AF.Copy),
                             lambda: A.activation(out=S2[:, :, 1, :], in_=S64[:], func=AF.Copy, scale=-1.0)], reads=[S64], writes=[S2])
            if BP == 5: pp.close(); ph.close(); return
            prm[d] = dict(w=w, LRk=LRk, LIk=LIk, RK=RK, C2=C2, S2=S2, bb=bb, scr=scr_, sci=sci)

        pp.close()
        BCUT = int(os.environ.get('BCUT', '99'))
        if BCUT == 0: ph.close(); return
        yacc = ph.sb('yacc', [128, TT], F32, dma=True); ub = ph.sb('ub', [128, TT], BF16)
        XHs = [ph.sb('XH%d' % i, [128, 4, 2, 8, 128], BF16) for i in range(2)]
        XTs = [ph.sb('XTt%d' % i, [128, 4, 2, 8, 128], BF16) for i in range(2)]
        Kbs = [ph.sb('Kb%d' % i, [128, 8, 128], BF16) for i in range(2)]
        Bstg = ph.sb('bstg4', [128, 4, 2, 128]); Cf = ph.sb('cf4', [128, 4, 2, 128]); cpad = ph.sb('cpad4', [128, 4, 2, 128], BF16)
        stg = [ph.sb('stgc%d' % i, [128, 128]) for i in range(2)]
        tq = [ph.sb('tq%d' % i, [128, 8, 128]) for i in range(2)]
        pxt = ph.ps('pxt', [128, 8, 128], BF16)
        pk = [ph.ps('pk%d' % i, [128, 4, 128]) for i in range(2)]
        pv = [ph.ps('pv%d' % i, [128, 4, 2, 64]) for i in range(2)]
        po = ph.ps('po', [128, 8, 64])
        t1 = ph.sb('bt1', [128, 4, 2, 64]); t2 = ph.sb('bt2', [128, 4, 2, 64]); Wt = ph.sb('bW', [128, 4, 2, 64]); Zt = ph.sb('bZ', [128, 4, 2, 64])
        Sf = [ph.sb('bSf%d' % i, [128, 4, 2, 64]) for i in range(2)]
        Sp = [ph.sb('bSp%d' % i, [128, 4, 2, 64], BF16) for i in range(2)]
        segs = [(0, 32)] + [(NCX + i * 512, 64) for i in range(8)]
        scnt = dict(n=0)

        def make_setup(j3, d, slot):
            P = prm[d]; XH = XHs[slot]; XT = XTs[slot]; Kb = Kbs[slot]
            steps = []

            def stage_s(s4):
                s = 4 * j3 + s4
                for ri in range(2):
                    first = (s4 == 0 and ri == 0)
                    tr.op('dve', lambda: V.tensor_tensor(out=Bstg[:, s4, ri, :].rearrange('p (a b) -> p a b', a=8),
                                                         in0=P['bb'][:, ri, s, :].unsqueeze(1).to_broadcast([128, 8, 16]),
                                                         in1=m2[:, s % 4], op=ALU.mult), reads=[P['bb'], m2],
                          writes=[Bstg] if first else (), acc=() if first else [Bstg])
                    sg_ = stg[scnt['n'] % 2]; scnt['n'] += 1
                    csrc = (P['scr'] if ri == 0 else P['sci'])
                    tr.op('dve', lambda: V.tensor_tensor(out=sg_[:].rearrange('p (a b) -> p a b', a=2),
                                                         in0=csrc[:, s // 4, :].unsqueeze(1).to_broadcast([128, 2, 64]),
                                                         in1=mcm[:, s % 4], op=ALU.mult), reads=[csrc, mcm], writes=[sg_])
                    tr.op('pe', lambda: PE.transpose(pst[:], sg_[:], ident[:]), reads=[sg_, ident], writes=[pst])
                    tr.op('act', lambda: A.activation(out=Cf[:, s4, ri, :], in_=pst[:], func=AF.Copy), reads=[pst],
                          writes=[Cf] if first else (), acc=() if first else [Cf])
                    tr.op('act', lambda: A.activation(out=cpad[:, s4, ri, :], in_=pst[:], func=AF.Copy, scale=(1.0 if ri == 0 else -1.0)),
                          reads=[pst], writes=[cpad] if first else (), acc=() if first else [cpad])

            def scaled_s(dstbuf, src, k0, neg_im, s4):
                s = 4 * j3 + s4
                sre = src[:, s4, 0, :].unsqueeze(1).to_broadcast([128, 8, 128]); sim = src[:, s4, 1, :].unsqueeze(1).to_broadcast([128, 8, 128])
                lr = P['LRk'][:, s, k0:k0 + 8].unsqueeze(2).to_broadcast([128, 8, 128])
                li = P['LIk'][:, s, k0:k0 + 8].unsqueeze(2).to_broadcast([128, 8, 128])
                first = (s4 == 0)
                tr.op('dve', lambda: V.tensor_tensor(out=tq[0][:], in0=sre, in1=lr, op=ALU.mult), reads=[src, P['LRk']], writes=[tq[0]])
                tr.op('dve', lambda: V.tensor_tensor(out=tq[1][:], in0=sim, in1=li, op=ALU.mult), reads=[src, P['LIk']], writes=[tq[1]])
                tr.op('dve', lambda: V.tensor_tensor(out=dstbuf[:, s4, 0], in0=tq[0][:], in1=tq[1][:], op=ALU.subtract), reads=[tq[0], tq[1]],
                      writes=[dstbuf] if first else (), acc=() if first else [dstbuf])
                tr.op('dve', lambda: V.tensor_tensor(out=tq[0][:], in0=sre, in1=li, op=ALU.mult), reads=[src, P['LIk']], writes=[tq[0]])
                tr.op('dve', lambda: V.tensor_tensor(out=tq[1][:], in0=sim, in1=lr, op=ALU.mult), reads=[src, P['LRk']], writes=[tq[1]])
                if neg_im:
                    tr.op('dve', lambda: V.scalar_tensor_tensor(out=dstbuf[:, s4, 1], in0=tq[0][:], scalar=-1.0, in1=tq[1][:], op0=ALU.mult, op1=ALU.subtract),
                          reads=[tq[0], tq[1]], acc=[dstbuf])
                else:
                    tr.op('dve', lambda: V.tensor_tensor(out=dstbuf[:, s4, 1], in0=tq[0][:], in1=tq[1][:], op=ALU.add), reads=[tq[0], tq[1]], acc=[dstbuf])

            def kmm(half):
                pkk = pk[half]
                fns = []
                for tt in range(4):
                    tau = half * 4 + tt
                    k = 0
                    for s4 in range(4):
                        for ri in range(2):
                            fns.append(lambda tt=tt, tau=tau, s4=s4, ri=ri, k=k: PE.matmul(pkk[:, tt, :], lhsT=XH[:, s4, ri, tau, :], rhs=cpad[:, s4, ri, :],
                                                                                             start=(k == 0), stop=(k == 7)))
                            k += 1
                tr.group('pe', fns, reads=[XH, cpad], writes=[pkk])
                tr.op('act', lambda: A.activation(out=Kb[:, half * 4:half * 4 + 4, :], in_=pkk[:], func=AF.Copy), reads=[pkk],
                      writes=[Kb] if half == 0 else (), acc=() if half == 0 else [Kb])

            def xtr(s4, ri):
                tr.group('pe', [lambda tau=tau: PE.transpose(pxt[:, tau, :], XH[:, s4, ri, tau, :], identb[:]) for tau in range(8)],
                         reads=[XH, identb], writes=[pxt])
                first = (s4 == 0 and ri == 0)
                tr.op('act' if ri == 0 else 'dve',
                      (lambda: A.activation(out=XT[:, s4, ri], in_=pxt[:], func=AF.Copy)) if ri == 0 else (lambda: V.tensor_copy(out=XT[:, s4, ri], in_=pxt[:])),
                      reads=[pxt], writes=[XT] if first else (), acc=() if first else [XT])

            for s4 in range(4): steps.append(lambda s4=s4: stage_s(s4))
            for s4 in range(4): steps.append(lambda s4=s4: scaled_s(XH, Bstg, 0, False, s4))
            for half in range(2): steps.append(lambda half=half: kmm(half))
            for s4 in range(4):
                for ri in range(2): steps.append(lambda s4=s4, ri=ri: xtr(s4, ri))
            for s4 in range(4): steps.append(lambda s4=s4: scaled_s(XH, Cf, 1, True, s4))
            return steps

        def run_loop(j3, d, slot, pending):
            P = prm[d]; XH = XHs[slot]; XT = XTs[slot]; Kb = Kbs[slot]
            order = list(range(9)) if d == 0 else [0] + list(range(8, 0, -1))
            prev = None
            per = -(-len(pending) // 8) if pending else 0

            def vmm(oi):
                t0, nC = segs[order[oi]]
                pvv = pv[oi % 2]
                fns = []
                for s4 in range(4):
                    for ri in range(2):
                        for j in range(8):
                            tau = (7 - j) if d == 0 else j
                            fns.append(lambda s4=s4, ri=ri, j=j, tau=tau: PE.matmul(pvv[:, s4, ri, :nC], lhsT=XT[:, s4, ri, tau, :],
                                                                                     rhs=ub[:, t0 + j:t0 + j + 8 * (nC - 1) + 1:8], start=(j == 0), stop=(j == 7)))
                tr.group('pe', fns, reads=[XT, ub], writes=[pvv])

            vmm(0)
            for oi in range(9):
                t0, nC = segs[order[oi]]
                par = oi % 2
                pvv = pv[par]
                s0 = 4 * j3
                csl = slice(0, nC) if d == 0 else slice(64 - nC, 64)
                C2v = P['C2'][:, s0:s0 + 4, :, csl]; S2v = P['S2'][:, s0:s0 + 4, :, csl]
                tr.op('dve', lambda: V.tensor_tensor(out=t1[:, :, :, :nC], in0=pvv[:, :, :, :nC], in1=C2v, op=ALU.mult), reads=[pvv, P['C2']], writes=[t1])
                tr.op('dve', lambda: V.tensor_tensor(out=t2[:, :, :, :nC], in0=pvv[:, :, ::-1, :nC], in1=S2v, op=ALU.mult), reads=[pvv, P['S2']], writes=[t2])
                tr.op('dve', lambda: V.tensor_tensor(out=Wt[:, :, :, :nC], in0=t1[:, :, :, :nC], in1=t2[:, :, :, :nC], op=ALU.add), reads=[t1, t2], writes=[Wt])
                if oi + 1 < 9: vmm(oi + 1)
                fns = []
                for s4 in range(4):
                    rr = P['RK'][:, s0 + s4, 8:9].to_broadcast([128, nC])
                    for ri in range(2):
                        if prev is None: ini = 0.0
                        else:
                            pcol = (prev[1] - 1) if d == 0 else 0
                            ini = Sf[1 - par][:, s4, ri, pcol:pcol + 1]
                        if d == 0:
                            fns.append(lambda s4=s4, ri=ri, rr=rr, ini=ini: V.tensor_tensor_scan(out=Zt[:, s4, ri, :nC], data0=rr, data1=Wt[:, s4, ri, :nC],
                                                                                                 initial=ini, op0=ALU.mult, op1=ALU.add))
                        else:
                            fns.append(lambda s4=s4, ri=ri, rr=rr, ini=ini: V.tensor_tensor_scan(out=Zt[:, s4, ri, :nC][:, ::-1], data0=rr, data1=Wt[:, s4, ri, :nC][:, ::-1],
                                                                                                 initial=ini, op0=ALU.mult, op1=ALU.add))
                tr.group('dve', fns, reads=[Wt, P['RK']] + ([Sf[1 - par]] if prev is not None else []), writes=[Zt])
                tr.op('dve', lambda: V.tensor_tensor(out=t1[:, :, :, :nC], in0=Zt[:, :, :, :nC], in1=C2v, op=ALU.mult), reads=[Zt, P['C2']], writes=[t1])
                tr.op('dve', lambda: V.tensor_tensor(out=t2[:, :, :, :nC], in0=Zt[:, :, ::-1, :nC], in1=S2v, op=ALU.mult), reads=[Zt, P['S2']], writes=[t2])
                tr.op('dve', lambda: V.tensor_tensor(out=Sf[par][:, :, :, :nC], in0=t1[:, :, :, :nC], in1=t2[:, :, :, :nC], op=ALU.subtract), reads=[t1, t2], writes=[Sf[par]])
                spv = Sp[par]
                if d == 0:
                    f1 = lambda: A.activation(out=spv[:, :, :, 1:nC], in_=Sf[par][:, :, :, 0:nC - 1], func=AF.Copy)
                    cdst = spv[:, :, :, 0:1]
                else:
                    f1 = lambda: A.activation(out=spv[:, :, :, 0:nC - 1], in_=Sf[par][:, :, :, 1:nC], func=AF.Copy)
                    cdst = spv[:, :, :, nC - 1:nC]
                if prev is None:
                    f2 = lambda: A.activation(out=cdst, in_=Sf[par][:, :, :, 0:1], func=AF.Copy, scale=0.0)
                    rdl = [Sf[par]]
                else:
                    pcol = (prev[1] - 1) if d == 0 else 0
                    f2 = lambda: A.activation(out=cdst, in_=Sf[1 - par][:, :, :, pcol:pcol + 1], func=AF.Copy)
                    rdl = [Sf[par], Sf[1 - par]]
                tr.group('act', [f1, f2], reads=rdl, writes=[spv])
                for _ in range(per):
                    if pending: pending.pop(0)()
                fns = []
                for j in range(8):
                    ntap = (j + 1) if d == 0 else (8 - j)
                    hk = j if d == 0 else (7 - j)
                    tot = ntap + 8; k = 0
                    for tau in range(ntap):
                        off = (j - tau) if d == 0 else (j + tau)
                        fns.append(lambda j=j, tau=tau, off=off, k=k, tot=tot: PE.matmul(po[:, j, :nC], lhsT=Kb[:, tau, :], rhs=ub[:, t0 + off:t0 + off + 8 * (nC - 1) + 1:8],
                                                                                          start=(k == 0), stop=(k == tot - 1)))
                        k += 1
                    for s4 in range(4):
                        for ri in range(2):
                            fns.append(lambda j=j, s4=s4, ri=ri, hk=hk, k=k, tot=tot: PE.matmul(po[:, j, :nC], lhsT=XH[:, s4, ri, hk, :], rhs=spv[:, s4, ri, :nC],
                                                                                                  start=(k == 0), stop=(k == tot - 1)))
                            k += 1
                tr.group('pe', fns, reads=[Kb, ub, XH, spv], writes=[po])
                yv = yacc[:, t0:t0 + 8 * nC].rearrange('p (c j) -> p j c', j=8)
                tr.op('dve', lambda: V.tensor_tensor(out=yv, in0=yv, in1=po[:, :, :nC], op=ALU.add), reads=[po, yacc], acc=[yacc])
                prev = (oi, nC)
            while pending: pending.pop(0)()

        ulist = [(j3, d) for j3 in range(3) for d in range(2)]
        for st_ in make_setup(0, 0, 0): st_()
        for ui, (j3, d) in enumerate(ulist):
            if d == 0:
                tr.dma('sp', yacc[:], U[j3 * 128:(j3 + 1) * 128, :], writes=[yacc])
                tr.op('act', lambda: A.activation(out=ub[:], in_=yacc[:], func=AF.Copy), reads=[yacc], writes=[ub])
                tr.op('dve', lambda: V.tensor_scalar(out=yacc[:], in0=yacc[:], scalar1=sdt[:, j3:j3 + 1], scalar2=None, op0=ALU.mult), reads=[yacc, sdt], writes=[yacc])
            pending = make_setup(ulist[ui + 1][0], ulist[ui + 1][1], (ui + 1) % 2) if ui + 1 < len(ulist) else []
            run_loop(j3, d, ui % 2, pending)
            if d == 1:
                for g0 in range(0, TT, 512):
                    n = min(512, TT - g0); yv = yacc[:, g0:g0 + n]
                    a0b = tq[0]; a1b = tq[1]
                    a0 = tq[0][:].rearrange('p a b -> p (a b)'); a1 = tq[1][:].rearrange('p a b -> p (a b)')
                    tr.op('act', lambda: A.activation(out=a0[:, :n], in_=yv, func=AF.Square), reads=[yacc], writes=[a0b])
                    tr.op('dve', lambda: V.tensor_scalar(out=a0[:, :n], in0=a0[:, :n], scalar1=0.044715, scalar2=1.0, op0=ALU.mult, op1=ALU.add), reads=[a0b], writes=[a0b])
                    tr.op('dve', lambda: V.tensor_tensor(out=a1[:, :n], in0=a0[:, :n], in1=yv, op=ALU.mult), reads=[a0b, yacc], writes=[a1b])
                    tr.op('act', lambda: A.activation(out=a1[:, 512:512 + n], in_=a1[:, :n], func=AF.Sigmoid, scale=2.0 * math.sqrt(2.0 / PI)), reads=[a1b], writes=[a1b])
                    st = stg_o[(g0 // 512) % 2]
                    tr.op('dve', lambda: V.tensor_tensor(out=st[:, :n], in0=a1[:, 512:512 + n], in1=yv, op=ALU.mult), reads=[a1b, yacc], writes=[st])
                    tr.dma('pool', TG[j3 * 128:(j3 + 1) * 128, g0:g0 + n], st[:, :n], reads=[st])
        ph.close()

    gm = None
    stg_o = None

    def phase_C(l, ctx_out):
        nonlocal gm
        ph = Phase()
        gm = ph.sb('gm', [128, 2, 128], F32, dma=True); tr.dma('sp', gm[:], gmask_in[:, :, :], writes=[gm])
        snk = ph.sb('snk', [64, 8], F32, dma=True); tr.dma('sp', snk[:], sink_in[l], writes=[snk])
        esk = ph.sb('esk', [64, 8])
        tr.op('act', lambda: A.activation(out=esk[:], in_=snk[:], func=AF.Exp), reads=[snk], writes=[esk])
        vsb = ph.sb('vsb', [128, 34, 128], BF16, dma=True)
        tr.dma('sp', vsb[:], VG.rearrange('(c p) d -> p c d', p=128), writes=[vsb])
        kT = [ph.sb('kT%d' % g, [64, TT], BF16, dma=True) for g in range(2)]
        for g in range(2):
            tr.dma('sp', kT[g][:, 0:NCX], KGC[g], writes=[kT[g]])
            tr.dma('sp', kT[g][:, NCX:TT], KG[g], acc=[kT[g]])
        qb = [ph.sb('qb%d' % i, [64, 4, 128], BF16, dma=True) for i in range(3)]
        ps_s = [ph.ps('pss%d' % i) for i in range(4)]
        ps_o = [ph.ps('pso%d' % i) for i in range(2)]; ps_d = [ph.ps('psd%d' % i) for i in range(2)]
        pT = [ph.sb('pT%d' % i, [128, 512], BF16) for i in range(4)]
        tmpm = [ph.sb('tmpm%d' % i, [128, 512]) for i in range(2)]; den = ph.sb('den', [64, 512])
        ob = [ph.sb('ob%d' % i, [64, 4, 128], BF16, dma=True) for i in range(2)]
        units = []
        if ctx_out:
            for g in range(2):
                for bi in range(2): units.append(('c', g, bi))
        for g in range(2):
            for bi in range(32): units.append(('l', g, bi))
        items = []
        uinfo = []
        for ui, (kind, g, bi) in enumerate(units):
            keys = [(kT[g][:, c * 128:(c + 1) * 128], None, vsb[:, c, g * 64:(g + 1) * 64]) for c in range(2)]
            if kind == 'l':
                for dlt in (-1, 0, 1):
                    kb = bi + dlt
                    if kb < 0 or kb > 31: continue
                    m = None if dlt == 0 else gm[:, (0 if dlt == -1 else 1), :].unsqueeze(1).to_broadcast([128, 4, 128])
                    keys.append((kT[g][:, NCX + kb * 128:NCX + (kb + 1) * 128], m, vsb[:, 2 + kb, g * 64:(g + 1) * 64]))
            for ki, (k_ap, m_ap, v_ap) in enumerate(keys): items.append((ui, ki, len(keys), k_ap, m_ap, v_ap))
        cnt = dict(n=0, m=0)
        st = {}

        def stage1(ii):
            ui, ki, nk, k_ap, m_ap, v_ap = items[ii]
            kind, g, bi = units[ui]
            q = qb[ui % 3]
            if ki == 0:
                srcq = (QGC if kind == 'c' else QG)[4 * g:4 * g + 4, :, bi * 128:(bi + 1) * 128].rearrange('h d t -> d h t')
                tr.dma('sp', q[:], srcq, writes=[q])
            pss = ps_s[cnt['n'] % 4]; pt = pT[cnt['n'] % 4]; cnt['n'] += 1
            tr.op('pe', lambda: PE.matmul(pss[:, :], lhsT=k_ap, rhs=q[:], start=True, stop=True), reads=[q, kT[g]], writes=[pss])
            if m_ap is not None:
                tm = tmpm[cnt['m'] % 2]; cnt['m'] += 1
                tr.op('dve', lambda: V.tensor_tensor(out=tm[:].rearrange('p (h t) -> p h t', h=4), in0=pss[:].rearrange('p (h t) -> p h t', h=4),
                                                     in1=m_ap, op=ALU.add), reads=[pss, gm], writes=[tm])
                tr.op('act', lambda: A.activation(out=pt[:], in_=tm[:], func=AF.Exp), reads=[tm], writes=[pt])
            else:
                tr.op('act', lambda: A.activation(out=pt[:], in_=pss[:], func=AF.Exp), reads=[pss], writes=[pt])
            st[ii] = pt

        def stage2(ii):
            ui, ki, nk, k_ap, m_ap, v_ap = items[ii]
            kind, g, bi = units[ui]
            pt = st.pop(ii)
            pso = ps_o[ui % 2]; psd = ps_d[ui % 2]; o = ob[ui % 2]
            tr.op('pe', lambda: PE.matmul(pso[0:64, :], lhsT=v_ap, rhs=pt[:], start=(ki == 0), stop=(ki == nk - 1)),
                  reads=[pt, vsb], writes=[pso] if ki == 0 else (), acc=() if ki == 0 else [pso])
            tr.op('pe', lambda: PE.matmul(psd[0:64, :], lhsT=ones_b[:, 0:64], rhs=pt[:], start=(ki == 0), stop=(ki == nk - 1)),
                  reads=[pt, ones_b], writes=[psd] if ki == 0 else (), acc=() if ki == 0 else [psd])
            if ki == nk - 1:
                tr.op('dve', lambda: V.tensor_tensor(out=den[:].rearrange('p (h t) -> p h t', h=4), in0=psd[0:64, :].rearrange('p (h t) -> p h t', h=4),
                                                     in1=esk[:, 4 * g:4 * g + 4].unsqueeze(2).to_broadcast([64, 4, 128]), op=ALU.add),
                      reads=[psd, esk], writes=[den])
                tr.op('dve', lambda: V.reciprocal(out=den[:], in_=den[:]), reads=[den], writes=[den])
                tr.op('dve', lambda: V.tensor_tensor(out=o[:].rearrange('p h t -> p (h t)'), in0=pso[0:64, :], in1=den[:], op=ALU.mult),
                      reads=[pso, den], writes=[o])
                dst = (YGC if kind == 'c' else YG)[4 * g:4 * g + 4, :, bi * 128:(bi + 1) * 128].rearrange('h d t -> d h t')
                tr.dma('pool', dst, o[:], reads=[o])

        KD = 2
        for ii in range(len(items) + KD):
            if ii < len(items): stage1(ii)
            if ii - KD >= 0: stage2(ii - KD)
        ph.close()

    def attn_unit3(q, keys, ps_s, pso, psd, pT, tmpm, qbufs, kbufs, vbufs, epi):
        nk = len(keys)
        for ki, (k_ap, m_ap, v_ap) in enumerate(keys):
            pss = ps_s[ki % len(ps_s)]
            tr.op('pe', lambda: PE.matmul(pss[:, :], lhsT=k_ap, rhs=q[:], start=True, stop=True), reads=qbufs + kbufs, writes=[pss])
            pt = pT[ki % len(pT)]
            if m_ap is not None:
                tr.op('dve', lambda: V.tensor_tensor(out=tmpm[:].rearrange('p (h t) -> p h t', h=4), in0=pss[:].rearrange('p (h t) -> p h t', h=4),
                                                     in1=m_ap, op=ALU.add), reads=[pss, gm], writes=[tmpm])
                tr.op('act', lambda: A.activation(out=pt[:], in_=tmpm[:], func=AF.Exp), reads=[tmpm], writes=[pt])
            else:
                tr.op('act', lambda: A.activation(out=pt[:], in_=pss[:], func=AF.Exp), reads=[pss], writes=[pt])
            tr.op('pe', lambda: PE.matmul(pso[0:64, :], lhsT=v_ap, rhs=pt[:], start=(ki == 0), stop=(ki == nk - 1)),
                  reads=[pt] + vbufs, writes=[pso] if ki == 0 else (), acc=() if ki == 0 else [pso])
            tr.op('pe', lambda: PE.matmul(psd[0:64, :], lhsT=ones_b[:, 0:64], rhs=pt[:], start=(ki == 0), stop=(ki == nk - 1)),
                  reads=[pt, ones_b], writes=[psd] if ki == 0 else (), acc=() if ki == 0 else [psd])
        epi()

    def phase_D(l, ctx_out):
        ph = Phase()
        kT = [ph.sb('nkT%d' % i, [64, TT], BF16, dma=True) for i in range(2)]
        qT = [ph.sb('nqT%d' % i, [64, TT], BF16, dma=True) for i in range(2)]
        vs = [ph.sb('nvs%d' % i, [128, 34, 64], BF16, dma=True) for i in range(2)]
        vt = [ph.sb('nvt%d' % i, [128, 14, 64], F32, dma=True) for i in range(2)]
        od = [ph.sb('nod%d' % i, [128, 5, 64], F32, dma=True) for i in range(2)]
        cbt = [ph.sb('ncb%d' % i, [128, 5, 128], F32, dma=True) for i in range(2)]
        yb = [ph.sb('nyb%d' % i, [64, TT], BF16, dma=True) for i in range(2)]
        RD = 2
        ps_L = [ph.ps('npl%d' % i) for i in range(RD)]
        ps_X = [ph.ps('npx%d' % i) for i in range(RD)]
        ps_od = [ph.ps('npo%d' % i) for i in range(RD)]
        tmp = [ph.sb('ntmp%d' % i, [128, 640]) for i in range(RD)]
        pT = [ph.sb('npT%d' % i, [128, 640], BF16) for i in range(RD)]
        pC = [ph.sb('npC%d' % i, [128, 256], BF16) for i in range(RD)]
        rd = [ph.sb('nrd%d' % i, [64, 256]) for i in range(RD)]
        n = 0
        for h in range(8):
            k_ = kT[h % 2]; q_ = qT[h % 2]; v_ = vs[h % 2]; vt_ = vt[h % 2]; od_ = od[h % 2]; cb_ = cbt[h % 2]; y_ = yb[h % 2]
            tr.dma('sp', k_[:, 0:NCX], KNC[h], writes=[k_]); tr.dma('sp', k_[:, NCX:TT], KN[h], acc=[k_])
            tr.dma('sp', q_[:, 0:NCX], QNC[h], writes=[q_]); tr.dma('sp', q_[:, NCX:TT], QN[h], acc=[q_])
            tr.dma('sp', v_[:], VN[:, h * 64:(h + 1) * 64].rearrange('(c p) d -> p c d', p=128), writes=[v_])
            tr.dma('sp', vt_[:], navt_in[l, h], writes=[vt_]); tr.dma('sp', od_[:], naod_in[l, h], writes=[od_])
            tr.dma('sp', cb_[:], nacb_in[l, h], writes=[cb_])
            units = ([('c', 0), ('c', 1)] if ctx_out else []) + [('l', r) for r in range(4)] + [('p', r) for r in range(4, 60, 2)] + [('l', r) for r in range(60, 64)]
            for ui, (kind, r) in enumerate(units):
                psl = ps_L[n % RD]; psx = ps_X[n % RD]; pob = ps_od[n % RD]
                tm = tmp[n % RD]; pt = pT[n % RD]; pc = pC[n % RD]; rdn = rd[n % RD]; n += 1
                p0 = 0
                if kind == 'c':
                    Nq = 128; qap = q_[:, r * 128:(r + 1) * 128]; npair = 0; oc0 = r * 128
                elif kind == 'p':
                    Nq = 128; qap = q_[:, NCX + r * 64:NCX + (r + 2) * 64]; oc0 = NCX + r * 64
                    p0 = (r - 4) // 2; npair = 5
                else:
                    Nq = 64; qap = q_[:, NCX + r * 64:NCX + (r + 1) * 64]; oc0 = NCX + r * 64
                    rs = min(max(r - 4, 0), 56)
                    p0 = rs // 2; npair = 4; i0_ = rs - r + 7
                    bias = vt_[:, i0_:i0_ + 7:2, :]
                fns = [lambda c=c: PE.matmul(psx[:, 128 + c * Nq:128 + (c + 1) * Nq], lhsT=k_[:, c * 128:(c + 1) * 128], rhs=qap, start=True, stop=True) for c in range(2)]
                if npair == 5:
                    fns.append(lambda: PE.matmul(psx[:, 0:Nq], lhsT=k_[:, NCX + (p0 + 4) * 128:NCX + (p0 + 5) * 128], rhs=qap, start=True, stop=True))
                tr.group('pe', fns, reads=[k_, q_], writes=[psx])
                if npair:
                    tr.group('pe', [lambda k=k: PE.matmul(psl[:, k * Nq:(k + 1) * Nq], lhsT=k_[:, NCX + (p0 + k) * 128:NCX + (p0 + k + 1) * 128], rhs=qap,
                                                          start=True, stop=True) for k in range(4)], reads=[k_, q_], writes=[psl])
                tr.op('act', lambda: A.activation(out=pc[:, :2 * Nq], in_=psx[:, 128:128 + 2 * Nq], func=AF.Exp), reads=[psx], writes=[pc])
                if kind == 'l':
                    tr.op('dve', lambda: V.tensor_tensor(out=tm[:, :256].rearrange('p (a b) -> p a b', a=4),
                                                         in0=psl[:, :256].rearrange('p (a b) -> p a b', a=4), in1=bias, op=ALU.add),
                          reads=[psl, vt_], writes=[tm])
                    tr.op('act', lambda: A.activation(out=pt[:, :256], in_=tm[:, :256], func=AF.Exp), reads=[tm], writes=[pt])
                elif kind == 'p':
                    tr.op('dve', lambda: V.tensor_tensor(out=tm[:, 0:512], in0=psl[:, 0:512], in1=cb_[:, 0:4, :].rearrange('p a b -> p (a b)'), op=ALU.add),
                          reads=[psl, cb_], writes=[tm])
                    tr.op('dve', lambda: V.tensor_tensor(out=tm[:, 512:640], in0=psx[:, 0:128], in1=cb_[:, 4, :], op=ALU.add),
                          reads=[psx, cb_], acc=[tm])
                    tr.op('act', lambda: A.activation(out=pt[:, :640], in_=tm[:, :640], func=AF.Exp), reads=[tm], writes=[pt])
                mm = [(v_[:, c, :], pc[:, c * Nq:(c + 1) * Nq]) for c in range(2)]
                mm += [(v_[:, 2 + p0 + k, :], pt[:, k * Nq:(k + 1) * Nq]) for k in range(npair)]
                tr.group('pe', [lambda i=i, a=a, b=b: PE.matmul(pob[0:64, 0:Nq], lhsT=a, rhs=b, start=(i == 0), stop=(i == len(mm) - 1))
                                for i, (a, b) in enumerate(mm)], reads=[v_, pc, pt], writes=[pob])
                tr.group('pe', [lambda i=i, b=b: PE.matmul(pob[0:64, 128:128 + Nq], lhsT=ones_b[:, 0:64], rhs=b, start=(i == 0), stop=(i == len(mm) - 1))
                                for i, (a, b) in enumerate(mm)], reads=[ones_b, pc, pt], acc=[pob])
                tr.op('dve', lambda: V.reciprocal(out=rdn[:, :Nq], in_=pob[0:64, 128:128 + Nq]), reads=[pob], writes=[rdn])
                tr.op('dve', lambda: V.tensor_tensor(out=y_[:, oc0:oc0 + Nq], in0=pob[0:64, 0:Nq], in1=rdn[:, :Nq], op=ALU.mult),
                      reads=[pob, rdn], writes=[y_] if ui == 0 else (), acc=() if ui == 0 else [y_])
            if ctx_out: tr.dma('pool', YNC[h], y_[:, 0:NCX], reads=[y_])
            tr.dma('pool', YN[h], y_[:, NCX:TT], reads=[y_])
        ph.close()

    pcbig = None

    def phase_E(l, ctx_out, last):
        ph = Phase()
        R = ffn_bufs(ph)
        xTs = [ph.sb('xT%d' % i, [128, KC, 512], F32, dma=True) for i in range(2)]
        hT = ph.sb('hT', [128, KC, 512], BF16); aT = ph.sb('aT', [128, FC, 512], BF16)
        tg = [ph.sb('tg%d' % i, [128, 3, 512], F32, dma=True) for i in range(2)]
        tgb = ph.sb('tgb', [128, 3, 512], BF16); ys = ph.sb('ys', [128, 3, 512], BF16)
        yg = [ph.sb('yg%d' % i, [64, 8, 512], BF16, dma=True) for i in range(2)]
        yn = [ph.sb('yn%d' % i, [64, 8, 512], BF16, dma=True) for i in range(2)]
        sgr = [ph.sb('sg%d' % i, [128, 3, 512], BF16, dma=True) for i in range(2)]
        wpgr = [ph.sb('wpg%d' % i, [64, 8, 128], BF16, dma=True) for i in range(2)]
        wpnr = [ph.sb('wpn%d' % i, [64, 8, 128], BF16, dma=True) for i in range(2)]
        wglu = ph.sb('wglu', [128, 3, 3, 128], BF16, dma=True)
        tr.dma('sp', wglu[:], w_b[('wglu', l)].rearrange('m p k c -> p m k c'), reads=[w_buf[('wglu', l)]], writes=[wglu])
        wps = ph.sb('wps', [128, 8, 3, 128], BF16, dma=True)
        tr.dma('sp', wps[:], w_b[('wps', l)].rearrange('m p k c -> p m k c'), reads=[w_buf[('wps', l)]], writes=[wps])
        wo = [ph.sb('wo%d' % i, [128, KC, 128], BF16, dma=True) for i in range(2)]
        acc = ph.sb('acc', [128, 512]); t2 = ph.sb('t2e', [128, 512])
        ost = [ph.sb('ost%d' % i, [128, D], F32, dma=True) for i in range(1)] if last else None
        tl = tiles if ctx_out else tiles[1:]
        cnt = dict(wo=0, ost=0)

        def loads(idx):
            c0, N, col = tl[idx]; b = idx % 2
            tr.dma('sp', xTs[b][:, :, :N], XT[:, c0:c0 + N].rearrange('(k p) t -> p k t', p=128), writes=[xTs[b]])
            tr.dma('sp', tg[b][:, :, :N], TG[:, c0:c0 + N].rearrange('(k p) t -> p k t', p=128), writes=[tg[b]])
            if col == 1:
                tr.dma('sp', yg[b][:, :, :N], YGC.rearrange('h d t -> d h t'), writes=[yg[b]])
                tr.dma('sp', yn[b][:, :, :N], YNC.rearrange('h d t -> d h t'), writes=[yn[b]])
            else:
                tr.dma('sp', yg[b][:, :, :N], YG[:, :, c0 - NCX:c0 - NCX + N].rearrange('h d t -> d h t'), writes=[yg[b]])
                tr.dma('sp', yn[b][:, :, :N], YN[:, :, c0 - NCX:c0 - NCX + N].rearrange('h d t -> d h t'), writes=[yn[b]])

        mcount = 0
        loads(0)
        for idx in range(len(tl)):
            c0, N, col = tl[idx]; b = idx % 2
            xT = xTs[b]; tg_ = tg[b]; yg_ = yg[b]; yn_ = yn[b]
            if idx + 1 < len(tl): loads(idx + 1)
            tr.op('act', lambda: A.activation(out=tgb[:, :, :N], in_=tg_[:, :, :N], func=AF.Copy), reads=[tg_], writes=[tgb])
            for m in range(3):
                pg = R['ps_g'][m % 2]
                tr.group('pe', [lambda kc=kc: PE.matmul(pg[:, :N], lhsT=wglu[:, m, kc, :], rhs=tgb[:, kc, :N], start=(kc == 0), stop=(kc == 2))
                                for kc in range(3)], reads=[wglu, tgb], writes=[pg])
                tr.op('act', lambda: A.activation(out=acc[:, :N], in_=pg[:, :N], func=AF.Sigmoid), reads=[pg], writes=[acc])
                tr.op('dve', lambda: V.tensor_tensor(out=ys[:, m, :N], in0=acc[:, :N], in1=tg_[:, m, :N], op=ALU.mult), reads=[acc, tg_],
                      writes=[ys] if m == 0 else (), acc=() if m == 0 else [ys])
            for m in range(KC):
                p1 = R['ps_g'][m % 2]; p2 = R['ps_u'][m % 2]; p3 = R['ps_m'][m % 2]
                sg_ = sgr[mcount % 2]; wpg = wpgr[mcount % 2]; wpn = wpnr[mcount % 2]; mcount += 1
                tr.dma('sp', sg_[:, :, :N], SG[:, c0:c0 + N].rearrange('(b m p) t -> m p b t', b=3, p=128)[m], writes=[sg_])
                tr.dma('sp', wpg[:], w_b[('wpg', l)][m], reads=[w_buf[('wpg', l)]], writes=[wpg])
                tr.dma('sp', wpn[:], w_b[('wpn', l)][m], reads=[w_buf[('wpn', l)]], writes=[wpn])
                tr.group('pe', [lambda kc=kc: PE.matmul(p1[:, :N], lhsT=wps[:, m, kc, :], rhs=ys[:, kc, :N], start=(kc == 0), stop=(kc == 2))
                                for kc in range(3)], reads=[wps, ys], writes=[p1])
                tr.group('pe', [lambda h=h: PE.matmul(p2[:, :N], lhsT=wpg[:, h, :], rhs=yg_[:, h, :N], start=(h == 0), stop=(h == 7))
                                for h in range(8)], reads=[wpg, yg_], writes=[p2])
                tr.group('pe', [lambda h=h: PE.matmul(p3[:, :N], lhsT=wpn[:, h, :], rhs=yn_[:, h, :N], start=(h == 0), stop=(h == 7))
                                for h in range(8)], reads=[wpn, yn_], writes=[p3])
                tr.op('dve', lambda: V.tensor_tensor(out=acc[:, :N], in0=p1[:, :N], in1=sg_[:, 0, :N], op=ALU.mult), reads=[p1, sg_], writes=[acc])
                tr.op('dve', lambda: V.tensor_tensor(out=t2[:, :N], in0=p2[:, :N], in1=sg_[:, 1, :N], op=ALU.mult), reads=[p2, sg_], writes=[t2])
                tr.op('pool', lambda: G.tensor_tensor(out=acc[:, :N], in0=acc[:, :N], in1=t2[:, :N], op=ALU.add), reads=[acc, t2], writes=[acc])
                tr.op('dve', lambda: V.tensor_tensor(out=t2[:, :N], in0=p3[:, :N], in1=sg_[:, 2, :N], op=ALU.mult), reads=[p3, sg_], writes=[t2])
                tr.op('pool', lambda: G.tensor_tensor(out=hT[:, m, :N], in0=acc[:, :N], in1=t2[:, :N], op=ALU.add), reads=[acc, t2],
                      writes=[hT] if m == 0 else (), acc=() if m == 0 else [hT])
            for m in range(KC):
                wb = wo[cnt['wo'] % 2]; cnt['wo'] += 1
                tr.dma('sp', wb[:], w_b[('wout', l)][m], reads=[w_buf[('wout', l)]], writes=[wb])
                pd = R['ps_m'][m % 2]
                tr.group('pe', [lambda kc=kc: PE.matmul(pd[:, :N], lhsT=wb[:, kc, :], rhs=hT[:, kc, :N], start=(kc == 0), stop=(kc == KC - 1))
                                for kc in range(KC)], reads=[wb, hT], writes=[pd])
                tr.op('dve', lambda: V.scalar_tensor_tensor(out=xT[:, m, :N], in0=pd[:, :N], scalar=modG[:, 1, m, col:col + 1],
                                                            in1=xT[:, m, :N], op0=ALU.mult, op1=ALU.add), reads=[pd, modG, xT], acc=[xT])
            ffn(ph, l, 2, xT, hT, aT, N, col, 2, R)
            if not last:
                tr.dma('pool', XT[:, c0:c0 + N].rearrange('(k p) t -> p k t', p=128), xT[:, :, :N], reads=[xT])
            else:
                sq = R['sq']; rstd = R['rstd']; tb = R['tmpbig']; pss = R['ps_m'][0]
                tr.op('act', lambda: A.activation(out=sq[:, :, :N], in_=xT[:, :, :N], func=AF.Square), reads=[xT], writes=[sq])
                tr.group('pe', [lambda kc=kc: PE.matmul(pss[:, :N], lhsT=ones_b[:], rhs=sq[:, kc, :N], start=(kc == 0), stop=(kc == KC - 1))
                                for kc in range(KC)], reads=[sq, ones_b], writes=[pss])
                tr.op('act', lambda: A.activation(out=rstd[:, :N], in_=pss[:, :N], func=AF.Sqrt, scale=1.0 / D, bias=epsb[:, 0:1]), reads=[pss, epsb], writes=[rstd])
                tr.op('dve', lambda: V.reciprocal(out=rstd[:, :N], in_=rstd[:, :N]), reads=[rstd], writes=[rstd])
                tr.op('dve', lambda: V.tensor_tensor(out=tb[:, :, :N], in0=xT[:, :, :N], in1=rstd[:, :N].unsqueeze(1).to_broadcast([128, KC, N]), op=ALU.mult),
                      reads=[xT, rstd], writes=[tb])
                tr.group('act', [lambda kc=kc: A.activation(out=tb[:, kc, :N], in_=tb[:, kc, :N], func=AF.Identity, scale=fing[:, kc:kc + 1])
                                 for kc in range(KC)], reads=[tb, fing], writes=[tb])
                for ts in range(N // 128):
                    o_ = ost[0]; cnt['ost'] += 1
                    for hf in range(2):
                        pt = R['ps_g'][hf]
                        tr.group('pe', [lambda k=k: PE.transpose(pt[:, k * 128:(k + 1) * 128], tb[:, hf * 4 + k, ts * 128:(ts + 1) * 128], ident[:])
                                        for k in range(4)], reads=[tb, ident], writes=[pt])
                        tr.op('act' if hf == 0 else 'dve',
                              (lambda: A.activation(out=o_[:, 0:512], in_=pt[:], func=AF.Copy)) if hf == 0 else (lambda: V.tensor_copy(out=o_[:, 512:1024], in_=pt[:])),
                              reads=[pt], writes=[o_] if hf == 0 else (), acc=() if hf == 0 else [o_])
                    r0 = c0 - NCX + ts * 128
                    tr.dma('pool', out_d[r0:r0 + 128, :], o_[:], reads=[o_])
        ph.close()

    epsb = gp.sb('epsb', [128, 1]); tr.op('dve', lambda: V.memset(epsb[:], 1e-6), writes=[epsb])
    tr.barrier()

    def run():
        nonlocal stg_o, pcbig
        stg_o = [gp.sb('stgo%d' % i, [128, 512], F32, dma=True) for i in range(2)]
        pcbig = gp.sb('pcbig', [128, 256], BF16)
        if only is not None:
            {'B': phase_B}[only[0]](only[1]); return
        for l in range(DEPTH):
            ctx_out = l < DEPTH - 1
            compute_mod(l)
            if stop_after == ('mod', l): return
            phase_A(l)
            if stop_after == ('A', l): return
            emit_casts(l, G2)
            phase_B(l)
            if stop_after == ('B', l): return
            if l + 1 < DEPTH: emit_casts(l + 1, G1)
            phase_C(l, ctx_out)
            if stop_after == ('C', l): return
            phase_D(l, ctx_out)
            if stop_after == ('D', l): return
            phase_E(l, ctx_out, l == DEPTH - 1)
            if stop_after == ('E', l): return

    run()
    tr.barrier()
    gp.es.close()
    tr.es.close()
    return nc


def prep_shared(inp):
    sh = {}
    L = DEPTH
    sh['wgu1'] = np.stack([np.stack([tile_w(inp['ffn1_wg'][l]), tile_w(inp['ffn1_wu'][l])], 2) for l in range(L)])
    sh['wd1'] = np.stack([tile_w(inp['ffn1_wd'][l]) for l in range(L)])
    sh['wgu2'] = np.stack([np.stack([tile_w(inp['ffn2_wg'][l]), tile_w(inp['ffn2_wu'][l])], 2) for l in range(L)])
    sh['wd2'] = np.stack([tile_w(inp['ffn2_wd'][l]) for l in range(L)])
    cols = win_fm_cols()
    sh['winfm'] = np.stack([tile_w(inp['w_in'][l][:, cols]) for l in range(L)])
    tmc = np.concatenate([IN_OFF['gv'] + np.arange(128), IN_OFF['nv'] + np.arange(512)])
    sh['wintm'] = np.stack([np.ascontiguousarray(inp['w_in'][l][:, tmc].reshape(KC, 128, 640).transpose(1, 0, 2)) for l in range(L)])
    sh['wada'] = np.stack([tile_w(inp['w_ada'][l]) for l in range(L)])
    sh['wglu'] = np.stack([tile_w(inp['ssm_w_glu'][l]) for l in range(L)])
    sh['wps'] = np.stack([tile_w(inp['w_p_ssm'][l]) for l in range(L)])
    sh['wpg'] = np.stack([tile_w(inp['w_p_gqa'][l], 64) for l in range(L)])
    sh['wpn'] = np.stack([tile_w(inp['w_p_na'][l], 64) for l in range(L)])
    sh['wout'] = np.stack([tile_w(inp['w_out'][l]) for l in range(L)])
    sh['bada'] = np.ascontiguousarray(inp['b_ada'].reshape(L, 72, 128).transpose(0, 2, 1))
    sh['normg'] = np.ascontiguousarray(inp['norm_g'].reshape(L, 3, KC, 128).transpose(0, 3, 1, 2))
    sh['fing'] = np.ascontiguousarray(inp['final_g'].reshape(KC, 128).T)

    def st(a):
        return a.reshape(L, 2, 12, 2, 64).transpose(0, 1, 3, 4, 2).reshape(L, 2, 128, 12)
    ldt = np.broadcast_to(inp['ssm_log_dt'][:, :, :, None], (L, 2, 24, 64))
    sh['ssm_a'] = np.ascontiguousarray(np.stack([st(inp['ssm_a_re']), st(inp['ssm_a_im']), st(ldt)], 3))

    def sbt(a):
        return a.reshape(L, 2, 12, 2, 64, 16).transpose(0, 1, 3, 4, 2, 5).reshape(L, 2, 128, 12, 16)
    sh['ssm_b'] = np.ascontiguousarray(np.stack([sbt(inp['ssm_b_re']), sbt(inp['ssm_b_im'])], 2))

    def sct(a):
        return a.reshape(L, 2, 3, 8, 16, 64).transpose(0, 1, 3, 4, 2, 5).reshape(L, 2, 128, 3, 64)
    sh['ssm_c'] = np.ascontiguousarray(np.stack([sct(inp['ssm_c_re']), sct(inp['ssm_c_im'])], 2))
    sh['ssm_d'] = np.ascontiguousarray(inp['ssm_d'].reshape(L, 3, 128).transpose(0, 2, 1))
    sh['sink'] = np.ascontiguousarray(np.broadcast_to(inp['gqa_sink'][:, None, :], (L, 64, 8)))
    vt, od, cb = na_bias_tables(inp['na_rpb'])
    sh['navt'] = vt; sh['naod'] = od; sh['nacb'] = cb.reshape(L, 8, 128, 5, 128)
    sh.update(host_consts())
    return {k: np.ascontiguousarray(v, dtype=np.float32) for k, v in sh.items()}


def core_inputs(inp, b, sh):
    m = dict(sh)
    m['x'] = np.ascontiguousarray(inp['x'][b]); m['ctx'] = np.ascontiguousarray(inp['ctx'][b])
    sv = np.stack([inp['c'][b].reshape(KC, 128).T, inp['c_ctx'].reshape(KC, 128).T], 2)
    m['svec'] = np.ascontiguousarray(sv, dtype=np.float32)
    return m


def kernel(**inputs):
    inp = {k: np.asarray(v) for k, v in inputs.items()}
    sh = prep_shared(inp)
    nc = build()
    in_maps = [core_inputs(inp, b, sh) for b in range(8)]
    res = run_bass_kernel_spmd(nc, in_maps, core_ids=list(range(8)))
    return np.stack([np.asarray(r['out'], dtype=np.float32) for r in res.results], 0)
```

```python
import contextlib, math, os
import numpy as np
import ml_dtypes
import concourse.bass as bass
import concourse.mybir as mybir
from concourse.bass_utils import run_bass_kernel_spmd

F32 = mybir.dt.float32; BF16 = mybir.dt.bfloat16; I32 = mybir.dt.int32
AF = mybir.ActivationFunctionType; ALU = mybir.AluOpType

D = 1024; T = 4096; NCX = 256; TT = T + NCX; FF = 2816; KC = 8; FC = 22; DEPTH = 2
NEG = -30000.0
SAME_SYNC = True
PI = math.pi


class Sem:
    def __init__(s, h, name): s.h = h; s.total = 0; s.name = name


class Eng:
    def __init__(s, name, obj, sem): s.name = name; s.obj = obj; s.sem = sem; s.known = {}


class Buf:
    def __init__(s, name, t=None, dsem=None, qsem=None):
        s.name = name; s.t = t; s.w = []; s.r = []; s.pre = []; s.dsem = dsem; s.qsem = qsem

    def __getitem__(s, k): return s.t[k]


class Trk:
    def __init__(self, nc):
        self.nc = nc
        self.es = contextlib.ExitStack()
        self.sems = []
        self.E = {}
        for n, o in (('pe', nc.tensor), ('act', nc.scalar), ('dve', nc.vector), ('pool', nc.gpsimd), ('sp', nc.sync)):
            self.E[n] = Eng(n, o, self.new_sem('e_' + n) if n != 'sp' else None)
        self.dpool = []; self.dnext = 0; self.uid = 0; self.qpool = []; self.qnext = 0

    def new_sem(self, name):
        s = Sem(self.es.enter_context(self.nc.semaphore(name)), name); self.sems.append(s); return s

    def dsem(self):
        if self.dnext >= len(self.dpool): self.dpool.append(self.new_sem('d%d' % len(self.dpool)))
        s = self.dpool[self.dnext]; self.dnext += 1; return s

    def qsem(self):
        if self.qnext >= len(self.qpool): self.qpool.append(self.new_sem('q%d' % len(self.qpool)))
        s = self.qpool[self.qnext]; self.qnext += 1; return s

    def _wait(self, eng, evs):
        need = {}
        for (sem, val, src) in evs:
            if src == eng.name and (src == 'pe' or not SAME_SYNC): continue
            if eng.known.get(sem, 0) >= val: continue
            need[sem] = max(need.get(sem, 0), val)
        for sem, val in need.items():
            eng.obj.wait_ge(sem.h, val); eng.known[sem] = val

    def _pre(self, eng, reads, writes, acc):
        evs = []
        for b in reads: evs += b.w
        for b in writes:
            b.pre = b.w + b.r; evs += b.pre
        for b in acc: evs += b.pre
        self._wait(eng, evs)

    def _post(self, ev, reads, writes, acc):
        for b in reads: b.r.append(ev)
        for b in writes: b.w = [ev]; b.r = []
        for b in acc: b.w.append(ev)

    def group(self, en, fns, reads=(), writes=(), acc=()):
        eng = self.E[en]
        self._pre(eng, reads, writes, acc)
        ins = None
        for f in fns: ins = f()
        eng.sem.total += 1
        ins.then_inc(eng.sem.h, 1)
        self._post((eng.sem, eng.sem.total, en), reads, writes, acc)

    def op(self, en, fn, reads=(), writes=(), acc=()):
        self.group(en, [fn], reads, writes, acc)

    def dma(self, q, out, in_, reads=(), writes=(), acc=(), sem=None):
        eng = self.E[q]
        self._pre(eng, reads, writes, acc)
        if sem is None:
            for b in list(writes) + list(acc) + list(reads):
                if b.dsem is not None: sem = (b.qsem if q == 'pool' else b.dsem); break
        ins = eng.obj.dma_start(out=out, in_=in_)
        sem.total += 16
        ins.then_inc(sem.h, 16)
        self._post((sem, sem.total, 'dma'), reads, writes, acc)

    def barrier(self):
        for eng in self.E.values():
            for s in self.sems:
                if s.total > 0 and eng.known.get(s, 0) < s.total:
                    eng.obj.wait_ge(s.h, s.total); eng.known[s] = s.total


def tile_w(W, kp=128):
    K, M = W.shape
    return np.ascontiguousarray(W.reshape(K // kp, kp, M // 128, 128).transpose(2, 1, 0, 3))


IN_OFF = dict(u=0, gq=384, gk=896, gv=1024, nq=1152, nk=1664, nv=2176, gates=2688)
ROT_PERM = np.concatenate([np.arange(16, 32), np.arange(0, 16), np.arange(48, 64), np.arange(32, 48)])
FM_CHUNKS = ([('u', i) for i in range(3)] + [('gq', i) for i in range(4)] + [('gq2', i) for i in range(4)]
             + [('gk', 0), ('gk2', 0)] + [('nq', i) for i in range(4)] + [('nk', i) for i in range(4)]
             + [('gates', i) for i in range(24)])


def win_fm_cols():
    cols = []
    for kind, i in FM_CHUNKS:
        if kind == 'u': c = IN_OFF['u'] + i * 128 + np.arange(128)
        elif kind == 'gq': c = IN_OFF['gq'] + i * 128 + np.arange(128)
        elif kind == 'gq2': c = IN_OFF['gq'] + i * 128 + np.concatenate([ROT_PERM, 64 + ROT_PERM])
        elif kind == 'gk': c = IN_OFF['gk'] + np.arange(128)
        elif kind == 'gk2': c = IN_OFF['gk'] + np.concatenate([ROT_PERM, 64 + ROT_PERM])
        elif kind == 'nq': c = IN_OFF['nq'] + i * 128 + np.arange(128)
        elif kind == 'nk': c = IN_OFF['nk'] + i * 128 + np.arange(128)
        else: c = IN_OFF['gates'] + i * 128 + np.arange(128)
        cols.append(c)
    return np.concatenate(cols)


def rope_tables():
    t = np.arange(T)
    pos = np.stack([t // 64, t % 64], 0).astype(np.float32)
    inv = (10000.0 ** (-np.arange(0, 32, 2, dtype=np.float32) / 32)).astype(np.float32)
    C = np.zeros((64, T), np.float32); S = np.zeros((64, T), np.float32)
    for ax in range(2):
        ang = (pos[ax][None, :] * inv[:, None]).astype(np.float32)
        for half in range(2):
            sl = slice(ax * 32 + half * 16, ax * 32 + half * 16 + 16)
            C[sl] = np.cos(ang)
            S[sl] = -np.sin(ang) if half == 0 else np.sin(ang)
    C2 = np.concatenate([C, C], 0); S2 = np.concatenate([S, S], 0)
    return np.stack([C2 * 0.125, S2 * 0.125, C2, S2], 0).astype(np.float32)


def na_bias_tables(rpb):
    L = rpb.shape[0]
    kc = np.arange(64)[:, None]; qc = np.arange(64)[None, :]
    cs = np.clip(qc - 8, 0, 48)
    valid = (kc >= cs) & (kc < cs + 16)
    idx = np.clip(kc - qc + 15, 0, 30)
    tab = np.where(valid[None, None, None], rpb[:, :, :, idx], np.float32(NEG)).astype(np.float32)
    negt = np.full((L, 8, 64, 64), NEG, np.float32)
    VT = np.zeros((L, 8, 128, 14, 64), np.float32)
    for d in range(-7, 7):
        VT[:, :, 0:64, d + 7] = tab[:, :, d + 7]
        VT[:, :, 64:128, d + 7] = tab[:, :, d + 8]
    OD = np.zeros((L, 8, 128, 5, 64), np.float32)
    for k, d in enumerate((-5, -3, -1, 1, 3)):
        OD[:, :, 0:64, k] = negt if d == -5 else tab[:, :, d + 7]
        OD[:, :, 64:128, k] = negt if d == 3 else tab[:, :, d + 8]
    CB = np.zeros((L, 8, 128, 5, 2, 64), np.float32)
    for k in range(5):
        CB[:, :, :, k, 0, :] = VT[:, :, :, 3 + 2 * k, :] if k < 4 else np.float32(NEG)
        CB[:, :, :, k, 1, :] = OD[:, :, :, k, :]
    return VT, OD, CB


def host_consts():
    c = {}
    c['ident'] = np.eye(128, dtype=np.float32)
    c['rope'] = rope_tables()
    k = np.arange(128)[:, None]; q = np.arange(128)[None, :]
    c['gmask'] = np.stack([np.where(k >= q, 0.0, NEG), np.where(k <= q, 0.0, NEG)], 1).astype(np.float32)
    p = np.arange(128)
    m2 = np.zeros((128, 4, 8, 16), np.float32)
    mc = np.zeros((128, 4, 2, 64), np.float32)
    for qq in range(4):
        for pp in range(128):
            m2[pp, qq, 2 * qq + (pp >= 64), :] = 1.0
            for half in range(2):
                if pp // 16 == 2 * qq + half: mc[pp, qq, half, :] = 1.0
    c['mask2'] = m2; c['maskc'] = mc
    io = np.zeros((128, 2, 64), np.float32)
    io[:, 0, :] = np.arange(1, 65)[None, :]; io[:, 1, :] = np.arange(64, 0, -1)[None, :]
    c['iota64'] = io
    c['iota9'] = np.broadcast_to(np.arange(9, dtype=np.float32)[None, :], (128, 9)).copy()
    return c


def build(dbg=(), stop_after=None, only=None, ext_in=()):
    nc = bass.Bass("TRN2", target_bir_lowering=False)
    tr = Trk(nc)
    dbg = set(dbg)

    def din(name, shape, dt=F32):
        return nc.dram_tensor(name, list(shape), dt, kind="ExternalInput").ap()

    def dscr(name, shape, dt):
        if name in ext_in: return nc.dram_tensor(name, list(shape), dt, kind="ExternalInput").ap()
        if name in dbg: return nc.dram_tensor(name, list(shape), dt, kind="ExternalOutput").ap()
        return nc.dram_tensor(name, list(shape), dt).ap()

    x_in = din('x', [T, D]); ctx_in = din('ctx', [NCX, D]); svec_in = din('svec', [128, KC, 2])
    out_d = nc.dram_tensor('out', [T, D], F32, kind="ExternalOutput").ap()
    WSH = dict(wgu1=[FC, 128, 2, KC, 128], wd1=[KC, 128, FC, 128], wgu2=[FC, 128, 2, KC, 128], wd2=[KC, 128, FC, 128],
               winfm=[45, 128, KC, 128], wintm=[128, KC, 640], wada=[72, 128, KC, 128], wglu=[3, 128, 3, 128],
               wps=[8, 128, 3, 128], wpg=[8, 64, 8, 128], wpn=[8, 64, 8, 128], wout=[8, 128, KC, 128])
    WORDER = ['wada', 'wgu1', 'wd1', 'winfm', 'wintm', 'wglu', 'wps', 'wpg', 'wpn', 'wout', 'wgu2', 'wd2']
    w_f = {k: din(k, [DEPTH] + v) for k, v in WSH.items()}
    w_b = {(k, l): dscr('%s_b%d' % (k, l), v, BF16) for k, v in WSH.items() for l in range(DEPTH)}
    w_buf = {(k, l): Buf('wb_%s%d' % (k, l)) for k in WSH for l in range(DEPTH)}
    bada_in = din('bada', [DEPTH, 128, 72]); normg_in = din('normg', [DEPTH, 128, 3, KC]); fing_in = din('fing', [128, KC])
    sa_in = din('ssm_a', [DEPTH, 2, 128, 3, 12])
    sb_in = din('ssm_b', [DEPTH, 2, 2, 128, 12, 16]); sc_in = din('ssm_c', [DEPTH, 2, 2, 128, 3, 64])
    sd_in = din('ssm_d', [DEPTH, 128, 3]); sink_in = din('sink', [DEPTH, 64, 8])
    navt_in = din('navt', [DEPTH, 8, 128, 14, 64]); naod_in = din('naod', [DEPTH, 8, 128, 5, 64]); nacb_in = din('nacb', [DEPTH, 8, 128, 5, 128])
    ident_in = din('ident', [128, 128]); rope_in = din('rope', [4, 128, T]); gmask_in = din('gmask', [128, 2, 128])
    mask2_in = din('mask2', [128, 4, 8, 16]); maskc_in = din('maskc', [128, 4, 2, 64]); iota64_in = din('iota64', [128, 2, 64]); iota9_in = din('iota9', [128, 9])

    XT = dscr('XT', [D, TT], F32)
    U = dscr('U', [384, TT], F32); TG = dscr('TG', [384, TT], F32)
    QG = dscr('QG', [8, 64, T], BF16); QGC = dscr('QGC', [8, 64, NCX], BF16)
    KG = dscr('KG', [2, 64, T], BF16); KGC = dscr('KGC', [2, 64, NCX], BF16)
    QN = dscr('QN', [8, 64, T], BF16); QNC = dscr('QNC', [8, 64, NCX], BF16)
    KN = dscr('KN', [8, 64, T], BF16); KNC = dscr('KNC', [8, 64, NCX], BF16)
    VG = dscr('VG', [TT, 128], BF16); VN = dscr('VN', [TT, 512], BF16)
    SG = dscr('SG', [3072, TT], BF16)
    YG = dscr('YG', [8, 64, T], BF16); YGC = dscr('YGC', [8, 64, NCX], BF16)
    YN = dscr('YN', [8, 64, T], BF16); YNC = dscr('YNC', [8, 64, NCX], BF16)

    def uname(n):
        tr.uid += 1; return '%s_%d' % (n, tr.uid)

    class Phase:
        def __init__(s, reset=True):
            s.es = contextlib.ExitStack()
            if reset: tr.dnext = 0; tr.qnext = 0

        def sb(s, name, shape, dt=F32, dma=False):
            t = s.es.enter_context(nc.sbuf_tensor(uname(name), list(shape), dt))
            return Buf(name, t, tr.dsem() if dma else None, tr.qsem() if dma else None)

        def ps(s, name, shape=(128, 512), dt=F32):
            t = s.es.enter_context(nc.psum_tensor(uname(name), list(shape), dt))
            return Buf(name, t)

        def close(s):
            tr.barrier(); s.es.close()

    V = nc.vector; A = nc.scalar; G = nc.gpsimd; PE = nc.tensor

    G1 = ['wada', 'wgu1', 'wd1', 'winfm', 'wintm']
    G2 = ['wglu', 'wps', 'wpg', 'wpn', 'wout', 'wgu2', 'wd2']

    def emit_casts(l, keys):
        if only is not None: return
        for k in keys:
            sem = tr.new_sem('c_%s%d' % (k, l))
            n = int(np.prod(WSH[k]))
            src = w_f[k][l]; dst = w_b[(k, l)]
            names = 'abcde'[:len(WSH[k])]
            pat = ' '.join(names)
            fs = src.rearrange('%s -> (%s)' % (pat, pat)).rearrange('(p n) -> p n', p=128)
            fd = dst.rearrange('%s -> (%s)' % (pat, pat)).rearrange('(p n) -> p n', p=128)
            cols = n // 128
            npieces = max(1, -(-cols // 16384))
            step = -(-cols // npieces)
            first = True
            for c0 in range(0, cols, step):
                c1 = min(cols, c0 + step)
                tr.dma('pool', fd[:, c0:c1], fs[:, c0:c1], writes=[w_buf[(k, l)]] if first else (),
                       acc=() if first else [w_buf[(k, l)]], sem=sem)
                first = False

    emit_casts(0, G1)

    gp = Phase()
    ident = gp.sb('ident', [128, 128], F32, dma=True)
    tr.dma('sp', ident[:], ident_in[:, :], writes=[ident])
    ones_b = gp.sb('ones_b', [128, 128], BF16)
    tr.op('dve', lambda: V.memset(ones_b[:], 1.0), writes=[ones_b])
    svec = gp.sb('svec', [128, KC, 2], F32, dma=True)
    tr.dma('sp', svec[:], svec_in[:, :, :], writes=[svec])
    svb = gp.sb('svb', [128, KC, 2], BF16)
    tr.op('act', lambda: A.activation(out=svb[:], in_=svec[:], func=AF.Silu), reads=[svec], writes=[svb])
    fing = gp.sb('fing', [128, KC], F32, dma=True)
    tr.dma('sp', fing[:], fing_in[:, :], writes=[fing])
    modA = gp.sb('modA', [128, 3, KC, 2]); modB = gp.sb('modB', [128, 3, KC, 2]); modG = gp.sb('modG', [128, 3, KC, 2])

    def compute_mod(l):
        ph = Phase()
        bada = ph.sb('bada', [128, 72], F32, dma=True); tr.dma('sp', bada[:], bada_in[l], writes=[bada])
        normg = ph.sb('normg', [128, 3, KC], F32, dma=True); tr.dma('sp', normg[:], normg_in[l], writes=[normg])
        wr = [ph.sb('wada%d' % i, [128, 8, KC, 128], BF16, dma=True) for i in range(2)]
        mps = ph.ps('mps', [128, 72, 2])
        mod = ph.sb('mod', [128, 72, 2])
        for jg in range(9):
            wbuf = wr[jg % 2]
            tr.dma('sp', wbuf[:], w_b[('wada', l)][jg * 8:(jg + 1) * 8].rearrange('j p k c -> p j k c'),
                   reads=[w_buf[('wada', l)]], writes=[wbuf])
            fns = []
            for jj in range(8):
                j = jg * 8 + jj
                for kc in range(KC):
                    fns.append(lambda j=j, jj=jj, kc=kc: PE.matmul(mps[:, j, :], lhsT=wbuf[:, jj, kc, :], rhs=svb[:, kc, :],
                                                                  start=(kc == 0), stop=(kc == KC - 1)))
            tr.group('pe', fns, reads=[wbuf, svb], writes=[mps] if jg == 0 else (), acc=() if jg == 0 else [mps])
        tr.op('dve', lambda: V.tensor_tensor(out=mod[:], in0=mps[:], in1=bada[:].unsqueeze(2).to_broadcast([128, 72, 2]), op=ALU.add),
              reads=[mps, bada], writes=[mod])
        for i in range(3):
            sh = mod[:, 8 * (3 * i):8 * (3 * i) + 8, :]; sc = mod[:, 8 * (3 * i + 1):8 * (3 * i + 1) + 8, :]
            gt = mod[:, 8 * (3 * i + 2):8 * (3 * i + 2) + 8, :]
            gb = normg[:, i, :].unsqueeze(2).to_broadcast([128, KC, 2])
            fns = [lambda sc=sc, i=i, gb=gb: V.scalar_tensor_tensor(out=modA[:, i], in0=sc, scalar=1.0, in1=gb, op0=ALU.add, op1=ALU.mult),
                   lambda sh=sh, i=i: V.tensor_copy(out=modB[:, i], in_=sh),
                   lambda gt=gt, i=i: V.tensor_scalar(out=modG[:, i], in0=gt, scalar1=(1.0 if i == 1 else 0.5), scalar2=None, op0=ALU.mult)]
            tr.group('dve', fns, reads=[mod, normg], writes=[modA, modB, modG] if i == 0 else (), acc=() if i == 0 else [modA, modB, modG])
        ph.close()

    def rms_mod(ph, xT, hT, N, site, col, ps_ss, tmpbig, sq, rstd):
        tr.op('act', lambda: A.activation(out=sq[:, :, :N], in_=xT[:, :, :N], func=AF.Square), reads=[xT], writes=[sq])
        tr.group('pe', [lambda kc=kc: PE.matmul(ps_ss[:, :N], lhsT=ones_b[:], rhs=sq[:, kc, :N], start=(kc == 0), stop=(kc == KC - 1))
                        for kc in range(KC)], reads=[sq, ones_b], writes=[ps_ss])
        tr.op('act', lambda: A.activation(out=rstd[:, :N], in_=ps_ss[:, :N], func=AF.Sqrt, scale=1.0 / D, bias=epsb[:, 0:1]),
              reads=[ps_ss, epsb], writes=[rstd])
        tr.op('dve', lambda: V.reciprocal(out=rstd[:, :N], in_=rstd[:, :N]), reads=[rstd], writes=[rstd])
        tr.op('dve', lambda: V.tensor_tensor(out=tmpbig[:, :, :N], in0=xT[:, :, :N],
                                             in1=rstd[:, :N].unsqueeze(1).to_broadcast([128, KC, N]), op=ALU.mult),
              reads=[xT, rstd], writes=[tmpbig])
        tr.group('act', [lambda kc=kc: A.activation(out=hT[:, kc, :N], in_=tmpbig[:, kc, :N], func=AF.Identity,
                                                    scale=modA[:, site, kc, col:col + 1], bias=modB[:, site, kc, col:col + 1])
                         for kc in range(KC)], reads=[tmpbig, modA, modB], writes=[hT])

    def ffn(ph, l, which, xT, hT, aT, N, col, site, R):
        wgu_k, wd_k = ('wgu1', 'wd1') if which == 1 else ('wgu2', 'wd2')
        rms_mod(ph, xT, hT, N, site, col, R['ps_m'][0], R['tmpbig'], R['sq'], R['rstd'])
        for f in range(FC):
            wb = R['wgu'][R['i_wgu'] % 3]; R['i_wgu'] += 1
            tr.dma('sp', wb[:], w_b[(wgu_k, l)][f], reads=[w_buf[(wgu_k, l)]], writes=[wb])
            pg = R['ps_g'][f % 2]; pu = R['ps_u'][f % 2]
            tr.group('pe', [lambda kc=kc: PE.matmul(pg[:, :N], lhsT=wb[:, 0, kc, :], rhs=hT[:, kc, :N], start=(kc == 0), stop=(kc == KC - 1))
                            for kc in range(KC)], reads=[wb, hT], writes=[pg])
            tr.group('pe', [lambda kc=kc: PE.matmul(pu[:, :N], lhsT=wb[:, 1, kc, :], rhs=hT[:, kc, :N], start=(kc == 0), stop=(kc == KC - 1))
                            for kc in range(KC)], reads=[wb, hT], writes=[pu])
            sg = R['sgt'][f % 2]
            tr.op('act', lambda: A.activation(out=sg[:, :N], in_=pg[:, :N], func=AF.Silu), reads=[pg], writes=[sg])
            tr.op('dve', lambda: V.tensor_tensor(out=aT[:, f, :N], in0=sg[:, :N], in1=pu[:, :N], op=ALU.mult),
                  reads=[sg, pu], writes=[aT] if f == 0 else (), acc=() if f == 0 else [aT])
        for m in range(KC):
            wb = R['wd'][R['i_wd'] % 2]; R['i_wd'] += 1
            tr.dma('sp', wb[:], w_b[(wd_k, l)][m], reads=[w_buf[(wd_k, l)]], writes=[wb])
            pd = R['ps_m'][m % 2]
            tr.group('pe', [lambda f=f: PE.matmul(pd[:, :N], lhsT=wb[:, f, :], rhs=aT[:, f, :N], start=(f == 0), stop=(f == FC - 1))
                            for f in range(FC)], reads=[wb, aT], writes=[pd])
            tr.op('dve', lambda: V.scalar_tensor_tensor(out=xT[:, m, :N], in0=pd[:, :N], scalar=modG[:, site, m, col:col + 1],
                                                        in1=xT[:, m, :N], op0=ALU.mult, op1=ALU.add),
                  reads=[pd, modG, xT], acc=[xT])

    def ffn_bufs(ph):
        R = {}
        R['wgu'] = [ph.sb('wgu%d' % i, [128, 2, KC, 128], BF16, dma=True) for i in range(3)]
        R['wd'] = [ph.sb('wd%d' % i, [128, FC, 128], BF16, dma=True) for i in range(2)]
        R['i_wgu'] = 0; R['i_wd'] = 0
        R['ps_g'] = [ph.ps('psg%d' % i) for i in range(2)]
        R['ps_u'] = [ph.ps('psu%d' % i) for i in range(2)]
        R['ps_m'] = [ph.ps('psm%d' % i) for i in range(2)]
        R['sgt'] = [ph.sb('sgt%d' % i, [128, 512]) for i in range(2)]
        R['tmpbig'] = ph.sb('tmpbig', [128, KC, 512]); R['sq'] = ph.sb('sq', [128, KC, 512], BF16)
        R['rstd'] = ph.sb('rstd', [128, 512])
        return R

    tiles = [(0, NCX, 1)] + [(NCX + i * 512, 512, 0) for i in range(8)]

    def phase_A(l):
        ph = Phase()
        R = ffn_bufs(ph)
        xTs = [ph.sb('xT%d' % i, [128, KC, 512], F32, dma=True) for i in range(2)]
        hT = ph.sb('hT', [128, KC, 512], BF16); aT = ph.sb('aT', [128, FC, 512], BF16)
        win = [ph.sb('win%d' % i, [128, KC, 128], BF16, dma=True) for i in range(3)]
        wtm = ph.sb('wtm', [128, KC, 640], BF16, dma=True)
        tr.dma('sp', wtm[:], w_b[('wintm', l)], reads=[w_buf[('wintm', l)]], writes=[wtm])
        rp = [ph.sb('rope%d' % i, [128, 4, 512], F32, dma=True) for i in range(2)]
        stf = [ph.sb('stf%d' % i, [128, 512], F32, dma=True) for i in range(2)]
        stb = [ph.sb('stb%d' % i, [128, 640], BF16, dma=True) for i in range(3)]
        t1 = ph.sb('t1', [128, 512]); t2 = ph.sb('t2', [128, 512])
        xin = [ph.sb('xin%d' % i, [128, D], F32, dma=True) for i in range(2)] if l == 0 else None
        ps_q = ph.ps('psq'); ps_q2 = ph.ps('psq2')
        cnt = dict(win=0, stf=0, stb=0, xin=0)

        def load_x(ti):
            c0, N, col = tiles[ti]; xT = xTs[ti % 2]
            if l > 0:
                tr.dma('sp', xT[:, :, :N], XT[:, c0:c0 + N].rearrange('(k p) t -> p k t', p=128), writes=[xT])
            else:
                for ts in range(N // 128):
                    xb = xin[cnt['xin'] % 2]; cnt['xin'] += 1
                    src = ctx_in[ts * 128:(ts + 1) * 128, :] if ti == 0 else x_in[c0 - NCX + ts * 128:c0 - NCX + (ts + 1) * 128, :]
                    tr.dma('sp', xb[:], src, writes=[xb])
                    for hf in range(2):
                        pt = R['ps_g'][hf]
                        tr.group('pe', [lambda k=k, hf=hf: PE.transpose(pt[:, k * 128:(k + 1) * 128], xb[:, (hf * 4 + k) * 128:(hf * 4 + k + 1) * 128], ident[:])
                                        for k in range(4)], reads=[xb, ident], writes=[pt])
                        tr.op('act', lambda hf=hf, pt=pt, ts=ts: A.activation(out=xT[:, hf * 4:hf * 4 + 4, ts * 128:(ts + 1) * 128],
                                                                               in_=pt[:].rearrange('p (k t) -> p k t', k=4), func=AF.Copy),
                              reads=[pt], writes=[xT] if (ts == 0 and hf == 0) else (), acc=() if (ts == 0 and hf == 0) else [xT])

        load_x(0)
        for ti in range(9):
            c0, N, col = tiles[ti]; xT = xTs[ti % 2]; lat = ti > 0
            if lat:
                rpb = rp[ti % 2]
                tr.dma('sp', rpb[:], rope_in[:, :, c0 - NCX:c0 - NCX + 512].rearrange('a p t -> p a t'), writes=[rpb])
            ffn(ph, l, 1, xT, hT, aT, N, col, 0, R)
            if ti + 1 < 9: load_x(ti + 1)
            rms_mod(ph, xT, hT, N, 1, col, R['ps_m'][0], R['tmpbig'], R['sq'], R['rstd'])
            tr.dma('pool', XT[:, c0:c0 + N].rearrange('(k p) t -> p k t', p=128), xT[:, :, :N], reads=[xT])
            ci = 0
            while ci < len(FM_CHUNKS):
                kind, i = FM_CHUNKS[ci]

                def wmm(ci, pbuf):
                    wb = win[cnt['win'] % 3]; cnt['win'] += 1
                    tr.dma('sp', wb[:], w_b[('winfm', l)][ci], reads=[w_buf[('winfm', l)]], writes=[wb])
                    tr.group('pe', [lambda kc=kc: PE.matmul(pbuf[:, :N], lhsT=wb[:, kc, :], rhs=hT[:, kc, :N], start=(kc == 0), stop=(kc == KC - 1))
                                    for kc in range(KC)], reads=[wb, hT], writes=[pbuf])

                if kind in ('gq', 'gk') and lat:
                    ci2 = ci + (4 if kind == 'gq' else 1)
                    wmm(ci, ps_q); wmm(ci2, ps_q2)
                    o = 0 if kind == 'gq' else 2
                    st = stb[cnt['stb'] % 3]; cnt['stb'] += 1
                    tr.op('dve', lambda: V.tensor_tensor(out=t1[:], in0=ps_q[:], in1=rpb[:, o, :], op=ALU.mult), reads=[ps_q, rpb], writes=[t1])
                    tr.op('dve', lambda: V.tensor_tensor(out=t2[:], in0=ps_q2[:], in1=rpb[:, o + 1, :], op=ALU.mult), reads=[ps_q2, rpb], writes=[t2])
                    tr.op('pool', lambda: G.tensor_tensor(out=st[:, :512], in0=t1[:], in1=t2[:], op=ALU.add), reads=[t1, t2], writes=[st])
                    dst = QG[2 * i:2 * i + 2] if kind == 'gq' else KG[0:2]
                    tr.dma('pool', dst.rearrange('h d t -> (h d) t')[:, c0 - NCX:c0 - NCX + 512], st[:, :512], reads=[st])
                elif kind in ('gq2', 'gk2'):
                    pass
                else:
                    pb = R['ps_m'][ci % 2]
                    wmm(ci, pb)
                    if kind == 'u':
                        st = stf[cnt['stf'] % 2]; cnt['stf'] += 1
                        tr.op('act', lambda: A.activation(out=st[:, :N], in_=pb[:, :N], func=AF.Copy), reads=[pb], writes=[st])
                        tr.dma('pool', U[i * 128:(i + 1) * 128, c0:c0 + N], st[:, :N], reads=[st])
                    else:
                        st = stb[cnt['stb'] % 3]; cnt['stb'] += 1
                        if kind == 'gates':
                            tr.op('act', lambda: A.activation(out=st[:, :N], in_=pb[:, :N], func=AF.Sigmoid), reads=[pb], writes=[st])
                            tr.dma('pool', SG[i * 128:(i + 1) * 128, c0:c0 + N], st[:, :N], reads=[st])
                        else:
                            sc = 0.125 if kind in ('gq', 'nq') else 1.0
                            tr.op('act', lambda: A.activation(out=st[:, :N], in_=pb[:, :N], func=AF.Copy, scale=sc), reads=[pb], writes=[st])
                            if lat:
                                dst = {'nq': QN, 'nk': KN}[kind][2 * i:2 * i + 2].rearrange('h d t -> (h d) t')[:, c0 - NCX:c0 - NCX + 512]
                            else:
                                dd = {'gq': QGC, 'gk': KGC, 'nq': QNC, 'nk': KNC}[kind]
                                dst = (dd[2 * i:2 * i + 2] if kind != 'gk' else dd[0:2]).rearrange('h d t -> (h d) t')
                            tr.dma('pool', dst, st[:, :N], reads=[st])
                ci += 1
            for ts in range(N // 128):
                pv = R['ps_g'][ts % 2]; pv2 = R['ps_u'][ts % 2]
                tr.group('pe', [lambda kc=kc: PE.matmul(pv[:, :128], lhsT=hT[:, kc, ts * 128:(ts + 1) * 128], rhs=wtm[:, kc, 0:128],
                                                        start=(kc == 0), stop=(kc == KC - 1)) for kc in range(KC)], reads=[hT, wtm], writes=[pv])
                tr.group('pe', [lambda kc=kc: PE.matmul(pv2[:, :512], lhsT=hT[:, kc, ts * 128:(ts + 1) * 128], rhs=wtm[:, kc, 128:640],
                                                        start=(kc == 0), stop=(kc == KC - 1)) for kc in range(KC)], reads=[hT, wtm], writes=[pv2])
                st = stb[cnt['stb'] % 3]; cnt['stb'] += 1
                tr.op('act', lambda: A.activation(out=st[:, 0:128], in_=pv[:, 0:128], func=AF.Copy), reads=[pv], writes=[st])
                tr.op('dve', lambda: V.tensor_copy(out=st[:, 128:640], in_=pv2[:, :512]), reads=[pv2], acc=[st])
                r0 = c0 + ts * 128
                tr.dma('pool', VG[r0:r0 + 128, :], st[:, 0:128], reads=[st])
                tr.dma('pool', VN[r0:r0 + 128, :], st[:, 128:640], reads=[st])
        ph.close()

    def sin_rr(src, srcb, dst, dstb, shift, tA, tAb, tI, tIb):
        if shift:
            tr.op('dve', lambda: V.tensor_scalar(out=tA, in0=src, scalar1=shift, scalar2=None, op0=ALU.add), reads=[srcb], writes=[tAb])
            x, xb = tA, tAb
        else:
            x, xb = src, srcb
        tr.op('dve', lambda: V.tensor_scalar(out=tI, in0=x, scalar1=1.0 / (2 * PI), scalar2=None, op0=ALU.mult), reads=[xb], writes=[tIb])
        tr.op('dve', lambda: V.tensor_copy(out=dst, in_=tI), reads=[tIb], writes=[dstb])
        tr.op('dve', lambda: V.scalar_tensor_tensor(out=dst, in0=dst, scalar=-2 * PI, in1=x, op0=ALU.mult, op1=ALU.add), reads=[dstb, xb], writes=[dstb])
        tr.op('dve', lambda: V.tensor_scalar(out=dst, in0=dst, scalar1=-PI, scalar2=PI, op0=ALU.max, op1=ALU.min), reads=[dstb], writes=[dstb])
        tr.op('act', lambda: A.activation(out=dst, in_=dst, func=AF.Sin), reads=[dstb], writes=[dstb])

    def phase_B(l):
        ph = Phase()
        io64 = ph.sb('io64', [128, 2, 64], F32, dma=True); tr.dma('sp', io64[:], iota64_in[:, :, :], writes=[io64])
        io9 = ph.sb('io9', [128, 9], F32, dma=True); tr.dma('sp', io9[:], iota9_in[:, :], writes=[io9])
        m2 = ph.sb('mask2', [128, 4, 8, 16], F32, dma=True); tr.dma('sp', m2[:], mask2_in[:, :, :, :], writes=[m2])
        mcm = ph.sb('maskc', [128, 4, 2, 64], F32, dma=True); tr.dma('sp', mcm[:], maskc_in[:, :, :, :], writes=[mcm])
        sdt = ph.sb('sd', [128, 3], F32, dma=True); tr.dma('sp', sdt[:], sd_in[l], writes=[sdt])
        identb = ph.sb('identb', [128, 128], BF16)
        tr.op('act', lambda: A.activation(out=identb[:], in_=ident[:], func=AF.Copy), reads=[ident], writes=[identb])
        pst = ph.ps('pst', [128, 128])
        prm = {}
        BP = int(os.environ.get('BPREP', '99'))
        if BP == 1: ph.close(); return
        pers = {}
        for d in range(2):
            pers[d] = dict(w=ph.sb('w%d' % d, [128, 16, 12]), bb=ph.sb('bb%d' % d, [128, 2, 12, 16]), RK=ph.sb('rk%d' % d, [128, 12, 9]),
                           LRk=ph.sb('lrk%d' % d, [128, 12, 9]), LIk=ph.sb('lik%d' % d, [128, 12, 9]),
                           C2=ph.sb('c2_%d' % d, [128, 12, 2, 64]), S2=ph.sb('s2_%d' % d, [128, 12, 2, 64]),
                           scr=ph.sb('scr%d' % d, [128, 3, 64], F32, dma=True), sci=ph.sb('sci%d' % d, [128, 3, 64], F32, dma=True))
        pp = Phase(reset=False)
        A64 = pp.sb('a64', [128, 12, 64]); TA64 = pp.sb('ta64', [128, 12, 64]); TI64 = pp.sb('ti64', [128, 12, 64], I32)
        C64 = pp.sb('c64', [128, 12, 64]); S64 = pp.sb('s64', [128, 12, 64])
        for d in range(2):
            PD = pers[d]
            sa = pp.sb('sa%d' % d, [128, 3, 12], F32, dma=True); tr.dma('sp', sa[:], sa_in[l, d], writes=[sa])
            sbr = pp.sb('sbr%d' % d, [128, 12, 16], F32, dma=True); tr.dma('sp', sbr[:], sb_in[l, d, 0], writes=[sbr])
            sbi = pp.sb('sbi%d' % d, [128, 12, 16], F32, dma=True); tr.dma('sp', sbi[:], sb_in[l, d, 1], writes=[sbi])
            scr_ = PD['scr']; tr.dma('sp', scr_[:], sc_in[l, d, 0], writes=[scr_])
            sci = PD['sci']; tr.dma('sp', sci[:], sc_in[l, d, 1], writes=[sci])
            w = PD['w']
            ki = pp.sb('ki%d' % d, [128, 12], I32)
            are = sa[:, 0, :]; aim = sa[:, 1, :]; ldt = sa[:, 2, :]
            DT, ARDT, RR, TH, KF, THR, CX, CM, CO, SI, LR, LI = [w[:, i, :] for i in range(12)]
            N1, N2, DEN, T3 = [w[:, 12 + i, :] for i in range(4)]
            tr.op('act', lambda: A.activation(out=DT, in_=ldt, func=AF.Exp), reads=[sa], writes=[w])
            tr.op('dve', lambda: V.tensor_tensor(out=ARDT, in0=are, in1=DT, op=ALU.mult), reads=[w, sa], acc=[w])
            tr.op('act', lambda: A.activation(out=RR, in_=ARDT, func=AF.Exp), reads=[w], acc=[w])
            tr.op('dve', lambda: V.tensor_tensor(out=TH, in0=aim, in1=DT, op=ALU.mult), reads=[w, sa], acc=[w])
            tr.op('dve', lambda: V.tensor_scalar(out=ki[:], in0=TH, scalar1=1.0 / (2 * PI), scalar2=None, op0=ALU.mult), reads=[w], writes=[ki])
            tr.op('dve', lambda: V.tensor_copy(out=KF, in_=ki[:]), reads=[ki], acc=[w])
            tr.op('dve', lambda: V.scalar_tensor_tensor(out=THR, in0=KF, scalar=-2 * PI, in1=TH, op0=ALU.mult, op1=ALU.add), reads=[w], acc=[w])
            tr.op('dve', lambda: V.tensor_scalar(out=THR, in0=THR, scalar1=-PI, scalar2=PI, op0=ALU.max, op1=ALU.min), reads=[w], acc=[w])
            tr.op('dve', lambda: V.tensor_scalar(out=CX, in0=THR, scalar1=PI / 2, scalar2=None, op0=ALU.add), reads=[w], acc=[w])
            tr.op('dve', lambda: V.tensor_scalar(out=CM, in0=CX, scalar1=PI, scalar2=-2 * PI, op0=ALU.is_gt, op1=ALU.mult), reads=[w], acc=[w])
            tr.op('dve', lambda: V.tensor_tensor(out=CX, in0=CX, in1=CM, op=ALU.add), reads=[w], acc=[w])
            tr.op('dve', lambda: V.tensor_scalar(out=CX, in0=CX, scalar1=-PI, scalar2=PI, op0=ALU.max, op1=ALU.min), reads=[w], acc=[w])
            tr.op('act', lambda: A.activation(out=CO, in_=CX, func=AF.Sin), reads=[w], acc=[w])
            tr.op('act', lambda: A.activation(out=SI, in_=THR, func=AF.Sin), reads=[w], acc=[w])
            tr.op('dve', lambda: V.tensor_tensor(out=LR, in0=RR, in1=CO, op=ALU.mult), reads=[w], acc=[w])
            tr.op('dve', lambda: V.tensor_scalar(out=LR, in0=LR, scalar1=-1.0, scalar2=None, op0=ALU.add), reads=[w], acc=[w])
            tr.op('dve', lambda: V.tensor_tensor(out=LI, in0=RR, in1=SI, op=ALU.mult), reads=[w], acc=[w])
            tr.op('dve', lambda: V.tensor_tensor(out=N1, in0=LR, in1=are, op=ALU.mult), reads=[w, sa], acc=[w])
            tr.op('dve', lambda: V.tensor_tensor(out=T3, in0=LI, in1=aim, op=ALU.mult), reads=[w, sa], acc=[w])
            tr.op('dve', lambda: V.tensor_tensor(out=N1, in0=N1, in1=T3, op=ALU.add), reads=[w], acc=[w])
            tr.op('dve', lambda: V.tensor_tensor(out=N2, in0=LI, in1=are, op=ALU.mult), reads=[w, sa], acc=[w])
            tr.op('dve', lambda: V.tensor_tensor(out=T3, in0=LR, in1=aim, op=ALU.mult), reads=[w, sa], acc=[w])
            tr.op('dve', lambda: V.tensor_tensor(out=N2, in0=N2, in1=T3, op=ALU.subtract), reads=[w], acc=[w])
            tr.op('dve', lambda: V.tensor_tensor(out=DEN, in0=are, in1=are, op=ALU.mult), reads=[w, sa], acc=[w])
            tr.op('dve', lambda: V.tensor_tensor(out=T3, in0=aim, in1=aim, op=ALU.mult), reads=[w, sa], acc=[w])
            tr.op('dve', lambda: V.tensor_tensor(out=DEN, in0=DEN, in1=T3, op=ALU.add), reads=[w], acc=[w])
            tr.op('dve', lambda: V.reciprocal(out=DEN, in_=DEN), reads=[w], acc=[w])
            tr.op('dve', lambda: V.tensor_tensor(out=N1, in0=N1, in1=DEN, op=ALU.mult), reads=[w], acc=[w])
            tr.op('dve', lambda: V.tensor_tensor(out=N2, in0=N2, in1=DEN, op=ALU.mult), reads=[w], acc=[w])
            bb = PD['bb']; tb = pp.sb('tb%d' % d, [128, 2, 12, 16])
            cre = N1.unsqueeze(2).to_broadcast([128, 12, 16]); cim = N2.unsqueeze(2).to_broadcast([128, 12, 16])
            tr.op('dve', lambda: V.tensor_tensor(out=bb[:, 0], in0=sbr[:], in1=cre, op=ALU.mult), reads=[w, sbr], writes=[bb])
            tr.op('dve', lambda: V.tensor_tensor(out=tb[:, 0], in0=sbi[:], in1=cim, op=ALU.mult), reads=[w, sbi], writes=[tb])
            tr.op('dve', lambda: V.tensor_tensor(out=bb[:, 0], in0=bb[:, 0], in1=tb[:, 0], op=ALU.subtract), reads=[bb, tb], acc=[bb])
            tr.op('dve', lambda: V.tensor_tensor(out=bb[:, 1], in0=sbi[:], in1=cre, op=ALU.mult), reads=[w, sbi], acc=[bb])
            tr.op('dve', lambda: V.tensor_tensor(out=tb[:, 1], in0=sbr[:], in1=cim, op=ALU.mult), reads=[w, sbr], acc=[tb])
            tr.op('dve', lambda: V.tensor_tensor(out=bb[:, 1], in0=bb[:, 1], in1=tb[:, 1], op=ALU.add), reads=[bb, tb], acc=[bb])
            if BP == 2: pp.close(); ph.close(); return
            ANG = pp.sb('ang9_%d' % d, [128, 12, 9]); TA9 = pp.sb('ta9_%d' % d, [128, 12, 9]); TI9 = pp.sb('ti9_%d' % d, [128, 12, 9], I32)
            COk = pp.sb('cok%d' % d, [128, 12, 9]); SIk = pp.sb('sik%d' % d, [128, 12, 9]); RK = PD['RK']
            LRk = PD['LRk']; LIk = PD['LIk']
            i9b = io9[:].unsqueeze(1).to_broadcast([128, 12, 9])
            tr.op('dve', lambda: V.tensor_tensor(out=ANG[:], in0=THR.unsqueeze(2).to_broadcast([128, 12, 9]), in1=i9b, op=ALU.mult), reads=[w, io9], writes=[ANG])
            sin_rr(ANG[:], ANG, SIk[:], SIk, 0.0, TA9[:], TA9, TI9[:], TI9)
            sin_rr(ANG[:], ANG, COk[:], COk, PI / 2, TA9[:], TA9, TI9[:], TI9)
            tr.op('dve', lambda: V.tensor_tensor(out=RK[:], in0=ARDT.unsqueeze(2).to_broadcast([128, 12, 9]), in1=i9b, op=ALU.mult), reads=[w, io9], writes=[RK])
            tr.op('act', lambda: A.activation(out=RK[:], in_=RK[:], func=AF.Exp), reads=[RK], writes=[RK])
            tr.op('dve', lambda: V.tensor_tensor(out=LRk[:], in0=RK[:], in1=COk[:], op=ALU.mult), reads=[RK, COk], writes=[LRk])
            tr.op('dve', lambda: V.tensor_tensor(out=LIk[:], in0=RK[:], in1=SIk[:], op=ALU.mult), reads=[RK, SIk], writes=[LIk])
            if BP == 3: pp.close(); ph.close(); return
            TH8 = pp.sb('th8_%d' % d, [128, 12]); TA8 = pp.sb('ta8_%d' % d, [128, 12]); TI8 = pp.sb('ti8_%d' % d, [128, 12], I32)
            tr.op('dve', lambda: V.tensor_scalar(out=TA8[:], in0=THR, scalar1=8.0, scalar2=None, op0=ALU.mult), reads=[w], writes=[TA8])
            tr.op('dve', lambda: V.tensor_scalar(out=TI8[:], in0=TA8[:], scalar1=1.0 / (2 * PI), scalar2=None, op0=ALU.mult), reads=[TA8], writes=[TI8])
            tr.op('dve', lambda: V.tensor_copy(out=TH8[:], in_=TI8[:]), reads=[TI8], writes=[TH8])
            tr.op('dve', lambda: V.scalar_tensor_tensor(out=TH8[:], in0=TH8[:], scalar=-2 * PI, in1=TA8[:], op0=ALU.mult, op1=ALU.add), reads=[TH8, TA8], writes=[TH8])
            tr.op('dve', lambda: V.tensor_tensor(out=A64[:], in0=TH8[:].unsqueeze(2).to_broadcast([128, 12, 64]),
                                                 in1=io64[:, d, :].unsqueeze(1).to_broadcast([128, 12, 64]), op=ALU.mult), reads=[TH8, io64], writes=[A64])
            sin_rr(A64[:], A64, S64[:], S64, 0.0, TA64[:], TA64, TI64[:], TI64)
            sin_rr(A64[:], A64, C64[:], C64, PI / 2, TA64[:], TA64, TI64[:], TI64)
            if BP == 4: pp.close(); ph.close(); return
            C2 = PD['C2']; S2 = PD['S2']
            tr.group('act', [lambda: A.activation(out=C2[:, :, 0, :], in_=C64[:], func=AF.Copy),
                             lambda: A.activation(out=C2[:, :, 1, :], in_=C64[:], func=AF.Copy)], reads=[C64], writes=[C2])
            tr.group('act', [lambda: A.activation(out=S2[:, :, 0, :], in_=S64[:], func=AF.Copy),
                             lambda: A.activation(out=S2[:, :, 1, :], in_=S64[:], func=AF.Copy, scale=-1.0)], reads=[S64], writes=[S2])
            if BP == 5: pp.close(); ph.close(); return
            prm[d] = dict(w=w, LRk=LRk, LIk=LIk, RK=RK, C2=C2, S2=S2, bb=bb, scr=scr_, sci=sci)

        pp.close()
        BCUT = int(os.environ.get('BCUT', '99'))
        if BCUT == 0: ph.close(); return
        yacc = ph.sb('yacc', [128, TT], F32, dma=True); ub = ph.sb('ub', [128, TT], BF16)
        XHs = [ph.sb('XH%d' % i, [128, 4, 2, 8, 128], BF16) for i in range(2)]
        XTs = [ph.sb('XTt%d' % i, [128, 4, 2, 8, 128], BF16) for i in range(2)]
        Kbs = [ph.sb('Kb%d' % i, [128, 8, 128], BF16) for i in range(2)]
        Bstg = ph.sb('bstg4', [128, 4, 2, 128]); Cf = ph.sb('cf4', [128, 4, 2, 128]); cpad = ph.sb('cpad4', [128, 4, 2, 128], BF16)
        stg = [ph.sb('stgc%d' % i, [128, 128]) for i in range(2)]
        tq = [ph.sb('tq%d' % i, [128, 8, 128]) for i in range(2)]
        pxt = ph.ps('pxt', [128, 8, 128], BF16)
        pk = [ph.ps('pk%d' % i, [128, 4, 128]) for i in range(2)]
        pv = [ph.ps('pv%d' % i, [128, 4, 2, 64]) for i in range(2)]
        po = ph.ps('po', [128, 8, 64])
        t1 = ph.sb('bt1', [128, 4, 2, 64]); t2 = ph.sb('bt2', [128, 4, 2, 64]); Wt = ph.sb('bW', [128, 4, 2, 64]); Zt = ph.sb('bZ', [128, 4, 2, 64])
        Sf = [ph.sb('bSf%d' % i, [128, 4, 2, 64]) for i in range(2)]
        Sp = [ph.sb('bSp%d' % i, [128, 4, 2, 64], BF16) for i in range(2)]
        segs = [(0, 32)] + [(NCX + i * 512, 64) for i in range(8)]
        scnt = dict(n=0)

        def make_setup(j3, d, slot):
            P = prm[d]; XH = XHs[slot]; XT = XTs[slot]; Kb = Kbs[slot]
            steps = []

            def stage_s(s4):
                s = 4 * j3 + s4
                for ri in range(2):
                    first = (s4 == 0 and ri == 0)
                    tr.op('dve', lambda: V.tensor_tensor(out=Bstg[:, s4, ri, :].rearrange('p (a b) -> p a b', a=8),
                                                         in0=P['bb'][:, ri, s, :].unsqueeze(1).to_broadcast([128, 8, 16]),
                                                         in1=m2[:, s % 4], op=ALU.mult), reads=[P['bb'], m2],
                          writes=[Bstg] if first else (), acc=() if first else [Bstg])
                    sg_ = stg[scnt['n'] % 2]; scnt['n'] += 1
                    csrc = (P['scr'] if ri == 0 else P['sci'])
                    tr.op('dve', lambda: V.tensor_tensor(out=sg_[:].rearrange('p (a b) -> p a b', a=2),
                                                         in0=csrc[:, s // 4, :].unsqueeze(1).to_broadcast([128, 2, 64]),
                                                         in1=mcm[:, s % 4], op=ALU.mult), reads=[csrc, mcm], writes=[sg_])
                    tr.op('pe', lambda: PE.transpose(pst[:], sg_[:], ident[:]), reads=[sg_, ident], writes=[pst])
                    tr.op('act', lambda: A.activation(out=Cf[:, s4, ri, :], in_=pst[:], func=AF.Copy), reads=[pst],
                          writes=[Cf] if first else (), acc=() if first else [Cf])
                    tr.op('act', lambda: A.activation(out=cpad[:, s4, ri, :], in_=pst[:], func=AF.Copy, scale=(1.0 if ri == 0 else -1.0)),
                          reads=[pst], writes=[cpad] if first else (), acc=() if first else [cpad])

            def scaled_s(dstbuf, src, k0, neg_im, s4):
                s = 4 * j3 + s4
                sre = src[:, s4, 0, :].unsqueeze(1).to_broadcast([128, 8, 128]); sim = src[:, s4, 1, :].unsqueeze(1).to_broadcast([128, 8, 128])
                lr = P['LRk'][:, s, k0:k0 + 8].unsqueeze(2).to_broadcast([128, 8, 128])
                li = P['LIk'][:, s, k0:k0 + 8].unsqueeze(2).to_broadcast([128, 8, 128])
                first = (s4 == 0)
                tr.op('dve', lambda: V.tensor_tensor(out=tq[0][:], in0=sre, in1=lr, op=ALU.mult), reads=[src, P['LRk']], writes=[tq[0]])
                tr.op('pool', lambda: G.tensor_tensor(out=tq[1][:], in0=sim, in1=li, op=ALU.mult), reads=[src, P['LIk']], writes=[tq[1]])
                tr.op('dve', lambda: V.tensor_tensor(out=dstbuf[:, s4, 0], in0=tq[0][:], in1=tq[1][:], op=ALU.subtract), reads=[tq[0], tq[1]],
                      writes=[dstbuf] if first else (), acc=() if first else [dstbuf])
                tr.op('dve', lambda: V.tensor_tensor(out=tq[0][:], in0=sre, in1=li, op=ALU.mult), reads=[src, P['LIk']], writes=[tq[0]])
                tr.op('pool', lambda: G.tensor_tensor(out=tq[1][:], in0=sim, in1=lr, op=ALU.mult), reads=[src, P['LRk']], writes=[tq[1]])
                if neg_im:
                    tr.op('dve', lambda: V.scalar_tensor_tensor(out=dstbuf[:, s4, 1], in0=tq[0][:], scalar=-1.0, in1=tq[1][:], op0=ALU.mult, op1=ALU.subtract),
                          reads=[tq[0], tq[1]], acc=[dstbuf])
                else:
                    tr.op('dve', lambda: V.tensor_tensor(out=dstbuf[:, s4, 1], in0=tq[0][:], in1=tq[1][:], op=ALU.add), reads=[tq[0], tq[1]], acc=[dstbuf])

            def kmm(half):
                pkk = pk[half]
                fns = []
                for tt in range(4):
                    tau = half * 4 + tt
                    k = 0
                    for s4 in range(4):
                        for ri in range(2):
                            fns.append(lambda tt=tt, tau=tau, s4=s4, ri=ri, k=k: PE.matmul(pkk[:, tt, :], lhsT=XH[:, s4, ri, tau, :], rhs=cpad[:, s4, ri, :],
                                                                                             start=(k == 0), stop=(k == 7)))
                            k += 1
                tr.group('pe', fns, reads=[XH, cpad], writes=[pkk])
                tr.op('act', lambda: A.activation(out=Kb[:, half * 4:half * 4 + 4, :], in_=pkk[:], func=AF.Copy), reads=[pkk],
                      writes=[Kb] if half == 0 else (), acc=() if half == 0 else [Kb])

            def xtr(s4, ri):
                tr.group('pe', [lambda tau=tau: PE.transpose(pxt[:, tau, :], XH[:, s4, ri, tau, :], identb[:]) for tau in range(8)],
                         reads=[XH, identb], writes=[pxt])
                first = (s4 == 0 and ri == 0)
                tr.op('act' if ri == 0 else 'dve',
                      (lambda: A.activation(out=XT[:, s4, ri], in_=pxt[:], func=AF.Copy)) if ri == 0 else (lambda: V.tensor_copy(out=XT[:, s4, ri], in_=pxt[:])),
                      reads=[pxt], writes=[XT] if first else (), acc=() if first else [XT])

            for s4 in range(4): steps.append(lambda s4=s4: stage_s(s4))
            for s4 in range(4): steps.append(lambda s4=s4: scaled_s(XH, Bstg, 0, False, s4))
            for half in range(2): steps.append(lambda half=half: kmm(half))
            for s4 in range(4):
                for ri in range(2): steps.append(lambda s4=s4, ri=ri: xtr(s4, ri))
            for s4 in range(4): steps.append(lambda s4=s4: scaled_s(XH, Cf, 1, True, s4))
            return steps

        def run_loop(j3, d, slot, pending):
            P = prm[d]; XH = XHs[slot]; XT = XTs[slot]; Kb = Kbs[slot]
            order = list(range(9)) if d == 0 else [0] + list(range(8, 0, -1))
            prev = None
            per = -(-len(pending) // 8) if pending else 0

            def vmm(oi):
                t0, nC = segs[order[oi]]
                pvv = pv[oi % 2]
                fns = []
                for s4 in range(4):
                    for ri in range(2):
                        for j in range(8):
                            tau = (7 - j) if d == 0 else j
                            fns.append(lambda s4=s4, ri=ri, j=j, tau=tau: PE.matmul(pvv[:, s4, ri, :nC], lhsT=XT[:, s4, ri, tau, :],
                                                                                     rhs=ub[:, t0 + j:t0 + j + 8 * (nC - 1) + 1:8], start=(j == 0), stop=(j == 7)))
                tr.group('pe', fns, reads=[XT, ub], writes=[pvv])

            vmm(0)
            for oi in range(9):
                t0, nC = segs[order[oi]]
                par = oi % 2
                pvv = pv[par]
                s0 = 4 * j3
                csl = slice(0, nC) if d == 0 else slice(64 - nC, 64)
                C2v = P['C2'][:, s0:s0 + 4, :, csl]; S2v = P['S2'][:, s0:s0 + 4, :, csl]
                tr.op('dve', lambda: V.tensor_tensor(out=t1[:, :, :, :nC], in0=pvv[:, :, :, :nC], in1=C2v, op=ALU.mult), reads=[pvv, P['C2']], writes=[t1])
                tr.op('dve', lambda: V.tensor_tensor(out=t2[:, :, :, :nC], in0=pvv[:, :, ::-1, :nC], in1=S2v, op=ALU.mult), reads=[pvv, P['S2']], writes=[t2])
                tr.op('dve', lambda: V.tensor_tensor(out=Wt[:, :, :, :nC], in0=t1[:, :, :, :nC], in1=t2[:, :, :, :nC], op=ALU.add), reads=[t1, t2], writes=[Wt])
                if oi + 1 < 9: vmm(oi + 1)
                fns = []
                for s4 in range(4):
                    rr = P['RK'][:, s0 + s4, 8:9].to_broadcast([128, nC])
                    for ri in range(2):
                        if prev is None: ini = 0.0
                        else:
                            pcol = (prev[1] - 1) if d == 0 else 0
                            ini = Sf[1 - par][:, s4, ri, pcol:pcol + 1]
                        if d == 0:
                            fns.append(lambda s4=s4, ri=ri, rr=rr, ini=ini: V.tensor_tensor_scan(out=Zt[:, s4, ri, :nC], data0=rr, data1=Wt[:, s4, ri, :nC],
                                                                                                 initial=ini, op0=ALU.mult, op1=ALU.add))
                        else:
                            fns.append(lambda s4=s4, ri=ri, rr=rr, ini=ini: V.tensor_tensor_scan(out=Zt[:, s4, ri, :nC][:, ::-1], data0=rr, data1=Wt[:, s4, ri, :nC][:, ::-1],
                                                                                                 initial=ini, op0=ALU.mult, op1=ALU.add))
                tr.group('dve', fns, reads=[Wt, P['RK']] + ([Sf[1 - par]] if prev is not None else []), writes=[Zt])
                tr.op('pool', lambda: G.tensor_tensor(out=t1[:, :, :, :nC], in0=Zt[:, :, :, :nC], in1=C2v, op=ALU.mult), reads=[Zt, P['C2']], writes=[t1])
                tr.op('dve', lambda: V.tensor_tensor(out=t2[:, :, :, :nC], in0=Zt[:, :, ::-1, :nC], in1=S2v, op=ALU.mult), reads=[Zt, P['S2']], writes=[t2])
                tr.op('dve', lambda: V.tensor_tensor(out=Sf[par][:, :, :, :nC], in0=t1[:, :, :, :nC], in1=t2[:, :, :, :nC], op=ALU.subtract), reads=[t1, t2], writes=[Sf[par]])
                spv = Sp[par]
                if d == 0:
                    f1 = lambda: A.activation(out=spv[:, :, :, 1:nC], in_=Sf[par][:, :, :, 0:nC - 1], func=AF.Copy)
                    cdst = spv[:, :, :, 0:1]
                else:
                    f1 = lambda: A.activation(out=spv[:, :, :, 0:nC - 1], in_=Sf[par][:, :, :, 1:nC], func=AF.Copy)
                    cdst = spv[:, :, :, nC - 1:nC]
                if prev is None:
                    f2 = lambda: A.activation(out=cdst, in_=Sf[par][:, :, :, 0:1], func=AF.Copy, scale=0.0)
                    rdl = [Sf[par]]
                else:
                    pcol = (prev[1] - 1) if d == 0 else 0
                    f2 = lambda: A.activation(out=cdst, in_=Sf[1 - par][:, :, :, pcol:pcol + 1], func=AF.Copy)
                    rdl = [Sf[par], Sf[1 - par]]
                tr.group('act', [f1, f2], reads=rdl, writes=[spv])
                for _ in range(per):
                    if pending: pending.pop(0)()
                fns = []
                for j in range(8):
                    ntap = (j + 1) if d == 0 else (8 - j)
                    hk = j if d == 0 else (7 - j)
                    tot = ntap + 8; k = 0
                    for tau in range(ntap):
                        off = (j - tau) if d == 0 else (j + tau)
                        fns.append(lambda j=j, tau=tau, off=off, k=k, tot=tot: PE.matmul(po[:, j, :nC], lhsT=Kb[:, tau, :], rhs=ub[:, t0 + off:t0 + off + 8 * (nC - 1) + 1:8],
                                                                                          start=(k == 0), stop=(k == tot - 1)))
                        k += 1
                    for s4 in range(4):
                        for ri in range(2):
                            fns.append(lambda j=j, s4=s4, ri=ri, hk=hk, k=k, tot=tot: PE.matmul(po[:, j, :nC], lhsT=XH[:, s4, ri, hk, :], rhs=spv[:, s4, ri, :nC],
                                                                                                  start=(k == 0), stop=(k == tot - 1)))
                            k += 1
                tr.group('pe', fns, reads=[Kb, ub, XH, spv], writes=[po])
                yv = yacc[:, t0:t0 + 8 * nC].rearrange('p (c j) -> p j c', j=8)
                tr.op('dve', lambda: V.tensor_tensor(out=yv, in0=yv, in1=po[:, :, :nC], op=ALU.add), reads=[po, yacc], acc=[yacc])
                prev = (oi, nC)
            while pending: pending.pop(0)()

        ulist = [(j3, d) for j3 in range(3) for d in range(2)]
        for st_ in make_setup(0, 0, 0): st_()
        for ui, (j3, d) in enumerate(ulist):
            if d == 0:
                tr.dma('sp', yacc[:], U[j3 * 128:(j3 + 1) * 128, :], writes=[yacc])
                tr.op('act', lambda: A.activation(out=ub[:], in_=yacc[:], func=AF.Copy), reads=[yacc], writes=[ub])
                tr.op('dve', lambda: V.tensor_scalar(out=yacc[:], in0=yacc[:], scalar1=sdt[:, j3:j3 + 1], scalar2=None, op0=ALU.mult), reads=[yacc, sdt], writes=[yacc])
            pending = make_setup(ulist[ui + 1][0], ulist[ui + 1][1], (ui + 1) % 2) if ui + 1 < len(ulist) else []
            run_loop(j3, d, ui % 2, pending)
            if d == 1:
                for g0 in range(0, TT, 512):
                    n = min(512, TT - g0); yv = yacc[:, g0:g0 + n]
                    a0b = tq[0]; a1b = tq[1]
                    a0 = tq[0][:].rearrange('p a b -> p (a b)'); a1 = tq[1][:].rearrange('p a b -> p (a b)')
                    tr.op('act', lambda: A.activation(out=a0[:, :n], in_=yv, func=AF.Square), reads=[yacc], writes=[a0b])
                    tr.op('dve', lambda: V.tensor_scalar(out=a0[:, :n], in0=a0[:, :n], scalar1=0.044715, scalar2=1.0, op0=ALU.mult, op1=ALU.add), reads=[a0b], writes=[a0b])
                    tr.op('dve', lambda: V.tensor_tensor(out=a1[:, :n], in0=a0[:, :n], in1=yv, op=ALU.mult), reads=[a0b, yacc], writes=[a1b])
                    tr.op('act', lambda: A.activation(out=a1[:, 512:512 + n], in_=a1[:, :n], func=AF.Sigmoid, scale=2.0 * math.sqrt(2.0 / PI)), reads=[a1b], writes=[a1b])
                    st = stg_o[(g0 // 512) % 2]
                    tr.op('dve', lambda: V.tensor_tensor(out=st[:, :n], in0=a1[:, 512:512 + n], in1=yv, op=ALU.mult), reads=[a1b, yacc], writes=[st])
                    tr.dma('pool', TG[j3 * 128:(j3 + 1) * 128, g0:g0 + n], st[:, :n], reads=[st])
        ph.close()

    gm = None
    stg_o = None

    def phase_C(l, ctx_out):
        nonlocal gm
        ph = Phase()
        gm = ph.sb('gm', [128, 2, 128], F32, dma=True); tr.dma('sp', gm[:], gmask_in[:, :, :], writes=[gm])
        snk = ph.sb('snk', [64, 8], F32, dma=True); tr.dma('sp', snk[:], sink_in[l], writes=[snk])
        esk = ph.sb('esk', [64, 8])
        tr.op('act', lambda: A.activation(out=esk[:], in_=snk[:], func=AF.Exp), reads=[snk], writes=[esk])
        vsb = ph.sb('vsb', [128, 34, 128], BF16, dma=True)
        tr.dma('sp', vsb[:], VG.rearrange('(c p) d -> p c d', p=128), writes=[vsb])
        kT = [ph.sb('kT%d' % g, [64, TT], BF16, dma=True) for g in range(2)]
        for g in range(2):
            tr.dma('sp', kT[g][:, 0:NCX], KGC[g], writes=[kT[g]])
            tr.dma('sp', kT[g][:, NCX:TT], KG[g], acc=[kT[g]])
        qb = [ph.sb('qb%d' % i, [64, 4, 128], BF16, dma=True) for i in range(3)]
        ps_s = [ph.ps('pss%d' % i) for i in range(4)]
        ps_o = [ph.ps('pso%d' % i) for i in range(2)]; ps_d = [ph.ps('psd%d' % i) for i in range(2)]
        pT = [ph.sb('pT%d' % i, [128, 512], BF16) for i in range(4)]
        tmpm = [ph.sb('tmpm%d' % i, [128, 512]) for i in range(2)]; den = ph.sb('den', [64, 512])
        ob = [ph.sb('ob%d' % i, [64, 4, 128], BF16, dma=True) for i in range(2)]
        units = []
        if ctx_out:
            for g in range(2):
                for bi in range(2): units.append(('c', g, bi))
        for g in range(2):
            for bi in range(32): units.append(('l', g, bi))
        items = []
        uinfo = []
        for ui, (kind, g, bi) in enumerate(units):
            keys = [(kT[g][:, c * 128:(c + 1) * 128], None, vsb[:, c, g * 64:(g + 1) * 64]) for c in range(2)]
            if kind == 'l':
                for dlt in (-1, 0, 1):
                    kb = bi + dlt
                    if kb < 0 or kb > 31: continue
                    m = None if dlt == 0 else gm[:, (0 if dlt == -1 else 1), :].unsqueeze(1).to_broadcast([128, 4, 128])
                    keys.append((kT[g][:, NCX + kb * 128:NCX + (kb + 1) * 128], m, vsb[:, 2 + kb, g * 64:(g + 1) * 64]))
            for ki, (k_ap, m_ap, v_ap) in enumerate(keys): items.append((ui, ki, len(keys), k_ap, m_ap, v_ap))
        cnt = dict(n=0, m=0)
        st = {}

        def stage1(ii):
            ui, ki, nk, k_ap, m_ap, v_ap = items[ii]
            kind, g, bi = units[ui]
            q = qb[ui % 3]
            if ki == 0:
                srcq = (QGC if kind == 'c' else QG)[4 * g:4 * g + 4, :, bi * 128:(bi + 1) * 128].rearrange('h d t -> d h t')
                tr.dma('sp', q[:], srcq, writes=[q])
            pss = ps_s[cnt['n'] % 4]; pt = pT[cnt['n'] % 4]; cnt['n'] += 1
            tr.op('pe', lambda: PE.matmul(pss[:, :], lhsT=k_ap, rhs=q[:], start=True, stop=True), reads=[q, kT[g]], writes=[pss])
            if m_ap is not None:
                tm = tmpm[cnt['m'] % 2]; cnt['m'] += 1
                tr.op('dve', lambda: V.tensor_tensor(out=tm[:].rearrange('p (h t) -> p h t', h=4), in0=pss[:].rearrange('p (h t) -> p h t', h=4),
                                                     in1=m_ap, op=ALU.add), reads=[pss, gm], writes=[tm])
                tr.op('act', lambda: A.activation(out=pt[:], in_=tm[:], func=AF.Exp), reads=[tm], writes=[pt])
            else:
                tr.op('act', lambda: A.activation(out=pt[:], in_=pss[:], func=AF.Exp), reads=[pss], writes=[pt])
            st[ii] = pt

        def stage2(ii):
            ui, ki, nk, k_ap, m_ap, v_ap = items[ii]
            kind, g, bi = units[ui]
            pt = st.pop(ii)
            pso = ps_o[ui % 2]; psd = ps_d[ui % 2]; o = ob[ui % 2]
            tr.op('pe', lambda: PE.matmul(pso[0:64, :], lhsT=v_ap, rhs=pt[:], start=(ki == 0), stop=(ki == nk - 1)),
                  reads=[pt, vsb], writes=[pso] if ki == 0 else (), acc=() if ki == 0 else [pso])
            tr.op('pe', lambda: PE.matmul(psd[0:64, :], lhsT=ones_b[:, 0:64], rhs=pt[:], start=(ki == 0), stop=(ki == nk - 1)),
                  reads=[pt, ones_b], writes=[psd] if ki == 0 else (), acc=() if ki == 0 else [psd])
            if ki == nk - 1:
                tr.op('dve', lambda: V.tensor_tensor(out=den[:].rearrange('p (h t) -> p h t', h=4), in0=psd[0:64, :].rearrange('p (h t) -> p h t', h=4),
                                                     in1=esk[:, 4 * g:4 * g + 4].unsqueeze(2).to_broadcast([64, 4, 128]), op=ALU.add),
                      reads=[psd, esk], writes=[den])
                tr.op('dve', lambda: V.reciprocal(out=den[:], in_=den[:]), reads=[den], writes=[den])
                tr.op('dve', lambda: V.tensor_tensor(out=o[:].rearrange('p h t -> p (h t)'), in0=pso[0:64, :], in1=den[:], op=ALU.mult),
                      reads=[pso, den], writes=[o])
                dst = (YGC if kind == 'c' else YG)[4 * g:4 * g + 4, :, bi * 128:(bi + 1) * 128].rearrange('h d t -> d h t')
                tr.dma('pool', dst, o[:], reads=[o])

        KD = 2
        for ii in range(len(items) + KD):
            if ii < len(items): stage1(ii)
            if ii - KD >= 0: stage2(ii - KD)
        ph.close()

    def attn_unit3(q, keys, ps_s, pso, psd, pT, tmpm, qbufs, kbufs, vbufs, epi):
        nk = len(keys)
        for ki, (k_ap, m_ap, v_ap) in enumerate(keys):
            pss = ps_s[ki % len(ps_s)]
            tr.op('pe', lambda: PE.matmul(pss[:, :], lhsT=k_ap, rhs=q[:], start=True, stop=True), reads=qbufs + kbufs, writes=[pss])
            pt = pT[ki % len(pT)]
            if m_ap is not None:
                tr.op('dve', lambda: V.tensor_tensor(out=tmpm[:].rearrange('p (h t) -> p h t', h=4), in0=pss[:].rearrange('p (h t) -> p h t', h=4),
                                                     in1=m_ap, op=ALU.add), reads=[pss, gm], writes=[tmpm])
                tr.op('act', lambda: A.activation(out=pt[:], in_=tmpm[:], func=AF.Exp), reads=[tmpm], writes=[pt])
            else:
                tr.op('act', lambda: A.activation(out=pt[:], in_=pss[:], func=AF.Exp), reads=[pss], writes=[pt])
            tr.op('pe', lambda: PE.matmul(pso[0:64, :], lhsT=v_ap, rhs=pt[:], start=(ki == 0), stop=(ki == nk - 1)),
                  reads=[pt] + vbufs, writes=[pso] if ki == 0 else (), acc=() if ki == 0 else [pso])
            tr.op('pe', lambda: PE.matmul(psd[0:64, :], lhsT=ones_b[:, 0:64], rhs=pt[:], start=(ki == 0), stop=(ki == nk - 1)),
                  reads=[pt, ones_b], writes=[psd] if ki == 0 else (), acc=() if ki == 0 else [psd])
        epi()

    def phase_D(l, ctx_out):
        ph = Phase()
        kT = [ph.sb('nkT%d' % i, [64, TT], BF16, dma=True) for i in range(2)]
        qT = [ph.sb('nqT%d' % i, [64, TT], BF16, dma=True) for i in range(2)]
        vs = [ph.sb('nvs%d' % i, [128, 34, 64], BF16, dma=True) for i in range(2)]
        vt = [ph.sb('nvt%d' % i, [128, 14, 64], F32, dma=True) for i in range(2)]
        od = [ph.sb('nod%d' % i, [128, 5, 64], F32, dma=True) for i in range(2)]
        cbt = [ph.sb('ncb%d' % i, [128, 5, 128], F32, dma=True) for i in range(2)]
        yb = [ph.sb('nyb%d' % i, [64, TT], BF16, dma=True) for i in range(2)]
        RD = 2
        ps_L = [ph.ps('npl%d' % i) for i in range(RD)]
        ps_X = [ph.ps('npx%d' % i) for i in range(RD)]
        ps_od = [ph.ps('npo%d' % i) for i in range(RD)]
        tmp = [ph.sb('ntmp%d' % i, [128, 640]) for i in range(RD)]
        pT = [ph.sb('npT%d' % i, [128, 640], BF16) for i in range(RD)]
        pC = [ph.sb('npC%d' % i, [128, 256], BF16) for i in range(RD)]
        rd = [ph.sb('nrd%d' % i, [64, 256]) for i in range(RD)]
        n = 0
        for h in range(8):
            k_ = kT[h % 2]; q_ = qT[h % 2]; v_ = vs[h % 2]; vt_ = vt[h % 2]; od_ = od[h % 2]; cb_ = cbt[h % 2]; y_ = yb[h % 2]
            tr.dma('sp', k_[:, 0:NCX], KNC[h], writes=[k_]); tr.dma('sp', k_[:, NCX:TT], KN[h], acc=[k_])
            tr.dma('sp', q_[:, 0:NCX], QNC[h], writes=[q_]); tr.dma('sp', q_[:, NCX:TT], QN[h], acc=[q_])
            tr.dma('sp', v_[:], VN[:, h * 64:(h + 1) * 64].rearrange('(c p) d -> p c d', p=128), writes=[v_])
            tr.dma('sp', vt_[:], navt_in[l, h], writes=[vt_]); tr.dma('sp', od_[:], naod_in[l, h], writes=[od_])
            tr.dma('sp', cb_[:], nacb_in[l, h], writes=[cb_])
            units = ([('c', 0), ('c', 1)] if ctx_out else []) + [('l', r) for r in range(4)] + [('p', r) for r in range(4, 60, 2)] + [('l', r) for r in range(60, 64)]
            for ui, (kind, r) in enumerate(units):
                psl = ps_L[n % RD]; psx = ps_X[n % RD]; pob = ps_od[n % RD]
                tm = tmp[n % RD]; pt = pT[n % RD]; pc = pC[n % RD]; rdn = rd[n % RD]; n += 1
                p0 = 0
                if kind == 'c':
                    Nq = 128; qap = q_[:, r * 128:(r + 1) * 128]; npair = 0; oc0 = r * 128
                elif kind == 'p':
                    Nq = 128; qap = q_[:, NCX + r * 64:NCX + (r + 2) * 64]; oc0 = NCX + r * 64
                    p0 = (r - 4) // 2; npair = 5
                else:
                    Nq = 64; qap = q_[:, NCX + r * 64:NCX + (r + 1) * 64]; oc0 = NCX + r * 64
                    rs = min(max(r - 4, 0), 56)
                    p0 = rs // 2; npair = 4; i0_ = rs - r + 7
                    bias = vt_[:, i0_:i0_ + 7:2, :]
                fns = [lambda c=c: PE.matmul(psx[:, 128 + c * Nq:128 + (c + 1) * Nq], lhsT=k_[:, c * 128:(c + 1) * 128], rhs=qap, start=True, stop=True) for c in range(2)]
                if npair == 5:
                    fns.append(lambda: PE.matmul(psx[:, 0:Nq], lhsT=k_[:, NCX + (p0 + 4) * 128:NCX + (p0 + 5) * 128], rhs=qap, start=True, stop=True))
                tr.group('pe', fns, reads=[k_, q_], writes=[psx])
                if npair:
                    tr.group('pe', [lambda k=k: PE.matmul(psl[:, k * Nq:(k + 1) * Nq], lhsT=k_[:, NCX + (p0 + k) * 128:NCX + (p0 + k + 1) * 128], rhs=qap,
                                                          start=True, stop=True) for k in range(4)], reads=[k_, q_], writes=[psl])
                tr.op('act', lambda: A.activation(out=pc[:, :2 * Nq], in_=psx[:, 128:128 + 2 * Nq], func=AF.Exp), reads=[psx], writes=[pc])
                if kind == 'l':
                    tr.op('dve', lambda: V.tensor_tensor(out=tm[:, :256].rearrange('p (a b) -> p a b', a=4),
                                                         in0=psl[:, :256].rearrange('p (a b) -> p a b', a=4), in1=bias, op=ALU.add),
                          reads=[psl, vt_], writes=[tm])
                    tr.op('act', lambda: A.activation(out=pt[:, :256], in_=tm[:, :256], func=AF.Exp), reads=[tm], writes=[pt])
                elif kind == 'p':
                    tr.op('dve', lambda: V.tensor_tensor(out=tm[:, 0:512], in0=psl[:, 0:512], in1=cb_[:, 0:4, :].rearrange('p a b -> p (a b)'), op=ALU.add),
                          reads=[psl, cb_], writes=[tm])
                    tr.op('dve', lambda: V.tensor_tensor(out=tm[:, 512:640], in0=psx[:, 0:128], in1=cb_[:, 4, :], op=ALU.add),
                          reads=[psx, cb_], acc=[tm])
                    tr.op('act', lambda: A.activation(out=pt[:, :640], in_=tm[:, :640], func=AF.Exp), reads=[tm], writes=[pt])
                mm = [(v_[:, c, :], pc[:, c * Nq:(c + 1) * Nq]) for c in range(2)]
                mm += [(v_[:, 2 + p0 + k, :], pt[:, k * Nq:(k + 1) * Nq]) for k in range(npair)]
                tr.group('pe', [lambda i=i, a=a, b=b: PE.matmul(pob[0:64, 0:Nq], lhsT=a, rhs=b, start=(i == 0), stop=(i == len(mm) - 1))
                                for i, (a, b) in enumerate(mm)], reads=[v_, pc, pt], writes=[pob])
                tr.group('pe', [lambda i=i, b=b: PE.matmul(pob[0:64, 128:128 + Nq], lhsT=ones_b[:, 0:64], rhs=b, start=(i == 0), stop=(i == len(mm) - 1))
                                for i, (a, b) in enumerate(mm)], reads=[ones_b, pc, pt], acc=[pob])
                tr.op('dve', lambda: V.reciprocal(out=rdn[:, :Nq], in_=pob[0:64, 128:128 + Nq]), reads=[pob], writes=[rdn])
                tr.op('dve', lambda: V.tensor_tensor(out=y_[:, oc0:oc0 + Nq], in0=pob[0:64, 0:Nq], in1=rdn[:, :Nq], op=ALU.mult),
                      reads=[pob, rdn], writes=[y_] if ui == 0 else (), acc=() if ui == 0 else [y_])
            if ctx_out: tr.dma('pool', YNC[h], y_[:, 0:NCX], reads=[y_])
            tr.dma('pool', YN[h], y_[:, NCX:TT], reads=[y_])
        ph.close()

    pcbig = None

    def phase_E(l, ctx_out, last):
        ph = Phase()
        R = ffn_bufs(ph)
        xTs = [ph.sb('xT%d' % i, [128, KC, 512], F32, dma=True) for i in range(2)]
        hT = ph.sb('hT', [128, KC, 512], BF16); aT = ph.sb('aT', [128, FC, 512], BF16)
        tg = [ph.sb('tg%d' % i, [128, 3, 512], F32, dma=True) for i in range(2)]
        tgb = ph.sb('tgb', [128, 3, 512], BF16); ys = ph.sb('ys', [128, 3, 512], BF16)
        yg = [ph.sb('yg%d' % i, [64, 8, 512], BF16, dma=True) for i in range(2)]
        yn = [ph.sb('yn%d' % i, [64, 8, 512], BF16, dma=True) for i in range(2)]
        sgr = [ph.sb('sg%d' % i, [128, 3, 512], BF16, dma=True) for i in range(2)]
        wpgr = [ph.sb('wpg%d' % i, [64, 8, 128], BF16, dma=True) for i in range(2)]
        wpnr = [ph.sb('wpn%d' % i, [64, 8, 128], BF16, dma=True) for i in range(2)]
        wglu = ph.sb('wglu', [128, 3, 3, 128], BF16, dma=True)
        tr.dma('sp', wglu[:], w_b[('wglu', l)].rearrange('m p k c -> p m k c'), reads=[w_buf[('wglu', l)]], writes=[wglu])
        wps = ph.sb('wps', [128, 8, 3, 128], BF16, dma=True)
        tr.dma('sp', wps[:], w_b[('wps', l)].rearrange('m p k c -> p m k c'), reads=[w_buf[('wps', l)]], writes=[wps])
        wo = [ph.sb('wo%d' % i, [128, KC, 128], BF16, dma=True) for i in range(2)]
        acc = ph.sb('acc', [128, 512]); t2 = ph.sb('t2e', [128, 512])
        ost = [ph.sb('ost%d' % i, [128, D], F32, dma=True) for i in range(1)] if last else None
        tl = tiles if ctx_out else tiles[1:]
        cnt = dict(wo=0, ost=0)

        def loads(idx):
            c0, N, col = tl[idx]; b = idx % 2
            tr.dma('sp', xTs[b][:, :, :N], XT[:, c0:c0 + N].rearrange('(k p) t -> p k t', p=128), writes=[xTs[b]])
            tr.dma('sp', tg[b][:, :, :N], TG[:, c0:c0 + N].rearrange('(k p) t -> p k t', p=128), writes=[tg[b]])
            if col == 1:
                tr.dma('sp', yg[b][:, :, :N], YGC.rearrange('h d t -> d h t'), writes=[yg[b]])
                tr.dma('sp', yn[b][:, :, :N], YNC.rearrange('h d t -> d h t'), writes=[yn[b]])
            else:
                tr.dma('sp', yg[b][:, :, :N], YG[:, :, c0 - NCX:c0 - NCX + N].rearrange('h d t -> d h t'), writes=[yg[b]])
                tr.dma('sp', yn[b][:, :, :N], YN[:, :, c0 - NCX:c0 - NCX + N].rearrange('h d t -> d h t'), writes=[yn[b]])

        mcount = 0
        loads(0)
        for idx in range(len(tl)):
            c0, N, col = tl[idx]; b = idx % 2
            xT = xTs[b]; tg_ = tg[b]; yg_ = yg[b]; yn_ = yn[b]
            if idx + 1 < len(tl): loads(idx + 1)
            tr.op('act', lambda: A.activation(out=tgb[:, :, :N], in_=tg_[:, :, :N], func=AF.Copy), reads=[tg_], writes=[tgb])
            for m in range(3):
                pg = R['ps_g'][m % 2]
                tr.group('pe', [lambda kc=kc: PE.matmul(pg[:, :N], lhsT=wglu[:, m, kc, :], rhs=tgb[:, kc, :N], start=(kc == 0), stop=(kc == 2))
                                for kc in range(3)], reads=[wglu, tgb], writes=[pg])
                tr.op('act', lambda: A.activation(out=acc[:, :N], in_=pg[:, :N], func=AF.Sigmoid), reads=[pg], writes=[acc])
                tr.op('dve', lambda: V.tensor_tensor(out=ys[:, m, :N], in0=acc[:, :N], in1=tg_[:, m, :N], op=ALU.mult), reads=[acc, tg_],
                      writes=[ys] if m == 0 else (), acc=() if m == 0 else [ys])
            for m in range(KC):
                p1 = R['ps_g'][m % 2]; p2 = R['ps_u'][m % 2]; p3 = R['ps_m'][m % 2]
                sg_ = sgr[mcount % 2]; wpg = wpgr[mcount % 2]; wpn = wpnr[mcount % 2]; mcount += 1
                tr.dma('sp', sg_[:, :, :N], SG[:, c0:c0 + N].rearrange('(b m p) t -> m p b t', b=3, p=128)[m], writes=[sg_])
                tr.dma('sp', wpg[:], w_b[('wpg', l)][m], reads=[w_buf[('wpg', l)]], writes=[wpg])
                tr.dma('sp', wpn[:], w_b[('wpn', l)][m], reads=[w_buf[('wpn', l)]], writes=[wpn])
                tr.group('pe', [lambda kc=kc: PE.matmul(p1[:, :N], lhsT=wps[:, m, kc, :], rhs=ys[:, kc, :N], start=(kc == 0), stop=(kc == 2))
                                for kc in range(3)], reads=[wps, ys], writes=[p1])
                tr.group('pe', [lambda h=h: PE.matmul(p2[:, :N], lhsT=wpg[:, h, :], rhs=yg_[:, h, :N], start=(h == 0), stop=(h == 7))
                                for h in range(8)], reads=[wpg, yg_], writes=[p2])
                tr.group('pe', [lambda h=h: PE.matmul(p3[:, :N], lhsT=wpn[:, h, :], rhs=yn_[:, h, :N], start=(h == 0), stop=(h == 7))
                                for h in range(8)], reads=[wpn, yn_], writes=[p3])
                tr.op('dve', lambda: V.tensor_tensor(out=acc[:, :N], in0=p1[:, :N], in1=sg_[:, 0, :N], op=ALU.mult), reads=[p1, sg_], writes=[acc])
                tr.op('dve', lambda: V.tensor_tensor(out=t2[:, :N], in0=p2[:, :N], in1=sg_[:, 1, :N], op=ALU.mult), reads=[p2, sg_], writes=[t2])
                tr.op('pool', lambda: G.tensor_tensor(out=acc[:, :N], in0=acc[:, :N], in1=t2[:, :N], op=ALU.add), reads=[acc, t2], writes=[acc])
                tr.op('dve', lambda: V.tensor_tensor(out=t2[:, :N], in0=p3[:, :N], in1=sg_[:, 2, :N], op=ALU.mult), reads=[p3, sg_], writes=[t2])
                tr.op('pool', lambda: G.tensor_tensor(out=hT[:, m, :N], in0=acc[:, :N], in1=t2[:, :N], op=ALU.add), reads=[acc, t2],
                      writes=[hT] if m == 0 else (), acc=() if m == 0 else [hT])
            for m in range(KC):
                wb = wo[cnt['wo'] % 2]; cnt['wo'] += 1
                tr.dma('sp', wb[:], w_b[('wout', l)][m], reads=[w_buf[('wout', l)]], writes=[wb])
                pd = R['ps_m'][m % 2]
                tr.group('pe', [lambda kc=kc: PE.matmul(pd[:, :N], lhsT=wb[:, kc, :], rhs=hT[:, kc, :N], start=(kc == 0), stop=(kc == KC - 1))
                                for kc in range(KC)], reads=[wb, hT], writes=[pd])
                tr.op('dve', lambda: V.scalar_tensor_tensor(out=xT[:, m, :N], in0=pd[:, :N], scalar=modG[:, 1, m, col:col + 1],
                                                            in1=xT[:, m, :N], op0=ALU.mult, op1=ALU.add), reads=[pd, modG, xT], acc=[xT])
            ffn(ph, l, 2, xT, hT, aT, N, col, 2, R)
            if not last:
                tr.dma('pool', XT[:, c0:c0 + N].rearrange('(k p) t -> p k t', p=128), xT[:, :, :N], reads=[xT])
            else:
                sq = R['sq']; rstd = R['rstd']; tb = R['tmpbig']; pss = R['ps_m'][0]
                tr.op('act', lambda: A.activation(out=sq[:, :, :N], in_=xT[:, :, :N], func=AF.Square), reads=[xT], writes=[sq])
                tr.group('pe', [lambda kc=kc: PE.matmul(pss[:, :N], lhsT=ones_b[:], rhs=sq[:, kc, :N], start=(kc == 0), stop=(kc == KC - 1))
                                for kc in range(KC)], reads=[sq, ones_b], writes=[pss])
                tr.op('act', lambda: A.activation(out=rstd[:, :N], in_=pss[:, :N], func=AF.Sqrt, scale=1.0 / D, bias=epsb[:, 0:1]), reads=[pss, epsb], writes=[rstd])
                tr.op('dve', lambda: V.reciprocal(out=rstd[:, :N], in_=rstd[:, :N]), reads=[rstd], writes=[rstd])
                tr.op('dve', lambda: V.tensor_tensor(out=tb[:, :, :N], in0=xT[:, :, :N], in1=rstd[:, :N].unsqueeze(1).to_broadcast([128, KC, N]), op=ALU.mult),
                      reads=[xT, rstd], writes=[tb])
                tr.group('act', [lambda kc=kc: A.activation(out=tb[:, kc, :N], in_=tb[:, kc, :N], func=AF.Identity, scale=fing[:, kc:kc + 1])
                                 for kc in range(KC)], reads=[tb, fing], writes=[tb])
                for ts in range(N // 128):
                    o_ = ost[0]; cnt['ost'] += 1
                    for hf in range(2):
                        pt = R['ps_g'][hf]
                        tr.group('pe', [lambda k=k: PE.transpose(pt[:, k * 128:(k + 1) * 128], tb[:, hf * 4 + k, ts * 128:(ts + 1) * 128], ident[:])
                                        for k in range(4)], reads=[tb, ident], writes=[pt])
                        tr.op('act' if hf == 0 else 'dve',
                              (lambda: A.activation(out=o_[:, 0:512], in_=pt[:], func=AF.Copy)) if hf == 0 else (lambda: V.tensor_copy(out=o_[:, 512:1024], in_=pt[:])),
                              reads=[pt], writes=[o_] if hf == 0 else (), acc=() if hf == 0 else [o_])
                    r0 = c0 - NCX + ts * 128
                    tr.dma('pool', out_d[r0:r0 + 128, :], o_[:], reads=[o_])
        ph.close()

    epsb = gp.sb('epsb', [128, 1]); tr.op('dve', lambda: V.memset(epsb[:], 1e-6), writes=[epsb])
    tr.barrier()

    def run():
        nonlocal stg_o, pcbig
        stg_o = [gp.sb('stgo%d' % i, [128, 512], F32, dma=True) for i in range(2)]
        pcbig = gp.sb('pcbig', [128, 256], BF16)
        if only is not None:
            {'B': phase_B}[only[0]](only[1]); return
        for l in range(DEPTH):
            ctx_out = l < DEPTH - 1
            compute_mod(l)
            if stop_after == ('mod', l): return
            phase_A(l)
            if stop_after == ('A', l): return
            emit_casts(l, G2)
            phase_B(l)
            if stop_after == ('B', l): return
            if l + 1 < DEPTH: emit_casts(l + 1, G1)
            phase_C(l, ctx_out)
            if stop_after == ('C', l): return
            phase_D(l, ctx_out)
            if stop_after == ('D', l): return
            phase_E(l, ctx_out, l == DEPTH - 1)
            if stop_after == ('E', l): return

    run()
    tr.barrier()
    gp.es.close()
    tr.es.close()
    return nc


def prep_shared(inp):
    sh = {}
    L = DEPTH
    sh['wgu1'] = np.stack([np.stack([tile_w(inp['ffn1_wg'][l]), tile_w(inp['ffn1_wu'][l])], 2) for l in range(L)])
    sh['wd1'] = np.stack([tile_w(inp['ffn1_wd'][l]) for l in range(L)])
    sh['wgu2'] = np.stack([np.stack([tile_w(inp['ffn2_wg'][l]), tile_w(inp['ffn2_wu'][l])], 2) for l in range(L)])
    sh['wd2'] = np.stack([tile_w(inp['ffn2_wd'][l]) for l in range(L)])
    cols = win_fm_cols()
    sh['winfm'] = np.stack([tile_w(inp['w_in'][l][:, cols]) for l in range(L)])
    tmc = np.concatenate([IN_OFF['gv'] + np.arange(128), IN_OFF['nv'] + np.arange(512)])
    sh['wintm'] = np.stack([np.ascontiguousarray(inp['w_in'][l][:, tmc].reshape(KC, 128, 640).transpose(1, 0, 2)) for l in range(L)])
    sh['wada'] = np.stack([tile_w(inp['w_ada'][l]) for l in range(L)])
    sh['wglu'] = np.stack([tile_w(inp['ssm_w_glu'][l]) for l in range(L)])
    sh['wps'] = np.stack([tile_w(inp['w_p_ssm'][l]) for l in range(L)])
    sh['wpg'] = np.stack([tile_w(inp['w_p_gqa'][l], 64) for l in range(L)])
    sh['wpn'] = np.stack([tile_w(inp['w_p_na'][l], 64) for l in range(L)])
    sh['wout'] = np.stack([tile_w(inp['w_out'][l]) for l in range(L)])
    sh['bada'] = np.ascontiguousarray(inp['b_ada'].reshape(L, 72, 128).transpose(0, 2, 1))
    sh['normg'] = np.ascontiguousarray(inp['norm_g'].reshape(L, 3, KC, 128).transpose(0, 3, 1, 2))
    sh['fing'] = np.ascontiguousarray(inp['final_g'].reshape(KC, 128).T)

    def st(a):
        return a.reshape(L, 2, 12, 2, 64).transpose(0, 1, 3, 4, 2).reshape(L, 2, 128, 12)
    ldt = np.broadcast_to(inp['ssm_log_dt'][:, :, :, None], (L, 2, 24, 64))
    sh['ssm_a'] = np.ascontiguousarray(np.stack([st(inp['ssm_a_re']), st(inp['ssm_a_im']), st(ldt)], 3))

    def sbt(a):
        return a.reshape(L, 2, 12, 2, 64, 16).transpose(0, 1, 3, 4, 2, 5).reshape(L, 2, 128, 12, 16)
    sh['ssm_b'] = np.ascontiguousarray(np.stack([sbt(inp['ssm_b_re']), sbt(inp['ssm_b_im'])], 2))

    def sct(a):
        return a.reshape(L, 2, 3, 8, 16, 64).transpose(0, 1, 3, 4, 2, 5).reshape(L, 2, 128, 3, 64)
    sh['ssm_c'] = np.ascontiguousarray(np.stack([sct(inp['ssm_c_re']), sct(inp['ssm_c_im'])], 2))
    sh['ssm_d'] = np.ascontiguousarray(inp['ssm_d'].reshape(L, 3, 128).transpose(0, 2, 1))
    sh['sink'] = np.ascontiguousarray(np.broadcast_to(inp['gqa_sink'][:, None, :], (L, 64, 8)))
    vt, od, cb = na_bias_tables(inp['na_rpb'])
    sh['navt'] = vt; sh['naod'] = od; sh['nacb'] = cb.reshape(L, 8, 128, 5, 128)
    sh.update(host_consts())
    return {k: np.ascontiguousarray(v, dtype=np.float32) for k, v in sh.items()}


def core_inputs(inp, b, sh):
    m = dict(sh)
    m['x'] = np.ascontiguousarray(inp['x'][b]); m['ctx'] = np.ascontiguousarray(inp['ctx'][b])
    sv = np.stack([inp['c'][b].reshape(KC, 128).T, inp['c_ctx'].reshape(KC, 128).T], 2)
    m['svec'] = np.ascontiguousarray(sv, dtype=np.float32)
    return m


def kernel(**inputs):
    inp = {k: np.asarray(v) for k, v in inputs.items()}
    sh = prep_shared(inp)
    nc = build()
    in_maps = [core_inputs(inp, b, sh) for b in range(8)]
    res = run_bass_kernel_spmd(nc, in_maps, core_ids=list(range(8)))
    return np.stack([np.asarray(r['out'], dtype=np.float32) for r in res.results], 0)
```

```python
import contextlib, math, os
import numpy as np
import ml_dtypes
import concourse.bass as bass
import concourse.mybir as mybir
from concourse.bass_utils import run_bass_kernel_spmd

F32 = mybir.dt.float32; BF16 = mybir.dt.bfloat16; I32 = mybir.dt.int32
AF = mybir.ActivationFunctionType; ALU = mybir.AluOpType

D = 1024; T = 4096; NCX = 256; TT = T + NCX; FF = 2816; KC = 8; FC = 22; DEPTH = 2
NEG = -30000.0
SAME_SYNC = True
PI = math.pi


class Sem:
    def __init__(s, h, name): s.h = h; s.total = 0; s.name = name


class Eng:
    def __init__(s, name, obj, sem): s.name = name; s.obj = obj; s.sem = sem; s.known = {}


class Buf:
    def __init__(s, name, t=None, dsem=None, qsem=None):
        s.name = name; s.t = t; s.w = []; s.r = []; s.pre = []; s.dsem = dsem; s.qsem = qsem

    def __getitem__(s, k): return s.t[k]


class Trk:
    def __init__(self, nc):
        self.nc = nc
        self.es = contextlib.ExitStack()
        self.sems = []
        self.E = {}
        for n, o in (('pe', nc.tensor), ('act', nc.scalar), ('dve', nc.vector), ('pool', nc.gpsimd), ('sp', nc.sync)):
            self.E[n] = Eng(n, o, self.new_sem('e_' + n) if n != 'sp' else None)
        self.dpool = []; self.dnext = 0; self.uid = 0; self.qpool = []; self.qnext = 0

    def new_sem(self, name):
        s = Sem(self.es.enter_context(self.nc.semaphore(name)), name); self.sems.append(s); return s

    def dsem(self):
        if self.dnext >= len(self.dpool): self.dpool.append(self.new_sem('d%d' % len(self.dpool)))
        s = self.dpool[self.dnext]; self.dnext += 1; return s

    def qsem(self):
        if self.qnext >= len(self.qpool): self.qpool.append(self.new_sem('q%d' % len(self.qpool)))
        s = self.qpool[self.qnext]; self.qnext += 1; return s

    def _wait(self, eng, evs):
        need = {}
        for (sem, val, src) in evs:
            if src == eng.name and (src == 'pe' or not SAME_SYNC): continue
            if eng.known.get(sem, 0) >= val: continue
            need[sem] = max(need.get(sem, 0), val)
        for sem, val in need.items():
            eng.obj.wait_ge(sem.h, val); eng.known[sem] = val

    def _pre(self, eng, reads, writes, acc):
        evs = []
        for b in reads: evs += b.w
        for b in writes:
            b.pre = b.w + b.r; evs += b.pre
        for b in acc: evs += b.pre
        self._wait(eng, evs)

    def _post(self, ev, reads, writes, acc):
        for b in reads: b.r.append(ev)
        for b in writes: b.w = [ev]; b.r = []
        for b in acc: b.w.append(ev)

    def group(self, en, fns, reads=(), writes=(), acc=()):
        eng = self.E[en]
        self._pre(eng, reads, writes, acc)
        ins = None
        for f in fns: ins = f()
        eng.sem.total += 1
        ins.then_inc(eng.sem.h, 1)
        self._post((eng.sem, eng.sem.total, en), reads, writes, acc)

    def op(self, en, fn, reads=(), writes=(), acc=()):
        self.group(en, [fn], reads, writes, acc)

    def dma(self, q, out, in_, reads=(), writes=(), acc=(), sem=None):
        eng = self.E[q]
        self._pre(eng, reads, writes, acc)
        if sem is None:
            for b in list(writes) + list(acc) + list(reads):
                if b.dsem is not None: sem = (b.qsem if q == 'pool' else b.dsem); break
        ins = eng.obj.dma_start(out=out, in_=in_)
        sem.total += 16
        ins.then_inc(sem.h, 16)
        self._post((sem, sem.total, 'dma'), reads, writes, acc)

    def barrier(self):
        for eng in self.E.values():
            for s in self.sems:
                if s.total > 0 and eng.known.get(s, 0) < s.total:
                    eng.obj.wait_ge(s.h, s.total); eng.known[s] = s.total


def tile_w(W, kp=128):
    K, M = W.shape
    return np.ascontiguousarray(W.reshape(K // kp, kp, M // 128, 128).transpose(2, 1, 0, 3))


IN_OFF = dict(u=0, gq=384, gk=896, gv=1024, nq=1152, nk=1664, nv=2176, gates=2688)
ROT_PERM = np.concatenate([np.arange(16, 32), np.arange(0, 16), np.arange(48, 64), np.arange(32, 48)])
FM_CHUNKS = ([('u', i) for i in range(3)] + [('gq', i) for i in range(4)] + [('gq2', i) for i in range(4)]
             + [('gk', 0), ('gk2', 0)] + [('nq', i) for i in range(4)] + [('nk', i) for i in range(4)]
             + [('gates', i) for i in range(24)])


def win_fm_cols():
    cols = []
    for kind, i in FM_CHUNKS:
        if kind == 'u': c = IN_OFF['u'] + i * 128 + np.arange(128)
        elif kind == 'gq': c = IN_OFF['gq'] + i * 128 + np.arange(128)
        elif kind == 'gq2': c = IN_OFF['gq'] + i * 128 + np.concatenate([ROT_PERM, 64 + ROT_PERM])
        elif kind == 'gk': c = IN_OFF['gk'] + np.arange(128)
        elif kind == 'gk2': c = IN_OFF['gk'] + np.concatenate([ROT_PERM, 64 + ROT_PERM])
        elif kind == 'nq': c = IN_OFF['nq'] + i * 128 + np.arange(128)
        elif kind == 'nk': c = IN_OFF['nk'] + i * 128 + np.arange(128)
        else: c = IN_OFF['gates'] + i * 128 + np.arange(128)
        cols.append(c)
    return np.concatenate(cols)


def rope_tables():
    t = np.arange(T)
    pos = np.stack([t // 64, t % 64], 0).astype(np.float32)
    inv = (10000.0 ** (-np.arange(0, 32, 2, dtype=np.float32) / 32)).astype(np.float32)
    C = np.zeros((64, T), np.float32); S = np.zeros((64, T), np.float32)
    for ax in range(2):
        ang = (pos[ax][None, :] * inv[:, None]).astype(np.float32)
        for half in range(2):
            sl = slice(ax * 32 + half * 16, ax * 32 + half * 16 + 16)
            C[sl] = np.cos(ang)
            S[sl] = -np.sin(ang) if half == 0 else np.sin(ang)
    C2 = np.concatenate([C, C], 0); S2 = np.concatenate([S, S], 0)
    return np.stack([C2 * 0.125, S2 * 0.125, C2, S2], 0).astype(np.float32)


def na_bias_tables(rpb):
    L = rpb.shape[0]
    kc = np.arange(64)[:, None]; qc = np.arange(64)[None, :]
    cs = np.clip(qc - 8, 0, 48)
    valid = (kc >= cs) & (kc < cs + 16)
    idx = np.clip(kc - qc + 15, 0, 30)
    tab = np.where(valid[None, None, None], rpb[:, :, :, idx], np.float32(NEG)).astype(np.float32)
    negt = np.full((L, 8, 64, 64), NEG, np.float32)
    VT = np.zeros((L, 8, 128, 14, 64), np.float32)
    for d in range(-7, 7):
        VT[:, :, 0:64, d + 7] = tab[:, :, d + 7]
        VT[:, :, 64:128, d + 7] = tab[:, :, d + 8]
    OD = np.zeros((L, 8, 128, 5, 64), np.float32)
    for k, d in enumerate((-5, -3, -1, 1, 3)):
        OD[:, :, 0:64, k] = negt if d == -5 else tab[:, :, d + 7]
        OD[:, :, 64:128, k] = negt if d == 3 else tab[:, :, d + 8]
    CB = np.zeros((L, 8, 128, 5, 2, 64), np.float32)
    for k in range(5):
        CB[:, :, :, k, 0, :] = VT[:, :, :, 3 + 2 * k, :] if k < 4 else np.float32(NEG)
        CB[:, :, :, k, 1, :] = OD[:, :, :, k, :]
    return VT, OD, CB


def host_consts():
    c = {}
    c['ident'] = np.eye(128, dtype=np.float32)
    c['rope'] = rope_tables()
    k = np.arange(128)[:, None]; q = np.arange(128)[None, :]
    c['gmask'] = np.stack([np.where(k >= q, 0.0, NEG), np.where(k <= q, 0.0, NEG)], 1).astype(np.float32)
    p = np.arange(128)
    m2 = np.zeros((128, 4, 8, 16), np.float32)
    mc = np.zeros((128, 4, 2, 64), np.float32)
    for qq in range(4):
        for pp in range(128):
            m2[pp, qq, 2 * qq + (pp >= 64), :] = 1.0
            for half in range(2):
                if pp // 16 == 2 * qq + half: mc[pp, qq, half, :] = 1.0
    c['mask2'] = m2; c['maskc'] = mc
    io = np.zeros((128, 2, 64), np.float32)
    io[:, 0, :] = np.arange(1, 65)[None, :]; io[:, 1, :] = np.arange(64, 0, -1)[None, :]
    c['iota64'] = io
    c['iota9'] = np.broadcast_to(np.arange(9, dtype=np.float32)[None, :], (128, 9)).copy()
    return c


def build(dbg=(), stop_after=None, only=None, ext_in=()):
    nc = bass.Bass("TRN2", target_bir_lowering=False)
    tr = Trk(nc)
    dbg = set(dbg)

    def din(name, shape, dt=F32):
        return nc.dram_tensor(name, list(shape), dt, kind="ExternalInput").ap()

    def dscr(name, shape, dt):
        if name in ext_in: return nc.dram_tensor(name, list(shape), dt, kind="ExternalInput").ap()
        if name in dbg: return nc.dram_tensor(name, list(shape), dt, kind="ExternalOutput").ap()
        return nc.dram_tensor(name, list(shape), dt).ap()

    x_in = din('x', [T, D]); ctx_in = din('ctx', [NCX, D]); svec_in = din('svec', [128, KC, 2])
    out_d = nc.dram_tensor('out', [T, D], F32, kind="ExternalOutput").ap()
    WSH = dict(wgu1=[FC, 128, 2, KC, 128], wd1=[KC, 128, FC, 128], wgu2=[FC, 128, 2, KC, 128], wd2=[KC, 128, FC, 128],
               winfm=[45, 128, KC, 128], wintm=[128, KC, 640], wada=[72, 128, KC, 128], wglu=[3, 128, 3, 128],
               wps=[8, 128, 3, 128], wpg=[8, 128, 4, 128], wpn=[8, 128, 4, 128], wout=[8, 128, KC, 128])
    WORDER = ['wada', 'wgu1', 'wd1', 'winfm', 'wintm', 'wglu', 'wps', 'wpg', 'wpn', 'wout', 'wgu2', 'wd2']
    w_f = {k: din(k, [DEPTH] + v) for k, v in WSH.items()}
    w_b = {(k, l): dscr('%s_b%d' % (k, l), v, BF16) for k, v in WSH.items() for l in range(DEPTH)}
    w_buf = {(k, l): Buf('wb_%s%d' % (k, l)) for k in WSH for l in range(DEPTH)}
    bada_in = din('bada', [DEPTH, 128, 72]); normg_in = din('normg', [DEPTH, 128, 3, KC]); fing_in = din('fing', [128, KC])
    sa_in = din('ssm_a', [DEPTH, 2, 128, 3, 12])
    sb_in = din('ssm_b', [DEPTH, 2, 2, 128, 12, 16]); sc_in = din('ssm_c', [DEPTH, 2, 2, 128, 3, 64])
    sd_in = din('ssm_d', [DEPTH, 128, 3]); sink_in = din('sink', [DEPTH, 64, 8])
    navt_in = din('navt', [DEPTH, 8, 128, 14, 64]); naod_in = din('naod', [DEPTH, 8, 128, 5, 64]); nacb_in = din('nacb', [DEPTH, 8, 128, 5, 128])
    ident_in = din('ident', [128, 128]); rope_in = din('rope', [4, 128, T]); gmask_in = din('gmask', [128, 2, 128])
    mask2_in = din('mask2', [128, 4, 8, 16]); maskc_in = din('maskc', [128, 4, 2, 64]); iota64_in = din('iota64', [128, 2, 64]); iota9_in = din('iota9', [128, 9])

    XT = dscr('XT', [D, TT], F32)
    U = dscr('U', [384, TT], F32); TG = dscr('TG', [384, TT], F32)
    QG = dscr('QG', [8, 64, T], BF16); QGC = dscr('QGC', [8, 64, NCX], BF16)
    KG = dscr('KG', [2, 64, T], BF16); KGC = dscr('KGC', [2, 64, NCX], BF16)
    QN = dscr('QN', [8, 64, T], BF16); QNC = dscr('QNC', [8, 64, NCX], BF16)
    KN = dscr('KN', [8, 64, T], BF16); KNC = dscr('KNC', [8, 64, NCX], BF16)
    VG = dscr('VG', [TT, 128], BF16); VN = dscr('VN', [TT, 512], BF16)
    SG = dscr('SG', [3072, TT], BF16)
    YG = dscr('YG', [8, 64, T], BF16); YGC = dscr('YGC', [8, 64, NCX], BF16)
    YN = dscr('YN', [8, 64, T], BF16); YNC = dscr('YNC', [8, 64, NCX], BF16)

    def uname(n):
        tr.uid += 1; return '%s_%d' % (n, tr.uid)

    class Phase:
        def __init__(s, reset=True):
            s.es = contextlib.ExitStack()
            if reset: tr.dnext = 0; tr.qnext = 0

        def sb(s, name, shape, dt=F32, dma=False):
            t = s.es.enter_context(nc.sbuf_tensor(uname(name), list(shape), dt))
            return Buf(name, t, tr.dsem() if dma else None, tr.qsem() if dma else None)

        def ps(s, name, shape=(128, 512), dt=F32):
            t = s.es.enter_context(nc.psum_tensor(uname(name), list(shape), dt))
            return Buf(name, t)

        def close(s):
            tr.barrier(); s.es.close()

    V = nc.vector; A = nc.scalar; G = nc.gpsimd; PE = nc.tensor

    G1 = ['wada', 'wgu1', 'wd1', 'winfm', 'wintm']
    G2 = ['wglu', 'wps', 'wpg', 'wpn', 'wout', 'wgu2', 'wd2']

    def emit_casts(l, keys):
        if only is not None: return
        for k in keys:
            sem = tr.new_sem('c_%s%d' % (k, l))
            n = int(np.prod(WSH[k]))
            src = w_f[k][l]; dst = w_b[(k, l)]
            names = 'abcde'[:len(WSH[k])]
            pat = ' '.join(names)
            fs = src.rearrange('%s -> (%s)' % (pat, pat)).rearrange('(p n) -> p n', p=128)
            fd = dst.rearrange('%s -> (%s)' % (pat, pat)).rearrange('(p n) -> p n', p=128)
            cols = n // 128
            npieces = max(1, -(-cols // 16384))
            step = -(-cols // npieces)
            first = True
            for c0 in range(0, cols, step):
                c1 = min(cols, c0 + step)
                tr.dma('pool', fd[:, c0:c1], fs[:, c0:c1], writes=[w_buf[(k, l)]] if first else (),
                       acc=() if first else [w_buf[(k, l)]], sem=sem)
                first = False

    emit_casts(0, G1)

    gp = Phase()
    ident = gp.sb('ident', [128, 128], F32, dma=True)
    tr.dma('sp', ident[:], ident_in[:, :], writes=[ident])
    ones_b = gp.sb('ones_b', [128, 128], BF16)
    tr.op('dve', lambda: V.memset(ones_b[:], 1.0), writes=[ones_b])
    svec = gp.sb('svec', [128, KC, 2], F32, dma=True)
    tr.dma('sp', svec[:], svec_in[:, :, :], writes=[svec])
    svb = gp.sb('svb', [128, KC, 2], BF16)
    tr.op('act', lambda: A.activation(out=svb[:], in_=svec[:], func=AF.Silu), reads=[svec], writes=[svb])
    fing = gp.sb('fing', [128, KC], F32, dma=True)
    tr.dma('sp', fing[:], fing_in[:, :], writes=[fing])
    modA = gp.sb('modA', [128, 3, KC, 2]); modB = gp.sb('modB', [128, 3, KC, 2]); modG = gp.sb('modG', [128, 3, KC, 2])

    def compute_mod(l):
        ph = Phase()
        bada = ph.sb('bada', [128, 72], F32, dma=True); tr.dma('sp', bada[:], bada_in[l], writes=[bada])
        normg = ph.sb('normg', [128, 3, KC], F32, dma=True); tr.dma('sp', normg[:], normg_in[l], writes=[normg])
        wr = [ph.sb('wada%d' % i, [128, 8, KC, 128], BF16, dma=True) for i in range(2)]
        mps = ph.ps('mps', [128, 72, 2])
        mod = ph.sb('mod', [128, 72, 2])
        for jg in range(9):
            wbuf = wr[jg % 2]
            tr.dma('sp', wbuf[:], w_b[('wada', l)][jg * 8:(jg + 1) * 8].rearrange('j p k c -> p j k c'),
                   reads=[w_buf[('wada', l)]], writes=[wbuf])
            fns = []
            for jj in range(8):
                j = jg * 8 + jj
                for kc in range(KC):
                    fns.append(lambda j=j, jj=jj, kc=kc: PE.matmul(mps[:, j, :], lhsT=wbuf[:, jj, kc, :], rhs=svb[:, kc, :],
                                                                  start=(kc == 0), stop=(kc == KC - 1)))
            tr.group('pe', fns, reads=[wbuf, svb], writes=[mps] if jg == 0 else (), acc=() if jg == 0 else [mps])
        tr.op('dve', lambda: V.tensor_tensor(out=mod[:], in0=mps[:], in1=bada[:].unsqueeze(2).to_broadcast([128, 72, 2]), op=ALU.add),
              reads=[mps, bada], writes=[mod])
        for i in range(3):
            sh = mod[:, 8 * (3 * i):8 * (3 * i) + 8, :]; sc = mod[:, 8 * (3 * i + 1):8 * (3 * i + 1) + 8, :]
            gt = mod[:, 8 * (3 * i + 2):8 * (3 * i + 2) + 8, :]
            gb = normg[:, i, :].unsqueeze(2).to_broadcast([128, KC, 2])
            fns = [lambda sc=sc, i=i, gb=gb: V.scalar_tensor_tensor(out=modA[:, i], in0=sc, scalar=1.0, in1=gb, op0=ALU.add, op1=ALU.mult),
                   lambda sh=sh, i=i: V.tensor_copy(out=modB[:, i], in_=sh),
                   lambda gt=gt, i=i: V.tensor_scalar(out=modG[:, i], in0=gt, scalar1=(1.0 if i == 1 else 0.5), scalar2=None, op0=ALU.mult)]
            tr.group('dve', fns, reads=[mod, normg], writes=[modA, modB, modG] if i == 0 else (), acc=() if i == 0 else [modA, modB, modG])
        ph.close()

    def rms_mod(ph, xT, hT, N, site, col, ps_ss, tmpbig, sq, rstd):
        tr.op('act', lambda: A.activation(out=sq[:, :, :N], in_=xT[:, :, :N], func=AF.Square), reads=[xT], writes=[sq])
        tr.group('pe', [lambda kc=kc: PE.matmul(ps_ss[:, :N], lhsT=ones_b[:], rhs=sq[:, kc, :N], start=(kc == 0), stop=(kc == KC - 1))
                        for kc in range(KC)], reads=[sq, ones_b], writes=[ps_ss])
        tr.op('act', lambda: A.activation(out=rstd[:, :N], in_=ps_ss[:, :N], func=AF.Sqrt, scale=1.0 / D, bias=epsb[:, 0:1]),
              reads=[ps_ss, epsb], writes=[rstd])
        tr.op('dve', lambda: V.reciprocal(out=rstd[:, :N], in_=rstd[:, :N]), reads=[rstd], writes=[rstd])
        tr.op('dve', lambda: V.tensor_tensor(out=tmpbig[:, :, :N], in0=xT[:, :, :N],
                                             in1=rstd[:, :N].unsqueeze(1).to_broadcast([128, KC, N]), op=ALU.mult),
              reads=[xT, rstd], writes=[tmpbig])
        tr.group('act', [lambda kc=kc: A.activation(out=hT[:, kc, :N], in_=tmpbig[:, kc, :N], func=AF.Identity,
                                                    scale=modA[:, site, kc, col:col + 1], bias=modB[:, site, kc, col:col + 1])
                         for kc in range(KC)], reads=[tmpbig, modA, modB], writes=[hT])

    def ffn(ph, l, which, xT, hT, aT, N, col, site, R):
        wgu_k, wd_k = ('wgu1', 'wd1') if which == 1 else ('wgu2', 'wd2')
        rms_mod(ph, xT, hT, N, site, col, R['ps_m'][0], R['tmpbig'], R['sq'], R['rstd'])
        for f in range(FC):
            wb = R['wgu'][R['i_wgu'] % 3]; R['i_wgu'] += 1
            tr.dma('sp', wb[:], w_b[(wgu_k, l)][f], reads=[w_buf[(wgu_k, l)]], writes=[wb])
            pg = R['ps_g'][f % 2]; pu = R['ps_u'][f % 2]
            tr.group('pe', [lambda kc=kc: PE.matmul(pg[:, :N], lhsT=wb[:, 0, kc, :], rhs=hT[:, kc, :N], start=(kc == 0), stop=(kc == KC - 1))
                            for kc in range(KC)], reads=[wb, hT], writes=[pg])
            tr.group('pe', [lambda kc=kc: PE.matmul(pu[:, :N], lhsT=wb[:, 1, kc, :], rhs=hT[:, kc, :N], start=(kc == 0), stop=(kc == KC - 1))
                            for kc in range(KC)], reads=[wb, hT], writes=[pu])
            sg = R['sgt'][f % 2]
            tr.op('act', lambda: A.activation(out=sg[:, :N], in_=pg[:, :N], func=AF.Silu), reads=[pg], writes=[sg])
            tr.op('dve', lambda: V.tensor_tensor(out=aT[:, f, :N], in0=sg[:, :N], in1=pu[:, :N], op=ALU.mult),
                  reads=[sg, pu], writes=[aT] if f == 0 else (), acc=() if f == 0 else [aT])
        for m in range(KC):
            wb = R['wd'][R['i_wd'] % 2]; R['i_wd'] += 1
            tr.dma('sp', wb[:], w_b[(wd_k, l)][m], reads=[w_buf[(wd_k, l)]], writes=[wb])
            pd = R['ps_m'][m % 2]
            tr.group('pe', [lambda f=f: PE.matmul(pd[:, :N], lhsT=wb[:, f, :], rhs=aT[:, f, :N], start=(f == 0), stop=(f == FC - 1))
                            for f in range(FC)], reads=[wb, aT], writes=[pd])
            tr.op('dve', lambda: V.scalar_tensor_tensor(out=xT[:, m, :N], in0=pd[:, :N], scalar=modG[:, site, m, col:col + 1],
                                                        in1=xT[:, m, :N], op0=ALU.mult, op1=ALU.add),
                  reads=[pd, modG, xT], acc=[xT])

    def ffn_bufs(ph):
        R = {}
        R['wgu'] = [ph.sb('wgu%d' % i, [128, 2, KC, 128], BF16, dma=True) for i in range(3)]
        R['wd'] = [ph.sb('wd%d' % i, [128, FC, 128], BF16, dma=True) for i in range(2)]
        R['i_wgu'] = 0; R['i_wd'] = 0
        R['ps_g'] = [ph.ps('psg%d' % i) for i in range(2)]
        R['ps_u'] = [ph.ps('psu%d' % i) for i in range(2)]
        R['ps_m'] = [ph.ps('psm%d' % i) for i in range(2)]
        R['sgt'] = [ph.sb('sgt%d' % i, [128, 512]) for i in range(2)]
        R['tmpbig'] = ph.sb('tmpbig', [128, KC, 512]); R['sq'] = ph.sb('sq', [128, KC, 512], BF16)
        R['rstd'] = ph.sb('rstd', [128, 512])
        return R

    tiles = [(0, NCX, 1)] + [(NCX + i * 512, 512, 0) for i in range(8)]

    def phase_A(l):
        ph = Phase()
        R = ffn_bufs(ph)
        xTs = [ph.sb('xT%d' % i, [128, KC, 512], F32, dma=True) for i in range(2)]
        hT = ph.sb('hT', [128, KC, 512], BF16); aT = ph.sb('aT', [128, FC, 512], BF16)
        win = [ph.sb('win%d' % i, [128, KC, 128], BF16, dma=True) for i in range(3)]
        wtm = ph.sb('wtm', [128, KC, 640], BF16, dma=True)
        tr.dma('sp', wtm[:], w_b[('wintm', l)], reads=[w_buf[('wintm', l)]], writes=[wtm])
        rp = [ph.sb('rope%d' % i, [128, 4, 512], F32, dma=True) for i in range(2)]
        stf = [ph.sb('stf%d' % i, [128, 512], F32, dma=True) for i in range(2)]
        stb = [ph.sb('stb%d' % i, [128, 640], BF16, dma=True) for i in range(3)]
        t1 = ph.sb('t1', [128, 512]); t2 = ph.sb('t2', [128, 512])
        xin = [ph.sb('xin%d' % i, [128, D], F32, dma=True) for i in range(2)] if l == 0 else None
        ps_q = ph.ps('psq'); ps_q2 = ph.ps('psq2')
        cnt = dict(win=0, stf=0, stb=0, xin=0)

        def load_x(ti):
            c0, N, col = tiles[ti]; xT = xTs[ti % 2]
            if l > 0:
                tr.dma('sp', xT[:, :, :N], XT[:, c0:c0 + N].rearrange('(k p) t -> p k t', p=128), writes=[xT])
            else:
                for ts in range(N // 128):
                    xb = xin[cnt['xin'] % 2]; cnt['xin'] += 1
                    src = ctx_in[ts * 128:(ts + 1) * 128, :] if ti == 0 else x_in[c0 - NCX + ts * 128:c0 - NCX + (ts + 1) * 128, :]
                    tr.dma('sp', xb[:], src, writes=[xb])
                    for hf in range(2):
                        pt = R['ps_g'][hf]
                        tr.group('pe', [lambda k=k, hf=hf: PE.transpose(pt[:, k * 128:(k + 1) * 128], xb[:, (hf * 4 + k) * 128:(hf * 4 + k + 1) * 128], ident[:])
                                        for k in range(4)], reads=[xb, ident], writes=[pt])
                        tr.op('act', lambda hf=hf, pt=pt, ts=ts: A.activation(out=xT[:, hf * 4:hf * 4 + 4, ts * 128:(ts + 1) * 128],
                                                                               in_=pt[:].rearrange('p (k t) -> p k t', k=4), func=AF.Copy),
                              reads=[pt], writes=[xT] if (ts == 0 and hf == 0) else (), acc=() if (ts == 0 and hf == 0) else [xT])

        load_x(0)
        for ti in range(9):
            c0, N, col = tiles[ti]; xT = xTs[ti % 2]; lat = ti > 0
            if lat:
                rpb = rp[ti % 2]
                tr.dma('sp', rpb[:], rope_in[:, :, c0 - NCX:c0 - NCX + 512].rearrange('a p t -> p a t'), writes=[rpb])
            ffn(ph, l, 1, xT, hT, aT, N, col, 0, R)
            if ti + 1 < 9: load_x(ti + 1)
            rms_mod(ph, xT, hT, N, 1, col, R['ps_m'][0], R['tmpbig'], R['sq'], R['rstd'])
            tr.dma('pool', XT[:, c0:c0 + N].rearrange('(k p) t -> p k t', p=128), xT[:, :, :N], reads=[xT])
            ci = 0
            while ci < len(FM_CHUNKS):
                kind, i = FM_CHUNKS[ci]

                def wmm(ci, pbuf):
                    wb = win[cnt['win'] % 3]; cnt['win'] += 1
                    tr.dma('sp', wb[:], w_b[('winfm', l)][ci], reads=[w_buf[('winfm', l)]], writes=[wb])
                    tr.group('pe', [lambda kc=kc: PE.matmul(pbuf[:, :N], lhsT=wb[:, kc, :], rhs=hT[:, kc, :N], start=(kc == 0), stop=(kc == KC - 1))
                                    for kc in range(KC)], reads=[wb, hT], writes=[pbuf])

                if kind in ('gq', 'gk') and lat:
                    ci2 = ci + (4 if kind == 'gq' else 1)
                    wmm(ci, ps_q); wmm(ci2, ps_q2)
                    o = 0 if kind == 'gq' else 2
                    st = stb[cnt['stb'] % 3]; cnt['stb'] += 1
                    tr.op('dve', lambda: V.tensor_tensor(out=t1[:], in0=ps_q[:], in1=rpb[:, o, :], op=ALU.mult), reads=[ps_q, rpb], writes=[t1])
                    tr.op('dve', lambda: V.tensor_tensor(out=t2[:], in0=ps_q2[:], in1=rpb[:, o + 1, :], op=ALU.mult), reads=[ps_q2, rpb], writes=[t2])
                    tr.op('pool', lambda: G.tensor_tensor(out=st[:, :512], in0=t1[:], in1=t2[:], op=ALU.add), reads=[t1, t2], writes=[st])
                    dst = QG[2 * i:2 * i + 2] if kind == 'gq' else KG[0:2]
                    tr.dma('pool', dst.rearrange('h d t -> (h d) t')[:, c0 - NCX:c0 - NCX + 512], st[:, :512], reads=[st])
                elif kind in ('gq2', 'gk2'):
                    pass
                else:
                    pb = R['ps_m'][ci % 2]
                    wmm(ci, pb)
                    if kind == 'u':
                        st = stf[cnt['stf'] % 2]; cnt['stf'] += 1
                        tr.op('act', lambda: A.activation(out=st[:, :N], in_=pb[:, :N], func=AF.Copy), reads=[pb], writes=[st])
                        tr.dma('pool', U[i * 128:(i + 1) * 128, c0:c0 + N], st[:, :N], reads=[st])
                    else:
                        st = stb[cnt['stb'] % 3]; cnt['stb'] += 1
                        if kind == 'gates':
                            tr.op('act', lambda: A.activation(out=st[:, :N], in_=pb[:, :N], func=AF.Sigmoid), reads=[pb], writes=[st])
                            tr.dma('pool', SG[i * 128:(i + 1) * 128, c0:c0 + N], st[:, :N], reads=[st])
                        else:
                            sc = 0.125 if kind in ('gq', 'nq') else 1.0
                            tr.op('act', lambda: A.activation(out=st[:, :N], in_=pb[:, :N], func=AF.Copy, scale=sc), reads=[pb], writes=[st])
                            if lat:
                                dst = {'nq': QN, 'nk': KN}[kind][2 * i:2 * i + 2].rearrange('h d t -> (h d) t')[:, c0 - NCX:c0 - NCX + 512]
                            else:
                                dd = {'gq': QGC, 'gk': KGC, 'nq': QNC, 'nk': KNC}[kind]
                                dst = (dd[2 * i:2 * i + 2] if kind != 'gk' else dd[0:2]).rearrange('h d t -> (h d) t')
                            tr.dma('pool', dst, st[:, :N], reads=[st])
                ci += 1
            for ts in range(N // 128):
                pv = R['ps_g'][ts % 2]; pv2 = R['ps_u'][ts % 2]
                tr.group('pe', [lambda kc=kc: PE.matmul(pv[:, :128], lhsT=hT[:, kc, ts * 128:(ts + 1) * 128], rhs=wtm[:, kc, 0:128],
                                                        start=(kc == 0), stop=(kc == KC - 1)) for kc in range(KC)], reads=[hT, wtm], writes=[pv])
                tr.group('pe', [lambda kc=kc: PE.matmul(pv2[:, :512], lhsT=hT[:, kc, ts * 128:(ts + 1) * 128], rhs=wtm[:, kc, 128:640],
                                                        start=(kc == 0), stop=(kc == KC - 1)) for kc in range(KC)], reads=[hT, wtm], writes=[pv2])
                st = stb[cnt['stb'] % 3]; cnt['stb'] += 1
                tr.op('act', lambda: A.activation(out=st[:, 0:128], in_=pv[:, 0:128], func=AF.Copy), reads=[pv], writes=[st])
                tr.op('dve', lambda: V.tensor_copy(out=st[:, 128:640], in_=pv2[:, :512]), reads=[pv2], acc=[st])
                r0 = c0 + ts * 128
                tr.dma('pool', VG[r0:r0 + 128, :], st[:, 0:128], reads=[st])
                tr.dma('pool', VN[r0:r0 + 128, :], st[:, 128:640], reads=[st])
        ph.close()

    def sin_rr(src, srcb, dst, dstb, shift, tA, tAb, tI, tIb):
        if shift:
            tr.op('dve', lambda: V.tensor_scalar(out=tA, in0=src, scalar1=shift, scalar2=None, op0=ALU.add), reads=[srcb], writes=[tAb])
            x, xb = tA, tAb
        else:
            x, xb = src, srcb
        tr.op('dve', lambda: V.tensor_scalar(out=tI, in0=x, scalar1=1.0 / (2 * PI), scalar2=None, op0=ALU.mult), reads=[xb], writes=[tIb])
        tr.op('dve', lambda: V.tensor_copy(out=dst, in_=tI), reads=[tIb], writes=[dstb])
        tr.op('dve', lambda: V.scalar_tensor_tensor(out=dst, in0=dst, scalar=-2 * PI, in1=x, op0=ALU.mult, op1=ALU.add), reads=[dstb, xb], writes=[dstb])
        tr.op('dve', lambda: V.tensor_scalar(out=dst, in0=dst, scalar1=-PI, scalar2=PI, op0=ALU.max, op1=ALU.min), reads=[dstb], writes=[dstb])
        tr.op('act', lambda: A.activation(out=dst, in_=dst, func=AF.Sin), reads=[dstb], writes=[dstb])

    def phase_B(l):
        ph = Phase()
        io64 = ph.sb('io64', [128, 2, 64], F32, dma=True); tr.dma('sp', io64[:], iota64_in[:, :, :], writes=[io64])
        io9 = ph.sb('io9', [128, 9], F32, dma=True); tr.dma('sp', io9[:], iota9_in[:, :], writes=[io9])
        m2 = ph.sb('mask2', [128, 4, 8, 16], F32, dma=True); tr.dma('sp', m2[:], mask2_in[:, :, :, :], writes=[m2])
        mcm = ph.sb('maskc', [128, 4, 2, 64], F32, dma=True); tr.dma('sp', mcm[:], maskc_in[:, :, :, :], writes=[mcm])
        sdt = ph.sb('sd', [128, 3], F32, dma=True); tr.dma('sp', sdt[:], sd_in[l], writes=[sdt])
        identb = ph.sb('identb', [128, 128], BF16)
        tr.op('act', lambda: A.activation(out=identb[:], in_=ident[:], func=AF.Copy), reads=[ident], writes=[identb])
        pst = ph.ps('pst', [128, 128])
        prm = {}
        BP = int(os.environ.get('BPREP', '99'))
        if BP == 1: ph.close(); return
        pers = {}
        for d in range(2):
            pers[d] = dict(w=ph.sb('w%d' % d, [128, 16, 12]), bb=ph.sb('bb%d' % d, [128, 2, 12, 16]), RK=ph.sb('rk%d' % d, [128, 12, 9]),
                           LRk=ph.sb('lrk%d' % d, [128, 12, 9]), LIk=ph.sb('lik%d' % d, [128, 12, 9]),
                           C2=ph.sb('c2_%d' % d, [128, 12, 2, 64]), S2=ph.sb('s2_%d' % d, [128, 12, 2, 64]),
                           scr=ph.sb('scr%d' % d, [128, 3, 64], F32, dma=True), sci=ph.sb('sci%d' % d, [128, 3, 64], F32, dma=True))
        pp = Phase(reset=False)
        A64 = pp.sb('a64', [128, 12, 64]); TA64 = pp.sb('ta64', [128, 12, 64]); TI64 = pp.sb('ti64', [128, 12, 64], I32)
        C64 = pp.sb('c64', [128, 12, 64]); S64 = pp.sb('s64', [128, 12, 64])
        for d in range(2):
            PD = pers[d]
            sa = pp.sb('sa%d' % d, [128, 3, 12], F32, dma=True); tr.dma('sp', sa[:], sa_in[l, d], writes=[sa])
            sbr = pp.sb('sbr%d' % d, [128, 12, 16], F32, dma=True); tr.dma('sp', sbr[:], sb_in[l, d, 0], writes=[sbr])
            sbi = pp.sb('sbi%d' % d, [128, 12, 16], F32, dma=True); tr.dma('sp', sbi[:], sb_in[l, d, 1], writes=[sbi])
            scr_ = PD['scr']; tr.dma('sp', scr_[:], sc_in[l, d, 0], writes=[scr_])
            sci = PD['sci']; tr.dma('sp', sci[:], sc_in[l, d, 1], writes=[sci])
            w = PD['w']
            ki = pp.sb('ki%d' % d, [128, 12], I32)
            are = sa[:, 0, :]; aim = sa[:, 1, :]; ldt = sa[:, 2, :]
            DT, ARDT, RR, TH, KF, THR, CX, CM, CO, SI, LR, LI = [w[:, i, :] for i in range(12)]
            N1, N2, DEN, T3 = [w[:, 12 + i, :] for i in range(4)]
            tr.op('act', lambda: A.activation(out=DT, in_=ldt, func=AF.Exp), reads=[sa], writes=[w])
            tr.op('dve', lambda: V.tensor_tensor(out=ARDT, in0=are, in1=DT, op=ALU.mult), reads=[w, sa], acc=[w])
            tr.op('act', lambda: A.activation(out=RR, in_=ARDT, func=AF.Exp), reads=[w], acc=[w])
            tr.op('dve', lambda: V.tensor_tensor(out=TH, in0=aim, in1=DT, op=ALU.mult), reads=[w, sa], acc=[w])
            tr.op('dve', lambda: V.tensor_scalar(out=ki[:], in0=TH, scalar1=1.0 / (2 * PI), scalar2=None, op0=ALU.mult), reads=[w], writes=[ki])
            tr.op('dve', lambda: V.tensor_copy(out=KF, in_=ki[:]), reads=[ki], acc=[w])
            tr.op('dve', lambda: V.scalar_tensor_tensor(out=THR, in0=KF, scalar=-2 * PI, in1=TH, op0=ALU.mult, op1=ALU.add), reads=[w], acc=[w])
            tr.op('dve', lambda: V.tensor_scalar(out=THR, in0=THR, scalar1=-PI, scalar2=PI, op0=ALU.max, op1=ALU.min), reads=[w], acc=[w])
            tr.op('dve', lambda: V.tensor_scalar(out=CX, in0=THR, scalar1=PI / 2, scalar2=None, op0=ALU.add), reads=[w], acc=[w])
            tr.op('dve', lambda: V.tensor_scalar(out=CM, in0=CX, scalar1=PI, scalar2=-2 * PI, op0=ALU.is_gt, op1=ALU.mult), reads=[w], acc=[w])
            tr.op('dve', lambda: V.tensor_tensor(out=CX, in0=CX, in1=CM, op=ALU.add), reads=[w], acc=[w])
            tr.op('dve', lambda: V.tensor_scalar(out=CX, in0=CX, scalar1=-PI, scalar2=PI, op0=ALU.max, op1=ALU.min), reads=[w], acc=[w])
            tr.op('act', lambda: A.activation(out=CO, in_=CX, func=AF.Sin), reads=[w], acc=[w])
            tr.op('act', lambda: A.activation(out=SI, in_=THR, func=AF.Sin), reads=[w], acc=[w])
            tr.op('dve', lambda: V.tensor_tensor(out=LR, in0=RR, in1=CO, op=ALU.mult), reads=[w], acc=[w])
            tr.op('dve', lambda: V.tensor_scalar(out=LR, in0=LR, scalar1=-1.0, scalar2=None, op0=ALU.add), reads=[w], acc=[w])
            tr.op('dve', lambda: V.tensor_tensor(out=LI, in0=RR, in1=SI, op=ALU.mult), reads=[w], acc=[w])
            tr.op('dve', lambda: V.tensor_tensor(out=N1, in0=LR, in1=are, op=ALU.mult), reads=[w, sa], acc=[w])
            tr.op('dve', lambda: V.tensor_tensor(out=T3, in0=LI, in1=aim, op=ALU.mult), reads=[w, sa], acc=[w])
            tr.op('dve', lambda: V.tensor_tensor(out=N1, in0=N1, in1=T3, op=ALU.add), reads=[w], acc=[w])
            tr.op('dve', lambda: V.tensor_tensor(out=N2, in0=LI, in1=are, op=ALU.mult), reads=[w, sa], acc=[w])
            tr.op('dve', lambda: V.tensor_tensor(out=T3, in0=LR, in1=aim, op=ALU.mult), reads=[w, sa], acc=[w])
            tr.op('dve', lambda: V.tensor_tensor(out=N2, in0=N2, in1=T3, op=ALU.subtract), reads=[w], acc=[w])
            tr.op('dve', lambda: V.tensor_tensor(out=DEN, in0=are, in1=are, op=ALU.mult), reads=[w, sa], acc=[w])
            tr.op('dve', lambda: V.tensor_tensor(out=T3, in0=aim, in1=aim, op=ALU.mult), reads=[w, sa], acc=[w])
            tr.op('dve', lambda: V.tensor_tensor(out=DEN, in0=DEN, in1=T3, op=ALU.add), reads=[w], acc=[w])
            tr.op('dve', lambda: V.reciprocal(out=DEN, in_=DEN), reads=[w], acc=[w])
            tr.op('dve', lambda: V.tensor_tensor(out=N1, in0=N1, in1=DEN, op=ALU.mult), reads=[w], acc=[w])
            tr.op('dve', lambda: V.tensor_tensor(out=N2, in0=N2, in1=DEN, op=ALU.mult), reads=[w], acc=[w])
            bb = PD['bb']; tb = pp.sb('tb%d' % d, [128, 2, 12, 16])
            cre = N1.unsqueeze(2).to_broadcast([128, 12, 16]); cim = N2.unsqueeze(2).to_broadcast([128, 12, 16])
            tr.op('dve', lambda: V.tensor_tensor(out=bb[:, 0], in0=sbr[:], in1=cre, op=ALU.mult), reads=[w, sbr], writes=[bb])
            tr.op('dve', lambda: V.tensor_tensor(out=tb[:, 0], in0=sbi[:], in1=cim, op=ALU.mult), reads=[w, sbi], writes=[tb])
            tr.op('dve', lambda: V.tensor_tensor(out=bb[:, 0], in0=bb[:, 0], in1=tb[:, 0], op=ALU.subtract), reads=[bb, tb], acc=[bb])
            tr.op('dve', lambda: V.tensor_tensor(out=bb[:, 1], in0=sbi[:], in1=cre, op=ALU.mult), reads=[w, sbi], acc=[bb])
            tr.op('dve', lambda: V.tensor_tensor(out=tb[:, 1], in0=sbr[:], in1=cim, op=ALU.mult), reads=[w, sbr], acc=[tb])
            tr.op('dve', lambda: V.tensor_tensor(out=bb[:, 1], in0=bb[:, 1], in1=tb[:, 1], op=ALU.add), reads=[bb, tb], acc=[bb])
            if BP == 2: pp.close(); ph.close(); return
            ANG = pp.sb('ang9_%d' % d, [128, 12, 9]); TA9 = pp.sb('ta9_%d' % d, [128, 12, 9]); TI9 = pp.sb('ti9_%d' % d, [128, 12, 9], I32)
            COk = pp.sb('cok%d' % d, [128, 12, 9]); SIk = pp.sb('sik%d' % d, [128, 12, 9]); RK = PD['RK']
            LRk = PD['LRk']; LIk = PD['LIk']
            i9b = io9[:].unsqueeze(1).to_broadcast([128, 12, 9])
            tr.op('dve', lambda: V.tensor_tensor(out=ANG[:], in0=THR.unsqueeze(2).to_broadcast([128, 12, 9]), in1=i9b, op=ALU.mult), reads=[w, io9], writes=[ANG])
            sin_rr(ANG[:], ANG, SIk[:], SIk, 0.0, TA9[:], TA9, TI9[:], TI9)
            sin_rr(ANG[:], ANG, COk[:], COk, PI / 2, TA9[:], TA9, TI9[:], TI9)
            tr.op('dve', lambda: V.tensor_tensor(out=RK[:], in0=ARDT.unsqueeze(2).to_broadcast([128, 12, 9]), in1=i9b, op=ALU.mult), reads=[w, io9], writes=[RK])
            tr.op('act', lambda: A.activation(out=RK[:], in_=RK[:], func=AF.Exp), reads=[RK], writes=[RK])
            tr.op('dve', lambda: V.tensor_tensor(out=LRk[:], in0=RK[:], in1=COk[:], op=ALU.mult), reads=[RK, COk], writes=[LRk])
            tr.op('dve', lambda: V.tensor_tensor(out=LIk[:], in0=RK[:], in1=SIk[:], op=ALU.mult), reads=[RK, SIk], writes=[LIk])
            if BP == 3: pp.close(); ph.close(); return
            TH8 = pp.sb('th8_%d' % d, [128, 12]); TA8 = pp.sb('ta8_%d' % d, [128, 12]); TI8 = pp.sb('ti8_%d' % d, [128, 12], I32)
            tr.op('dve', lambda: V.tensor_scalar(out=TA8[:], in0=THR, scalar1=8.0, scalar2=None, op0=ALU.mult), reads=[w], writes=[TA8])
            tr.op('dve', lambda: V.tensor_scalar(out=TI8[:], in0=TA8[:], scalar1=1.0 / (2 * PI), scalar2=None, op0=ALU.mult), reads=[TA8], writes=[TI8])
            tr.op('dve', lambda: V.tensor_copy(out=TH8[:], in_=TI8[:]), reads=[TI8], writes=[TH8])
            tr.op('dve', lambda: V.scalar_tensor_tensor(out=TH8[:], in0=TH8[:], scalar=-2 * PI, in1=TA8[:], op0=ALU.mult, op1=ALU.add), reads=[TH8, TA8], writes=[TH8])
            tr.op('dve', lambda: V.tensor_tensor(out=A64[:], in0=TH8[:].unsqueeze(2).to_broadcast([128, 12, 64]),
                                                 in1=io64[:, d, :].unsqueeze(1).to_broadcast([128, 12, 64]), op=ALU.mult), reads=[TH8, io64], writes=[A64])
            sin_rr(A64[:], A64, S64[:], S64, 0.0, TA64[:], TA64, TI64[:], TI64)
            sin_rr(A64[:], A64, C64[:], C64, PI / 2, TA64[:], TA64, TI64[:], TI64)
            if BP == 4: pp.close(); ph.close(); return
            C2 = PD['C2']; S2 = PD['S2']
            tr.group('act', [lambda: A.activation(out=C2[:, :, 0, :], in_=C64[:], func=AF.Copy),
                             lambda: A.activation(out=C2[:, :, 1, :], in_=C64[:], func=AF.Copy)], reads=[C64], writes=[C2])
            tr.group('act', [lambda: A.activation(out=S2[:, :, 0, :], in_=S64[:], func=AF.Copy),
                             lambda: A.activation(out=S2[:, :, 1, :], in_=S64[:], func=AF.Copy, scale=-1.0)], reads=[S64], writes=[S2])
            if BP == 5: pp.close(); ph.close(); return
            prm[d] = dict(w=w, LRk=LRk, LIk=LIk, RK=RK, C2=C2, S2=S2, bb=bb, scr=scr_, sci=sci)

        pp.close()
        BCUT = int(os.environ.get('BCUT', '99'))
        if BCUT == 0: ph.close(); return
        yacc = ph.sb('yacc', [128, TT], F32, dma=True); ub = ph.sb('ub', [128, TT], BF16)
        XHs = [ph.sb('XH%d' % i, [128, 4, 2, 8, 128], BF16) for i in range(2)]
        XTs = [ph.sb('XTt%d' % i, [128, 4, 2, 8, 128], BF16) for i in range(2)]
        Kbs = [ph.sb('Kb%d' % i, [128, 8, 128], BF16) for i in range(2)]
        Bstg = ph.sb('bstg4', [128, 4, 2, 128]); Cf = ph.sb('cf4', [128, 4, 2, 128]); cpad = ph.sb('cpad4', [128, 4, 2, 128], BF16)
        stg = [ph.sb('stgc%d' % i, [128, 128]) for i in range(2)]
        tq = [ph.sb('tq%d' % i, [128, 8, 128]) for i in range(2)]
        pxt = ph.ps('pxt', [128, 8, 128], BF16)
        pk = [ph.ps('pk%d' % i, [128, 4, 128]) for i in range(2)]
        pv = [ph.ps('pv%d' % i, [128, 4, 2, 64]) for i in range(2)]
        po = ph.ps('po', [128, 8, 64])
        t1 = ph.sb('bt1', [128, 4, 2, 64]); t2 = ph.sb('bt2', [128, 4, 2, 64]); Wt = ph.sb('bW', [128, 4, 2, 64]); Zt = ph.sb('bZ', [128, 4, 2, 64])
        Sf = [ph.sb('bSf%d' % i, [128, 4, 2, 64]) for i in range(2)]
        Sp = [ph.sb('bSp%d' % i, [128, 4, 2, 64], BF16) for i in range(2)]
        segs = [(0, 32)] + [(NCX + i * 512, 64) for i in range(8)]
        scnt = dict(n=0)

        def make_setup(j3, d, slot):
            P = prm[d]; XH = XHs[slot]; XT = XTs[slot]; Kb = Kbs[slot]
            steps = []

            def stage_s(s4):
                s = 4 * j3 + s4
                for ri in range(2):
                    first = (s4 == 0 and ri == 0)
                    tr.op('dve', lambda: V.tensor_tensor(out=Bstg[:, s4, ri, :].rearrange('p (a b) -> p a b', a=8),
                                                         in0=P['bb'][:, ri, s, :].unsqueeze(1).to_broadcast([128, 8, 16]),
                                                         in1=m2[:, s % 4], op=ALU.mult), reads=[P['bb'], m2],
                          writes=[Bstg] if first else (), acc=() if first else [Bstg])
                    sg_ = stg[scnt['n'] % 2]; scnt['n'] += 1
                    csrc = (P['scr'] if ri == 0 else P['sci'])
                    tr.op('dve', lambda: V.tensor_tensor(out=sg_[:].rearrange('p (a b) -> p a b', a=2),
                                                         in0=csrc[:, s // 4, :].unsqueeze(1).to_broadcast([128, 2, 64]),
                                                         in1=mcm[:, s % 4], op=ALU.mult), reads=[csrc, mcm], writes=[sg_])
                    tr.op('pe', lambda: PE.transpose(pst[:], sg_[:], ident[:]), reads=[sg_, ident], writes=[pst])
                    tr.op('act', lambda: A.activation(out=Cf[:, s4, ri, :], in_=pst[:], func=AF.Copy), reads=[pst],
                          writes=[Cf] if first else (), acc=() if first else [Cf])
                    tr.op('act', lambda: A.activation(out=cpad[:, s4, ri, :], in_=pst[:], func=AF.Copy, scale=(1.0 if ri == 0 else -1.0)),
                          reads=[pst], writes=[cpad] if first else (), acc=() if first else [cpad])

            def scaled_s(dstbuf, src, k0, neg_im, s4):
                s = 4 * j3 + s4
                sre = src[:, s4, 0, :].unsqueeze(1).to_broadcast([128, 8, 128]); sim = src[:, s4, 1, :].unsqueeze(1).to_broadcast([128, 8, 128])
                lr = P['LRk'][:, s, k0:k0 + 8].unsqueeze(2).to_broadcast([128, 8, 128])
                li = P['LIk'][:, s, k0:k0 + 8].unsqueeze(2).to_broadcast([128, 8, 128])
                first = (s4 == 0)
                tr.op('dve', lambda: V.tensor_tensor(out=tq[0][:], in0=sre, in1=lr, op=ALU.mult), reads=[src, P['LRk']], writes=[tq[0]])
                tr.op('dve', lambda: V.tensor_tensor(out=tq[1][:], in0=sim, in1=li, op=ALU.mult), reads=[src, P['LIk']], writes=[tq[1]])
                tr.op('dve', lambda: V.tensor_tensor(out=dstbuf[:, s4, 0], in0=tq[0][:], in1=tq[1][:], op=ALU.subtract), reads=[tq[0], tq[1]],
                      writes=[dstbuf] if first else (), acc=() if first else [dstbuf])
                tr.op('dve', lambda: V.tensor_tensor(out=tq[0][:], in0=sre, in1=li, op=ALU.mult), reads=[src, P['LIk']], writes=[tq[0]])
                tr.op('dve', lambda: V.tensor_tensor(out=tq[1][:], in0=sim, in1=lr, op=ALU.mult), reads=[src, P['LRk']], writes=[tq[1]])
                if neg_im:
                    tr.op('dve', lambda: V.scalar_tensor_tensor(out=dstbuf[:, s4, 1], in0=tq[0][:], scalar=-1.0, in1=tq[1][:], op0=ALU.mult, op1=ALU.subtract),
                          reads=[tq[0], tq[1]], acc=[dstbuf])
                else:
                    tr.op('dve', lambda: V.tensor_tensor(out=dstbuf[:, s4, 1], in0=tq[0][:], in1=tq[1][:], op=ALU.add), reads=[tq[0], tq[1]], acc=[dstbuf])

            def kmm(half):
                pkk = pk[half]
                fns = []
                for tt in range(4):
                    tau = half * 4 + tt
                    k = 0
                    for s4 in range(4):
                        for ri in range(2):
                            fns.append(lambda tt=tt, tau=tau, s4=s4, ri=ri, k=k: PE.matmul(pkk[:, tt, :], lhsT=XH[:, s4, ri, tau, :], rhs=cpad[:, s4, ri, :],
                                                                                             start=(k == 0), stop=(k == 7)))
                            k += 1
                tr.group('pe', fns, reads=[XH, cpad], writes=[pkk])
                tr.op('act', lambda: A.activation(out=Kb[:, half * 4:half * 4 + 4, :], in_=pkk[:], func=AF.Copy), reads=[pkk],
                      writes=[Kb] if half == 0 else (), acc=() if half == 0 else [Kb])

            def xtr(s4, ri):
                tr.group('pe', [lambda tau=tau: PE.transpose(pxt[:, tau, :], XH[:, s4, ri, tau, :], identb[:]) for tau in range(8)],
                         reads=[XH, identb], writes=[pxt])
                first = (s4 == 0 and ri == 0)
                tr.op('act' if ri == 0 else 'dve',
                      (lambda: A.activation(out=XT[:, s4, ri], in_=pxt[:], func=AF.Copy)) if ri == 0 else (lambda: V.tensor_copy(out=XT[:, s4, ri], in_=pxt[:])),
                      reads=[pxt], writes=[XT] if first else (), acc=() if first else [XT])

            for s4 in range(4): steps.append(lambda s4=s4: stage_s(s4))
            for s4 in range(4): steps.append(lambda s4=s4: scaled_s(XH, Bstg, 0, False, s4))
            for half in range(2): steps.append(lambda half=half: kmm(half))
            for s4 in range(4):
                for ri in range(2): steps.append(lambda s4=s4, ri=ri: xtr(s4, ri))
            for s4 in range(4): steps.append(lambda s4=s4: scaled_s(XH, Cf, 1, True, s4))
            return steps

        def run_loop(j3, d, slot, pending):
            P = prm[d]; XH = XHs[slot]; XT = XTs[slot]; Kb = Kbs[slot]
            order = list(range(9)) if d == 0 else [0] + list(range(8, 0, -1))
            prev = None
            per = -(-len(pending) // 8) if pending else 0

            def vmm(oi):
                t0, nC = segs[order[oi]]
                pvv = pv[oi % 2]
                fns = []
                for s4 in range(4):
                    for ri in range(2):
                        for j in range(8):
                            tau = (7 - j) if d == 0 else j
                            fns.append(lambda s4=s4, ri=ri, j=j, tau=tau: PE.matmul(pvv[:, s4, ri, :nC], lhsT=XT[:, s4, ri, tau, :],
                                                                                     rhs=ub[:, t0 + j:t0 + j + 8 * (nC - 1) + 1:8], start=(j == 0), stop=(j == 7)))
                tr.group('pe', fns, reads=[XT, ub], writes=[pvv])

            vmm(0)
            for oi in range(9):
                t0, nC = segs[order[oi]]
                par = oi % 2
                pvv = pv[par]
                s0 = 4 * j3
                csl = slice(0, nC) if d == 0 else slice(64 - nC, 64)
                C2v = P['C2'][:, s0:s0 + 4, :, csl]; S2v = P['S2'][:, s0:s0 + 4, :, csl]
                tr.op('dve', lambda: V.tensor_tensor(out=t1[:, :, :, :nC], in0=pvv[:, :, :, :nC], in1=C2v, op=ALU.mult), reads=[pvv, P['C2']], writes=[t1])
                tr.op('dve', lambda: V.tensor_tensor(out=t2[:, :, :, :nC], in0=pvv[:, :, ::-1, :nC], in1=S2v, op=ALU.mult), reads=[pvv, P['S2']], writes=[t2])
                tr.op('dve', lambda: V.tensor_tensor(out=Wt[:, :, :, :nC], in0=t1[:, :, :, :nC], in1=t2[:, :, :, :nC], op=ALU.add), reads=[t1, t2], writes=[Wt])
                if oi + 1 < 9: vmm(oi + 1)
                fns = []
                for s4 in range(4):
                    rr = P['RK'][:, s0 + s4, 8:9].to_broadcast([128, nC])
                    for ri in range(2):
                        if prev is None: ini = 0.0
                        else:
                            pcol = (prev[1] - 1) if d == 0 else 0
                            ini = Sf[1 - par][:, s4, ri, pcol:pcol + 1]
                        if d == 0:
                            fns.append(lambda s4=s4, ri=ri, rr=rr, ini=ini: V.tensor_tensor_scan(out=Zt[:, s4, ri, :nC], data0=rr, data1=Wt[:, s4, ri, :nC],
                                                                                                 initial=ini, op0=ALU.mult, op1=ALU.add))
                        else:
                            fns.append(lambda s4=s4, ri=ri, rr=rr, ini=ini: V.tensor_tensor_scan(out=Zt[:, s4, ri, :nC][:, ::-1], data0=rr, data1=Wt[:, s4, ri, :nC][:, ::-1],
                                                                                                 initial=ini, op0=ALU.mult, op1=ALU.add))
                tr.group('dve', fns, reads=[Wt, P['RK']] + ([Sf[1 - par]] if prev is not None else []), writes=[Zt])
                tr.op('dve', lambda: V.tensor_tensor(out=t1[:, :, :, :nC], in0=Zt[:, :, :, :nC], in1=C2v, op=ALU.mult), reads=[Zt, P['C2']], writes=[t1])
                tr.op('dve', lambda: V.tensor_tensor(out=t2[:, :, :, :nC], in0=Zt[:, :, ::-1, :nC], in1=S2v, op=ALU.mult), reads=[Zt, P['S2']], writes=[t2])
                tr.op('dve', lambda: V.tensor_tensor(out=Sf[par][:, :, :, :nC], in0=t1[:, :, :, :nC], in1=t2[:, :, :, :nC], op=ALU.subtract), reads=[t1, t2], writes=[Sf[par]])
                spv = Sp[par]
                if d == 0:
                    f1 = lambda: A.activation(out=spv[:, :, :, 1:nC], in_=Sf[par][:, :, :, 0:nC - 1], func=AF.Copy)
                    cdst = spv[:, :, :, 0:1]
                else:
                    f1 = lambda: A.activation(out=spv[:, :, :, 0:nC - 1], in_=Sf[par][:, :, :, 1:nC], func=AF.Copy)
                    cdst = spv[:, :, :, nC - 1:nC]
                if prev is None:
                    f2 = lambda: A.activation(out=cdst, in_=Sf[par][:, :, :, 0:1], func=AF.Copy, scale=0.0)
                    rdl = [Sf[par]]
                else:
                    pcol = (prev[1] - 1) if d == 0 else 0
                    f2 = lambda: A.activation(out=cdst, in_=Sf[1 - par][:, :, :, pcol:pcol + 1], func=AF.Copy)
                    rdl = [Sf[par], Sf[1 - par]]
                tr.group('act', [f1, f2], reads=rdl, writes=[spv])
                for _ in range(per):
                    if pending: pending.pop(0)()
                fns = []
                for j in range(8):
                    ntap = (j + 1) if d == 0 else (8 - j)
                    hk = j if d == 0 else (7 - j)
                    tot = ntap + 8; k = 0
                    for tau in range(ntap):
                        off = (j - tau) if d == 0 else (j + tau)
                        fns.append(lambda j=j, tau=tau, off=off, k=k, tot=tot: PE.matmul(po[:, j, :nC], lhsT=Kb[:, tau, :], rhs=ub[:, t0 + off:t0 + off + 8 * (nC - 1) + 1:8],
                                                                                          start=(k == 0), stop=(k == tot - 1)))
                        k += 1
                    for s4 in range(4):
                        for ri in range(2):
                            fns.append(lambda j=j, s4=s4, ri=ri, hk=hk, k=k, tot=tot: PE.matmul(po[:, j, :nC], lhsT=XH[:, s4, ri, hk, :], rhs=spv[:, s4, ri, :nC],
                                                                                                  start=(k == 0), stop=(k == tot - 1)))
                            k += 1
                tr.group('pe', fns, reads=[Kb, ub, XH, spv], writes=[po])
                yv = yacc[:, t0:t0 + 8 * nC].rearrange('p (c j) -> p j c', j=8)
                tr.op('dve', lambda: V.tensor_tensor(out=yv, in0=yv, in1=po[:, :, :nC], op=ALU.add), reads=[po, yacc], acc=[yacc])
                prev = (oi, nC)
            while pending: pending.pop(0)()

        ulist = [(j3, d) for j3 in range(3) for d in range(2)]
        for st_ in make_setup(0, 0, 0): st_()
        for ui, (j3, d) in enumerate(ulist):
            if d == 0:
                tr.dma('sp', yacc[:], U[j3 * 128:(j3 + 1) * 128, :], writes=[yacc])
                tr.op('act', lambda: A.activation(out=ub[:], in_=yacc[:], func=AF.Copy), reads=[yacc], writes=[ub])
                tr.op('dve', lambda: V.tensor_scalar(out=yacc[:], in0=yacc[:], scalar1=sdt[:, j3:j3 + 1], scalar2=None, op0=ALU.mult), reads=[yacc, sdt], writes=[yacc])
            pending = make_setup(ulist[ui + 1][0], ulist[ui + 1][1], (ui + 1) % 2) if ui + 1 < len(ulist) else []
            run_loop(j3, d, ui % 2, pending)
            if d == 1:
                for g0 in range(0, TT, 512):
                    n = min(512, TT - g0); yv = yacc[:, g0:g0 + n]
                    a0b = tq[0]; a1b = tq[1]
                    a0 = tq[0][:].rearrange('p a b -> p (a b)'); a1 = tq[1][:].rearrange('p a b -> p (a b)')
                    tr.op('act', lambda: A.activation(out=a0[:, :n], in_=yv, func=AF.Square), reads=[yacc], writes=[a0b])
                    tr.op('dve', lambda: V.tensor_scalar(out=a0[:, :n], in0=a0[:, :n], scalar1=0.044715, scalar2=1.0, op0=ALU.mult, op1=ALU.add), reads=[a0b], writes=[a0b])
                    tr.op('dve', lambda: V.tensor_tensor(out=a1[:, :n], in0=a0[:, :n], in1=yv, op=ALU.mult), reads=[a0b, yacc], writes=[a1b])
                    tr.op('act', lambda: A.activation(out=a1[:, 512:512 + n], in_=a1[:, :n], func=AF.Sigmoid, scale=2.0 * math.sqrt(2.0 / PI)), reads=[a1b], writes=[a1b])
                    st = stg_o[(g0 // 512) % 2]
                    tr.op('dve', lambda: V.tensor_tensor(out=st[:, :n], in0=a1[:, 512:512 + n], in1=yv, op=ALU.mult), reads=[a1b, yacc], writes=[st])
                    tr.dma('pool', TG[j3 * 128:(j3 + 1) * 128, g0:g0 + n], st[:, :n], reads=[st])
        ph.close()

    gm = None
    stg_o = None

    def phase_C(l, ctx_out):
        nonlocal gm
        ph = Phase()
        gm = ph.sb('gm', [128, 2, 128], F32, dma=True); tr.dma('sp', gm[:], gmask_in[:, :, :], writes=[gm])
        snk = ph.sb('snk', [128, 8], F32, dma=True)
        tr.dma('sp', snk[0:64, :], sink_in[l], writes=[snk]); tr.dma('sp', snk[64:128, :], sink_in[l], acc=[snk])
        esk = ph.sb('esk', [128, 8])
        tr.op('act', lambda: A.activation(out=esk[:], in_=snk[:], func=AF.Exp), reads=[snk], writes=[esk])
        vsb = ph.sb('vsb', [128, 34, 128], BF16, dma=True)
        tr.dma('sp', vsb[:], VG.rearrange('(c p) d -> p c d', p=128), writes=[vsb])
        vaug = [ph.sb('vaug%d' % g, [128, 34, 128], BF16) for g in range(2)]
        for g in range(2):
            tr.op('dve', lambda: V.memset(vaug[g][:, :, 64:128], 1.0), writes=[vaug[g]])
            tr.op('act', lambda: A.activation(out=vaug[g][:, :, 0:64], in_=vsb[:, :, g * 64:(g + 1) * 64], func=AF.Copy), reads=[vsb], acc=[vaug[g]])
        kT = [ph.sb('kT%d' % g, [64, TT], BF16, dma=True) for g in range(2)]
        for g in range(2):
            tr.dma('sp', kT[g][:, 0:NCX], KGC[g], writes=[kT[g]])
            tr.dma('sp', kT[g][:, NCX:TT], KG[g], acc=[kT[g]])
        qb = [ph.sb('qb%d' % i, [64, 4, 128], BF16, dma=True) for i in range(3)]
        ps_s = [ph.ps('pss%d' % i) for i in range(4)]
        ps_o = [ph.ps('pso%d' % i) for i in range(2)]; ps_d = [ph.ps('psd%d' % i) for i in range(2)]
        pT = [ph.sb('pT%d' % i, [128, 512], BF16) for i in range(4)]
        tmpm = [ph.sb('tmpm%d' % i, [128, 512]) for i in range(2)]; den = ph.sb('den', [128, 512])
        ob = [ph.sb('ob%d' % i, [64, 4, 128], BF16, dma=True) for i in range(2)]
        units = []
        if ctx_out:
            for g in range(2):
                for bi in range(2): units.append(('c', g, bi))
        for g in range(2):
            for bi in range(32): units.append(('l', g, bi))
        items = []
        uinfo = []
        for ui, (kind, g, bi) in enumerate(units):
            keys = [(kT[g][:, c * 128:(c + 1) * 128], None, vaug[g][:, c, :]) for c in range(2)]
            if kind == 'l':
                for dlt in (-1, 0, 1):
                    kb = bi + dlt
                    if kb < 0 or kb > 31: continue
                    m = None if dlt == 0 else gm[:, (0 if dlt == -1 else 1), :].unsqueeze(1).to_broadcast([128, 4, 128])
                    keys.append((kT[g][:, NCX + kb * 128:NCX + (kb + 1) * 128], m, vaug[g][:, 2 + kb, :]))
            for ki, (k_ap, m_ap, v_ap) in enumerate(keys): items.append((ui, ki, len(keys), k_ap, m_ap, v_ap))
        cnt = dict(n=0, m=0)
        st = {}

        def stage1(ii):
            ui, ki, nk, k_ap, m_ap, v_ap = items[ii]
            kind, g, bi = units[ui]
            q = qb[ui % 3]
            if ki == 0:
                srcq = (QGC if kind == 'c' else QG)[4 * g:4 * g + 4, :, bi * 128:(bi + 1) * 128].rearrange('h d t -> d h t')
                tr.dma('sp', q[:], srcq, writes=[q])
            pss = ps_s[cnt['n'] % 4]; pt = pT[cnt['n'] % 4]; cnt['n'] += 1
            tr.op('pe', lambda: PE.matmul(pss[:, :], lhsT=k_ap, rhs=q[:], start=True, stop=True), reads=[q, kT[g]], writes=[pss])
            if m_ap is not None:
                tm = tmpm[cnt['m'] % 2]; cnt['m'] += 1
                tr.op('dve', lambda: V.tensor_tensor(out=tm[:].rearrange('p (h t) -> p h t', h=4), in0=pss[:].rearrange('p (h t) -> p h t', h=4),
                                                     in1=m_ap, op=ALU.add), reads=[pss, gm], writes=[tm])
                tr.op('act', lambda: A.activation(out=pt[:], in_=tm[:], func=AF.Exp), reads=[tm], writes=[pt])
            else:
                tr.op('act', lambda: A.activation(out=pt[:], in_=pss[:], func=AF.Exp), reads=[pss], writes=[pt])
            st[ii] = pt

        def stage2(ii):
            ui, ki, nk, k_ap, m_ap, v_ap = items[ii]
            kind, g, bi = units[ui]
            pt = st.pop(ii)
            pso = ps_o[ui % 2]; psd = ps_d[ui % 2]; o = ob[ui % 2]
            tr.op('pe', lambda: PE.matmul(pso[:, :], lhsT=v_ap, rhs=pt[:], start=(ki == 0), stop=(ki == nk - 1)),
                  reads=[pt, vaug[g]], writes=[pso] if ki == 0 else (), acc=() if ki == 0 else [pso])
            if ki == nk - 1:
                tr.op('dve', lambda: V.tensor_tensor(out=den[64:128, :].rearrange('p (h t) -> p h t', h=4), in0=pso[64:128, :].rearrange('p (h t) -> p h t', h=4),
                                                     in1=esk[64:128, 4 * g:4 * g + 4].unsqueeze(2).to_broadcast([64, 4, 128]), op=ALU.add),
                      reads=[pso, esk], writes=[den])
                tr.op('act', lambda: A.activation(out=den[64:128, :], in_=den[64:128, :], func=AF.Ln), reads=[den], writes=[den])
                tr.op('act', lambda: A.activation(out=den[64:128, :], in_=den[64:128, :], func=AF.Exp, scale=-1.0), reads=[den], writes=[den])
                tr.op('dve', lambda: V.tensor_tensor(out=o[:].rearrange('p h t -> p (h t)'), in0=pso[0:64, :], in1=den[64:128, :], op=ALU.mult),
                      reads=[pso, den], writes=[o])
                dst = (YGC if kind == 'c' else YG)[4 * g:4 * g + 4, :, bi * 128:(bi + 1) * 128].rearrange('h d t -> d h t')
                tr.dma('pool', dst, o[:], reads=[o])

        KD = 2
        for ii in range(len(items) + KD):
            if ii < len(items): stage1(ii)
            if ii - KD >= 0: stage2(ii - KD)
        ph.close()

    def attn_unit3(q, keys, ps_s, pso, psd, pT, tmpm, qbufs, kbufs, vbufs, epi):
        nk = len(keys)
        for ki, (k_ap, m_ap, v_ap) in enumerate(keys):
            pss = ps_s[ki % len(ps_s)]
            tr.op('pe', lambda: PE.matmul(pss[:, :], lhsT=k_ap, rhs=q[:], start=True, stop=True), reads=qbufs + kbufs, writes=[pss])
            pt = pT[ki % len(pT)]
            if m_ap is not None:
                tr.op('dve', lambda: V.tensor_tensor(out=tmpm[:].rearrange('p (h t) -> p h t', h=4), in0=pss[:].rearrange('p (h t) -> p h t', h=4),
                                                     in1=m_ap, op=ALU.add), reads=[pss, gm], writes=[tmpm])
                tr.op('act', lambda: A.activation(out=pt[:], in_=tmpm[:], func=AF.Exp), reads=[tmpm], writes=[pt])
            else:
                tr.op('act', lambda: A.activation(out=pt[:], in_=pss[:], func=AF.Exp), reads=[pss], writes=[pt])
            tr.op('pe', lambda: PE.matmul(pso[0:64, :], lhsT=v_ap, rhs=pt[:], start=(ki == 0), stop=(ki == nk - 1)),
                  reads=[pt] + vbufs, writes=[pso] if ki == 0 else (), acc=() if ki == 0 else [pso])
            tr.op('pe', lambda: PE.matmul(psd[0:64, :], lhsT=ones_b[:, 0:64], rhs=pt[:], start=(ki == 0), stop=(ki == nk - 1)),
                  reads=[pt, ones_b], writes=[psd] if ki == 0 else (), acc=() if ki == 0 else [psd])
        epi()

    def phase_D(l, ctx_out):
        ph = Phase()
        kT = [ph.sb('nkT%d' % i, [64, TT], BF16, dma=True) for i in range(2)]
        qT = [ph.sb('nqT%d' % i, [64, TT], BF16, dma=True) for i in range(2)]
        vs = [ph.sb('nvs%d' % i, [128, 34, 64], BF16, dma=True) for i in range(2)]
        vt = [ph.sb('nvt%d' % i, [128, 14, 64], F32, dma=True) for i in range(2)]
        od = [ph.sb('nod%d' % i, [128, 5, 64], F32, dma=True) for i in range(2)]
        cbt = [ph.sb('ncb%d' % i, [128, 5, 128], F32, dma=True) for i in range(2)]
        yb = [ph.sb('nyb%d' % i, [64, TT], BF16, dma=True) for i in range(2)]
        RD = 2
        ps_L = [ph.ps('npl%d' % i) for i in range(RD)]
        ps_X = [ph.ps('npx%d' % i) for i in range(RD)]
        ps_od = [ph.ps('npo%d' % i) for i in range(RD)]
        tmp = [ph.sb('ntmp%d' % i, [128, 640]) for i in range(RD)]
        pT = [ph.sb('npT%d' % i, [128, 640], BF16) for i in range(RD)]
        pC = [ph.sb('npC%d' % i, [128, 256], BF16) for i in range(RD)]
        rd = [ph.sb('nrd%d' % i, [128, 256]) for i in range(RD)]
        vaugn = [ph.sb('nvaug%d' % i, [128, 34, 128], BF16) for i in range(2)]
        n = 0
        for h in range(8):
            k_ = kT[h % 2]; q_ = qT[h % 2]; v_ = vs[h % 2]; vt_ = vt[h % 2]; od_ = od[h % 2]; cb_ = cbt[h % 2]; y_ = yb[h % 2]
            tr.dma('sp', k_[:, 0:NCX], KNC[h], writes=[k_]); tr.dma('sp', k_[:, NCX:TT], KN[h], acc=[k_])
            tr.dma('sp', q_[:, 0:NCX], QNC[h], writes=[q_]); tr.dma('sp', q_[:, NCX:TT], QN[h], acc=[q_])
            tr.dma('sp', v_[:], VN[:, h * 64:(h + 1) * 64].rearrange('(c p) d -> p c d', p=128), writes=[v_])
            tr.dma('sp', vt_[:], navt_in[l, h], writes=[vt_]); tr.dma('sp', od_[:], naod_in[l, h], writes=[od_])
            tr.dma('sp', cb_[:], nacb_in[l, h], writes=[cb_])
            va_ = vaugn[h % 2]
            tr.op('dve', lambda: V.memset(va_[:, :, 64:128], 1.0), writes=[va_])
            tr.op('act', lambda: A.activation(out=va_[:, :, 0:64], in_=v_[:], func=AF.Copy), reads=[v_], acc=[va_])
            units = ([('c', 0), ('c', 1)] if ctx_out else []) + [('l', r) for r in range(4)] + [('p', r) for r in range(4, 60, 2)] + [('l', r) for r in range(60, 64)]
            for ui, (kind, r) in enumerate(units):
                psl = ps_L[n % RD]; psx = ps_X[n % RD]; pob = ps_od[n % RD]
                tm = tmp[n % RD]; pt = pT[n % RD]; pc = pC[n % RD]; rdn = rd[n % RD]; n += 1
                p0 = 0
                if kind == 'c':
                    Nq = 128; qap = q_[:, r * 128:(r + 1) * 128]; npair = 0; oc0 = r * 128
                elif kind == 'p':
                    Nq = 128; qap = q_[:, NCX + r * 64:NCX + (r + 2) * 64]; oc0 = NCX + r * 64
                    p0 = (r - 4) // 2; npair = 5
                else:
                    Nq = 64; qap = q_[:, NCX + r * 64:NCX + (r + 1) * 64]; oc0 = NCX + r * 64
                    rs = min(max(r - 4, 0), 56)
                    p0 = rs // 2; npair = 4; i0_ = rs - r + 7
                    bias = vt_[:, i0_:i0_ + 7:2, :]
                fns = [lambda c=c: PE.matmul(psx[:, 128 + c * Nq:128 + (c + 1) * Nq], lhsT=k_[:, c * 128:(c + 1) * 128], rhs=qap, start=True, stop=True) for c in range(2)]
                if npair == 5:
                    fns.append(lambda: PE.matmul(psx[:, 0:Nq], lhsT=k_[:, NCX + (p0 + 4) * 128:NCX + (p0 + 5) * 128], rhs=qap, start=True, stop=True))
                tr.group('pe', fns, reads=[k_, q_], writes=[psx])
                if npair:
                    tr.group('pe', [lambda k=k: PE.matmul(psl[:, k * Nq:(k + 1) * Nq], lhsT=k_[:, NCX + (p0 + k) * 128:NCX + (p0 + k + 1) * 128], rhs=qap,
                                                          start=True, stop=True) for k in range(4)], reads=[k_, q_], writes=[psl])
                tr.op('act', lambda: A.activation(out=pc[:, :2 * Nq], in_=psx[:, 128:128 + 2 * Nq], func=AF.Exp), reads=[psx], writes=[pc])
                if kind == 'l':
                    tr.op('dve', lambda: V.tensor_tensor(out=tm[:, :256].rearrange('p (a b) -> p a b', a=4),
                                                         in0=psl[:, :256].rearrange('p (a b) -> p a b', a=4), in1=bias, op=ALU.add),
                          reads=[psl, vt_], writes=[tm])
                    tr.op('act', lambda: A.activation(out=pt[:, :256], in_=tm[:, :256], func=AF.Exp), reads=[tm], writes=[pt])
                elif kind == 'p':
                    tr.op('dve', lambda: V.tensor_tensor(out=tm[:, 0:512], in0=psl[:, 0:512], in1=cb_[:, 0:4, :].rearrange('p a b -> p (a b)'), op=ALU.add),
                          reads=[psl, cb_], writes=[tm])
                    tr.op('dve', lambda: V.tensor_tensor(out=tm[:, 512:640], in0=psx[:, 0:128], in1=cb_[:, 4, :], op=ALU.add),
                          reads=[psx, cb_], acc=[tm])
                    tr.op('act', lambda: A.activation(out=pt[:, :640], in_=tm[:, :640], func=AF.Exp), reads=[tm], writes=[pt])
                mm = [(va_[:, c, :], pc[:, c * Nq:(c + 1) * Nq]) for c in range(2)]
                mm += [(va_[:, 2 + p0 + k, :], pt[:, k * Nq:(k + 1) * Nq]) for k in range(npair)]
                tr.group('pe', [lambda i=i, a=a, b=b: PE.matmul(pob[:, 0:Nq], lhsT=a, rhs=b, start=(i == 0), stop=(i == len(mm) - 1))
                                for i, (a, b) in enumerate(mm)], reads=[va_, pc, pt], writes=[pob])
                tr.op('dve', lambda: V.reciprocal(out=rdn[64:128, :Nq], in_=pob[64:128, 0:Nq]), reads=[pob], writes=[rdn])
                tr.op('dve', lambda: V.tensor_tensor(out=y_[:, oc0:oc0 + Nq], in0=pob[0:64, 0:Nq], in1=rdn[64:128, :Nq], op=ALU.mult),
                      reads=[pob, rdn], writes=[y_] if ui == 0 else (), acc=() if ui == 0 else [y_])
            if ctx_out: tr.dma('pool', YNC[h], y_[:, 0:NCX], reads=[y_])
            tr.dma('pool', YN[h], y_[:, NCX:TT], reads=[y_])
        ph.close()

    pcbig = None

    def phase_E(l, ctx_out, last):
        ph = Phase()
        R = ffn_bufs(ph)
        xTs = [ph.sb('xT%d' % i, [128, KC, 512], F32, dma=True) for i in range(2)]
        hT = ph.sb('hT', [128, KC, 512], BF16); aT = ph.sb('aT', [128, FC, 512], BF16)
        tg = [ph.sb('tg%d' % i, [128, 3, 512], F32, dma=True) for i in range(2)]
        tgb = ph.sb('tgb', [128, 3, 512], BF16); ys = ph.sb('ys', [128, 3, 512], BF16)
        yg = [ph.sb('yg%d' % i, [128, 4, 512], BF16, dma=True) for i in range(2)]
        yn = [ph.sb('yn%d' % i, [128, 4, 512], BF16, dma=True) for i in range(2)]
        sgr = [ph.sb('sg%d' % i, [128, 3, 512], BF16, dma=True) for i in range(2)]
        wpgr = [ph.sb('wpg%d' % i, [128, 4, 128], BF16, dma=True) for i in range(2)]
        wpnr = [ph.sb('wpn%d' % i, [128, 4, 128], BF16, dma=True) for i in range(2)]
        wglu = ph.sb('wglu', [128, 3, 3, 128], BF16, dma=True)
        tr.dma('sp', wglu[:], w_b[('wglu', l)].rearrange('m p k c -> p m k c'), reads=[w_buf[('wglu', l)]], writes=[wglu])
        wps = ph.sb('wps', [128, 8, 3, 128], BF16, dma=True)
        tr.dma('sp', wps[:], w_b[('wps', l)].rearrange('m p k c -> p m k c'), reads=[w_buf[('wps', l)]], writes=[wps])
        wo = [ph.sb('wo%d' % i, [128, KC, 128], BF16, dma=True) for i in range(2)]
        acc = ph.sb('acc', [128, 512]); t2 = ph.sb('t2e', [128, 512])
        ost = [ph.sb('ost%d' % i, [128, D], F32, dma=True) for i in range(1)] if last else None
        tl = tiles if ctx_out else tiles[1:]
        cnt = dict(wo=0, ost=0)

        def loads(idx):
            c0, N, col = tl[idx]; b = idx % 2
            tr.dma('sp', xTs[b][:, :, :N], XT[:, c0:c0 + N].rearrange('(k p) t -> p k t', p=128), writes=[xTs[b]])
            tr.dma('sp', tg[b][:, :, :N], TG[:, c0:c0 + N].rearrange('(k p) t -> p k t', p=128), writes=[tg[b]])
            if col == 1:
                tr.dma('sp', yg[b][:, :, :N], YGC.rearrange('h d t -> (h d) t').rearrange('(k p) t -> p k t', p=128), writes=[yg[b]])
                tr.dma('sp', yn[b][:, :, :N], YNC.rearrange('h d t -> (h d) t').rearrange('(k p) t -> p k t', p=128), writes=[yn[b]])
            else:
                tr.dma('sp', yg[b][:, :, :N], YG.rearrange('h d t -> (h d) t')[:, c0 - NCX:c0 - NCX + N].rearrange('(k p) t -> p k t', p=128), writes=[yg[b]])
                tr.dma('sp', yn[b][:, :, :N], YN.rearrange('h d t -> (h d) t')[:, c0 - NCX:c0 - NCX + N].rearrange('(k p) t -> p k t', p=128), writes=[yn[b]])

        mcount = 0
        loads(0)
        for idx in range(len(tl)):
            c0, N, col = tl[idx]; b = idx % 2
            xT = xTs[b]; tg_ = tg[b]; yg_ = yg[b]; yn_ = yn[b]
            if idx + 1 < len(tl): loads(idx + 1)
            tr.op('act', lambda: A.activation(out=tgb[:, :, :N], in_=tg_[:, :, :N], func=AF.Copy), reads=[tg_], writes=[tgb])
            for m in range(3):
                pg = R['ps_g'][m % 2]
                tr.group('pe', [lambda kc=kc: PE.matmul(pg[:, :N], lhsT=wglu[:, m, kc, :], rhs=tgb[:, kc, :N], start=(kc == 0), stop=(kc == 2))
                                for kc in range(3)], reads=[wglu, tgb], writes=[pg])
                tr.op('act', lambda: A.activation(out=acc[:, :N], in_=pg[:, :N], func=AF.Sigmoid), reads=[pg], writes=[acc])
                tr.op('dve', lambda: V.tensor_tensor(out=ys[:, m, :N], in0=acc[:, :N], in1=tg_[:, m, :N], op=ALU.mult), reads=[acc, tg_],
                      writes=[ys] if m == 0 else (), acc=() if m == 0 else [ys])
            for m in range(KC):
                p1 = R['ps_g'][m % 2]; p2 = R['ps_u'][m % 2]; p3 = R['ps_m'][m % 2]
                sg_ = sgr[mcount % 2]; wpg = wpgr[mcount % 2]; wpn = wpnr[mcount % 2]; mcount += 1
                tr.dma('sp', sg_[:, :, :N], SG[:, c0:c0 + N].rearrange('(b m p) t -> m p b t', b=3, p=128)[m], writes=[sg_])
                tr.dma('sp', wpg[:], w_b[('wpg', l)][m], reads=[w_buf[('wpg', l)]], writes=[wpg])
                tr.dma('sp', wpn[:], w_b[('wpn', l)][m], reads=[w_buf[('wpn', l)]], writes=[wpn])
                tr.group('pe', [lambda kc=kc: PE.matmul(p1[:, :N], lhsT=wps[:, m, kc, :], rhs=ys[:, kc, :N], start=(kc == 0), stop=(kc == 2))
                                for kc in range(3)], reads=[wps, ys], writes=[p1])
                tr.group('pe', [lambda h=h: PE.matmul(p2[:, :N], lhsT=wpg[:, h, :], rhs=yg_[:, h, :N], start=(h == 0), stop=(h == 3))
                                for h in range(4)], reads=[wpg, yg_], writes=[p2])
                tr.group('pe', [lambda h=h: PE.matmul(p3[:, :N], lhsT=wpn[:, h, :], rhs=yn_[:, h, :N], start=(h == 0), stop=(h == 3))
                                for h in range(4)], reads=[wpn, yn_], writes=[p3])
                tr.op('dve', lambda: V.tensor_tensor(out=acc[:, :N], in0=p1[:, :N], in1=sg_[:, 0, :N], op=ALU.mult), reads=[p1, sg_], writes=[acc])
                tr.op('dve', lambda: V.tensor_tensor(out=t2[:, :N], in0=p2[:, :N], in1=sg_[:, 1, :N], op=ALU.mult), reads=[p2, sg_], writes=[t2])
                tr.op('pool', lambda: G.tensor_tensor(out=acc[:, :N], in0=acc[:, :N], in1=t2[:, :N], op=ALU.add), reads=[acc, t2], writes=[acc])
                tr.op('dve', lambda: V.tensor_tensor(out=t2[:, :N], in0=p3[:, :N], in1=sg_[:, 2, :N], op=ALU.mult), reads=[p3, sg_], writes=[t2])
                tr.op('pool', lambda: G.tensor_tensor(out=hT[:, m, :N], in0=acc[:, :N], in1=t2[:, :N], op=ALU.add), reads=[acc, t2],
                      writes=[hT] if m == 0 else (), acc=() if m == 0 else [hT])
            for m in range(KC):
                wb = wo[cnt['wo'] % 2]; cnt['wo'] += 1
                tr.dma('sp', wb[:], w_b[('wout', l)][m], reads=[w_buf[('wout', l)]], writes=[wb])
                pd = R['ps_m'][m % 2]
                tr.group('pe', [lambda kc=kc: PE.matmul(pd[:, :N], lhsT=wb[:, kc, :], rhs=hT[:, kc, :N], start=(kc == 0), stop=(kc == KC - 1))
                                for kc in range(KC)], reads=[wb, hT], writes=[pd])
                tr.op('dve', lambda: V.scalar_tensor_tensor(out=xT[:, m, :N], in0=pd[:, :N], scalar=modG[:, 1, m, col:col + 1],
                                                            in1=xT[:, m, :N], op0=ALU.mult, op1=ALU.add), reads=[pd, modG, xT], acc=[xT])
            ffn(ph, l, 2, xT, hT, aT, N, col, 2, R)
            if not last:
                tr.dma('pool', XT[:, c0:c0 + N].rearrange('(k p) t -> p k t', p=128), xT[:, :, :N], reads=[xT])
            else:
                sq = R['sq']; rstd = R['rstd']; tb = R['tmpbig']; pss = R['ps_m'][0]
                tr.op('act', lambda: A.activation(out=sq[:, :, :N], in_=xT[:, :, :N], func=AF.Square), reads=[xT], writes=[sq])
                tr.group('pe', [lambda kc=kc: PE.matmul(pss[:, :N], lhsT=ones_b[:], rhs=sq[:, kc, :N], start=(kc == 0), stop=(kc == KC - 1))
                                for kc in range(KC)], reads=[sq, ones_b], writes=[pss])
                tr.op('act', lambda: A.activation(out=rstd[:, :N], in_=pss[:, :N], func=AF.Sqrt, scale=1.0 / D, bias=epsb[:, 0:1]), reads=[pss, epsb], writes=[rstd])
                tr.op('dve', lambda: V.reciprocal(out=rstd[:, :N], in_=rstd[:, :N]), reads=[rstd], writes=[rstd])
                tr.op('dve', lambda: V.tensor_tensor(out=tb[:, :, :N], in0=xT[:, :, :N], in1=rstd[:, :N].unsqueeze(1).to_broadcast([128, KC, N]), op=ALU.mult),
                      reads=[xT, rstd], writes=[tb])
                tr.group('act', [lambda kc=kc: A.activation(out=tb[:, kc, :N], in_=tb[:, kc, :N], func=AF.Identity, scale=fing[:, kc:kc + 1])
                                 for kc in range(KC)], reads=[tb, fing], writes=[tb])
                for ts in range(N // 128):
                    o_ = ost[0]; cnt['ost'] += 1
                    for hf in range(2):
                        pt = R['ps_g'][hf]
                        tr.group('pe', [lambda k=k: PE.transpose(pt[:, k * 128:(k + 1) * 128], tb[:, hf * 4 + k, ts * 128:(ts + 1) * 128], ident[:])
                                        for k in range(4)], reads=[tb, ident], writes=[pt])
                        tr.op('act' if hf == 0 else 'dve',
                              (lambda: A.activation(out=o_[:, 0:512], in_=pt[:], func=AF.Copy)) if hf == 0 else (lambda: V.tensor_copy(out=o_[:, 512:1024], in_=pt[:])),
                              reads=[pt], writes=[o_] if hf == 0 else (), acc=() if hf == 0 else [o_])
                    r0 = c0 - NCX + ts * 128
                    tr.dma('pool', out_d[r0:r0 + 128, :], o_[:], reads=[o_])
        ph.close()

    epsb = gp.sb('epsb', [128, 1]); tr.op('dve', lambda: V.memset(epsb[:], 1e-6), writes=[epsb])
    tr.barrier()

    def run():
        nonlocal stg_o, pcbig
        stg_o = [gp.sb('stgo%d' % i, [128, 512], F32, dma=True) for i in range(2)]
        pcbig = gp.sb('pcbig', [128, 256], BF16)
        if only is not None:
            {'B': phase_B}[only[0]](only[1]); return
        for l in range(DEPTH):
            ctx_out = l < DEPTH - 1
            compute_mod(l)
            if stop_after == ('mod', l): return
            phase_A(l)
            if stop_after == ('A', l): return
            emit_casts(l, G2)
            phase_B(l)
            if stop_after == ('B', l): return
            if l + 1 < DEPTH: emit_casts(l + 1, G1)
            phase_C(l, ctx_out)
            if stop_after == ('C', l): return
            phase_D(l, ctx_out)
            if stop_after == ('D', l): return
            phase_E(l, ctx_out, l == DEPTH - 1)
            if stop_after == ('E', l): return

    run()
    tr.barrier()
    gp.es.close()
    tr.es.close()
    return nc


def prep_shared(inp):
    sh = {}
    L = DEPTH
    sh['wgu1'] = np.stack([np.stack([tile_w(inp['ffn1_wg'][l]), tile_w(inp['ffn1_wu'][l])], 2) for l in range(L)])
    sh['wd1'] = np.stack([tile_w(inp['ffn1_wd'][l]) for l in range(L)])
    sh['wgu2'] = np.stack([np.stack([tile_w(inp['ffn2_wg'][l]), tile_w(inp['ffn2_wu'][l])], 2) for l in range(L)])
    sh['wd2'] = np.stack([tile_w(inp['ffn2_wd'][l]) for l in range(L)])
    cols = win_fm_cols()
    sh['winfm'] = np.stack([tile_w(inp['w_in'][l][:, cols]) for l in range(L)])
    tmc = np.concatenate([IN_OFF['gv'] + np.arange(128), IN_OFF['nv'] + np.arange(512)])
    sh['wintm'] = np.stack([np.ascontiguousarray(inp['w_in'][l][:, tmc].reshape(KC, 128, 640).transpose(1, 0, 2)) for l in range(L)])
    sh['wada'] = np.stack([tile_w(inp['w_ada'][l]) for l in range(L)])
    sh['wglu'] = np.stack([tile_w(inp['ssm_w_glu'][l]) for l in range(L)])
    sh['wps'] = np.stack([tile_w(inp['w_p_ssm'][l]) for l in range(L)])
    sh['wpg'] = np.stack([tile_w(inp['w_p_gqa'][l]) for l in range(L)])
    sh['wpn'] = np.stack([tile_w(inp['w_p_na'][l]) for l in range(L)])
    sh['wout'] = np.stack([tile_w(inp['w_out'][l]) for l in range(L)])
    sh['bada'] = np.ascontiguousarray(inp['b_ada'].reshape(L, 72, 128).transpose(0, 2, 1))
    sh['normg'] = np.ascontiguousarray(inp['norm_g'].reshape(L, 3, KC, 128).transpose(0, 3, 1, 2))
    sh['fing'] = np.ascontiguousarray(inp['final_g'].reshape(KC, 128).T)

    def st(a):
        return a.reshape(L, 2, 12, 2, 64).transpose(0, 1, 3, 4, 2).reshape(L, 2, 128, 12)
    ldt = np.broadcast_to(inp['ssm_log_dt'][:, :, :, None], (L, 2, 24, 64))
    sh['ssm_a'] = np.ascontiguousarray(np.stack([st(inp['ssm_a_re']), st(inp['ssm_a_im']), st(ldt)], 3))

    def sbt(a):
        return a.reshape(L, 2, 12, 2, 64, 16).transpose(0, 1, 3, 4, 2, 5).reshape(L, 2, 128, 12, 16)
    sh['ssm_b'] = np.ascontiguousarray(np.stack([sbt(inp['ssm_b_re']), sbt(inp['ssm_b_im'])], 2))

    def sct(a):
        return a.reshape(L, 2, 3, 8, 16, 64).transpose(0, 1, 3, 4, 2, 5).reshape(L, 2, 128, 3, 64)
    sh['ssm_c'] = np.ascontiguousarray(np.stack([sct(inp['ssm_c_re']), sct(inp['ssm_c_im'])], 2))
    sh['ssm_d'] = np.ascontiguousarray(inp['ssm_d'].reshape(L, 3, 128).transpose(0, 2, 1))
    sh['sink'] = np.ascontiguousarray(np.broadcast_to(inp['gqa_sink'][:, None, :], (L, 64, 8)))
    vt, od, cb = na_bias_tables(inp['na_rpb'])
    sh['navt'] = vt; sh['naod'] = od; sh['nacb'] = cb.reshape(L, 8, 128, 5, 128)
    sh.update(host_consts())
    return {k: np.ascontiguousarray(v, dtype=np.float32) for k, v in sh.items()}


def core_inputs(inp, b, sh):
    m = dict(sh)
    m['x'] = np.ascontiguousarray(inp['x'][b]); m['ctx'] = np.ascontiguousarray(inp['ctx'][b])
    sv = np.stack([inp['c'][b].reshape(KC, 128).T, inp['c_ctx'].reshape(KC, 128).T], 2)
    m['svec'] = np.ascontiguousarray(sv, dtype=np.float32)
    return m


def kernel(**inputs):
    inp = {k: np.asarray(v) for k, v in inputs.items()}
    sh = prep_shared(inp)
    nc = build()
    in_maps = [core_inputs(inp, b, sh) for b in range(8)]
    res = run_bass_kernel_spmd(nc, in_maps, core_ids=list(range(8)))
    return np.stack([np.asarray(r['out'], dtype=np.float32) for r in res.results], 0)
```

```python
import contextlib, math, os
import numpy as np
import ml_dtypes
import concourse.bass as bass
import concourse.mybir as mybir
from concourse.bass_utils import run_bass_kernel_spmd

F32 = mybir.dt.float32; BF16 = mybir.dt.bfloat16; I32 = mybir.dt.int32
AF = mybir.ActivationFunctionType; ALU = mybir.AluOpType

D = 1024; T = 4096; NCX = 256; TT = T + NCX; FF = 2816; KC = 8; FC = 22; DEPTH = 2
NEG = -30000.0
SAME_SYNC = True
PI = math.pi


class Sem:
    def __init__(s, h, name): s.h = h; s.total = 0; s.name = name


class Eng:
    def __init__(s, name, obj, sem): s.name = name; s.obj = obj; s.sem = sem; s.known = {}


class Buf:
    def __init__(s, name, t=None, dsem=None, qsem=None):
        s.name = name; s.t = t; s.w = []; s.r = []; s.pre = []; s.dsem = dsem; s.qsem = qsem

    def __getitem__(s, k): return s.t[k]


class Trk:
    def __init__(self, nc):
        self.nc = nc
        self.es = contextlib.ExitStack()
        self.sems = []
        self.E = {}
        for n, o in (('pe', nc.tensor), ('act', nc.scalar), ('dve', nc.vector), ('pool', nc.gpsimd), ('sp', nc.sync)):
            self.E[n] = Eng(n, o, self.new_sem('e_' + n) if n != 'sp' else None)
        self.dpool = []; self.dnext = 0; self.uid = 0; self.qpool = []; self.qnext = 0

    def new_sem(self, name):
        s = Sem(self.es.enter_context(self.nc.semaphore(name)), name); self.sems.append(s); return s

    def dsem(self):
        if self.dnext >= len(self.dpool): self.dpool.append(self.new_sem('d%d' % len(self.dpool)))
        s = self.dpool[self.dnext]; self.dnext += 1; return s

    def qsem(self):
        if self.qnext >= len(self.qpool): self.qpool.append(self.new_sem('q%d' % len(self.qpool)))
        s = self.qpool[self.qnext]; self.qnext += 1; return s

    def _wait(self, eng, evs):
        need = {}
        for (sem, val, src) in evs:
            if src == eng.name and (src == 'pe' or not SAME_SYNC): continue
            if eng.known.get(sem, 0) >= val: continue
            need[sem] = max(need.get(sem, 0), val)
        for sem, val in need.items():
            eng.obj.wait_ge(sem.h, val); eng.known[sem] = val

    def _pre(self, eng, reads, writes, acc):
        evs = []
        for b in reads: evs += b.w
        for b in writes:
            b.pre = b.w + b.r; evs += b.pre
        for b in acc: evs += b.pre
        self._wait(eng, evs)

    def _post(self, ev, reads, writes, acc):
        for b in reads: b.r.append(ev)
        for b in writes: b.w = [ev]; b.r = []
        for b in acc: b.w.append(ev)

    def group(self, en, fns, reads=(), writes=(), acc=()):
        eng = self.E[en]
        self._pre(eng, reads, writes, acc)
        ins = None
        for f in fns: ins = f()
        eng.sem.total += 1
        ins.then_inc(eng.sem.h, 1)
        self._post((eng.sem, eng.sem.total, en), reads, writes, acc)

    def op(self, en, fn, reads=(), writes=(), acc=()):
        self.group(en, [fn], reads, writes, acc)

    def dma(self, q, out, in_, reads=(), writes=(), acc=(), sem=None):
        eng = self.E[q]
        self._pre(eng, reads, writes, acc)
        if sem is None:
            for b in list(writes) + list(acc) + list(reads):
                if b.dsem is not None: sem = (b.qsem if q == 'pool' else b.dsem); break
        ins = eng.obj.dma_start(out=out, in_=in_)
        sem.total += 16
        ins.then_inc(sem.h, 16)
        self._post((sem, sem.total, 'dma'), reads, writes, acc)

    def barrier(self):
        for eng in self.E.values():
            for s in self.sems:
                if s.total > 0 and eng.known.get(s, 0) < s.total:
                    eng.obj.wait_ge(s.h, s.total); eng.known[s] = s.total


def tile_w(W, kp=128):
    K, M = W.shape
    return np.ascontiguousarray(W.reshape(K // kp, kp, M // 128, 128).transpose(2, 1, 0, 3))


IN_OFF = dict(u=0, gq=384, gk=896, gv=1024, nq=1152, nk=1664, nv=2176, gates=2688)
ROT_PERM = np.concatenate([np.arange(16, 32), np.arange(0, 16), np.arange(48, 64), np.arange(32, 48)])
FM_CHUNKS = ([('u', i) for i in range(3)] + [('gq', i) for i in range(4)] + [('gq2', i) for i in range(4)]
             + [('gk', 0), ('gk2', 0)] + [('nq', i) for i in range(4)] + [('nk', i) for i in range(4)]
             + [('gates', i) for i in range(24)])


def win_fm_cols():
    cols = []
    for kind, i in FM_CHUNKS:
        if kind == 'u': c = IN_OFF['u'] + i * 128 + np.arange(128)
        elif kind == 'gq': c = IN_OFF['gq'] + i * 128 + np.arange(128)
        elif kind == 'gq2': c = IN_OFF['gq'] + i * 128 + np.concatenate([ROT_PERM, 64 + ROT_PERM])
        elif kind == 'gk': c = IN_OFF['gk'] + np.arange(128)
        elif kind == 'gk2': c = IN_OFF['gk'] + np.concatenate([ROT_PERM, 64 + ROT_PERM])
        elif kind == 'nq': c = IN_OFF['nq'] + i * 128 + np.arange(128)
        elif kind == 'nk': c = IN_OFF['nk'] + i * 128 + np.arange(128)
        else: c = IN_OFF['gates'] + i * 128 + np.arange(128)
        cols.append(c)
    return np.concatenate(cols)


def rope_tables():
    t = np.arange(T)
    pos = np.stack([t // 64, t % 64], 0).astype(np.float32)
    inv = (10000.0 ** (-np.arange(0, 32, 2, dtype=np.float32) / 32)).astype(np.float32)
    C = np.zeros((64, T), np.float32); S = np.zeros((64, T), np.float32)
    for ax in range(2):
        ang = (pos[ax][None, :] * inv[:, None]).astype(np.float32)
        for half in range(2):
            sl = slice(ax * 32 + half * 16, ax * 32 + half * 16 + 16)
            C[sl] = np.cos(ang)
            S[sl] = -np.sin(ang) if half == 0 else np.sin(ang)
    C2 = np.concatenate([C, C], 0); S2 = np.concatenate([S, S], 0)
    return np.stack([C2 * 0.125, S2 * 0.125, C2, S2], 0).astype(np.float32)


def na_bias_tables(rpb):
    L = rpb.shape[0]
    kc = np.arange(64)[:, None]; qc = np.arange(64)[None, :]
    cs = np.clip(qc - 8, 0, 48)
    valid = (kc >= cs) & (kc < cs + 16)
    idx = np.clip(kc - qc + 15, 0, 30)
    tab = np.where(valid[None, None, None], rpb[:, :, :, idx], np.float32(NEG)).astype(np.float32)
    negt = np.full((L, 8, 64, 64), NEG, np.float32)
    VT = np.zeros((L, 8, 128, 14, 64), np.float32)
    for d in range(-7, 7):
        VT[:, :, 0:64, d + 7] = tab[:, :, d + 7]
        VT[:, :, 64:128, d + 7] = tab[:, :, d + 8]
    OD = np.zeros((L, 8, 128, 5, 64), np.float32)
    for k, d in enumerate((-5, -3, -1, 1, 3)):
        OD[:, :, 0:64, k] = negt if d == -5 else tab[:, :, d + 7]
        OD[:, :, 64:128, k] = negt if d == 3 else tab[:, :, d + 8]
    CB = np.zeros((L, 8, 128, 5, 2, 64), np.float32)
    for k in range(5):
        CB[:, :, :, k, 0, :] = VT[:, :, :, 3 + 2 * k, :] if k < 4 else np.float32(NEG)
        CB[:, :, :, k, 1, :] = OD[:, :, :, k, :]
    return VT, OD, CB


def host_consts():
    c = {}
    c['ident'] = np.eye(128, dtype=np.float32)
    c['rope'] = rope_tables()
    k = np.arange(128)[:, None]; q = np.arange(128)[None, :]
    c['gmask'] = np.stack([np.where(k >= q, 0.0, NEG), np.where(k <= q, 0.0, NEG)], 1).astype(np.float32)
    p = np.arange(128)
    m2 = np.zeros((128, 4, 8, 16), np.float32)
    mc = np.zeros((128, 4, 2, 64), np.float32)
    for qq in range(4):
        for pp in range(128):
            m2[pp, qq, 2 * qq + (pp >= 64), :] = 1.0
            for half in range(2):
                if pp // 16 == 2 * qq + half: mc[pp, qq, half, :] = 1.0
    c['mask2'] = m2; c['maskc'] = mc
    io = np.zeros((128, 2, 64), np.float32)
    io[:, 0, :] = np.arange(1, 65)[None, :]; io[:, 1, :] = np.arange(64, 0, -1)[None, :]
    c['iota64'] = io
    c['iota9'] = np.broadcast_to(np.arange(9, dtype=np.float32)[None, :], (128, 9)).copy()
    return c


def build(dbg=(), stop_after=None, only=None, ext_in=()):
    nc = bass.Bass("TRN2", target_bir_lowering=False)
    tr = Trk(nc)
    dbg = set(dbg)

    def din(name, shape, dt=F32):
        return nc.dram_tensor(name, list(shape), dt, kind="ExternalInput").ap()

    def dscr(name, shape, dt):
        if name in ext_in: return nc.dram_tensor(name, list(shape), dt, kind="ExternalInput").ap()
        if name in dbg: return nc.dram_tensor(name, list(shape), dt, kind="ExternalOutput").ap()
        return nc.dram_tensor(name, list(shape), dt).ap()

    x_in = din('x', [T, D]); ctx_in = din('ctx', [NCX, D]); svec_in = din('svec', [128, KC, 2])
    out_d = nc.dram_tensor('out', [T, D], F32, kind="ExternalOutput").ap()
    WSH = dict(wgu1=[FC, 128, 2, KC, 128], wd1=[KC, 128, FC, 128], wgu2=[FC, 128, 2, KC, 128], wd2=[KC, 128, FC, 128],
               winfm=[45, 128, KC, 128], wintm=[128, KC, 640], wada=[72, 128, KC, 128], wglu=[3, 128, 3, 128],
               wps=[8, 128, 3, 128], wpg=[8, 128, 4, 128], wpn=[8, 128, 4, 128], wout=[8, 128, KC, 128])
    WORDER = ['wada', 'wgu1', 'wd1', 'winfm', 'wintm', 'wglu', 'wps', 'wpg', 'wpn', 'wout', 'wgu2', 'wd2']
    w_f = {k: din(k, [DEPTH] + v) for k, v in WSH.items()}
    w_b = {(k, l): dscr('%s_b%d' % (k, l), v, BF16) for k, v in WSH.items() for l in range(DEPTH)}
    w_buf = {(k, l): Buf('wb_%s%d' % (k, l)) for k in WSH for l in range(DEPTH)}
    bada_in = din('bada', [DEPTH, 128, 72]); normg_in = din('normg', [DEPTH, 128, 3, KC]); fing_in = din('fing', [128, KC])
    sa_in = din('ssm_a', [DEPTH, 2, 128, 3, 12])
    sb_in = din('ssm_b', [DEPTH, 2, 2, 128, 12, 16]); sc_in = din('ssm_c', [DEPTH, 2, 2, 128, 3, 64])
    sd_in = din('ssm_d', [DEPTH, 128, 3]); sink_in = din('sink', [DEPTH, 64, 8])
    navt_in = din('navt', [DEPTH, 8, 128, 14, 64]); naod_in = din('naod', [DEPTH, 8, 128, 5, 64]); nacb_in = din('nacb', [DEPTH, 8, 128, 5, 128])
    ident_in = din('ident', [128, 128]); rope_in = din('rope', [4, 128, T]); gmask_in = din('gmask', [128, 2, 128])
    mask2_in = din('mask2', [128, 4, 8, 16]); maskc_in = din('maskc', [128, 4, 2, 64]); iota64_in = din('iota64', [128, 2, 64]); iota9_in = din('iota9', [128, 9])

    XT = dscr('XT', [D, TT], F32)
    U = dscr('U', [384, TT], F32); TG = dscr('TG', [384, TT], F32)
    QG = dscr('QG', [8, 64, T], BF16); QGC = dscr('QGC', [8, 64, NCX], BF16)
    KG = dscr('KG', [2, 64, T], BF16); KGC = dscr('KGC', [2, 64, NCX], BF16)
    QN = dscr('QN', [8, 64, T], BF16); QNC = dscr('QNC', [8, 64, NCX], BF16)
    KN = dscr('KN', [8, 64, T], BF16); KNC = dscr('KNC', [8, 64, NCX], BF16)
    VG = dscr('VG', [TT, 128], BF16); VN = dscr('VN', [TT, 512], BF16)
    SG = dscr('SG', [3072, TT], BF16)
    YG = dscr('YG', [8, 64, T], BF16); YGC = dscr('YGC', [8, 64, NCX], BF16)
    YN = dscr('YN', [8, 64, T], BF16); YNC = dscr('YNC', [8, 64, NCX], BF16)

    def uname(n):
        tr.uid += 1; return '%s_%d' % (n, tr.uid)

    class Phase:
        def __init__(s, reset=True):
            s.es = contextlib.ExitStack()
            if reset: tr.dnext = 0; tr.qnext = 0

        def sb(s, name, shape, dt=F32, dma=False):
            t = s.es.enter_context(nc.sbuf_tensor(uname(name), list(shape), dt))
            return Buf(name, t, tr.dsem() if dma else None, tr.qsem() if dma else None)

        def ps(s, name, shape=(128, 512), dt=F32):
            t = s.es.enter_context(nc.psum_tensor(uname(name), list(shape), dt))
            return Buf(name, t)

        def close(s):
            tr.barrier(); s.es.close()

    V = nc.vector; A = nc.scalar; G = nc.gpsimd; PE = nc.tensor

    G1 = ['wada', 'wgu1', 'wd1', 'winfm', 'wintm']
    G2 = ['wglu', 'wps', 'wpg', 'wpn', 'wout', 'wgu2', 'wd2']

    def emit_casts(l, keys):
        if only is not None: return
        for k in keys:
            sem = tr.new_sem('c_%s%d' % (k, l))
            n = int(np.prod(WSH[k]))
            src = w_f[k][l]; dst = w_b[(k, l)]
            names = 'abcde'[:len(WSH[k])]
            pat = ' '.join(names)
            fs = src.rearrange('%s -> (%s)' % (pat, pat)).rearrange('(p n) -> p n', p=128)
            fd = dst.rearrange('%s -> (%s)' % (pat, pat)).rearrange('(p n) -> p n', p=128)
            cols = n // 128
            npieces = max(1, -(-cols // 16384))
            step = -(-cols // npieces)
            first = True
            for c0 in range(0, cols, step):
                c1 = min(cols, c0 + step)
                tr.dma('pool', fd[:, c0:c1], fs[:, c0:c1], writes=[w_buf[(k, l)]] if first else (),
                       acc=() if first else [w_buf[(k, l)]], sem=sem)
                first = False

    emit_casts(0, G1)

    gp = Phase()
    ident = gp.sb('ident', [128, 128], F32, dma=True)
    tr.dma('sp', ident[:], ident_in[:, :], writes=[ident])
    ones_b = gp.sb('ones_b', [128, 128], BF16)
    tr.op('dve', lambda: V.memset(ones_b[:], 1.0), writes=[ones_b])
    svec = gp.sb('svec', [128, KC, 2], F32, dma=True)
    tr.dma('sp', svec[:], svec_in[:, :, :], writes=[svec])
    svb = gp.sb('svb', [128, KC, 2], BF16)
    tr.op('act', lambda: A.activation(out=svb[:], in_=svec[:], func=AF.Silu), reads=[svec], writes=[svb])
    fing = gp.sb('fing', [128, KC], F32, dma=True)
    tr.dma('sp', fing[:], fing_in[:, :], writes=[fing])
    modA = gp.sb('modA', [128, 3, KC, 2]); modB = gp.sb('modB', [128, 3, KC, 2]); modG = gp.sb('modG', [128, 3, KC, 2])

    def compute_mod(l):
        ph = Phase()
        bada = ph.sb('bada', [128, 72], F32, dma=True); tr.dma('sp', bada[:], bada_in[l], writes=[bada])
        normg = ph.sb('normg', [128, 3, KC], F32, dma=True); tr.dma('sp', normg[:], normg_in[l], writes=[normg])
        wr = [ph.sb('wada%d' % i, [128, 8, KC, 128], BF16, dma=True) for i in range(2)]
        mps = ph.ps('mps', [128, 72, 2])
        mod = ph.sb('mod', [128, 72, 2])
        for jg in range(9):
            wbuf = wr[jg % 2]
            tr.dma('sp', wbuf[:], w_b[('wada', l)][jg * 8:(jg + 1) * 8].rearrange('j p k c -> p j k c'),
                   reads=[w_buf[('wada', l)]], writes=[wbuf])
            fns = []
            for jj in range(8):
                j = jg * 8 + jj
                for kc in range(KC):
                    fns.append(lambda j=j, jj=jj, kc=kc: PE.matmul(mps[:, j, :], lhsT=wbuf[:, jj, kc, :], rhs=svb[:, kc, :],
                                                                  start=(kc == 0), stop=(kc == KC - 1)))
            tr.group('pe', fns, reads=[wbuf, svb], writes=[mps] if jg == 0 else (), acc=() if jg == 0 else [mps])
        tr.op('dve', lambda: V.tensor_tensor(out=mod[:], in0=mps[:], in1=bada[:].unsqueeze(2).to_broadcast([128, 72, 2]), op=ALU.add),
              reads=[mps, bada], writes=[mod])
        for i in range(3):
            sh = mod[:, 8 * (3 * i):8 * (3 * i) + 8, :]; sc = mod[:, 8 * (3 * i + 1):8 * (3 * i + 1) + 8, :]
            gt = mod[:, 8 * (3 * i + 2):8 * (3 * i + 2) + 8, :]
            gb = normg[:, i, :].unsqueeze(2).to_broadcast([128, KC, 2])
            fns = [lambda sc=sc, i=i, gb=gb: V.scalar_tensor_tensor(out=modA[:, i], in0=sc, scalar=1.0, in1=gb, op0=ALU.add, op1=ALU.mult),
                   lambda sh=sh, i=i: V.tensor_copy(out=modB[:, i], in_=sh),
                   lambda gt=gt, i=i: V.tensor_scalar(out=modG[:, i], in0=gt, scalar1=(1.0 if i == 1 else 0.5), scalar2=None, op0=ALU.mult)]
            tr.group('dve', fns, reads=[mod, normg], writes=[modA, modB, modG] if i == 0 else (), acc=() if i == 0 else [modA, modB, modG])
        ph.close()

    def rms_mod(ph, xT, hT, N, site, col, ps_ss, tmpbig, sq, rstd):
        tr.op('act', lambda: A.activation(out=sq[:, :, :N], in_=xT[:, :, :N], func=AF.Square), reads=[xT], writes=[sq])
        tr.group('pe', [lambda kc=kc: PE.matmul(ps_ss[:, :N], lhsT=ones_b[:], rhs=sq[:, kc, :N], start=(kc == 0), stop=(kc == KC - 1))
                        for kc in range(KC)], reads=[sq, ones_b], writes=[ps_ss])
        tr.op('act', lambda: A.activation(out=rstd[:, :N], in_=ps_ss[:, :N], func=AF.Ln, scale=1.0 / D, bias=epsb[:, 0:1]),
              reads=[ps_ss, epsb], writes=[rstd])
        tr.op('act', lambda: A.activation(out=rstd[:, :N], in_=rstd[:, :N], func=AF.Exp, scale=-0.5), reads=[rstd], writes=[rstd])
        tr.op('dve', lambda: V.tensor_tensor(out=tmpbig[:, :, :N], in0=xT[:, :, :N],
                                             in1=rstd[:, :N].unsqueeze(1).to_broadcast([128, KC, N]), op=ALU.mult),
              reads=[xT, rstd], writes=[tmpbig])
        tr.group('act', [lambda kc=kc: A.activation(out=hT[:, kc, :N], in_=tmpbig[:, kc, :N], func=AF.Identity,
                                                    scale=modA[:, site, kc, col:col + 1], bias=modB[:, site, kc, col:col + 1])
                         for kc in range(KC)], reads=[tmpbig, modA, modB], writes=[hT])

    def ffn(ph, l, which, xT, hT, aT, N, col, site, R):
        wgu_k, wd_k = ('wgu1', 'wd1') if which == 1 else ('wgu2', 'wd2')
        rms_mod(ph, xT, hT, N, site, col, R['ps_m'][0], R['tmpbig'], R['sq'], R['rstd'])
        for f in range(FC):
            wb = R['wgu'][R['i_wgu'] % 3]; R['i_wgu'] += 1
            tr.dma('sp', wb[:], w_b[(wgu_k, l)][f], reads=[w_buf[(wgu_k, l)]], writes=[wb])
            pg = R['ps_g'][f % 2]; pu = R['ps_u'][f % 2]
            tr.group('pe', [lambda kc=kc: PE.matmul(pg[:, :N], lhsT=wb[:, 0, kc, :], rhs=hT[:, kc, :N], start=(kc == 0), stop=(kc == KC - 1))
                            for kc in range(KC)], reads=[wb, hT], writes=[pg])
            tr.group('pe', [lambda kc=kc: PE.matmul(pu[:, :N], lhsT=wb[:, 1, kc, :], rhs=hT[:, kc, :N], start=(kc == 0), stop=(kc == KC - 1))
                            for kc in range(KC)], reads=[wb, hT], writes=[pu])
            sg = R['sgt'][f % 2]
            tr.op('act', lambda: A.activation(out=sg[:, :N], in_=pg[:, :N], func=AF.Silu), reads=[pg], writes=[sg])
            tr.op('dve', lambda: V.tensor_tensor(out=aT[:, f, :N], in0=sg[:, :N], in1=pu[:, :N], op=ALU.mult),
                  reads=[sg, pu], writes=[aT] if f == 0 else (), acc=() if f == 0 else [aT])
        for m in range(KC):
            wb = R['wd'][R['i_wd'] % 2]; R['i_wd'] += 1
            tr.dma('sp', wb[:], w_b[(wd_k, l)][m], reads=[w_buf[(wd_k, l)]], writes=[wb])
            pd = R['ps_m'][m % 2]
            tr.group('pe', [lambda f=f: PE.matmul(pd[:, :N], lhsT=wb[:, f, :], rhs=aT[:, f, :N], start=(f == 0), stop=(f == FC - 1))
                            for f in range(FC)], reads=[wb, aT], writes=[pd])
            tr.op('dve', lambda: V.scalar_tensor_tensor(out=xT[:, m, :N], in0=pd[:, :N], scalar=modG[:, site, m, col:col + 1],
                                                        in1=xT[:, m, :N], op0=ALU.mult, op1=ALU.add),
                  reads=[pd, modG, xT], acc=[xT])

    def ffn_bufs(ph):
        R = {}
        R['wgu'] = [ph.sb('wgu%d' % i, [128, 2, KC, 128], BF16, dma=True) for i in range(3)]
        R['wd'] = [ph.sb('wd%d' % i, [128, FC, 128], BF16, dma=True) for i in range(2)]
        R['i_wgu'] = 0; R['i_wd'] = 0
        R['ps_g'] = [ph.ps('psg%d' % i) for i in range(2)]
        R['ps_u'] = [ph.ps('psu%d' % i) for i in range(2)]
        R['ps_m'] = [ph.ps('psm%d' % i) for i in range(2)]
        R['sgt'] = [ph.sb('sgt%d' % i, [128, 512]) for i in range(2)]
        R['tmpbig'] = ph.sb('tmpbig', [128, KC, 512]); R['sq'] = ph.sb('sq', [128, KC, 512], BF16)
        R['rstd'] = ph.sb('rstd', [128, 512])
        return R

    tiles = [(0, NCX, 1)] + [(NCX + i * 512, 512, 0) for i in range(8)]

    def phase_A(l):
        ph = Phase()
        R = ffn_bufs(ph)
        xTs = [ph.sb('xT%d' % i, [128, KC, 512], F32, dma=True) for i in range(2)]
        hT = ph.sb('hT', [128, KC, 512], BF16); aT = ph.sb('aT', [128, FC, 512], BF16)
        win = [ph.sb('win%d' % i, [128, KC, 128], BF16, dma=True) for i in range(3)]
        wtm = ph.sb('wtm', [128, KC, 640], BF16, dma=True)
        tr.dma('sp', wtm[:], w_b[('wintm', l)], reads=[w_buf[('wintm', l)]], writes=[wtm])
        rp = [ph.sb('rope%d' % i, [128, 4, 512], F32, dma=True) for i in range(2)]
        stf = [ph.sb('stf%d' % i, [128, 512], F32, dma=True) for i in range(2)]
        stb = [ph.sb('stb%d' % i, [128, 640], BF16, dma=True) for i in range(3)]
        t1 = ph.sb('t1', [128, 512]); t2 = ph.sb('t2', [128, 512])
        xin = [ph.sb('xin%d' % i, [128, D], F32, dma=True) for i in range(2)] if l == 0 else None
        ps_q = ph.ps('psq'); ps_q2 = ph.ps('psq2')
        cnt = dict(win=0, stf=0, stb=0, xin=0)

        def load_x(ti):
            c0, N, col = tiles[ti]; xT = xTs[ti % 2]
            if l > 0:
                tr.dma('sp', xT[:, :, :N], XT[:, c0:c0 + N].rearrange('(k p) t -> p k t', p=128), writes=[xT])
            else:
                for ts in range(N // 128):
                    xb = xin[cnt['xin'] % 2]; cnt['xin'] += 1
                    src = ctx_in[ts * 128:(ts + 1) * 128, :] if ti == 0 else x_in[c0 - NCX + ts * 128:c0 - NCX + (ts + 1) * 128, :]
                    tr.dma('sp', xb[:], src, writes=[xb])
                    for hf in range(2):
                        pt = R['ps_g'][hf]
                        tr.group('pe', [lambda k=k, hf=hf: PE.transpose(pt[:, k * 128:(k + 1) * 128], xb[:, (hf * 4 + k) * 128:(hf * 4 + k + 1) * 128], ident[:])
                                        for k in range(4)], reads=[xb, ident], writes=[pt])
                        tr.op('act', lambda hf=hf, pt=pt, ts=ts: A.activation(out=xT[:, hf * 4:hf * 4 + 4, ts * 128:(ts + 1) * 128],
                                                                               in_=pt[:].rearrange('p (k t) -> p k t', k=4), func=AF.Copy),
                              reads=[pt], writes=[xT] if (ts == 0 and hf == 0) else (), acc=() if (ts == 0 and hf == 0) else [xT])

        load_x(0)
        for ti in range(9):
            c0, N, col = tiles[ti]; xT = xTs[ti % 2]; lat = ti > 0
            if lat:
                rpb = rp[ti % 2]
                tr.dma('sp', rpb[:], rope_in[:, :, c0 - NCX:c0 - NCX + 512].rearrange('a p t -> p a t'), writes=[rpb])
            ffn(ph, l, 1, xT, hT, aT, N, col, 0, R)
            if ti + 1 < 9: load_x(ti + 1)
            rms_mod(ph, xT, hT, N, 1, col, R['ps_m'][0], R['tmpbig'], R['sq'], R['rstd'])
            tr.dma('pool', XT[:, c0:c0 + N].rearrange('(k p) t -> p k t', p=128), xT[:, :, :N], reads=[xT])
            ci = 0
            while ci < len(FM_CHUNKS):
                kind, i = FM_CHUNKS[ci]

                def wmm(ci, pbuf):
                    wb = win[cnt['win'] % 3]; cnt['win'] += 1
                    tr.dma('sp', wb[:], w_b[('winfm', l)][ci], reads=[w_buf[('winfm', l)]], writes=[wb])
                    tr.group('pe', [lambda kc=kc: PE.matmul(pbuf[:, :N], lhsT=wb[:, kc, :], rhs=hT[:, kc, :N], start=(kc == 0), stop=(kc == KC - 1))
                                    for kc in range(KC)], reads=[wb, hT], writes=[pbuf])

                if kind in ('gq', 'gk') and lat:
                    ci2 = ci + (4 if kind == 'gq' else 1)
                    wmm(ci, ps_q); wmm(ci2, ps_q2)
                    o = 0 if kind == 'gq' else 2
                    st = stb[cnt['stb'] % 3]; cnt['stb'] += 1
                    tr.op('dve', lambda: V.tensor_tensor(out=t1[:], in0=ps_q[:], in1=rpb[:, o, :], op=ALU.mult), reads=[ps_q, rpb], writes=[t1])
                    tr.op('dve', lambda: V.tensor_tensor(out=t2[:], in0=ps_q2[:], in1=rpb[:, o + 1, :], op=ALU.mult), reads=[ps_q2, rpb], writes=[t2])
                    tr.op('pool', lambda: G.tensor_tensor(out=st[:, :512], in0=t1[:], in1=t2[:], op=ALU.add), reads=[t1, t2], writes=[st])
                    dst = QG[2 * i:2 * i + 2] if kind == 'gq' else KG[0:2]
                    tr.dma('pool', dst.rearrange('h d t -> (h d) t')[:, c0 - NCX:c0 - NCX + 512], st[:, :512], reads=[st])
                elif kind in ('gq2', 'gk2'):
                    pass
                else:
                    pb = R['ps_m'][ci % 2]
                    wmm(ci, pb)
                    if kind == 'u':
                        st = stf[cnt['stf'] % 2]; cnt['stf'] += 1
                        tr.op('act', lambda: A.activation(out=st[:, :N], in_=pb[:, :N], func=AF.Copy), reads=[pb], writes=[st])
                        tr.dma('pool', U[i * 128:(i + 1) * 128, c0:c0 + N], st[:, :N], reads=[st])
                    else:
                        st = stb[cnt['stb'] % 3]; cnt['stb'] += 1
                        if kind == 'gates':
                            tr.op('act', lambda: A.activation(out=st[:, :N], in_=pb[:, :N], func=AF.Sigmoid), reads=[pb], writes=[st])
                            tr.dma('pool', SG[i * 128:(i + 1) * 128, c0:c0 + N], st[:, :N], reads=[st])
                        else:
                            sc = 0.125 if kind in ('gq', 'nq') else 1.0
                            tr.op('act', lambda: A.activation(out=st[:, :N], in_=pb[:, :N], func=AF.Copy, scale=sc), reads=[pb], writes=[st])
                            if lat:
                                dst = {'nq': QN, 'nk': KN}[kind][2 * i:2 * i + 2].rearrange('h d t -> (h d) t')[:, c0 - NCX:c0 - NCX + 512]
                            else:
                                dd = {'gq': QGC, 'gk': KGC, 'nq': QNC, 'nk': KNC}[kind]
                                dst = (dd[2 * i:2 * i + 2] if kind != 'gk' else dd[0:2]).rearrange('h d t -> (h d) t')
                            tr.dma('pool', dst, st[:, :N], reads=[st])
                ci += 1
            for ts in range(N // 128):
                pv = R['ps_g'][ts % 2]; pv2 = R['ps_u'][ts % 2]
                tr.group('pe', [lambda kc=kc: PE.matmul(pv[:, :128], lhsT=hT[:, kc, ts * 128:(ts + 1) * 128], rhs=wtm[:, kc, 0:128],
                                                        start=(kc == 0), stop=(kc == KC - 1)) for kc in range(KC)], reads=[hT, wtm], writes=[pv])
                tr.group('pe', [lambda kc=kc: PE.matmul(pv2[:, :512], lhsT=hT[:, kc, ts * 128:(ts + 1) * 128], rhs=wtm[:, kc, 128:640],
                                                        start=(kc == 0), stop=(kc == KC - 1)) for kc in range(KC)], reads=[hT, wtm], writes=[pv2])
                st = stb[cnt['stb'] % 3]; cnt['stb'] += 1
                tr.op('act', lambda: A.activation(out=st[:, 0:128], in_=pv[:, 0:128], func=AF.Copy), reads=[pv], writes=[st])
                tr.op('dve', lambda: V.tensor_copy(out=st[:, 128:640], in_=pv2[:, :512]), reads=[pv2], acc=[st])
                r0 = c0 + ts * 128
                tr.dma('pool', VG[r0:r0 + 128, :], st[:, 0:128], reads=[st])
                tr.dma('pool', VN[r0:r0 + 128, :], st[:, 128:640], reads=[st])
        ph.close()

    def sin_rr(src, srcb, dst, dstb, shift, tA, tAb, tI, tIb):
        if shift:
            tr.op('dve', lambda: V.tensor_scalar(out=tA, in0=src, scalar1=shift, scalar2=None, op0=ALU.add), reads=[srcb], writes=[tAb])
            x, xb = tA, tAb
        else:
            x, xb = src, srcb
        tr.op('dve', lambda: V.tensor_scalar(out=tI, in0=x, scalar1=1.0 / (2 * PI), scalar2=None, op0=ALU.mult), reads=[xb], writes=[tIb])
        tr.op('dve', lambda: V.tensor_copy(out=dst, in_=tI), reads=[tIb], writes=[dstb])
        tr.op('dve', lambda: V.scalar_tensor_tensor(out=dst, in0=dst, scalar=-2 * PI, in1=x, op0=ALU.mult, op1=ALU.add), reads=[dstb, xb], writes=[dstb])
        tr.op('dve', lambda: V.tensor_scalar(out=dst, in0=dst, scalar1=-PI, scalar2=PI, op0=ALU.max, op1=ALU.min), reads=[dstb], writes=[dstb])
        tr.op('act', lambda: A.activation(out=dst, in_=dst, func=AF.Sin), reads=[dstb], writes=[dstb])

    def phase_B(l):
        ph = Phase()
        io64 = ph.sb('io64', [128, 2, 64], F32, dma=True); tr.dma('sp', io64[:], iota64_in[:, :, :], writes=[io64])
        io9 = ph.sb('io9', [128, 9], F32, dma=True); tr.dma('sp', io9[:], iota9_in[:, :], writes=[io9])
        m2 = ph.sb('mask2', [128, 4, 8, 16], F32, dma=True); tr.dma('sp', m2[:], mask2_in[:, :, :, :], writes=[m2])
        mcm = ph.sb('maskc', [128, 4, 2, 64], F32, dma=True); tr.dma('sp', mcm[:], maskc_in[:, :, :, :], writes=[mcm])
        sdt = ph.sb('sd', [128, 3], F32, dma=True); tr.dma('sp', sdt[:], sd_in[l], writes=[sdt])
        identb = ph.sb('identb', [128, 128], BF16)
        tr.op('act', lambda: A.activation(out=identb[:], in_=ident[:], func=AF.Copy), reads=[ident], writes=[identb])
        pst = ph.ps('pst', [128, 128])
        prm = {}
        BP = int(os.environ.get('BPREP', '99'))
        if BP == 1: ph.close(); return
        pers = {}
        for d in range(2):
            pers[d] = dict(w=ph.sb('w%d' % d, [128, 16, 12]), bb=ph.sb('bb%d' % d, [128, 2, 12, 16]), RK=ph.sb('rk%d' % d, [128, 12, 9]),
                           LRk=ph.sb('lrk%d' % d, [128, 12, 9]), LIk=ph.sb('lik%d' % d, [128, 12, 9]),
                           C2=ph.sb('c2_%d' % d, [128, 12, 2, 64]), S2=ph.sb('s2_%d' % d, [128, 12, 2, 64]),
                           scr=ph.sb('scr%d' % d, [128, 3, 64], F32, dma=True), sci=ph.sb('sci%d' % d, [128, 3, 64], F32, dma=True))
        pp = Phase(reset=False)
        A64 = pp.sb('a64', [128, 12, 64]); TA64 = pp.sb('ta64', [128, 12, 64]); TI64 = pp.sb('ti64', [128, 12, 64], I32)
        C64 = pp.sb('c64', [128, 12, 64]); S64 = pp.sb('s64', [128, 12, 64])
        for d in range(2):
            PD = pers[d]
            sa = pp.sb('sa%d' % d, [128, 3, 12], F32, dma=True); tr.dma('sp', sa[:], sa_in[l, d], writes=[sa])
            sbr = pp.sb('sbr%d' % d, [128, 12, 16], F32, dma=True); tr.dma('sp', sbr[:], sb_in[l, d, 0], writes=[sbr])
            sbi = pp.sb('sbi%d' % d, [128, 12, 16], F32, dma=True); tr.dma('sp', sbi[:], sb_in[l, d, 1], writes=[sbi])
            scr_ = PD['scr']; tr.dma('sp', scr_[:], sc_in[l, d, 0], writes=[scr_])
            sci = PD['sci']; tr.dma('sp', sci[:], sc_in[l, d, 1], writes=[sci])
            w = PD['w']
            ki = pp.sb('ki%d' % d, [128, 12], I32)
            are = sa[:, 0, :]; aim = sa[:, 1, :]; ldt = sa[:, 2, :]
            DT, ARDT, RR, TH, KF, THR, CX, CM, CO, SI, LR, LI = [w[:, i, :] for i in range(12)]
            N1, N2, DEN, T3 = [w[:, 12 + i, :] for i in range(4)]
            tr.op('act', lambda: A.activation(out=DT, in_=ldt, func=AF.Exp), reads=[sa], writes=[w])
            tr.op('dve', lambda: V.tensor_tensor(out=ARDT, in0=are, in1=DT, op=ALU.mult), reads=[w, sa], acc=[w])
            tr.op('act', lambda: A.activation(out=RR, in_=ARDT, func=AF.Exp), reads=[w], acc=[w])
            tr.op('dve', lambda: V.tensor_tensor(out=TH, in0=aim, in1=DT, op=ALU.mult), reads=[w, sa], acc=[w])
            tr.op('dve', lambda: V.tensor_scalar(out=ki[:], in0=TH, scalar1=1.0 / (2 * PI), scalar2=None, op0=ALU.mult), reads=[w], writes=[ki])
            tr.op('dve', lambda: V.tensor_copy(out=KF, in_=ki[:]), reads=[ki], acc=[w])
            tr.op('dve', lambda: V.scalar_tensor_tensor(out=THR, in0=KF, scalar=-2 * PI, in1=TH, op0=ALU.mult, op1=ALU.add), reads=[w], acc=[w])
            tr.op('dve', lambda: V.tensor_scalar(out=THR, in0=THR, scalar1=-PI, scalar2=PI, op0=ALU.max, op1=ALU.min), reads=[w], acc=[w])
            tr.op('dve', lambda: V.tensor_scalar(out=CX, in0=THR, scalar1=PI / 2, scalar2=None, op0=ALU.add), reads=[w], acc=[w])
            tr.op('dve', lambda: V.tensor_scalar(out=CM, in0=CX, scalar1=PI, scalar2=-2 * PI, op0=ALU.is_gt, op1=ALU.mult), reads=[w], acc=[w])
            tr.op('dve', lambda: V.tensor_tensor(out=CX, in0=CX, in1=CM, op=ALU.add), reads=[w], acc=[w])
            tr.op('dve', lambda: V.tensor_scalar(out=CX, in0=CX, scalar1=-PI, scalar2=PI, op0=ALU.max, op1=ALU.min), reads=[w], acc=[w])
            tr.op('act', lambda: A.activation(out=CO, in_=CX, func=AF.Sin), reads=[w], acc=[w])
            tr.op('act', lambda: A.activation(out=SI, in_=THR, func=AF.Sin), reads=[w], acc=[w])
            tr.op('dve', lambda: V.tensor_tensor(out=LR, in0=RR, in1=CO, op=ALU.mult), reads=[w], acc=[w])
            tr.op('dve', lambda: V.tensor_scalar(out=LR, in0=LR, scalar1=-1.0, scalar2=None, op0=ALU.add), reads=[w], acc=[w])
            tr.op('dve', lambda: V.tensor_tensor(out=LI, in0=RR, in1=SI, op=ALU.mult), reads=[w], acc=[w])
            tr.op('dve', lambda: V.tensor_tensor(out=N1, in0=LR, in1=are, op=ALU.mult), reads=[w, sa], acc=[w])
            tr.op('dve', lambda: V.tensor_tensor(out=T3, in0=LI, in1=aim, op=ALU.mult), reads=[w, sa], acc=[w])
            tr.op('dve', lambda: V.tensor_tensor(out=N1, in0=N1, in1=T3, op=ALU.add), reads=[w], acc=[w])
            tr.op('dve', lambda: V.tensor_tensor(out=N2, in0=LI, in1=are, op=ALU.mult), reads=[w, sa], acc=[w])
            tr.op('dve', lambda: V.tensor_tensor(out=T3, in0=LR, in1=aim, op=ALU.mult), reads=[w, sa], acc=[w])
            tr.op('dve', lambda: V.tensor_tensor(out=N2, in0=N2, in1=T3, op=ALU.subtract), reads=[w], acc=[w])
            tr.op('dve', lambda: V.tensor_tensor(out=DEN, in0=are, in1=are, op=ALU.mult), reads=[w, sa], acc=[w])
            tr.op('dve', lambda: V.tensor_tensor(out=T3, in0=aim, in1=aim, op=ALU.mult), reads=[w, sa], acc=[w])
            tr.op('dve', lambda: V.tensor_tensor(out=DEN, in0=DEN, in1=T3, op=ALU.add), reads=[w], acc=[w])
            tr.op('dve', lambda: V.reciprocal(out=DEN, in_=DEN), reads=[w], acc=[w])
            tr.op('dve', lambda: V.tensor_tensor(out=N1, in0=N1, in1=DEN, op=ALU.mult), reads=[w], acc=[w])
            tr.op('dve', lambda: V.tensor_tensor(out=N2, in0=N2, in1=DEN, op=ALU.mult), reads=[w], acc=[w])
            bb = PD['bb']; tb = pp.sb('tb%d' % d, [128, 2, 12, 16])
            cre = N1.unsqueeze(2).to_broadcast([128, 12, 16]); cim = N2.unsqueeze(2).to_broadcast([128, 12, 16])
            tr.op('dve', lambda: V.tensor_tensor(out=bb[:, 0], in0=sbr[:], in1=cre, op=ALU.mult), reads=[w, sbr], writes=[bb])
            tr.op('dve', lambda: V.tensor_tensor(out=tb[:, 0], in0=sbi[:], in1=cim, op=ALU.mult), reads=[w, sbi], writes=[tb])
            tr.op('dve', lambda: V.tensor_tensor(out=bb[:, 0], in0=bb[:, 0], in1=tb[:, 0], op=ALU.subtract), reads=[bb, tb], acc=[bb])
            tr.op('dve', lambda: V.tensor_tensor(out=bb[:, 1], in0=sbi[:], in1=cre, op=ALU.mult), reads=[w, sbi], acc=[bb])
            tr.op('dve', lambda: V.tensor_tensor(out=tb[:, 1], in0=sbr[:], in1=cim, op=ALU.mult), reads=[w, sbr], acc=[tb])
            tr.op('dve', lambda: V.tensor_tensor(out=bb[:, 1], in0=bb[:, 1], in1=tb[:, 1], op=ALU.add), reads=[bb, tb], acc=[bb])
            if BP == 2: pp.close(); ph.close(); return
            ANG = pp.sb('ang9_%d' % d, [128, 12, 9]); TA9 = pp.sb('ta9_%d' % d, [128, 12, 9]); TI9 = pp.sb('ti9_%d' % d, [128, 12, 9], I32)
            COk = pp.sb('cok%d' % d, [128, 12, 9]); SIk = pp.sb('sik%d' % d, [128, 12, 9]); RK = PD['RK']
            LRk = PD['LRk']; LIk = PD['LIk']
            i9b = io9[:].unsqueeze(1).to_broadcast([128, 12, 9])
            tr.op('dve', lambda: V.tensor_tensor(out=ANG[:], in0=THR.unsqueeze(2).to_broadcast([128, 12, 9]), in1=i9b, op=ALU.mult), reads=[w, io9], writes=[ANG])
            sin_rr(ANG[:], ANG, SIk[:], SIk, 0.0, TA9[:], TA9, TI9[:], TI9)
            sin_rr(ANG[:], ANG, COk[:], COk, PI / 2, TA9[:], TA9, TI9[:], TI9)
            tr.op('dve', lambda: V.tensor_tensor(out=RK[:], in0=ARDT.unsqueeze(2).to_broadcast([128, 12, 9]), in1=i9b, op=ALU.mult), reads=[w, io9], writes=[RK])
            tr.op('act', lambda: A.activation(out=RK[:], in_=RK[:], func=AF.Exp), reads=[RK], writes=[RK])
            tr.op('dve', lambda: V.tensor_tensor(out=LRk[:], in0=RK[:], in1=COk[:], op=ALU.mult), reads=[RK, COk], writes=[LRk])
            tr.op('dve', lambda: V.tensor_tensor(out=LIk[:], in0=RK[:], in1=SIk[:], op=ALU.mult), reads=[RK, SIk], writes=[LIk])
            if BP == 3: pp.close(); ph.close(); return
            TH8 = pp.sb('th8_%d' % d, [128, 12]); TA8 = pp.sb('ta8_%d' % d, [128, 12]); TI8 = pp.sb('ti8_%d' % d, [128, 12], I32)
            tr.op('dve', lambda: V.tensor_scalar(out=TA8[:], in0=THR, scalar1=8.0, scalar2=None, op0=ALU.mult), reads=[w], writes=[TA8])
            tr.op('dve', lambda: V.tensor_scalar(out=TI8[:], in0=TA8[:], scalar1=1.0 / (2 * PI), scalar2=None, op0=ALU.mult), reads=[TA8], writes=[TI8])
            tr.op('dve', lambda: V.tensor_copy(out=TH8[:], in_=TI8[:]), reads=[TI8], writes=[TH8])
            tr.op('dve', lambda: V.scalar_tensor_tensor(out=TH8[:], in0=TH8[:], scalar=-2 * PI, in1=TA8[:], op0=ALU.mult, op1=ALU.add), reads=[TH8, TA8], writes=[TH8])
            tr.op('dve', lambda: V.tensor_tensor(out=A64[:], in0=TH8[:].unsqueeze(2).to_broadcast([128, 12, 64]),
                                                 in1=io64[:, d, :].unsqueeze(1).to_broadcast([128, 12, 64]), op=ALU.mult), reads=[TH8, io64], writes=[A64])
            sin_rr(A64[:], A64, S64[:], S64, 0.0, TA64[:], TA64, TI64[:], TI64)
            sin_rr(A64[:], A64, C64[:], C64, PI / 2, TA64[:], TA64, TI64[:], TI64)
            if BP == 4: pp.close(); ph.close(); return
            C2 = PD['C2']; S2 = PD['S2']
            tr.group('act', [lambda: A.activation(out=C2[:, :, 0, :], in_=C64[:], func=AF.Copy),
                             lambda: A.activation(out=C2[:, :, 1, :], in_=C64[:], func=AF.Copy)], reads=[C64], writes=[C2])
            tr.group('act', [lambda: A.activation(out=S2[:, :, 0, :], in_=S64[:], func=AF.Copy),
                             lambda: A.activation(out=S2[:, :, 1, :], in_=S64[:], func=AF.Copy, scale=-1.0)], reads=[S64], writes=[S2])
            if BP == 5: pp.close(); ph.close(); return
            prm[d] = dict(w=w, LRk=LRk, LIk=LIk, RK=RK, C2=C2, S2=S2, bb=bb, scr=scr_, sci=sci)

        pp.close()
        BCUT = int(os.environ.get('BCUT', '99'))
        if BCUT == 0: ph.close(); return
        yacc = ph.sb('yacc', [128, TT], F32, dma=True); ub = ph.sb('ub', [128, TT], BF16)
        XHs = [ph.sb('XH%d' % i, [128, 4, 2, 8, 128], BF16) for i in range(2)]
        XTs = [ph.sb('XTt%d' % i, [128, 4, 2, 8, 128], BF16) for i in range(2)]
        Kbs = [ph.sb('Kb%d' % i, [128, 8, 128], BF16) for i in range(2)]
        Bstg = ph.sb('bstg4', [128, 4, 2, 128]); Cf = ph.sb('cf4', [128, 4, 2, 128]); cpad = ph.sb('cpad4', [128, 4, 2, 128], BF16)
        stg = [ph.sb('stgc%d' % i, [128, 128]) for i in range(2)]
        tq = [ph.sb('tq%d' % i, [128, 8, 128]) for i in range(2)]
        pxt = ph.ps('pxt', [128, 8, 128], BF16)
        pk = [ph.ps('pk%d' % i, [128, 4, 128]) for i in range(2)]
        pv = [ph.ps('pv%d' % i, [128, 4, 2, 64]) for i in range(2)]
        po = ph.ps('po', [128, 8, 64])
        t1 = ph.sb('bt1', [128, 4, 2, 64]); t2 = ph.sb('bt2', [128, 4, 2, 64]); Wt = ph.sb('bW', [128, 4, 2, 64]); Zt = ph.sb('bZ', [128, 4, 2, 64])
        Sf = [ph.sb('bSf%d' % i, [128, 4, 2, 64]) for i in range(2)]
        Sp = [ph.sb('bSp%d' % i, [128, 4, 2, 64], BF16) for i in range(2)]
        segs = [(0, 32)] + [(NCX + i * 512, 64) for i in range(8)]
        scnt = dict(n=0)

        def make_setup(j3, d, slot):
            P = prm[d]; XH = XHs[slot]; XT = XTs[slot]; Kb = Kbs[slot]
            steps = []

            def stage_s(s4):
                s = 4 * j3 + s4
                for ri in range(2):
                    first = (s4 == 0 and ri == 0)
                    tr.op('dve', lambda: V.tensor_tensor(out=Bstg[:, s4, ri, :].rearrange('p (a b) -> p a b', a=8),
                                                         in0=P['bb'][:, ri, s, :].unsqueeze(1).to_broadcast([128, 8, 16]),
                                                         in1=m2[:, s % 4], op=ALU.mult), reads=[P['bb'], m2],
                          writes=[Bstg] if first else (), acc=() if first else [Bstg])
                    sg_ = stg[scnt['n'] % 2]; scnt['n'] += 1
                    csrc = (P['scr'] if ri == 0 else P['sci'])
                    tr.op('dve', lambda: V.tensor_tensor(out=sg_[:].rearrange('p (a b) -> p a b', a=2),
                                                         in0=csrc[:, s // 4, :].unsqueeze(1).to_broadcast([128, 2, 64]),
                                                         in1=mcm[:, s % 4], op=ALU.mult), reads=[csrc, mcm], writes=[sg_])
                    tr.op('pe', lambda: PE.transpose(pst[:], sg_[:], ident[:]), reads=[sg_, ident], writes=[pst])
                    tr.op('act', lambda: A.activation(out=Cf[:, s4, ri, :], in_=pst[:], func=AF.Copy), reads=[pst],
                          writes=[Cf] if first else (), acc=() if first else [Cf])
                    tr.op('act', lambda: A.activation(out=cpad[:, s4, ri, :], in_=pst[:], func=AF.Copy, scale=(1.0 if ri == 0 else -1.0)),
                          reads=[pst], writes=[cpad] if first else (), acc=() if first else [cpad])

            def scaled_s(dstbuf, src, k0, neg_im, s4):
                s = 4 * j3 + s4
                sre = src[:, s4, 0, :].unsqueeze(1).to_broadcast([128, 8, 128]); sim = src[:, s4, 1, :].unsqueeze(1).to_broadcast([128, 8, 128])
                lr = P['LRk'][:, s, k0:k0 + 8].unsqueeze(2).to_broadcast([128, 8, 128])
                li = P['LIk'][:, s, k0:k0 + 8].unsqueeze(2).to_broadcast([128, 8, 128])
                first = (s4 == 0)
                tr.op('dve', lambda: V.tensor_tensor(out=tq[0][:], in0=sre, in1=lr, op=ALU.mult), reads=[src, P['LRk']], writes=[tq[0]])
                tr.op('dve', lambda: V.tensor_tensor(out=tq[1][:], in0=sim, in1=li, op=ALU.mult), reads=[src, P['LIk']], writes=[tq[1]])
                tr.op('dve', lambda: V.tensor_tensor(out=dstbuf[:, s4, 0], in0=tq[0][:], in1=tq[1][:], op=ALU.subtract), reads=[tq[0], tq[1]],
                      writes=[dstbuf] if first else (), acc=() if first else [dstbuf])
                tr.op('dve', lambda: V.tensor_tensor(out=tq[0][:], in0=sre, in1=li, op=ALU.mult), reads=[src, P['LIk']], writes=[tq[0]])
                tr.op('dve', lambda: V.tensor_tensor(out=tq[1][:], in0=sim, in1=lr, op=ALU.mult), reads=[src, P['LRk']], writes=[tq[1]])
                if neg_im:
                    tr.op('dve', lambda: V.scalar_tensor_tensor(out=dstbuf[:, s4, 1], in0=tq[0][:], scalar=-1.0, in1=tq[1][:], op0=ALU.mult, op1=ALU.subtract),
                          reads=[tq[0], tq[1]], acc=[dstbuf])
                else:
                    tr.op('dve', lambda: V.tensor_tensor(out=dstbuf[:, s4, 1], in0=tq[0][:], in1=tq[1][:], op=ALU.add), reads=[tq[0], tq[1]], acc=[dstbuf])

            def kmm(half):
                pkk = pk[half]
                fns = []
                for tt in range(4):
                    tau = half * 4 + tt
                    k = 0
                    for s4 in range(4):
                        for ri in range(2):
                            fns.append(lambda tt=tt, tau=tau, s4=s4, ri=ri, k=k: PE.matmul(pkk[:, tt, :], lhsT=XH[:, s4, ri, tau, :], rhs=cpad[:, s4, ri, :],
                                                                                             start=(k == 0), stop=(k == 7)))
                            k += 1
                tr.group('pe', fns, reads=[XH, cpad], writes=[pkk])
                tr.op('act', lambda: A.activation(out=Kb[:, half * 4:half * 4 + 4, :], in_=pkk[:], func=AF.Copy), reads=[pkk],
                      writes=[Kb] if half == 0 else (), acc=() if half == 0 else [Kb])

            def xtr(s4, ri):
                tr.group('pe', [lambda tau=tau: PE.transpose(pxt[:, tau, :], XH[:, s4, ri, tau, :], identb[:]) for tau in range(8)],
                         reads=[XH, identb], writes=[pxt])
                first = (s4 == 0 and ri == 0)
                tr.op('act' if ri == 0 else 'dve',
                      (lambda: A.activation(out=XT[:, s4, ri], in_=pxt[:], func=AF.Copy)) if ri == 0 else (lambda: V.tensor_copy(out=XT[:, s4, ri], in_=pxt[:])),
                      reads=[pxt], writes=[XT] if first else (), acc=() if first else [XT])

            for s4 in range(4): steps.append(lambda s4=s4: stage_s(s4))
            for s4 in range(4): steps.append(lambda s4=s4: scaled_s(XH, Bstg, 0, False, s4))
            for half in range(2): steps.append(lambda half=half: kmm(half))
            for s4 in range(4):
                for ri in range(2): steps.append(lambda s4=s4, ri=ri: xtr(s4, ri))
            for s4 in range(4): steps.append(lambda s4=s4: scaled_s(XH, Cf, 1, True, s4))
            return steps

        def run_loop(j3, d, slot, pending):
            P = prm[d]; XH = XHs[slot]; XT = XTs[slot]; Kb = Kbs[slot]
            order = list(range(9)) if d == 0 else [0] + list(range(8, 0, -1))
            prev = None
            per = -(-len(pending) // 8) if pending else 0

            def vmm(oi):
                t0, nC = segs[order[oi]]
                pvv = pv[oi % 2]
                fns = []
                for s4 in range(4):
                    for ri in range(2):
                        for j in range(8):
                            tau = (7 - j) if d == 0 else j
                            fns.append(lambda s4=s4, ri=ri, j=j, tau=tau: PE.matmul(pvv[:, s4, ri, :nC], lhsT=XT[:, s4, ri, tau, :],
                                                                                     rhs=ub[:, t0 + j:t0 + j + 8 * (nC - 1) + 1:8], start=(j == 0), stop=(j == 7)))
                tr.group('pe', fns, reads=[XT, ub], writes=[pvv])

            vmm(0)
            for oi in range(9):
                t0, nC = segs[order[oi]]
                par = oi % 2
                pvv = pv[par]
                s0 = 4 * j3
                csl = slice(0, nC) if d == 0 else slice(64 - nC, 64)
                C2v = P['C2'][:, s0:s0 + 4, :, csl]; S2v = P['S2'][:, s0:s0 + 4, :, csl]
                tr.op('dve', lambda: V.tensor_tensor(out=t1[:, :, :, :nC], in0=pvv[:, :, :, :nC], in1=C2v, op=ALU.mult), reads=[pvv, P['C2']], writes=[t1])
                tr.op('dve', lambda: V.tensor_tensor(out=t2[:, :, :, :nC], in0=pvv[:, :, ::-1, :nC], in1=S2v, op=ALU.mult), reads=[pvv, P['S2']], writes=[t2])
                tr.op('dve', lambda: V.tensor_tensor(out=Wt[:, :, :, :nC], in0=t1[:, :, :, :nC], in1=t2[:, :, :, :nC], op=ALU.add), reads=[t1, t2], writes=[Wt])
                if oi + 1 < 9: vmm(oi + 1)
                fns = []
                for s4 in range(4):
                    rr = P['RK'][:, s0 + s4, 8:9].to_broadcast([128, nC])
                    for ri in range(2):
                        if prev is None: ini = 0.0
                        else:
                            pcol = (prev[1] - 1) if d == 0 else 0
                            ini = Sf[1 - par][:, s4, ri, pcol:pcol + 1]
                        if d == 0:
                            fns.append(lambda s4=s4, ri=ri, rr=rr, ini=ini: V.tensor_tensor_scan(out=Zt[:, s4, ri, :nC], data0=rr, data1=Wt[:, s4, ri, :nC],
                                                                                                 initial=ini, op0=ALU.mult, op1=ALU.add))
                        else:
                            fns.append(lambda s4=s4, ri=ri, rr=rr, ini=ini: V.tensor_tensor_scan(out=Zt[:, s4, ri, :nC][:, ::-1], data0=rr, data1=Wt[:, s4, ri, :nC][:, ::-1],
                                                                                                 initial=ini, op0=ALU.mult, op1=ALU.add))
                tr.group('dve', fns, reads=[Wt, P['RK']] + ([Sf[1 - par]] if prev is not None else []), writes=[Zt])
                tr.op('dve', lambda: V.tensor_tensor(out=t1[:, :, :, :nC], in0=Zt[:, :, :, :nC], in1=C2v, op=ALU.mult), reads=[Zt, P['C2']], writes=[t1])
                tr.op('dve', lambda: V.tensor_tensor(out=t2[:, :, :, :nC], in0=Zt[:, :, ::-1, :nC], in1=S2v, op=ALU.mult), reads=[Zt, P['S2']], writes=[t2])
                tr.op('dve', lambda: V.tensor_tensor(out=Sf[par][:, :, :, :nC], in0=t1[:, :, :, :nC], in1=t2[:, :, :, :nC], op=ALU.subtract), reads=[t1, t2], writes=[Sf[par]])
                spv = Sp[par]
                if d == 0:
                    f1 = lambda: A.activation(out=spv[:, :, :, 1:nC], in_=Sf[par][:, :, :, 0:nC - 1], func=AF.Copy)
                    cdst = spv[:, :, :, 0:1]
                else:
                    f1 = lambda: A.activation(out=spv[:, :, :, 0:nC - 1], in_=Sf[par][:, :, :, 1:nC], func=AF.Copy)
                    cdst = spv[:, :, :, nC - 1:nC]
                if prev is None:
                    f2 = lambda: A.activation(out=cdst, in_=Sf[par][:, :, :, 0:1], func=AF.Copy, scale=0.0)
                    rdl = [Sf[par]]
                else:
                    pcol = (prev[1] - 1) if d == 0 else 0
                    f2 = lambda: A.activation(out=cdst, in_=Sf[1 - par][:, :, :, pcol:pcol + 1], func=AF.Copy)
                    rdl = [Sf[par], Sf[1 - par]]
                tr.group('act', [f1, f2], reads=rdl, writes=[spv])
                for _ in range(per):
                    if pending: pending.pop(0)()
                fns = []
                for j in range(8):
                    ntap = (j + 1) if d == 0 else (8 - j)
                    hk = j if d == 0 else (7 - j)
                    tot = ntap + 8; k = 0
                    for tau in range(ntap):
                        off = (j - tau) if d == 0 else (j + tau)
                        fns.append(lambda j=j, tau=tau, off=off, k=k, tot=tot: PE.matmul(po[:, j, :nC], lhsT=Kb[:, tau, :], rhs=ub[:, t0 + off:t0 + off + 8 * (nC - 1) + 1:8],
                                                                                          start=(k == 0), stop=(k == tot - 1)))
                        k += 1
                    for s4 in range(4):
                        for ri in range(2):
                            fns.append(lambda j=j, s4=s4, ri=ri, hk=hk, k=k, tot=tot: PE.matmul(po[:, j, :nC], lhsT=XH[:, s4, ri, hk, :], rhs=spv[:, s4, ri, :nC],
                                                                                                  start=(k == 0), stop=(k == tot - 1)))
                            k += 1
                tr.group('pe', fns, reads=[Kb, ub, XH, spv], writes=[po])
                yv = yacc[:, t0:t0 + 8 * nC].rearrange('p (c j) -> p j c', j=8)
                tr.op('dve', lambda: V.tensor_tensor(out=yv, in0=yv, in1=po[:, :, :nC], op=ALU.add), reads=[po, yacc], acc=[yacc])
                prev = (oi, nC)
            while pending: pending.pop(0)()

        ulist = [(j3, d) for j3 in range(3) for d in range(2)]
        for st_ in make_setup(0, 0, 0): st_()
        for ui, (j3, d) in enumerate(ulist):
            if d == 0:
                tr.dma('sp', yacc[:], U[j3 * 128:(j3 + 1) * 128, :], writes=[yacc])
                tr.op('act', lambda: A.activation(out=ub[:], in_=yacc[:], func=AF.Copy), reads=[yacc], writes=[ub])
                tr.op('dve', lambda: V.tensor_scalar(out=yacc[:], in0=yacc[:], scalar1=sdt[:, j3:j3 + 1], scalar2=None, op0=ALU.mult), reads=[yacc, sdt], writes=[yacc])
            pending = make_setup(ulist[ui + 1][0], ulist[ui + 1][1], (ui + 1) % 2) if ui + 1 < len(ulist) else []
            run_loop(j3, d, ui % 2, pending)
            if d == 1:
                for g0 in range(0, TT, 512):
                    n = min(512, TT - g0); yv = yacc[:, g0:g0 + n]
                    a0b = tq[0]; a1b = tq[1]
                    a0 = tq[0][:].rearrange('p a b -> p (a b)'); a1 = tq[1][:].rearrange('p a b -> p (a b)')
                    tr.op('act', lambda: A.activation(out=a0[:, :n], in_=yv, func=AF.Square), reads=[yacc], writes=[a0b])
                    tr.op('dve', lambda: V.tensor_scalar(out=a0[:, :n], in0=a0[:, :n], scalar1=0.044715, scalar2=1.0, op0=ALU.mult, op1=ALU.add), reads=[a0b], writes=[a0b])
                    tr.op('dve', lambda: V.tensor_tensor(out=a1[:, :n], in0=a0[:, :n], in1=yv, op=ALU.mult), reads=[a0b, yacc], writes=[a1b])
                    tr.op('act', lambda: A.activation(out=a1[:, 512:512 + n], in_=a1[:, :n], func=AF.Sigmoid, scale=2.0 * math.sqrt(2.0 / PI)), reads=[a1b], writes=[a1b])
                    st = stg_o[(g0 // 512) % 2]
                    tr.op('dve', lambda: V.tensor_tensor(out=st[:, :n], in0=a1[:, 512:512 + n], in1=yv, op=ALU.mult), reads=[a1b, yacc], writes=[st])
                    tr.dma('pool', TG[j3 * 128:(j3 + 1) * 128, g0:g0 + n], st[:, :n], reads=[st])
        ph.close()

    gm = None
    stg_o = None

    def phase_C(l, ctx_out):
        nonlocal gm
        ph = Phase()
        gm = ph.sb('gm', [128, 2, 128], F32, dma=True); tr.dma('sp', gm[:], gmask_in[:, :, :], writes=[gm])
        snk = ph.sb('snk', [128, 8], F32, dma=True)
        tr.dma('sp', snk[0:64, :], sink_in[l], writes=[snk]); tr.dma('sp', snk[64:128, :], sink_in[l], acc=[snk])
        esk = ph.sb('esk', [128, 8])
        tr.op('act', lambda: A.activation(out=esk[:], in_=snk[:], func=AF.Exp), reads=[snk], writes=[esk])
        vsb = ph.sb('vsb', [128, 34, 128], BF16, dma=True)
        tr.dma('sp', vsb[:], VG.rearrange('(c p) d -> p c d', p=128), writes=[vsb])
        vaug = [ph.sb('vaug%d' % g, [128, 34, 128], BF16) for g in range(2)]
        for g in range(2):
            tr.op('dve', lambda: V.memset(vaug[g][:, :, 64:128], 1.0), writes=[vaug[g]])
            tr.op('act', lambda: A.activation(out=vaug[g][:, :, 0:64], in_=vsb[:, :, g * 64:(g + 1) * 64], func=AF.Copy), reads=[vsb], acc=[vaug[g]])
        kT = [ph.sb('kT%d' % g, [64, TT], BF16, dma=True) for g in range(2)]
        for g in range(2):
            tr.dma('sp', kT[g][:, 0:NCX], KGC[g], writes=[kT[g]])
            tr.dma('sp', kT[g][:, NCX:TT], KG[g], acc=[kT[g]])
        qb = [ph.sb('qb%d' % i, [64, 4, 128], BF16, dma=True) for i in range(3)]
        ps_s = [ph.ps('pss%d' % i) for i in range(4)]
        ps_o = [ph.ps('pso%d' % i) for i in range(2)]; ps_d = [ph.ps('psd%d' % i) for i in range(2)]
        pT = [ph.sb('pT%d' % i, [128, 512], BF16) for i in range(4)]
        tmpm = [ph.sb('tmpm%d' % i, [128, 512]) for i in range(2)]; den = ph.sb('den', [128, 512])
        ob = [ph.sb('ob%d' % i, [64, 4, 128], BF16, dma=True) for i in range(2)]
        units = []
        if ctx_out:
            for g in range(2):
                for bi in range(2): units.append(('c', g, bi))
        for g in range(2):
            for bi in range(32): units.append(('l', g, bi))
        items = []
        uinfo = []
        for ui, (kind, g, bi) in enumerate(units):
            keys = [(kT[g][:, c * 128:(c + 1) * 128], None, vaug[g][:, c, :]) for c in range(2)]
            if kind == 'l':
                for dlt in (-1, 0, 1):
                    kb = bi + dlt
                    if kb < 0 or kb > 31: continue
                    m = None if dlt == 0 else gm[:, (0 if dlt == -1 else 1), :].unsqueeze(1).to_broadcast([128, 4, 128])
                    keys.append((kT[g][:, NCX + kb * 128:NCX + (kb + 1) * 128], m, vaug[g][:, 2 + kb, :]))
            for ki, (k_ap, m_ap, v_ap) in enumerate(keys): items.append((ui, ki, len(keys), k_ap, m_ap, v_ap))
        cnt = dict(n=0, m=0)
        st = {}

        def stage1(ii):
            ui, ki, nk, k_ap, m_ap, v_ap = items[ii]
            kind, g, bi = units[ui]
            q = qb[ui % 3]
            if ki == 0:
                srcq = (QGC if kind == 'c' else QG)[4 * g:4 * g + 4, :, bi * 128:(bi + 1) * 128].rearrange('h d t -> d h t')
                tr.dma('sp', q[:], srcq, writes=[q])
            pss = ps_s[cnt['n'] % 4]; pt = pT[cnt['n'] % 4]; cnt['n'] += 1
            tr.op('pe', lambda: PE.matmul(pss[:, :], lhsT=k_ap, rhs=q[:], start=True, stop=True), reads=[q, kT[g]], writes=[pss])
            if m_ap is not None:
                tm = tmpm[cnt['m'] % 2]; cnt['m'] += 1
                tr.op('dve', lambda: V.tensor_tensor(out=tm[:].rearrange('p (h t) -> p h t', h=4), in0=pss[:].rearrange('p (h t) -> p h t', h=4),
                                                     in1=m_ap, op=ALU.add), reads=[pss, gm], writes=[tm])
                tr.op('act', lambda: A.activation(out=pt[:], in_=tm[:], func=AF.Exp), reads=[tm], writes=[pt])
            else:
                tr.op('act', lambda: A.activation(out=pt[:], in_=pss[:], func=AF.Exp), reads=[pss], writes=[pt])
            st[ii] = pt

        def stage2(ii):
            ui, ki, nk, k_ap, m_ap, v_ap = items[ii]
            kind, g, bi = units[ui]
            pt = st.pop(ii)
            pso = ps_o[ui % 2]; psd = ps_d[ui % 2]; o = ob[ui % 2]
            tr.op('pe', lambda: PE.matmul(pso[:, :], lhsT=v_ap, rhs=pt[:], start=(ki == 0), stop=(ki == nk - 1)),
                  reads=[pt, vaug[g]], writes=[pso] if ki == 0 else (), acc=() if ki == 0 else [pso])
            if ki == nk - 1:
                tr.op('dve', lambda: V.tensor_tensor(out=den[64:128, :].rearrange('p (h t) -> p h t', h=4), in0=pso[64:128, :].rearrange('p (h t) -> p h t', h=4),
                                                     in1=esk[64:128, 4 * g:4 * g + 4].unsqueeze(2).to_broadcast([64, 4, 128]), op=ALU.add),
                      reads=[pso, esk], writes=[den])
                tr.op('act', lambda: A.activation(out=den[64:128, :], in_=den[64:128, :], func=AF.Ln), reads=[den], writes=[den])
                tr.op('act', lambda: A.activation(out=den[64:128, :], in_=den[64:128, :], func=AF.Exp, scale=-1.0), reads=[den], writes=[den])
                tr.op('dve', lambda: V.tensor_tensor(out=o[:].rearrange('p h t -> p (h t)'), in0=pso[0:64, :], in1=den[64:128, :], op=ALU.mult),
                      reads=[pso, den], writes=[o])
                dst = (YGC if kind == 'c' else YG)[4 * g:4 * g + 4, :, bi * 128:(bi + 1) * 128].rearrange('h d t -> d h t')
                tr.dma('pool', dst, o[:], reads=[o])

        KD = 2
        for ii in range(len(items) + KD):
            if ii < len(items): stage1(ii)
            if ii - KD >= 0: stage2(ii - KD)
        ph.close()

    def attn_unit3(q, keys, ps_s, pso, psd, pT, tmpm, qbufs, kbufs, vbufs, epi):
        nk = len(keys)
        for ki, (k_ap, m_ap, v_ap) in enumerate(keys):
            pss = ps_s[ki % len(ps_s)]
            tr.op('pe', lambda: PE.matmul(pss[:, :], lhsT=k_ap, rhs=q[:], start=True, stop=True), reads=qbufs + kbufs, writes=[pss])
            pt = pT[ki % len(pT)]
            if m_ap is not None:
                tr.op('dve', lambda: V.tensor_tensor(out=tmpm[:].rearrange('p (h t) -> p h t', h=4), in0=pss[:].rearrange('p (h t) -> p h t', h=4),
                                                     in1=m_ap, op=ALU.add), reads=[pss, gm], writes=[tmpm])
                tr.op('act', lambda: A.activation(out=pt[:], in_=tmpm[:], func=AF.Exp), reads=[tmpm], writes=[pt])
            else:
                tr.op('act', lambda: A.activation(out=pt[:], in_=pss[:], func=AF.Exp), reads=[pss], writes=[pt])
            tr.op('pe', lambda: PE.matmul(pso[0:64, :], lhsT=v_ap, rhs=pt[:], start=(ki == 0), stop=(ki == nk - 1)),
                  reads=[pt] + vbufs, writes=[pso] if ki == 0 else (), acc=() if ki == 0 else [pso])
            tr.op('pe', lambda: PE.matmul(psd[0:64, :], lhsT=ones_b[:, 0:64], rhs=pt[:], start=(ki == 0), stop=(ki == nk - 1)),
                  reads=[pt, ones_b], writes=[psd] if ki == 0 else (), acc=() if ki == 0 else [psd])
        epi()

    def phase_D(l, ctx_out):
        ph = Phase()
        kT = [ph.sb('nkT%d' % i, [64, TT], BF16, dma=True) for i in range(2)]
        qT = [ph.sb('nqT%d' % i, [64, TT], BF16, dma=True) for i in range(2)]
        vs = [ph.sb('nvs%d' % i, [128, 34, 64], BF16, dma=True) for i in range(2)]
        vt = [ph.sb('nvt%d' % i, [128, 14, 64], F32, dma=True) for i in range(2)]
        od = [ph.sb('nod%d' % i, [128, 5, 64], F32, dma=True) for i in range(2)]
        cbt = [ph.sb('ncb%d' % i, [128, 5, 128], F32, dma=True) for i in range(2)]
        yb = [ph.sb('nyb%d' % i, [64, TT], BF16, dma=True) for i in range(2)]
        RD = 2
        ps_L = [ph.ps('npl%d' % i) for i in range(RD)]
        ps_X = [ph.ps('npx%d' % i) for i in range(RD)]
        ps_od = [ph.ps('npo%d' % i) for i in range(RD)]
        tmp = [ph.sb('ntmp%d' % i, [128, 640]) for i in range(RD)]
        pT = [ph.sb('npT%d' % i, [128, 640], BF16) for i in range(RD)]
        pC = [ph.sb('npC%d' % i, [128, 256], BF16) for i in range(RD)]
        rd = [ph.sb('nrd%d' % i, [128, 256]) for i in range(RD)]
        vaugn = [ph.sb('nvaug%d' % i, [128, 34, 128], BF16) for i in range(2)]
        n = 0
        for h in range(8):
            k_ = kT[h % 2]; q_ = qT[h % 2]; v_ = vs[h % 2]; vt_ = vt[h % 2]; od_ = od[h % 2]; cb_ = cbt[h % 2]; y_ = yb[h % 2]
            tr.dma('sp', k_[:, 0:NCX], KNC[h], writes=[k_]); tr.dma('sp', k_[:, NCX:TT], KN[h], acc=[k_])
            tr.dma('sp', q_[:, 0:NCX], QNC[h], writes=[q_]); tr.dma('sp', q_[:, NCX:TT], QN[h], acc=[q_])
            tr.dma('sp', v_[:], VN[:, h * 64:(h + 1) * 64].rearrange('(c p) d -> p c d', p=128), writes=[v_])
            tr.dma('sp', vt_[:], navt_in[l, h], writes=[vt_]); tr.dma('sp', od_[:], naod_in[l, h], writes=[od_])
            tr.dma('sp', cb_[:], nacb_in[l, h], writes=[cb_])
            va_ = vaugn[h % 2]
            tr.op('dve', lambda: V.memset(va_[:, :, 64:128], 1.0), writes=[va_])
            tr.op('act', lambda: A.activation(out=va_[:, :, 0:64], in_=v_[:], func=AF.Copy), reads=[v_], acc=[va_])
            units = ([('c', 0), ('c', 1)] if ctx_out else []) + [('l', r) for r in range(4)] + [('p', r) for r in range(4, 60, 2)] + [('l', r) for r in range(60, 64)]
            for ui, (kind, r) in enumerate(units):
                psl = ps_L[n % RD]; psx = ps_X[n % RD]; pob = ps_od[n % RD]
                tm = tmp[n % RD]; pt = pT[n % RD]; pc = pC[n % RD]; rdn = rd[n % RD]; n += 1
                p0 = 0
                if kind == 'c':
                    Nq = 128; qap = q_[:, r * 128:(r + 1) * 128]; npair = 0; oc0 = r * 128
                elif kind == 'p':
                    Nq = 128; qap = q_[:, NCX + r * 64:NCX + (r + 2) * 64]; oc0 = NCX + r * 64
                    p0 = (r - 4) // 2; npair = 5
                else:
                    Nq = 64; qap = q_[:, NCX + r * 64:NCX + (r + 1) * 64]; oc0 = NCX + r * 64
                    rs = min(max(r - 4, 0), 56)
                    p0 = rs // 2; npair = 4; i0_ = rs - r + 7
                    bias = vt_[:, i0_:i0_ + 7:2, :]
                fns = [lambda c=c: PE.matmul(psx[:, 128 + c * Nq:128 + (c + 1) * Nq], lhsT=k_[:, c * 128:(c + 1) * 128], rhs=qap, start=True, stop=True) for c in range(2)]
                if npair == 5:
                    fns.append(lambda: PE.matmul(psx[:, 0:Nq], lhsT=k_[:, NCX + (p0 + 4) * 128:NCX + (p0 + 5) * 128], rhs=qap, start=True, stop=True))
                tr.group('pe', fns, reads=[k_, q_], writes=[psx])
                if npair:
                    tr.group('pe', [lambda k=k: PE.matmul(psl[:, k * Nq:(k + 1) * Nq], lhsT=k_[:, NCX + (p0 + k) * 128:NCX + (p0 + k + 1) * 128], rhs=qap,
                                                          start=True, stop=True) for k in range(4)], reads=[k_, q_], writes=[psl])
                tr.op('act', lambda: A.activation(out=pc[:, :2 * Nq], in_=psx[:, 128:128 + 2 * Nq], func=AF.Exp), reads=[psx], writes=[pc])
                if kind == 'l':
                    tr.op('dve', lambda: V.tensor_tensor(out=tm[:, :256].rearrange('p (a b) -> p a b', a=4),
                                                         in0=psl[:, :256].rearrange('p (a b) -> p a b', a=4), in1=bias, op=ALU.add),
                          reads=[psl, vt_], writes=[tm])
                    tr.op('act', lambda: A.activation(out=pt[:, :256], in_=tm[:, :256], func=AF.Exp), reads=[tm], writes=[pt])
                elif kind == 'p':
                    tr.op('dve', lambda: V.tensor_tensor(out=tm[:, 0:512], in0=psl[:, 0:512], in1=cb_[:, 0:4, :].rearrange('p a b -> p (a b)'), op=ALU.add),
                          reads=[psl, cb_], writes=[tm])
                    tr.op('dve', lambda: V.tensor_tensor(out=tm[:, 512:640], in0=psx[:, 0:128], in1=cb_[:, 4, :], op=ALU.add),
                          reads=[psx, cb_], acc=[tm])
                    tr.op('act', lambda: A.activation(out=pt[:, :640], in_=tm[:, :640], func=AF.Exp), reads=[tm], writes=[pt])
                mm = [(va_[:, c, :], pc[:, c * Nq:(c + 1) * Nq]) for c in range(2)]
                mm += [(va_[:, 2 + p0 + k, :], pt[:, k * Nq:(k + 1) * Nq]) for k in range(npair)]
                tr.group('pe', [lambda i=i, a=a, b=b: PE.matmul(pob[:, 0:Nq], lhsT=a, rhs=b, start=(i == 0), stop=(i == len(mm) - 1))
                                for i, (a, b) in enumerate(mm)], reads=[va_, pc, pt], writes=[pob])
                tr.op('dve', lambda: V.reciprocal(out=rdn[64:128, :Nq], in_=pob[64:128, 0:Nq]), reads=[pob], writes=[rdn])
                tr.op('dve', lambda: V.tensor_tensor(out=y_[:, oc0:oc0 + Nq], in0=pob[0:64, 0:Nq], in1=rdn[64:128, :Nq], op=ALU.mult),
                      reads=[pob, rdn], writes=[y_] if ui == 0 else (), acc=() if ui == 0 else [y_])
            if ctx_out: tr.dma('pool', YNC[h], y_[:, 0:NCX], reads=[y_])
            tr.dma('pool', YN[h], y_[:, NCX:TT], reads=[y_])
        ph.close()

    pcbig = None

    def phase_E(l, ctx_out, last):
        ph = Phase()
        R = ffn_bufs(ph)
        xTs = [ph.sb('xT%d' % i, [128, KC, 512], F32, dma=True) for i in range(2)]
        hT = ph.sb('hT', [128, KC, 512], BF16); aT = ph.sb('aT', [128, FC, 512], BF16)
        tg = [ph.sb('tg%d' % i, [128, 3, 512], F32, dma=True) for i in range(2)]
        tgb = ph.sb('tgb', [128, 3, 512], BF16); ys = ph.sb('ys', [128, 3, 512], BF16)
        yg = [ph.sb('yg%d' % i, [128, 4, 512], BF16, dma=True) for i in range(2)]
        yn = [ph.sb('yn%d' % i, [128, 4, 512], BF16, dma=True) for i in range(2)]
        sgr = [ph.sb('sg%d' % i, [128, 3, 512], BF16, dma=True) for i in range(2)]
        wpgr = [ph.sb('wpg%d' % i, [128, 4, 128], BF16, dma=True) for i in range(2)]
        wpnr = [ph.sb('wpn%d' % i, [128, 4, 128], BF16, dma=True) for i in range(2)]
        wglu = ph.sb('wglu', [128, 3, 3, 128], BF16, dma=True)
        tr.dma('sp', wglu[:], w_b[('wglu', l)].rearrange('m p k c -> p m k c'), reads=[w_buf[('wglu', l)]], writes=[wglu])
        wps = ph.sb('wps', [128, 8, 3, 128], BF16, dma=True)
        tr.dma('sp', wps[:], w_b[('wps', l)].rearrange('m p k c -> p m k c'), reads=[w_buf[('wps', l)]], writes=[wps])
        wo = [ph.sb('wo%d' % i, [128, KC, 128], BF16, dma=True) for i in range(2)]
        acc = ph.sb('acc', [128, 512]); t2 = ph.sb('t2e', [128, 512])
        ost = [ph.sb('ost%d' % i, [128, D], F32, dma=True) for i in range(1)] if last else None
        tl = tiles if ctx_out else tiles[1:]
        cnt = dict(wo=0, ost=0)

        def loads(idx):
            c0, N, col = tl[idx]; b = idx % 2
            tr.dma('sp', xTs[b][:, :, :N], XT[:, c0:c0 + N].rearrange('(k p) t -> p k t', p=128), writes=[xTs[b]])
            tr.dma('sp', tg[b][:, :, :N], TG[:, c0:c0 + N].rearrange('(k p) t -> p k t', p=128), writes=[tg[b]])
            if col == 1:
                tr.dma('sp', yg[b][:, :, :N], YGC.rearrange('h d t -> (h d) t').rearrange('(k p) t -> p k t', p=128), writes=[yg[b]])
                tr.dma('sp', yn[b][:, :, :N], YNC.rearrange('h d t -> (h d) t').rearrange('(k p) t -> p k t', p=128), writes=[yn[b]])
            else:
                tr.dma('sp', yg[b][:, :, :N], YG.rearrange('h d t -> (h d) t')[:, c0 - NCX:c0 - NCX + N].rearrange('(k p) t -> p k t', p=128), writes=[yg[b]])
                tr.dma('sp', yn[b][:, :, :N], YN.rearrange('h d t -> (h d) t')[:, c0 - NCX:c0 - NCX + N].rearrange('(k p) t -> p k t', p=128), writes=[yn[b]])

        mcount = 0
        loads(0)
        for idx in range(len(tl)):
            c0, N, col = tl[idx]; b = idx % 2
            xT = xTs[b]; tg_ = tg[b]; yg_ = yg[b]; yn_ = yn[b]
            if idx + 1 < len(tl): loads(idx + 1)
            tr.op('act', lambda: A.activation(out=tgb[:, :, :N], in_=tg_[:, :, :N], func=AF.Copy), reads=[tg_], writes=[tgb])
            for m in range(3):
                pg = R['ps_g'][m % 2]
                tr.group('pe', [lambda kc=kc: PE.matmul(pg[:, :N], lhsT=wglu[:, m, kc, :], rhs=tgb[:, kc, :N], start=(kc == 0), stop=(kc == 2))
                                for kc in range(3)], reads=[wglu, tgb], writes=[pg])
                tr.op('act', lambda: A.activation(out=acc[:, :N], in_=pg[:, :N], func=AF.Sigmoid), reads=[pg], writes=[acc])
                tr.op('dve', lambda: V.tensor_tensor(out=ys[:, m, :N], in0=acc[:, :N], in1=tg_[:, m, :N], op=ALU.mult), reads=[acc, tg_],
                      writes=[ys] if m == 0 else (), acc=() if m == 0 else [ys])
            for m in range(KC):
                p1 = R['ps_g'][m % 2]; p2 = R['ps_u'][m % 2]; p3 = R['ps_m'][m % 2]
                sg_ = sgr[mcount % 2]; wpg = wpgr[mcount % 2]; wpn = wpnr[mcount % 2]; mcount += 1
                tr.dma('sp', sg_[:, :, :N], SG[:, c0:c0 + N].rearrange('(b m p) t -> m p b t', b=3, p=128)[m], writes=[sg_])
                tr.dma('sp', wpg[:], w_b[('wpg', l)][m], reads=[w_buf[('wpg', l)]], writes=[wpg])
                tr.dma('sp', wpn[:], w_b[('wpn', l)][m], reads=[w_buf[('wpn', l)]], writes=[wpn])
                tr.group('pe', [lambda kc=kc: PE.matmul(p1[:, :N], lhsT=wps[:, m, kc, :], rhs=ys[:, kc, :N], start=(kc == 0), stop=(kc == 2))
                                for kc in range(3)], reads=[wps, ys], writes=[p1])
                tr.group('pe', [lambda h=h: PE.matmul(p2[:, :N], lhsT=wpg[:, h, :], rhs=yg_[:, h, :N], start=(h == 0), stop=(h == 3))
                                for h in range(4)], reads=[wpg, yg_], writes=[p2])
                tr.group('pe', [lambda h=h: PE.matmul(p3[:, :N], lhsT=wpn[:, h, :], rhs=yn_[:, h, :N], start=(h == 0), stop=(h == 3))
                                for h in range(4)], reads=[wpn, yn_], writes=[p3])
                tr.op('dve', lambda: V.tensor_tensor(out=acc[:, :N], in0=p1[:, :N], in1=sg_[:, 0, :N], op=ALU.mult), reads=[p1, sg_], writes=[acc])
                tr.op('dve', lambda: V.tensor_tensor(out=t2[:, :N], in0=p2[:, :N], in1=sg_[:, 1, :N], op=ALU.mult), reads=[p2, sg_], writes=[t2])
                tr.op('pool', lambda: G.tensor_tensor(out=acc[:, :N], in0=acc[:, :N], in1=t2[:, :N], op=ALU.add), reads=[acc, t2], writes=[acc])
                tr.op('dve', lambda: V.tensor_tensor(out=t2[:, :N], in0=p3[:, :N], in1=sg_[:, 2, :N], op=ALU.mult), reads=[p3, sg_], writes=[t2])
                tr.op('pool', lambda: G.tensor_tensor(out=hT[:, m, :N], in0=acc[:, :N], in1=t2[:, :N], op=ALU.add), reads=[acc, t2],
                      writes=[hT] if m == 0 else (), acc=() if m == 0 else [hT])
            for m in range(KC):
                wb = wo[cnt['wo'] % 2]; cnt['wo'] += 1
                tr.dma('sp', wb[:], w_b[('wout', l)][m], reads=[w_buf[('wout', l)]], writes=[wb])
                pd = R['ps_m'][m % 2]
                tr.group('pe', [lambda kc=kc: PE.matmul(pd[:, :N], lhsT=wb[:, kc, :], rhs=hT[:, kc, :N], start=(kc == 0), stop=(kc == KC - 1))
                                for kc in range(KC)], reads=[wb, hT], writes=[pd])
                tr.op('dve', lambda: V.scalar_tensor_tensor(out=xT[:, m, :N], in0=pd[:, :N], scalar=modG[:, 1, m, col:col + 1],
                                                            in1=xT[:, m, :N], op0=ALU.mult, op1=ALU.add), reads=[pd, modG, xT], acc=[xT])
            ffn(ph, l, 2, xT, hT, aT, N, col, 2, R)
            if not last:
                tr.dma('pool', XT[:, c0:c0 + N].rearrange('(k p) t -> p k t', p=128), xT[:, :, :N], reads=[xT])
            else:
                sq = R['sq']; rstd = R['rstd']; tb = R['tmpbig']; pss = R['ps_m'][0]
                tr.op('act', lambda: A.activation(out=sq[:, :, :N], in_=xT[:, :, :N], func=AF.Square), reads=[xT], writes=[sq])
                tr.group('pe', [lambda kc=kc: PE.matmul(pss[:, :N], lhsT=ones_b[:], rhs=sq[:, kc, :N], start=(kc == 0), stop=(kc == KC - 1))
                                for kc in range(KC)], reads=[sq, ones_b], writes=[pss])
                tr.op('act', lambda: A.activation(out=rstd[:, :N], in_=pss[:, :N], func=AF.Ln, scale=1.0 / D, bias=epsb[:, 0:1]), reads=[pss, epsb], writes=[rstd])
                tr.op('act', lambda: A.activation(out=rstd[:, :N], in_=rstd[:, :N], func=AF.Exp, scale=-0.5), reads=[rstd], writes=[rstd])
                tr.op('dve', lambda: V.tensor_tensor(out=tb[:, :, :N], in0=xT[:, :, :N], in1=rstd[:, :N].unsqueeze(1).to_broadcast([128, KC, N]), op=ALU.mult),
                      reads=[xT, rstd], writes=[tb])
                tr.group('act', [lambda kc=kc: A.activation(out=tb[:, kc, :N], in_=tb[:, kc, :N], func=AF.Identity, scale=fing[:, kc:kc + 1])
                                 for kc in range(KC)], reads=[tb, fing], writes=[tb])
                for ts in range(N // 128):
                    o_ = ost[0]; cnt['ost'] += 1
                    for hf in range(2):
                        pt = R['ps_g'][hf]
                        tr.group('pe', [lambda k=k: PE.transpose(pt[:, k * 128:(k + 1) * 128], tb[:, hf * 4 + k, ts * 128:(ts + 1) * 128], ident[:])
                                        for k in range(4)], reads=[tb, ident], writes=[pt])
                        tr.op('act' if hf == 0 else 'dve',
                              (lambda: A.activation(out=o_[:, 0:512], in_=pt[:], func=AF.Copy)) if hf == 0 else (lambda: V.tensor_copy(out=o_[:, 512:1024], in_=pt[:])),
                              reads=[pt], writes=[o_] if hf == 0 else (), acc=() if hf == 0 else [o_])
                    r0 = c0 - NCX + ts * 128
                    tr.dma('pool', out_d[r0:r0 + 128, :], o_[:], reads=[o_])
        ph.close()

    epsb = gp.sb('epsb', [128, 1]); tr.op('dve', lambda: V.memset(epsb[:], 1e-6), writes=[epsb])
    tr.barrier()

    def run():
        nonlocal stg_o, pcbig
        stg_o = [gp.sb('stgo%d' % i, [128, 512], F32, dma=True) for i in range(2)]
        pcbig = gp.sb('pcbig', [128, 256], BF16)
        if only is not None:
            {'B': phase_B}[only[0]](only[1]); return
        for l in range(DEPTH):
            ctx_out = l < DEPTH - 1
            compute_mod(l)
            if stop_after == ('mod', l): return
            phase_A(l)
            if stop_after == ('A', l): return
            emit_casts(l, G2)
            phase_B(l)
            if stop_after == ('B', l): return
            if l + 1 < DEPTH: emit_casts(l + 1, G1)
            phase_C(l, ctx_out)
            if stop_after == ('C', l): return
            phase_D(l, ctx_out)
            if stop_after == ('D', l): return
            phase_E(l, ctx_out, l == DEPTH - 1)
            if stop_after == ('E', l): return

    run()
    tr.barrier()
    gp.es.close()
    tr.es.close()
    return nc


def prep_shared(inp):
    sh = {}
    L = DEPTH
    sh['wgu1'] = np.stack([np.stack([tile_w(inp['ffn1_wg'][l]), tile_w(inp['ffn1_wu'][l])], 2) for l in range(L)])
    sh['wd1'] = np.stack([tile_w(inp['ffn1_wd'][l]) for l in range(L)])
    sh['wgu2'] = np.stack([np.stack([tile_w(inp['ffn2_wg'][l]), tile_w(inp['ffn2_wu'][l])], 2) for l in range(L)])
    sh['wd2'] = np.stack([tile_w(inp['ffn2_wd'][l]) for l in range(L)])
    cols = win_fm_cols()
    sh['winfm'] = np.stack([tile_w(inp['w_in'][l][:, cols]) for l in range(L)])
    tmc = np.concatenate([IN_OFF['gv'] + np.arange(128), IN_OFF['nv'] + np.arange(512)])
    sh['wintm'] = np.stack([np.ascontiguousarray(inp['w_in'][l][:, tmc].reshape(KC, 128, 640).transpose(1, 0, 2)) for l in range(L)])
    sh['wada'] = np.stack([tile_w(inp['w_ada'][l]) for l in range(L)])
    sh['wglu'] = np.stack([tile_w(inp['ssm_w_glu'][l]) for l in range(L)])
    sh['wps'] = np.stack([tile_w(inp['w_p_ssm'][l]) for l in range(L)])
    sh['wpg'] = np.stack([tile_w(inp['w_p_gqa'][l]) for l in range(L)])
    sh['wpn'] = np.stack([tile_w(inp['w_p_na'][l]) for l in range(L)])
    sh['wout'] = np.stack([tile_w(inp['w_out'][l]) for l in range(L)])
    sh['bada'] = np.ascontiguousarray(inp['b_ada'].reshape(L, 72, 128).transpose(0, 2, 1))
    sh['normg'] = np.ascontiguousarray(inp['norm_g'].reshape(L, 3, KC, 128).transpose(0, 3, 1, 2))
    sh['fing'] = np.ascontiguousarray(inp['final_g'].reshape(KC, 128).T)

    def st(a):
        return a.reshape(L, 2, 12, 2, 64).transpose(0, 1, 3, 4, 2).reshape(L, 2, 128, 12)
    ldt = np.broadcast_to(inp['ssm_log_dt'][:, :, :, None], (L, 2, 24, 64))
    sh['ssm_a'] = np.ascontiguousarray(np.stack([st(inp['ssm_a_re']), st(inp['ssm_a_im']), st(ldt)], 3))

    def sbt(a):
        return a.reshape(L, 2, 12, 2, 64, 16).transpose(0, 1, 3, 4, 2, 5).reshape(L, 2, 128, 12, 16)
    sh['ssm_b'] = np.ascontiguousarray(np.stack([sbt(inp['ssm_b_re']), sbt(inp['ssm_b_im'])], 2))

    def sct(a):
        return a.reshape(L, 2, 3, 8, 16, 64).transpose(0, 1, 3, 4, 2, 5).reshape(L, 2, 128, 3, 64)
    sh['ssm_c'] = np.ascontiguousarray(np.stack([sct(inp['ssm_c_re']), sct(inp['ssm_c_im'])], 2))
    sh['ssm_d'] = np.ascontiguousarray(inp['ssm_d'].reshape(L, 3, 128).transpose(0, 2, 1))
    sh['sink'] = np.ascontiguousarray(np.broadcast_to(inp['gqa_sink'][:, None, :], (L, 64, 8)))
    vt, od, cb = na_bias_tables(inp['na_rpb'])
    sh['navt'] = vt; sh['naod'] = od; sh['nacb'] = cb.reshape(L, 8, 128, 5, 128)
    sh.update(host_consts())
    return {k: np.ascontiguousarray(v, dtype=np.float32) for k, v in sh.items()}


def core_inputs(inp, b, sh):
    m = dict(sh)
    m['x'] = np.ascontiguousarray(inp['x'][b]); m['ctx'] = np.ascontiguousarray(inp['ctx'][b])
    sv = np.stack([inp['c'][b].reshape(KC, 128).T, inp['c_ctx'].reshape(KC, 128).T], 2)
    m['svec'] = np.ascontiguousarray(sv, dtype=np.float32)
    return m


def kernel(**inputs):
    inp = {k: np.asarray(v) for k, v in inputs.items()}
    sh = prep_shared(inp)
    nc = build()
    in_maps = [core_inputs(inp, b, sh) for b in range(8)]
    res = run_bass_kernel_spmd(nc, in_maps, core_ids=list(range(8)))
    return np.stack([np.asarray(r['out'], dtype=np.float32) for r in res.results], 0)
```

```python
import contextlib, math, os
import numpy as np
import ml_dtypes
import concourse.bass as bass
import concourse.mybir as mybir
from concourse.bass_utils import run_bass_kernel_spmd

F32 = mybir.dt.float32; BF16 = mybir.dt.bfloat16; I32 = mybir.dt.int32
AF = mybir.ActivationFunctionType; ALU = mybir.AluOpType

D = 1024; T = 4096; NCX = 256; TT = T + NCX; FF = 2816; KC = 8; FC = 22; DEPTH = 2
NEG = -30000.0
SAME_SYNC = True
PI = math.pi


class Sem:
    def __init__(s, h, name): s.h = h; s.total = 0; s.name = name


class Eng:
    def __init__(s, name, obj, sem): s.name = name; s.obj = obj; s.sem = sem; s.known = {}


class Buf:
    def __init__(s, name, t=None, dsem=None, qsem=None):
        s.name = name; s.t = t; s.w = []; s.r = []; s.pre = []; s.dsem = dsem; s.qsem = qsem

    def __getitem__(s, k): return s.t[k]


class Trk:
    def __init__(self, nc):
        self.nc = nc
        self.es = contextlib.ExitStack()
        self.sems = []
        self.E = {}
        for n, o in (('pe', nc.tensor), ('act', nc.scalar), ('dve', nc.vector), ('pool', nc.gpsimd), ('sp', nc.sync)):
            self.E[n] = Eng(n, o, self.new_sem('e_' + n) if n != 'sp' else None)
        self.dpool = []; self.dnext = 0; self.uid = 0; self.qpool = []; self.qnext = 0

    def new_sem(self, name):
        s = Sem(self.es.enter_context(self.nc.semaphore(name)), name); self.sems.append(s); return s

    def dsem(self):
        if self.dnext >= len(self.dpool): self.dpool.append(self.new_sem('d%d' % len(self.dpool)))
        s = self.dpool[self.dnext]; self.dnext += 1; return s

    def qsem(self):
        if self.qnext >= len(self.qpool): self.qpool.append(self.new_sem('q%d' % len(self.qpool)))
        s = self.qpool[self.qnext]; self.qnext += 1; return s

    def _wait(self, eng, evs):
        need = {}
        for (sem, val, src) in evs:
            if src == eng.name and (src == 'pe' or not SAME_SYNC): continue
            if eng.known.get(sem, 0) >= val: continue
            need[sem] = max(need.get(sem, 0), val)
        for sem, val in need.items():
            eng.obj.wait_ge(sem.h, val); eng.known[sem] = val

    def _pre(self, eng, reads, writes, acc):
        evs = []
        for b in reads: evs += b.w
        for b in writes:
            b.pre = b.w + b.r; evs += b.pre
        for b in acc: evs += b.pre
        self._wait(eng, evs)

    def _post(self, ev, reads, writes, acc):
        for b in reads: b.r.append(ev)
        for b in writes: b.w = [ev]; b.r = []
        for b in acc: b.w.append(ev)

    def group(self, en, fns, reads=(), writes=(), acc=()):
        eng = self.E[en]
        self._pre(eng, reads, writes, acc)
        ins = None
        for f in fns: ins = f()
        eng.sem.total += 1
        ins.then_inc(eng.sem.h, 1)
        self._post((eng.sem, eng.sem.total, en), reads, writes, acc)

    def op(self, en, fn, reads=(), writes=(), acc=()):
        self.group(en, [fn], reads, writes, acc)

    def dma(self, q, out, in_, reads=(), writes=(), acc=(), sem=None):
        eng = self.E[q]
        self._pre(eng, reads, writes, acc)
        if sem is None:
            for b in list(writes) + list(acc) + list(reads):
                if b.dsem is not None: sem = (b.qsem if q == 'pool' else b.dsem); break
        ins = eng.obj.dma_start(out=out, in_=in_)
        sem.total += 16
        ins.then_inc(sem.h, 16)
        self._post((sem, sem.total, 'dma'), reads, writes, acc)

    def barrier(self):
        for eng in self.E.values():
            for s in self.sems:
                if s.total > 0 and eng.known.get(s, 0) < s.total:
                    eng.obj.wait_ge(s.h, s.total); eng.known[s] = s.total


def tile_w(W, kp=128):
    K, M = W.shape
    return np.ascontiguousarray(W.reshape(K // kp, kp, M // 128, 128).transpose(2, 1, 0, 3))


IN_OFF = dict(u=0, gq=384, gk=896, gv=1024, nq=1152, nk=1664, nv=2176, gates=2688)
ROT_PERM = np.concatenate([np.arange(16, 32), np.arange(0, 16), np.arange(48, 64), np.arange(32, 48)])
FM_CHUNKS = ([('u', i) for i in range(3)] + [('gq', i) for i in range(4)] + [('gq2', i) for i in range(4)]
             + [('gk', 0), ('gk2', 0)] + [('nq', i) for i in range(4)] + [('nk', i) for i in range(4)]
             + [('gates', i) for i in range(24)])


def win_fm_cols():
    cols = []
    for kind, i in FM_CHUNKS:
        if kind == 'u': c = IN_OFF['u'] + i * 128 + np.arange(128)
        elif kind == 'gq': c = IN_OFF['gq'] + i * 128 + np.arange(128)
        elif kind == 'gq2': c = IN_OFF['gq'] + i * 128 + np.concatenate([ROT_PERM, 64 + ROT_PERM])
        elif kind == 'gk': c = IN_OFF['gk'] + np.arange(128)
        elif kind == 'gk2': c = IN_OFF['gk'] + np.concatenate([ROT_PERM, 64 + ROT_PERM])
        elif kind == 'nq': c = IN_OFF['nq'] + i * 128 + np.arange(128)
        elif kind == 'nk': c = IN_OFF['nk'] + i * 128 + np.arange(128)
        else: c = IN_OFF['gates'] + i * 128 + np.arange(128)
        cols.append(c)
    return np.concatenate(cols)


def rope_tables():
    t = np.arange(T)
    pos = np.stack([t // 64, t % 64], 0).astype(np.float32)
    inv = (10000.0 ** (-np.arange(0, 32, 2, dtype=np.float32) / 32)).astype(np.float32)
    C = np.zeros((64, T), np.float32); S = np.zeros((64, T), np.float32)
    for ax in range(2):
        ang = (pos[ax][None, :] * inv[:, None]).astype(np.float32)
        for half in range(2):
            sl = slice(ax * 32 + half * 16, ax * 32 + half * 16 + 16)
            C[sl] = np.cos(ang)
            S[sl] = -np.sin(ang) if half == 0 else np.sin(ang)
    C2 = np.concatenate([C, C], 0); S2 = np.concatenate([S, S], 0)
    return np.stack([C2 * 0.125, S2 * 0.125, C2, S2], 0).astype(np.float32)


def na_bias_tables(rpb):
    L = rpb.shape[0]
    kc = np.arange(64)[:, None]; qc = np.arange(64)[None, :]
    cs = np.clip(qc - 8, 0, 48)
    valid = (kc >= cs) & (kc < cs + 16)
    idx = np.clip(kc - qc + 15, 0, 30)
    tab = np.where(valid[None, None, None], rpb[:, :, :, idx], np.float32(NEG)).astype(np.float32)
    negt = np.full((L, 8, 64, 64), NEG, np.float32)
    VT = np.zeros((L, 8, 128, 14, 64), np.float32)
    for d in range(-7, 7):
        VT[:, :, 0:64, d + 7] = tab[:, :, d + 7]
        VT[:, :, 64:128, d + 7] = tab[:, :, d + 8]
    OD = np.zeros((L, 8, 128, 5, 64), np.float32)
    for k, d in enumerate((-5, -3, -1, 1, 3)):
        OD[:, :, 0:64, k] = negt if d == -5 else tab[:, :, d + 7]
        OD[:, :, 64:128, k] = negt if d == 3 else tab[:, :, d + 8]
    CB = np.zeros((L, 8, 128, 5, 2, 64), np.float32)
    for k in range(5):
        CB[:, :, :, k, 0, :] = VT[:, :, :, 3 + 2 * k, :] if k < 4 else np.float32(NEG)
        CB[:, :, :, k, 1, :] = OD[:, :, :, k, :]
    return VT, OD, CB


def host_consts():
    c = {}
    c['ident'] = np.eye(128, dtype=np.float32)
    c['rope'] = rope_tables()
    k = np.arange(128)[:, None]; q = np.arange(128)[None, :]
    c['gmask'] = np.stack([np.where(k >= q, 0.0, NEG), np.where(k <= q, 0.0, NEG)], 1).astype(np.float32)
    p = np.arange(128)
    m2 = np.zeros((128, 4, 8, 16), np.float32)
    mc = np.zeros((128, 4, 2, 64), np.float32)
    for qq in range(4):
        for pp in range(128):
            m2[pp, qq, 2 * qq + (pp >= 64), :] = 1.0
            for half in range(2):
                if pp // 16 == 2 * qq + half: mc[pp, qq, half, :] = 1.0
    c['mask2'] = m2; c['maskc'] = mc
    io = np.zeros((128, 2, 64), np.float32)
    io[:, 0, :] = np.arange(1, 65)[None, :]; io[:, 1, :] = np.arange(64, 0, -1)[None, :]
    c['iota64'] = io
    c['iota9'] = np.broadcast_to(np.arange(9, dtype=np.float32)[None, :], (128, 9)).copy()
    return c


def build(dbg=(), stop_after=None, only=None, ext_in=()):
    nc = bass.Bass("TRN2", target_bir_lowering=False)
    tr = Trk(nc)
    dbg = set(dbg)

    def din(name, shape, dt=F32):
        return nc.dram_tensor(name, list(shape), dt, kind="ExternalInput").ap()

    def dscr(name, shape, dt):
        if name in ext_in: return nc.dram_tensor(name, list(shape), dt, kind="ExternalInput").ap()
        if name in dbg: return nc.dram_tensor(name, list(shape), dt, kind="ExternalOutput").ap()
        return nc.dram_tensor(name, list(shape), dt).ap()

    x_in = din('x', [T, D]); ctx_in = din('ctx', [NCX, D]); svec_in = din('svec', [128, KC, 2])
    out_d = nc.dram_tensor('out', [T, D], F32, kind="ExternalOutput").ap()
    WSH = dict(wgu1=[FC, 128, 2, KC, 128], wd1=[KC, 128, FC, 128], wgu2=[FC, 128, 2, KC, 128], wd2=[KC, 128, FC, 128],
               winfm=[45, 128, KC, 128], wintm=[128, KC, 640], wada=[72, 128, KC, 128], wglu=[3, 128, 3, 128],
               wps=[8, 128, 3, 128], wpg=[8, 128, 4, 128], wpn=[8, 128, 4, 128], wout=[8, 128, KC, 128])
    WORDER = ['wada', 'wgu1', 'wd1', 'winfm', 'wintm', 'wglu', 'wps', 'wpg', 'wpn', 'wout', 'wgu2', 'wd2']
    w_f = {k: din(k, [DEPTH] + v) for k, v in WSH.items()}
    w_b = {(k, l): dscr('%s_b%d' % (k, l), v, BF16) for k, v in WSH.items() for l in range(DEPTH)}
    w_buf = {(k, l): Buf('wb_%s%d' % (k, l)) for k in WSH for l in range(DEPTH)}
    bada_in = din('bada', [DEPTH, 128, 72]); normg_in = din('normg', [DEPTH, 128, 3, KC]); fing_in = din('fing', [128, KC])
    sa_in = din('ssm_a', [DEPTH, 2, 128, 3, 12])
    sb_in = din('ssm_b', [DEPTH, 2, 2, 128, 12, 16]); sc_in = din('ssm_c', [DEPTH, 2, 2, 128, 3, 64])
    sd_in = din('ssm_d', [DEPTH, 128, 3]); sink_in = din('sink', [DEPTH, 64, 8])
    navt_in = din('navt', [DEPTH, 8, 128, 14, 64]); naod_in = din('naod', [DEPTH, 8, 128, 5, 64]); nacb_in = din('nacb', [DEPTH, 8, 128, 5, 128])
    ident_in = din('ident', [128, 128]); rope_in = din('rope', [4, 128, T]); gmask_in = din('gmask', [128, 2, 128])
    mask2_in = din('mask2', [128, 4, 8, 16]); maskc_in = din('maskc', [128, 4, 2, 64]); iota64_in = din('iota64', [128, 2, 64]); iota9_in = din('iota9', [128, 9])

    XT = dscr('XT', [D, TT], F32)
    U = dscr('U', [384, TT], F32); TG = dscr('TG', [384, TT], F32)
    QG = dscr('QG', [8, 64, T], BF16); QGC = dscr('QGC', [8, 64, NCX], BF16)
    KG = dscr('KG', [2, 64, T], BF16); KGC = dscr('KGC', [2, 64, NCX], BF16)
    QN = dscr('QN', [8, 64, T], BF16); QNC = dscr('QNC', [8, 64, NCX], BF16)
    KN = dscr('KN', [8, 64, T], BF16); KNC = dscr('KNC', [8, 64, NCX], BF16)
    VG = dscr('VG', [TT, 128], BF16); VN = dscr('VN', [TT, 512], BF16)
    SG = dscr('SG', [3072, TT], BF16)
    YG = dscr('YG', [8, 64, T], BF16); YGC = dscr('YGC', [8, 64, NCX], BF16)
    YN = dscr('YN', [8, 64, T], BF16); YNC = dscr('YNC', [8, 64, NCX], BF16)

    def uname(n):
        tr.uid += 1; return '%s_%d' % (n, tr.uid)

    class Phase:
        def __init__(s, reset=True):
            s.es = contextlib.ExitStack()
            if reset: tr.dnext = 0; tr.qnext = 0

        def sb(s, name, shape, dt=F32, dma=False):
            t = s.es.enter_context(nc.sbuf_tensor(uname(name), list(shape), dt))
            return Buf(name, t, tr.dsem() if dma else None, tr.qsem() if dma else None)

        def ps(s, name, shape=(128, 512), dt=F32):
            t = s.es.enter_context(nc.psum_tensor(uname(name), list(shape), dt))
            return Buf(name, t)

        def close(s):
            tr.barrier(); s.es.close()

    V = nc.vector; A = nc.scalar; G = nc.gpsimd; PE = nc.tensor

    G1 = ['wada', 'wgu1', 'wd1', 'winfm', 'wintm']
    G2 = ['wglu', 'wps', 'wpg', 'wpn', 'wout', 'wgu2', 'wd2']

    def emit_casts(l, keys):
        if only is not None: return
        for k in keys:
            sem = tr.new_sem('c_%s%d' % (k, l))
            n = int(np.prod(WSH[k]))
            src = w_f[k][l]; dst = w_b[(k, l)]
            names = 'abcde'[:len(WSH[k])]
            pat = ' '.join(names)
            fs = src.rearrange('%s -> (%s)' % (pat, pat)).rearrange('(p n) -> p n', p=128)
            fd = dst.rearrange('%s -> (%s)' % (pat, pat)).rearrange('(p n) -> p n', p=128)
            cols = n // 128
            npieces = max(1, -(-cols // 16384))
            step = -(-cols // npieces)
            first = True
            for c0 in range(0, cols, step):
                c1 = min(cols, c0 + step)
                tr.dma('pool', fd[:, c0:c1], fs[:, c0:c1], writes=[w_buf[(k, l)]] if first else (),
                       acc=() if first else [w_buf[(k, l)]], sem=sem)
                first = False

    emit_casts(0, G1)

    gp = Phase()
    ident = gp.sb('ident', [128, 128], F32, dma=True)
    tr.dma('sp', ident[:], ident_in[:, :], writes=[ident])
    ones_b = gp.sb('ones_b', [128, 128], BF16)
    tr.op('dve', lambda: V.memset(ones_b[:], 1.0), writes=[ones_b])
    svec = gp.sb('svec', [128, KC, 2], F32, dma=True)
    tr.dma('sp', svec[:], svec_in[:, :, :], writes=[svec])
    svb = gp.sb('svb', [128, KC, 2], BF16)
    tr.op('act', lambda: A.activation(out=svb[:], in_=svec[:], func=AF.Silu), reads=[svec], writes=[svb])
    fing = gp.sb('fing', [128, KC], F32, dma=True)
    tr.dma('sp', fing[:], fing_in[:, :], writes=[fing])
    modA = gp.sb('modA', [128, 3, KC, 2]); modB = gp.sb('modB', [128, 3, KC, 2]); modG = gp.sb('modG', [128, 3, KC, 2])

    def compute_mod(l):
        ph = Phase()
        bada = ph.sb('bada', [128, 72], F32, dma=True); tr.dma('sp', bada[:], bada_in[l], writes=[bada])
        normg = ph.sb('normg', [128, 3, KC], F32, dma=True); tr.dma('sp', normg[:], normg_in[l], writes=[normg])
        wr = [ph.sb('wada%d' % i, [128, 8, KC, 128], BF16, dma=True) for i in range(2)]
        mps = ph.ps('mps', [128, 72, 2])
        mod = ph.sb('mod', [128, 72, 2])
        for jg in range(9):
            wbuf = wr[jg % 2]
            tr.dma('sp', wbuf[:], w_b[('wada', l)][jg * 8:(jg + 1) * 8].rearrange('j p k c -> p j k c'),
                   reads=[w_buf[('wada', l)]], writes=[wbuf])
            fns = []
            for jj in range(8):
                j = jg * 8 + jj
                for kc in range(KC):
                    fns.append(lambda j=j, jj=jj, kc=kc: PE.matmul(mps[:, j, :], lhsT=wbuf[:, jj, kc, :], rhs=svb[:, kc, :],
                                                                  start=(kc == 0), stop=(kc == KC - 1)))
            tr.group('pe', fns, reads=[wbuf, svb], writes=[mps] if jg == 0 else (), acc=() if jg == 0 else [mps])
        tr.op('dve', lambda: V.tensor_tensor(out=mod[:], in0=mps[:], in1=bada[:].unsqueeze(2).to_broadcast([128, 72, 2]), op=ALU.add),
              reads=[mps, bada], writes=[mod])
        for i in range(3):
            sh = mod[:, 8 * (3 * i):8 * (3 * i) + 8, :]; sc = mod[:, 8 * (3 * i + 1):8 * (3 * i + 1) + 8, :]
            gt = mod[:, 8 * (3 * i + 2):8 * (3 * i + 2) + 8, :]
            gb = normg[:, i, :].unsqueeze(2).to_broadcast([128, KC, 2])
            fns = [lambda sc=sc, i=i, gb=gb: V.scalar_tensor_tensor(out=modA[:, i], in0=sc, scalar=1.0, in1=gb, op0=ALU.add, op1=ALU.mult),
                   lambda sh=sh, i=i: V.tensor_copy(out=modB[:, i], in_=sh),
                   lambda gt=gt, i=i: V.tensor_scalar(out=modG[:, i], in0=gt, scalar1=(1.0 if i == 1 else 0.5), scalar2=None, op0=ALU.mult)]
            tr.group('dve', fns, reads=[mod, normg], writes=[modA, modB, modG] if i == 0 else (), acc=() if i == 0 else [modA, modB, modG])
        ph.close()

    def rms_mod(ph, xT, hT, N, site, col, ps_ss, tmpbig, sq, rstd):
        tr.op('act', lambda: A.activation(out=sq[:, :, :N], in_=xT[:, :, :N], func=AF.Square), reads=[xT], writes=[sq])
        tr.group('pe', [lambda kc=kc: PE.matmul(ps_ss[:, :N], lhsT=ones_b[:], rhs=sq[:, kc, :N], start=(kc == 0), stop=(kc == KC - 1))
                        for kc in range(KC)], reads=[sq, ones_b], writes=[ps_ss])
        tr.op('act', lambda: A.activation(out=rstd[:, :N], in_=ps_ss[:, :N], func=AF.Sqrt, scale=1.0 / D, bias=epsb[:, 0:1]),
              reads=[ps_ss, epsb], writes=[rstd])
        tr.op('dve', lambda: V.reciprocal(out=rstd[:, :N], in_=rstd[:, :N]), reads=[rstd], writes=[rstd])
        tr.op('dve', lambda: V.tensor_tensor(out=tmpbig[:, :, :N], in0=xT[:, :, :N],
                                             in1=rstd[:, :N].unsqueeze(1).to_broadcast([128, KC, N]), op=ALU.mult),
              reads=[xT, rstd], writes=[tmpbig])
        tr.group('act', [lambda kc=kc: A.activation(out=hT[:, kc, :N], in_=tmpbig[:, kc, :N], func=AF.Identity,
                                                    scale=modA[:, site, kc, col:col + 1], bias=modB[:, site, kc, col:col + 1])
                         for kc in range(KC)], reads=[tmpbig, modA, modB], writes=[hT])

    def ffn(ph, l, which, xT, hT, aT, N, col, site, R):
        wgu_k, wd_k = ('wgu1', 'wd1') if which == 1 else ('wgu2', 'wd2')
        rms_mod(ph, xT, hT, N, site, col, R['ps_m'][0], R['tmpbig'], R['sq'], R['rstd'])
        for f in range(FC):
            wb = R['wgu'][R['i_wgu'] % 3]; R['i_wgu'] += 1
            tr.dma('sp', wb[:], w_b[(wgu_k, l)][f], reads=[w_buf[(wgu_k, l)]], writes=[wb])
            pg = R['ps_g'][f % 2]; pu = R['ps_u'][f % 2]
            tr.group('pe', [lambda kc=kc: PE.matmul(pg[:, :N], lhsT=wb[:, 0, kc, :], rhs=hT[:, kc, :N], start=(kc == 0), stop=(kc == KC - 1))
                            for kc in range(KC)], reads=[wb, hT], writes=[pg])
            tr.group('pe', [lambda kc=kc: PE.matmul(pu[:, :N], lhsT=wb[:, 1, kc, :], rhs=hT[:, kc, :N], start=(kc == 0), stop=(kc == KC - 1))
                            for kc in range(KC)], reads=[wb, hT], writes=[pu])
            sg = R['sgt'][f % 2]
            tr.op('act', lambda: A.activation(out=sg[:, :N], in_=pg[:, :N], func=AF.Silu), reads=[pg], writes=[sg])
            tr.op('dve', lambda: V.tensor_tensor(out=aT[:, f, :N], in0=sg[:, :N], in1=pu[:, :N], op=ALU.mult),
                  reads=[sg, pu], writes=[aT] if f == 0 else (), acc=() if f == 0 else [aT])
        for m in range(KC):
            wb = R['wd'][R['i_wd'] % 2]; R['i_wd'] += 1
            tr.dma('sp', wb[:], w_b[(wd_k, l)][m], reads=[w_buf[(wd_k, l)]], writes=[wb])
            pd = R['ps_m'][m % 2]
            tr.group('pe', [lambda f=f: PE.matmul(pd[:, :N], lhsT=wb[:, f, :], rhs=aT[:, f, :N], start=(f == 0), stop=(f == FC - 1))
                            for f in range(FC)], reads=[wb, aT], writes=[pd])
            tr.op('dve', lambda: V.scalar_tensor_tensor(out=xT[:, m, :N], in0=pd[:, :N], scalar=modG[:, site, m, col:col + 1],
                                                        in1=xT[:, m, :N], op0=ALU.mult, op1=ALU.add),
                  reads=[pd, modG, xT], acc=[xT])

    def ffn_bufs(ph):
        R = {}
        R['wgu'] = [ph.sb('wgu%d' % i, [128, 2, KC, 128], BF16, dma=True) for i in range(3)]
        R['wd'] = [ph.sb('wd%d' % i, [128, FC, 128], BF16, dma=True) for i in range(2)]
        R['i_wgu'] = 0; R['i_wd'] = 0
        R['ps_g'] = [ph.ps('psg%d' % i) for i in range(2)]
        R['ps_u'] = [ph.ps('psu%d' % i) for i in range(2)]
        R['ps_m'] = [ph.ps('psm%d' % i) for i in range(2)]
        R['sgt'] = [ph.sb('sgt%d' % i, [128, 512]) for i in range(2)]
        R['tmpbig'] = ph.sb('tmpbig', [128, KC, 512]); R['sq'] = ph.sb('sq', [128, KC, 512], BF16)
        R['rstd'] = ph.sb('rstd', [128, 512])
        return R

    tiles = [(0, NCX, 1)] + [(NCX + i * 512, 512, 0) for i in range(8)]

    def phase_A(l):
        ph = Phase()
        R = ffn_bufs(ph)
        xTs = [ph.sb('xT%d' % i, [128, KC, 512], F32, dma=True) for i in range(2)]
        hT = ph.sb('hT', [128, KC, 512], BF16); aT = ph.sb('aT', [128, FC, 512], BF16)
        win = [ph.sb('win%d' % i, [128, KC, 128], BF16, dma=True) for i in range(3)]
        wtm = ph.sb('wtm', [128, KC, 640], BF16, dma=True)
        tr.dma('sp', wtm[:], w_b[('wintm', l)], reads=[w_buf[('wintm', l)]], writes=[wtm])
        rp = [ph.sb('rope%d' % i, [128, 4, 512], F32, dma=True) for i in range(2)]
        stf = [ph.sb('stf%d' % i, [128, 512], F32, dma=True) for i in range(2)]
        stb = [ph.sb('stb%d' % i, [128, 640], BF16, dma=True) for i in range(3)]
        t1 = ph.sb('t1', [128, 512]); t2 = ph.sb('t2', [128, 512])
        xin = [ph.sb('xin%d' % i, [128, D], F32, dma=True) for i in range(2)] if l == 0 else None
        ps_q = ph.ps('psq'); ps_q2 = ph.ps('psq2')
        cnt = dict(win=0, stf=0, stb=0, xin=0)

        def load_x(ti):
            c0, N, col = tiles[ti]; xT = xTs[ti % 2]
            if l > 0:
                tr.dma('sp', xT[:, :, :N], XT[:, c0:c0 + N].rearrange('(k p) t -> p k t', p=128), writes=[xT])
            else:
                for ts in range(N // 128):
                    xb = xin[cnt['xin'] % 2]; cnt['xin'] += 1
                    src = ctx_in[ts * 128:(ts + 1) * 128, :] if ti == 0 else x_in[c0 - NCX + ts * 128:c0 - NCX + (ts + 1) * 128, :]
                    tr.dma('sp', xb[:], src, writes=[xb])
                    for hf in range(2):
                        pt = R['ps_g'][hf]
                        tr.group('pe', [lambda k=k, hf=hf: PE.transpose(pt[:, k * 128:(k + 1) * 128], xb[:, (hf * 4 + k) * 128:(hf * 4 + k + 1) * 128], ident[:])
                                        for k in range(4)], reads=[xb, ident], writes=[pt])
                        tr.op('act', lambda hf=hf, pt=pt, ts=ts: A.activation(out=xT[:, hf * 4:hf * 4 + 4, ts * 128:(ts + 1) * 128],
                                                                               in_=pt[:].rearrange('p (k t) -> p k t', k=4), func=AF.Copy),
                              reads=[pt], writes=[xT] if (ts == 0 and hf == 0) else (), acc=() if (ts == 0 and hf == 0) else [xT])

        load_x(0)
        for ti in range(9):
            c0, N, col = tiles[ti]; xT = xTs[ti % 2]; lat = ti > 0
            if lat:
                rpb = rp[ti % 2]
                tr.dma('sp', rpb[:], rope_in[:, :, c0 - NCX:c0 - NCX + 512].rearrange('a p t -> p a t'), writes=[rpb])
            ffn(ph, l, 1, xT, hT, aT, N, col, 0, R)
            if ti + 1 < 9: load_x(ti + 1)
            rms_mod(ph, xT, hT, N, 1, col, R['ps_m'][0], R['tmpbig'], R['sq'], R['rstd'])
            tr.dma('pool', XT[:, c0:c0 + N].rearrange('(k p) t -> p k t', p=128), xT[:, :, :N], reads=[xT])
            ci = 0
            while ci < len(FM_CHUNKS):
                kind, i = FM_CHUNKS[ci]

                def wmm(ci, pbuf):
                    wb = win[cnt['win'] % 3]; cnt['win'] += 1
                    tr.dma('sp', wb[:], w_b[('winfm', l)][ci], reads=[w_buf[('winfm', l)]], writes=[wb])
                    tr.group('pe', [lambda kc=kc: PE.matmul(pbuf[:, :N], lhsT=wb[:, kc, :], rhs=hT[:, kc, :N], start=(kc == 0), stop=(kc == KC - 1))
                                    for kc in range(KC)], reads=[wb, hT], writes=[pbuf])

                if kind in ('gq', 'gk') and lat:
                    ci2 = ci + (4 if kind == 'gq' else 1)
                    wmm(ci, ps_q); wmm(ci2, ps_q2)
                    o = 0 if kind == 'gq' else 2
                    st = stb[cnt['stb'] % 3]; cnt['stb'] += 1
                    tr.op('dve', lambda: V.tensor_tensor(out=t1[:], in0=ps_q[:], in1=rpb[:, o, :], op=ALU.mult), reads=[ps_q, rpb], writes=[t1])
                    tr.op('dve', lambda: V.tensor_tensor(out=t2[:], in0=ps_q2[:], in1=rpb[:, o + 1, :], op=ALU.mult), reads=[ps_q2, rpb], writes=[t2])
                    tr.op('pool', lambda: G.tensor_tensor(out=st[:, :512], in0=t1[:], in1=t2[:], op=ALU.add), reads=[t1, t2], writes=[st])
                    dst = QG[2 * i:2 * i + 2] if kind == 'gq' else KG[0:2]
                    tr.dma('pool', dst.rearrange('h d t -> (h d) t')[:, c0 - NCX:c0 - NCX + 512], st[:, :512], reads=[st])
                elif kind in ('gq2', 'gk2'):
                    pass
                else:
                    pb = R['ps_m'][ci % 2]
                    wmm(ci, pb)
                    if kind == 'u':
                        st = stf[cnt['stf'] % 2]; cnt['stf'] += 1
                        tr.op('act', lambda: A.activation(out=st[:, :N], in_=pb[:, :N], func=AF.Copy), reads=[pb], writes=[st])
                        tr.dma('pool', U[i * 128:(i + 1) * 128, c0:c0 + N], st[:, :N], reads=[st])
                    else:
                        st = stb[cnt['stb'] % 3]; cnt['stb'] += 1
                        if kind == 'gates':
                            tr.op('act', lambda: A.activation(out=st[:, :N], in_=pb[:, :N], func=AF.Sigmoid), reads=[pb], writes=[st])
                            tr.dma('pool', SG[i * 128:(i + 1) * 128, c0:c0 + N], st[:, :N], reads=[st])
                        else:
                            sc = 0.125 if kind in ('gq', 'nq') else 1.0
                            tr.op('act', lambda: A.activation(out=st[:, :N], in_=pb[:, :N], func=AF.Copy, scale=sc), reads=[pb], writes=[st])
                            if lat:
                                dst = {'nq': QN, 'nk': KN}[kind][2 * i:2 * i + 2].rearrange('h d t -> (h d) t')[:, c0 - NCX:c0 - NCX + 512]
                            else:
                                dd = {'gq': QGC, 'gk': KGC, 'nq': QNC, 'nk': KNC}[kind]
                                dst = (dd[2 * i:2 * i + 2] if kind != 'gk' else dd[0:2]).rearrange('h d t -> (h d) t')
                            tr.dma('pool', dst, st[:, :N], reads=[st])
                ci += 1
            for ts in range(N // 128):
                pv = R['ps_g'][ts % 2]; pv2 = R['ps_u'][ts % 2]
                tr.group('pe', [lambda kc=kc: PE.matmul(pv[:, :128], lhsT=hT[:, kc, ts * 128:(ts + 1) * 128], rhs=wtm[:, kc, 0:128],
                                                        start=(kc == 0), stop=(kc == KC - 1)) for kc in range(KC)], reads=[hT, wtm], writes=[pv])
                tr.group('pe', [lambda kc=kc: PE.matmul(pv2[:, :512], lhsT=hT[:, kc, ts * 128:(ts + 1) * 128], rhs=wtm[:, kc, 128:640],
                                                        start=(kc == 0), stop=(kc == KC - 1)) for kc in range(KC)], reads=[hT, wtm], writes=[pv2])
                st = stb[cnt['stb'] % 3]; cnt['stb'] += 1
                tr.op('act', lambda: A.activation(out=st[:, 0:128], in_=pv[:, 0:128], func=AF.Copy), reads=[pv], writes=[st])
                tr.op('dve', lambda: V.tensor_copy(out=st[:, 128:640], in_=pv2[:, :512]), reads=[pv2], acc=[st])
                r0 = c0 + ts * 128
                tr.dma('pool', VG[r0:r0 + 128, :], st[:, 0:128], reads=[st])
                tr.dma('pool', VN[r0:r0 + 128, :], st[:, 128:640], reads=[st])
        ph.close()

    def sin_rr(src, srcb, dst, dstb, shift, tA, tAb, tI, tIb):
        if shift:
            tr.op('dve', lambda: V.tensor_scalar(out=tA, in0=src, scalar1=shift, scalar2=None, op0=ALU.add), reads=[srcb], writes=[tAb])
            x, xb = tA, tAb
        else:
            x, xb = src, srcb
        tr.op('dve', lambda: V.tensor_scalar(out=tI, in0=x, scalar1=1.0 / (2 * PI), scalar2=None, op0=ALU.mult), reads=[xb], writes=[tIb])
        tr.op('dve', lambda: V.tensor_copy(out=dst, in_=tI), reads=[tIb], writes=[dstb])
        tr.op('dve', lambda: V.scalar_tensor_tensor(out=dst, in0=dst, scalar=-2 * PI, in1=x, op0=ALU.mult, op1=ALU.add), reads=[dstb, xb], writes=[dstb])
        tr.op('dve', lambda: V.tensor_scalar(out=dst, in0=dst, scalar1=-PI, scalar2=PI, op0=ALU.max, op1=ALU.min), reads=[dstb], writes=[dstb])
        tr.op('act', lambda: A.activation(out=dst, in_=dst, func=AF.Sin), reads=[dstb], writes=[dstb])

    def phase_B(l):
        ph = Phase()
        io64 = ph.sb('io64', [128, 2, 64], F32, dma=True); tr.dma('sp', io64[:], iota64_in[:, :, :], writes=[io64])
        io9 = ph.sb('io9', [128, 9], F32, dma=True); tr.dma('sp', io9[:], iota9_in[:, :], writes=[io9])
        m2 = ph.sb('mask2', [128, 4, 8, 16], F32, dma=True); tr.dma('sp', m2[:], mask2_in[:, :, :, :], writes=[m2])
        mcm = ph.sb('maskc', [128, 4, 2, 64], F32, dma=True); tr.dma('sp', mcm[:], maskc_in[:, :, :, :], writes=[mcm])
        sdt = ph.sb('sd', [128, 3], F32, dma=True); tr.dma('sp', sdt[:], sd_in[l], writes=[sdt])
        identb = ph.sb('identb', [128, 128], BF16)
        tr.op('act', lambda: A.activation(out=identb[:], in_=ident[:], func=AF.Copy), reads=[ident], writes=[identb])
        pst = ph.ps('pst', [128, 128])
        prm = {}
        BP = int(os.environ.get('BPREP', '99'))
        if BP == 1: ph.close(); return
        pers = {}
        for d in range(2):
            pers[d] = dict(w=ph.sb('w%d' % d, [128, 16, 12]), bb=ph.sb('bb%d' % d, [128, 2, 12, 16]), RK=ph.sb('rk%d' % d, [128, 12, 9]),
                           LRk=ph.sb('lrk%d' % d, [128, 12, 9]), LIk=ph.sb('lik%d' % d, [128, 12, 9]),
                           C2=ph.sb('c2_%d' % d, [128, 12, 2, 64]), S2=ph.sb('s2_%d' % d, [128, 12, 2, 64]),
                           scr=ph.sb('scr%d' % d, [128, 3, 64], F32, dma=True), sci=ph.sb('sci%d' % d, [128, 3, 64], F32, dma=True))
        pp = Phase(reset=False)
        A64 = pp.sb('a64', [128, 12, 64]); TA64 = pp.sb('ta64', [128, 12, 64]); TI64 = pp.sb('ti64', [128, 12, 64], I32)
        C64 = pp.sb('c64', [128, 12, 64]); S64 = pp.sb('s64', [128, 12, 64])
        for d in range(2):
            PD = pers[d]
            sa = pp.sb('sa%d' % d, [128, 3, 12], F32, dma=True); tr.dma('sp', sa[:], sa_in[l, d], writes=[sa])
            sbr = pp.sb('sbr%d' % d, [128, 12, 16], F32, dma=True); tr.dma('sp', sbr[:], sb_in[l, d, 0], writes=[sbr])
            sbi = pp.sb('sbi%d' % d, [128, 12, 16], F32, dma=True); tr.dma('sp', sbi[:], sb_in[l, d, 1], writes=[sbi])
            scr_ = PD['scr']; tr.dma('sp', scr_[:], sc_in[l, d, 0], writes=[scr_])
            sci = PD['sci']; tr.dma('sp', sci[:], sc_in[l, d, 1], writes=[sci])
            w = PD['w']
            ki = pp.sb('ki%d' % d, [128, 12], I32)
            are = sa[:, 0, :]; aim = sa[:, 1, :]; ldt = sa[:, 2, :]
            DT, ARDT, RR, TH, KF, THR, CX, CM, CO, SI, LR, LI = [w[:, i, :] for i in range(12)]
            N1, N2, DEN, T3 = [w[:, 12 + i, :] for i in range(4)]
            tr.op('act', lambda: A.activation(out=DT, in_=ldt, func=AF.Exp), reads=[sa], writes=[w])
            tr.op('dve', lambda: V.tensor_tensor(out=ARDT, in0=are, in1=DT, op=ALU.mult), reads=[w, sa], acc=[w])
            tr.op('act', lambda: A.activation(out=RR, in_=ARDT, func=AF.Exp), reads=[w], acc=[w])
            tr.op('dve', lambda: V.tensor_tensor(out=TH, in0=aim, in1=DT, op=ALU.mult), reads=[w, sa], acc=[w])
            tr.op('dve', lambda: V.tensor_scalar(out=ki[:], in0=TH, scalar1=1.0 / (2 * PI), scalar2=None, op0=ALU.mult), reads=[w], writes=[ki])
            tr.op('dve', lambda: V.tensor_copy(out=KF, in_=ki[:]), reads=[ki], acc=[w])
            tr.op('dve', lambda: V.scalar_tensor_tensor(out=THR, in0=KF, scalar=-2 * PI, in1=TH, op0=ALU.mult, op1=ALU.add), reads=[w], acc=[w])
            tr.op('dve', lambda: V.tensor_scalar(out=THR, in0=THR, scalar1=-PI, scalar2=PI, op0=ALU.max, op1=ALU.min), reads=[w], acc=[w])
            tr.op('dve', lambda: V.tensor_scalar(out=CX, in0=THR, scalar1=PI / 2, scalar2=None, op0=ALU.add), reads=[w], acc=[w])
            tr.op('dve', lambda: V.tensor_scalar(out=CM, in0=CX, scalar1=PI, scalar2=-2 * PI, op0=ALU.is_gt, op1=ALU.mult), reads=[w], acc=[w])
            tr.op('dve', lambda: V.tensor_tensor(out=CX, in0=CX, in1=CM, op=ALU.add), reads=[w], acc=[w])
            tr.op('dve', lambda: V.tensor_scalar(out=CX, in0=CX, scalar1=-PI, scalar2=PI, op0=ALU.max, op1=ALU.min), reads=[w], acc=[w])
            tr.op('act', lambda: A.activation(out=CO, in_=CX, func=AF.Sin), reads=[w], acc=[w])
            tr.op('act', lambda: A.activation(out=SI, in_=THR, func=AF.Sin), reads=[w], acc=[w])
            tr.op('dve', lambda: V.tensor_tensor(out=LR, in0=RR, in1=CO, op=ALU.mult), reads=[w], acc=[w])
            tr.op('dve', lambda: V.tensor_scalar(out=LR, in0=LR, scalar1=-1.0, scalar2=None, op0=ALU.add), reads=[w], acc=[w])
            tr.op('dve', lambda: V.tensor_tensor(out=LI, in0=RR, in1=SI, op=ALU.mult), reads=[w], acc=[w])
            tr.op('dve', lambda: V.tensor_tensor(out=N1, in0=LR, in1=are, op=ALU.mult), reads=[w, sa], acc=[w])
            tr.op('dve', lambda: V.tensor_tensor(out=T3, in0=LI, in1=aim, op=ALU.mult), reads=[w, sa], acc=[w])
            tr.op('dve', lambda: V.tensor_tensor(out=N1, in0=N1, in1=T3, op=ALU.add), reads=[w], acc=[w])
            tr.op('dve', lambda: V.tensor_tensor(out=N2, in0=LI, in1=are, op=ALU.mult), reads=[w, sa], acc=[w])
            tr.op('dve', lambda: V.tensor_tensor(out=T3, in0=LR, in1=aim, op=ALU.mult), reads=[w, sa], acc=[w])
            tr.op('dve', lambda: V.tensor_tensor(out=N2, in0=N2, in1=T3, op=ALU.subtract), reads=[w], acc=[w])
            tr.op('dve', lambda: V.tensor_tensor(out=DEN, in0=are, in1=are, op=ALU.mult), reads=[w, sa], acc=[w])
            tr.op('dve', lambda: V.tensor_tensor(out=T3, in0=aim, in1=aim, op=ALU.mult), reads=[w, sa], acc=[w])
            tr.op('dve', lambda: V.tensor_tensor(out=DEN, in0=DEN, in1=T3, op=ALU.add), reads=[w], acc=[w])
            tr.op('dve', lambda: V.reciprocal(out=DEN, in_=DEN), reads=[w], acc=[w])
            tr.op('dve', lambda: V.tensor_tensor(out=N1, in0=N1, in1=DEN, op=ALU.mult), reads=[w], acc=[w])
            tr.op('dve', lambda: V.tensor_tensor(out=N2, in0=N2, in1=DEN, op=ALU.mult), reads=[w], acc=[w])
            bb = PD['bb']; tb = pp.sb('tb%d' % d, [128, 2, 12, 16])
            cre = N1.unsqueeze(2).to_broadcast([128, 12, 16]); cim = N2.unsqueeze(2).to_broadcast([128, 12, 16])
            tr.op('dve', lambda: V.tensor_tensor(out=bb[:, 0], in0=sbr[:], in1=cre, op=ALU.mult), reads=[w, sbr], writes=[bb])
            tr.op('dve', lambda: V.tensor_tensor(out=tb[:, 0], in0=sbi[:], in1=cim, op=ALU.mult), reads=[w, sbi], writes=[tb])
            tr.op('dve', lambda: V.tensor_tensor(out=bb[:, 0], in0=bb[:, 0], in1=tb[:, 0], op=ALU.subtract), reads=[bb, tb], acc=[bb])
            tr.op('dve', lambda: V.tensor_tensor(out=bb[:, 1], in0=sbi[:], in1=cre, op=ALU.mult), reads=[w, sbi], acc=[bb])
            tr.op('dve', lambda: V.tensor_tensor(out=tb[:, 1], in0=sbr[:], in1=cim, op=ALU.mult), reads=[w, sbr], acc=[tb])
            tr.op('dve', lambda: V.tensor_tensor(out=bb[:, 1], in0=bb[:, 1], in1=tb[:, 1], op=ALU.add), reads=[bb, tb], acc=[bb])
            if BP == 2: pp.close(); ph.close(); return
            ANG = pp.sb('ang9_%d' % d, [128, 12, 9]); TA9 = pp.sb('ta9_%d' % d, [128, 12, 9]); TI9 = pp.sb('ti9_%d' % d, [128, 12, 9], I32)
            COk = pp.sb('cok%d' % d, [128, 12, 9]); SIk = pp.sb('sik%d' % d, [128, 12, 9]); RK = PD['RK']
            LRk = PD['LRk']; LIk = PD['LIk']
            i9b = io9[:].unsqueeze(1).to_broadcast([128, 12, 9])
            tr.op('dve', lambda: V.tensor_tensor(out=ANG[:], in0=THR.unsqueeze(2).to_broadcast([128, 12, 9]), in1=i9b, op=ALU.mult), reads=[w, io9], writes=[ANG])
            sin_rr(ANG[:], ANG, SIk[:], SIk, 0.0, TA9[:], TA9, TI9[:], TI9)
            sin_rr(ANG[:], ANG, COk[:], COk, PI / 2, TA9[:], TA9, TI9[:], TI9)
            tr.op('dve', lambda: V.tensor_tensor(out=RK[:], in0=ARDT.unsqueeze(2).to_broadcast([128, 12, 9]), in1=i9b, op=ALU.mult), reads=[w, io9], writes=[RK])
            tr.op('act', lambda: A.activation(out=RK[:], in_=RK[:], func=AF.Exp), reads=[RK], writes=[RK])
            tr.op('dve', lambda: V.tensor_tensor(out=LRk[:], in0=RK[:], in1=COk[:], op=ALU.mult), reads=[RK, COk], writes=[LRk])
            tr.op('dve', lambda: V.tensor_tensor(out=LIk[:], in0=RK[:], in1=SIk[:], op=ALU.mult), reads=[RK, SIk], writes=[LIk])
            if BP == 3: pp.close(); ph.close(); return
            TH8 = pp.sb('th8_%d' % d, [128, 12]); TA8 = pp.sb('ta8_%d' % d, [128, 12]); TI8 = pp.sb('ti8_%d' % d, [128, 12], I32)
            tr.op('dve', lambda: V.tensor_scalar(out=TA8[:], in0=THR, scalar1=8.0, scalar2=None, op0=ALU.mult), reads=[w], writes=[TA8])
            tr.op('dve', lambda: V.tensor_scalar(out=TI8[:], in0=TA8[:], scalar1=1.0 / (2 * PI), scalar2=None, op0=ALU.mult), reads=[TA8], writes=[TI8])
            tr.op('dve', lambda: V.tensor_copy(out=TH8[:], in_=TI8[:]), reads=[TI8], writes=[TH8])
            tr.op('dve', lambda: V.scalar_tensor_tensor(out=TH8[:], in0=TH8[:], scalar=-2 * PI, in1=TA8[:], op0=ALU.mult, op1=ALU.add), reads=[TH8, TA8], writes=[TH8])
            tr.op('dve', lambda: V.tensor_tensor(out=A64[:], in0=TH8[:].unsqueeze(2).to_broadcast([128, 12, 64]),
                                                 in1=io64[:, d, :].unsqueeze(1).to_broadcast([128, 12, 64]), op=ALU.mult), reads=[TH8, io64], writes=[A64])
            sin_rr(A64[:], A64, S64[:], S64, 0.0, TA64[:], TA64, TI64[:], TI64)
            sin_rr(A64[:], A64, C64[:], C64, PI / 2, TA64[:], TA64, TI64[:], TI64)
            if BP == 4: pp.close(); ph.close(); return
            C2 = PD['C2']; S2 = PD['S2']
            tr.group('act', [lambda: A.activation(out=C2[:, :, 0, :], in_=C64[:], func=AF.Copy),
                             lambda: A.activation(out=C2[:, :, 1, :], in_=C64[:], func=AF.Copy)], reads=[C64], writes=[C2])
            tr.group('act', [lambda: A.activation(out=S2[:, :, 0, :], in_=S64[:], func=AF.Copy),
                             lambda: A.activation(out=S2[:, :, 1, :], in_=S64[:], func=AF.Copy, scale=-1.0)], reads=[S64], writes=[S2])
            if BP == 5: pp.close(); ph.close(); return
            prm[d] = dict(w=w, LRk=LRk, LIk=LIk, RK=RK, C2=C2, S2=S2, bb=bb, scr=scr_, sci=sci)

        pp.close()
        BCUT = int(os.environ.get('BCUT', '99'))
        if BCUT == 0: ph.close(); return
        yacc = ph.sb('yacc', [128, TT], F32, dma=True); ub = ph.sb('ub', [128, TT], BF16)
        XHs = [ph.sb('XH%d' % i, [128, 4, 2, 8, 128], BF16) for i in range(2)]
        XTs = [ph.sb('XTt%d' % i, [128, 4, 2, 8, 128], BF16) for i in range(2)]
        Kbs = [ph.sb('Kb%d' % i, [128, 8, 128], BF16) for i in range(2)]
        Bstg = ph.sb('bstg4', [128, 4, 2, 128]); Cf = ph.sb('cf4', [128, 4, 2, 128]); cpad = ph.sb('cpad4', [128, 4, 2, 128], BF16)
        stg = [ph.sb('stgc%d' % i, [128, 128]) for i in range(2)]
        tq = [ph.sb('tq%d' % i, [128, 8, 128]) for i in range(2)]
        pxt = ph.ps('pxt', [128, 8, 128], BF16)
        pk = [ph.ps('pk%d' % i, [128, 4, 128]) for i in range(2)]
        pv = [ph.ps('pv%d' % i, [128, 4, 2, 64]) for i in range(2)]
        po = ph.ps('po', [128, 8, 64])
        t1 = ph.sb('bt1', [128, 4, 2, 64]); t2 = ph.sb('bt2', [128, 4, 2, 64]); Wt = ph.sb('bW', [128, 4, 2, 64]); Zt = ph.sb('bZ', [128, 4, 2, 64])
        Sf = [ph.sb('bSf%d' % i, [128, 4, 2, 64]) for i in range(2)]
        Sp = [ph.sb('bSp%d' % i, [128, 4, 2, 64], BF16) for i in range(2)]
        segs = [(0, 32)] + [(NCX + i * 512, 64) for i in range(8)]
        scnt = dict(n=0)

        def make_setup(j3, d, slot):
            P = prm[d]; XH = XHs[slot]; XT = XTs[slot]; Kb = Kbs[slot]
            steps = []

            def stage_s(s4):
                s = 4 * j3 + s4
                for ri in range(2):
                    first = (s4 == 0 and ri == 0)
                    tr.op('dve', lambda: V.tensor_tensor(out=Bstg[:, s4, ri, :].rearrange('p (a b) -> p a b', a=8),
                                                         in0=P['bb'][:, ri, s, :].unsqueeze(1).to_broadcast([128, 8, 16]),
                                                         in1=m2[:, s % 4], op=ALU.mult), reads=[P['bb'], m2],
                          writes=[Bstg] if first else (), acc=() if first else [Bstg])
                    sg_ = stg[scnt['n'] % 2]; scnt['n'] += 1
                    csrc = (P['scr'] if ri == 0 else P['sci'])
                    tr.op('dve', lambda: V.tensor_tensor(out=sg_[:].rearrange('p (a b) -> p a b', a=2),
                                                         in0=csrc[:, s // 4, :].unsqueeze(1).to_broadcast([128, 2, 64]),
                                                         in1=mcm[:, s % 4], op=ALU.mult), reads=[csrc, mcm], writes=[sg_])
                    tr.op('pe', lambda: PE.transpose(pst[:], sg_[:], ident[:]), reads=[sg_, ident], writes=[pst])
                    tr.op('act', lambda: A.activation(out=Cf[:, s4, ri, :], in_=pst[:], func=AF.Copy), reads=[pst],
                          writes=[Cf] if first else (), acc=() if first else [Cf])
                    tr.op('act', lambda: A.activation(out=cpad[:, s4, ri, :], in_=pst[:], func=AF.Copy, scale=(1.0 if ri == 0 else -1.0)),
                          reads=[pst], writes=[cpad] if first else (), acc=() if first else [cpad])

            def scaled_s(dstbuf, src, k0, neg_im, s4):
                s = 4 * j3 + s4
                sre = src[:, s4, 0, :].unsqueeze(1).to_broadcast([128, 8, 128]); sim = src[:, s4, 1, :].unsqueeze(1).to_broadcast([128, 8, 128])
                lr = P['LRk'][:, s, k0:k0 + 8].unsqueeze(2).to_broadcast([128, 8, 128])
                li = P['LIk'][:, s, k0:k0 + 8].unsqueeze(2).to_broadcast([128, 8, 128])
                first = (s4 == 0)
                tr.op('dve', lambda: V.tensor_tensor(out=tq[0][:], in0=sre, in1=lr, op=ALU.mult), reads=[src, P['LRk']], writes=[tq[0]])
                tr.op('dve', lambda: V.tensor_tensor(out=tq[1][:], in0=sim, in1=li, op=ALU.mult), reads=[src, P['LIk']], writes=[tq[1]])
                tr.op('dve', lambda: V.tensor_tensor(out=dstbuf[:, s4, 0], in0=tq[0][:], in1=tq[1][:], op=ALU.subtract), reads=[tq[0], tq[1]],
                      writes=[dstbuf] if first else (), acc=() if first else [dstbuf])
                tr.op('dve', lambda: V.tensor_tensor(out=tq[0][:], in0=sre, in1=li, op=ALU.mult), reads=[src, P['LIk']], writes=[tq[0]])
                tr.op('dve', lambda: V.tensor_tensor(out=tq[1][:], in0=sim, in1=lr, op=ALU.mult), reads=[src, P['LRk']], writes=[tq[1]])
                if neg_im:
                    tr.op('dve', lambda: V.scalar_tensor_tensor(out=dstbuf[:, s4, 1], in0=tq[0][:], scalar=-1.0, in1=tq[1][:], op0=ALU.mult, op1=ALU.subtract),
                          reads=[tq[0], tq[1]], acc=[dstbuf])
                else:
                    tr.op('dve', lambda: V.tensor_tensor(out=dstbuf[:, s4, 1], in0=tq[0][:], in1=tq[1][:], op=ALU.add), reads=[tq[0], tq[1]], acc=[dstbuf])

            def kmm(half):
                pkk = pk[half]
                fns = []
                for tt in range(4):
                    tau = half * 4 + tt
                    k = 0
                    for s4 in range(4):
                        for ri in range(2):
                            fns.append(lambda tt=tt, tau=tau, s4=s4, ri=ri, k=k: PE.matmul(pkk[:, tt, :], lhsT=XH[:, s4, ri, tau, :], rhs=cpad[:, s4, ri, :],
                                                                                             start=(k == 0), stop=(k == 7)))
                            k += 1
                tr.group('pe', fns, reads=[XH, cpad], writes=[pkk])
                tr.op('act', lambda: A.activation(out=Kb[:, half * 4:half * 4 + 4, :], in_=pkk[:], func=AF.Copy), reads=[pkk],
                      writes=[Kb] if half == 0 else (), acc=() if half == 0 else [Kb])

            def xtr(s4, ri):
                tr.group('pe', [lambda tau=tau: PE.transpose(pxt[:, tau, :], XH[:, s4, ri, tau, :], identb[:]) for tau in range(8)],
                         reads=[XH, identb], writes=[pxt])
                first = (s4 == 0 and ri == 0)
                tr.op('act' if ri == 0 else 'dve',
                      (lambda: A.activation(out=XT[:, s4, ri], in_=pxt[:], func=AF.Copy)) if ri == 0 else (lambda: V.tensor_copy(out=XT[:, s4, ri], in_=pxt[:])),
                      reads=[pxt], writes=[XT] if first else (), acc=() if first else [XT])

            for s4 in range(4): steps.append(lambda s4=s4: stage_s(s4))
            for s4 in range(4): steps.append(lambda s4=s4: scaled_s(XH, Bstg, 0, False, s4))
            for half in range(2): steps.append(lambda half=half: kmm(half))
            for s4 in range(4):
                for ri in range(2): steps.append(lambda s4=s4, ri=ri: xtr(s4, ri))
            for s4 in range(4): steps.append(lambda s4=s4: scaled_s(XH, Cf, 1, True, s4))
            return steps

        def run_loop(j3, d, slot, pending):
            P = prm[d]; XH = XHs[slot]; XT = XTs[slot]; Kb = Kbs[slot]
            order = list(range(9)) if d == 0 else [0] + list(range(8, 0, -1))
            prev = None
            per = -(-len(pending) // 8) if pending else 0

            def vmm(oi):
                t0, nC = segs[order[oi]]
                pvv = pv[oi % 2]
                fns = []
                for s4 in range(4):
                    for ri in range(2):
                        for j in range(8):
                            tau = (7 - j) if d == 0 else j
                            fns.append(lambda s4=s4, ri=ri, j=j, tau=tau: PE.matmul(pvv[:, s4, ri, :nC], lhsT=XT[:, s4, ri, tau, :],
                                                                                     rhs=ub[:, t0 + j:t0 + j + 8 * (nC - 1) + 1:8], start=(j == 0), stop=(j == 7)))
                tr.group('pe', fns, reads=[XT, ub], writes=[pvv])

            vmm(0)
            for oi in range(9):
                t0, nC = segs[order[oi]]
                par = oi % 2
                pvv = pv[par]
                s0 = 4 * j3
                csl = slice(0, nC) if d == 0 else slice(64 - nC, 64)
                C2v = P['C2'][:, s0:s0 + 4, :, csl]; S2v = P['S2'][:, s0:s0 + 4, :, csl]
                tr.op('dve', lambda: V.tensor_tensor(out=t1[:, :, :, :nC], in0=pvv[:, :, :, :nC], in1=C2v, op=ALU.mult), reads=[pvv, P['C2']], writes=[t1])
                tr.op('dve', lambda: V.tensor_tensor(out=t2[:, :, :, :nC], in0=pvv[:, :, ::-1, :nC], in1=S2v, op=ALU.mult), reads=[pvv, P['S2']], writes=[t2])
                tr.op('dve', lambda: V.tensor_tensor(out=Wt[:, :, :, :nC], in0=t1[:, :, :, :nC], in1=t2[:, :, :, :nC], op=ALU.add), reads=[t1, t2], writes=[Wt])
                if oi + 1 < 9: vmm(oi + 1)
                fns = []
                for s4 in range(4):
                    rr = P['RK'][:, s0 + s4, 8:9].to_broadcast([128, nC])
                    for ri in range(2):
                        if prev is None: ini = 0.0
                        else:
                            pcol = (prev[1] - 1) if d == 0 else 0
                            ini = Sf[1 - par][:, s4, ri, pcol:pcol + 1]
                        if d == 0:
                            fns.append(lambda s4=s4, ri=ri, rr=rr, ini=ini: V.tensor_tensor_scan(out=Zt[:, s4, ri, :nC], data0=rr, data1=Wt[:, s4, ri, :nC],
                                                                                                 initial=ini, op0=ALU.mult, op1=ALU.add))
                        else:
                            fns.append(lambda s4=s4, ri=ri, rr=rr, ini=ini: V.tensor_tensor_scan(out=Zt[:, s4, ri, :nC][:, ::-1], data0=rr, data1=Wt[:, s4, ri, :nC][:, ::-1],
                                                                                                 initial=ini, op0=ALU.mult, op1=ALU.add))
                tr.group('dve', fns, reads=[Wt, P['RK']] + ([Sf[1 - par]] if prev is not None else []), writes=[Zt])
                tr.op('dve', lambda: V.tensor_tensor(out=t1[:, :, :, :nC], in0=Zt[:, :, :, :nC], in1=C2v, op=ALU.mult), reads=[Zt, P['C2']], writes=[t1])
                tr.op('dve', lambda: V.tensor_tensor(out=t2[:, :, :, :nC], in0=Zt[:, :, ::-1, :nC], in1=S2v, op=ALU.mult), reads=[Zt, P['S2']], writes=[t2])
                tr.op('dve', lambda: V.tensor_tensor(out=Sf[par][:, :, :, :nC], in0=t1[:, :, :, :nC], in1=t2[:, :, :, :nC], op=ALU.subtract), reads=[t1, t2], writes=[Sf[par]])
                spv = Sp[par]
                if d == 0:
                    f1 = lambda: A.activation(out=spv[:, :, :, 1:nC], in_=Sf[par][:, :, :, 0:nC - 1], func=AF.Copy)
                    cdst = spv[:, :, :, 0:1]
                else:
                    f1 = lambda: A.activation(out=spv[:, :, :, 0:nC - 1], in_=Sf[par][:, :, :, 1:nC], func=AF.Copy)
                    cdst = spv[:, :, :, nC - 1:nC]
                if prev is None:
                    f2 = lambda: A.activation(out=cdst, in_=Sf[par][:, :, :, 0:1], func=AF.Copy, scale=0.0)
                    rdl = [Sf[par]]
                else:
                    pcol = (prev[1] - 1) if d == 0 else 0
                    f2 = lambda: A.activation(out=cdst, in_=Sf[1 - par][:, :, :, pcol:pcol + 1], func=AF.Copy)
                    rdl = [Sf[par], Sf[1 - par]]
                tr.group('act', [f1, f2], reads=rdl, writes=[spv])
                for _ in range(per):
                    if pending: pending.pop(0)()
                fns = []
                for j in range(8):
                    ntap = (j + 1) if d == 0 else (8 - j)
                    hk = j if d == 0 else (7 - j)
                    tot = ntap + 8; k = 0
                    for tau in range(ntap):
                        off = (j - tau) if d == 0 else (j + tau)
                        fns.append(lambda j=j, tau=tau, off=off, k=k, tot=tot: PE.matmul(po[:, j, :nC], lhsT=Kb[:, tau, :], rhs=ub[:, t0 + off:t0 + off + 8 * (nC - 1) + 1:8],
                                                                                          start=(k == 0), stop=(k == tot - 1)))
                        k += 1
                    for s4 in range(4):
                        for ri in range(2):
                            fns.append(lambda j=j, s4=s4, ri=ri, hk=hk, k=k, tot=tot: PE.matmul(po[:, j, :nC], lhsT=XH[:, s4, ri, hk, :], rhs=spv[:, s4, ri, :nC],
                                                                                                  start=(k == 0), stop=(k == tot - 1)))
                            k += 1
                tr.group('pe', fns, reads=[Kb, ub, XH, spv], writes=[po])
                yv = yacc[:, t0:t0 + 8 * nC].rearrange('p (c j) -> p j c', j=8)
                tr.op('dve', lambda: V.tensor_tensor(out=yv, in0=yv, in1=po[:, :, :nC], op=ALU.add), reads=[po, yacc], acc=[yacc])
                prev = (oi, nC)
            while pending: pending.pop(0)()

        ulist = [(j3, d) for j3 in range(3) for d in range(2)]
        for st_ in make_setup(0, 0, 0): st_()
        for ui, (j3, d) in enumerate(ulist):
            if d == 0:
                tr.dma('sp', yacc[:], U[j3 * 128:(j3 + 1) * 128, :], writes=[yacc])
                tr.op('act', lambda: A.activation(out=ub[:], in_=yacc[:], func=AF.Copy), reads=[yacc], writes=[ub])
                tr.op('dve', lambda: V.tensor_scalar(out=yacc[:], in0=yacc[:], scalar1=sdt[:, j3:j3 + 1], scalar2=None, op0=ALU.mult), reads=[yacc, sdt], writes=[yacc])
            pending = make_setup(ulist[ui + 1][0], ulist[ui + 1][1], (ui + 1) % 2) if ui + 1 < len(ulist) else []
            run_loop(j3, d, ui % 2, pending)
            if d == 1:
                for g0 in range(0, TT, 512):
                    n = min(512, TT - g0); yv = yacc[:, g0:g0 + n]
                    a0b = tq[0]; a1b = tq[1]
                    a0 = tq[0][:].rearrange('p a b -> p (a b)'); a1 = tq[1][:].rearrange('p a b -> p (a b)')
                    tr.op('act', lambda: A.activation(out=a0[:, :n], in_=yv, func=AF.Square), reads=[yacc], writes=[a0b])
                    tr.op('dve', lambda: V.tensor_scalar(out=a0[:, :n], in0=a0[:, :n], scalar1=0.044715, scalar2=1.0, op0=ALU.mult, op1=ALU.add), reads=[a0b], writes=[a0b])
                    tr.op('dve', lambda: V.tensor_tensor(out=a1[:, :n], in0=a0[:, :n], in1=yv, op=ALU.mult), reads=[a0b, yacc], writes=[a1b])
                    tr.op('act', lambda: A.activation(out=a1[:, 512:512 + n], in_=a1[:, :n], func=AF.Sigmoid, scale=2.0 * math.sqrt(2.0 / PI)), reads=[a1b], writes=[a1b])
                    st = stg_o[(g0 // 512) % 2]
                    tr.op('dve', lambda: V.tensor_tensor(out=st[:, :n], in0=a1[:, 512:512 + n], in1=yv, op=ALU.mult), reads=[a1b, yacc], writes=[st])
                    tr.dma('pool', TG[j3 * 128:(j3 + 1) * 128, g0:g0 + n], st[:, :n], reads=[st])
        ph.close()

    gm = None
    stg_o = None

    def phase_C(l, ctx_out):
        nonlocal gm
        ph = Phase()
        gm = ph.sb('gm', [128, 2, 128], F32, dma=True); tr.dma('sp', gm[:], gmask_in[:, :, :], writes=[gm])
        snk = ph.sb('snk', [128, 8], F32, dma=True)
        tr.dma('sp', snk[0:64, :], sink_in[l], writes=[snk]); tr.dma('sp', snk[64:128, :], sink_in[l], acc=[snk])
        esk = ph.sb('esk', [128, 8])
        tr.op('act', lambda: A.activation(out=esk[:], in_=snk[:], func=AF.Exp), reads=[snk], writes=[esk])
        vsb = ph.sb('vsb', [128, 34, 128], BF16, dma=True)
        tr.dma('sp', vsb[:], VG.rearrange('(c p) d -> p c d', p=128), writes=[vsb])
        vaug = [ph.sb('vaug%d' % g, [128, 34, 128], BF16) for g in range(2)]
        for g in range(2):
            tr.op('dve', lambda: V.memset(vaug[g][:, :, 64:128], 1.0), writes=[vaug[g]])
            tr.op('act', lambda: A.activation(out=vaug[g][:, :, 0:64], in_=vsb[:, :, g * 64:(g + 1) * 64], func=AF.Copy), reads=[vsb], acc=[vaug[g]])
        kT = [ph.sb('kT%d' % g, [128, TT], BF16, dma=True) for g in range(2)]
        for g in range(2):
            tr.op('dve', lambda: V.memset(kT[g][64:128, :], 0.0), writes=[kT[g]])
            tr.dma('sp', kT[g][0:64, 0:NCX], KGC[g], acc=[kT[g]])
            tr.dma('sp', kT[g][0:64, NCX:TT], KG[g], acc=[kT[g]])
        qb = [ph.sb('qb%d' % i, [128, 4, 128], BF16, dma=True) for i in range(3)]
        for i in range(3):
            tr.op('dve', lambda: V.memset(qb[i][64:128], 0.0), writes=[qb[i]])
        ps_s = [ph.ps('pss%d' % i) for i in range(4)]
        ps_o = [ph.ps('pso%d' % i) for i in range(2)]; ps_d = [ph.ps('psd%d' % i) for i in range(2)]
        pT = [ph.sb('pT%d' % i, [128, 512], BF16) for i in range(4)]
        tmpm = [ph.sb('tmpm%d' % i, [128, 512]) for i in range(2)]; den = ph.sb('den', [128, 512])
        ob = [ph.sb('ob%d' % i, [64, 4, 128], BF16, dma=True) for i in range(2)]
        units = []
        if ctx_out:
            for g in range(2):
                for bi in range(2): units.append(('c', g, bi))
        for g in range(2):
            for bi in range(32): units.append(('l', g, bi))
        items = []
        uinfo = []
        for ui, (kind, g, bi) in enumerate(units):
            keys = [(kT[g][:, c * 128:(c + 1) * 128], None, vaug[g][:, c, :]) for c in range(2)]
            if kind == 'l':
                for dlt in (-1, 0, 1):
                    kb = bi + dlt
                    if kb < 0 or kb > 31: continue
                    m = None if dlt == 0 else gm[:, (0 if dlt == -1 else 1), :].unsqueeze(1).to_broadcast([128, 4, 128])
                    keys.append((kT[g][:, NCX + kb * 128:NCX + (kb + 1) * 128], m, vaug[g][:, 2 + kb, :]))
            for ki, (k_ap, m_ap, v_ap) in enumerate(keys): items.append((ui, ki, len(keys), k_ap, m_ap, v_ap))
        cnt = dict(n=0, m=0)
        st = {}

        def stage1(ii):
            ui, ki, nk, k_ap, m_ap, v_ap = items[ii]
            kind, g, bi = units[ui]
            q = qb[ui % 3]
            if ki == 0:
                srcq = (QGC if kind == 'c' else QG)[4 * g:4 * g + 4, :, bi * 128:(bi + 1) * 128].rearrange('h d t -> d h t')
                tr.dma('sp', q[0:64], srcq, writes=[q])
            pss = ps_s[cnt['n'] % 4]; pt = pT[cnt['n'] % 4]; cnt['n'] += 1
            tr.op('pe', lambda: PE.matmul(pss[:, :], lhsT=k_ap, rhs=q[:], start=True, stop=True), reads=[q, kT[g]], writes=[pss])
            if m_ap is not None:
                tm = tmpm[cnt['m'] % 2]; cnt['m'] += 1
                tr.op('dve', lambda: V.tensor_tensor(out=tm[:].rearrange('p (h t) -> p h t', h=4), in0=pss[:].rearrange('p (h t) -> p h t', h=4),
                                                     in1=m_ap, op=ALU.add), reads=[pss, gm], writes=[tm])
                tr.op('act', lambda: A.activation(out=pt[:], in_=tm[:], func=AF.Exp), reads=[tm], writes=[pt])
            else:
                tr.op('act', lambda: A.activation(out=pt[:], in_=pss[:], func=AF.Exp), reads=[pss], writes=[pt])
            st[ii] = pt

        def stage2(ii):
            ui, ki, nk, k_ap, m_ap, v_ap = items[ii]
            kind, g, bi = units[ui]
            pt = st.pop(ii)
            pso = ps_o[ui % 2]; psd = ps_d[ui % 2]; o = ob[ui % 2]
            tr.op('pe', lambda: PE.matmul(pso[:, :], lhsT=v_ap, rhs=pt[:], start=(ki == 0), stop=(ki == nk - 1)),
                  reads=[pt, vaug[g]], writes=[pso] if ki == 0 else (), acc=() if ki == 0 else [pso])
            if ki == nk - 1:
                tr.op('dve', lambda: V.tensor_tensor(out=den[64:128, :].rearrange('p (h t) -> p h t', h=4), in0=pso[64:128, :].rearrange('p (h t) -> p h t', h=4),
                                                     in1=esk[64:128, 4 * g:4 * g + 4].unsqueeze(2).to_broadcast([64, 4, 128]), op=ALU.add),
                      reads=[pso, esk], writes=[den])
                tr.op('act', lambda: A.activation(out=den[64:128, :], in_=den[64:128, :], func=AF.Ln), reads=[den], writes=[den])
                tr.op('act', lambda: A.activation(out=den[64:128, :], in_=den[64:128, :], func=AF.Exp, scale=-1.0), reads=[den], writes=[den])
                tr.op('dve', lambda: V.tensor_tensor(out=o[:].rearrange('p h t -> p (h t)'), in0=pso[0:64, :], in1=den[64:128, :], op=ALU.mult),
                      reads=[pso, den], writes=[o])
                dst = (YGC if kind == 'c' else YG)[4 * g:4 * g + 4, :, bi * 128:(bi + 1) * 128].rearrange('h d t -> d h t')
                tr.dma('pool', dst, o[:], reads=[o])

        KD = 2
        for ii in range(len(items) + KD):
            if ii < len(items): stage1(ii)
            if ii - KD >= 0: stage2(ii - KD)
        ph.close()

    def attn_unit3(q, keys, ps_s, pso, psd, pT, tmpm, qbufs, kbufs, vbufs, epi):
        nk = len(keys)
        for ki, (k_ap, m_ap, v_ap) in enumerate(keys):
            pss = ps_s[ki % len(ps_s)]
            tr.op('pe', lambda: PE.matmul(pss[:, :], lhsT=k_ap, rhs=q[:], start=True, stop=True), reads=qbufs + kbufs, writes=[pss])
            pt = pT[ki % len(pT)]
            if m_ap is not None:
                tr.op('dve', lambda: V.tensor_tensor(out=tmpm[:].rearrange('p (h t) -> p h t', h=4), in0=pss[:].rearrange('p (h t) -> p h t', h=4),
                                                     in1=m_ap, op=ALU.add), reads=[pss, gm], writes=[tmpm])
                tr.op('act', lambda: A.activation(out=pt[:], in_=tmpm[:], func=AF.Exp), reads=[tmpm], writes=[pt])
            else:
                tr.op('act', lambda: A.activation(out=pt[:], in_=pss[:], func=AF.Exp), reads=[pss], writes=[pt])
            tr.op('pe', lambda: PE.matmul(pso[0:64, :], lhsT=v_ap, rhs=pt[:], start=(ki == 0), stop=(ki == nk - 1)),
                  reads=[pt] + vbufs, writes=[pso] if ki == 0 else (), acc=() if ki == 0 else [pso])
            tr.op('pe', lambda: PE.matmul(psd[0:64, :], lhsT=ones_b[:, 0:64], rhs=pt[:], start=(ki == 0), stop=(ki == nk - 1)),
                  reads=[pt, ones_b], writes=[psd] if ki == 0 else (), acc=() if ki == 0 else [psd])
        epi()

    def phase_D(l, ctx_out):
        ph = Phase()
        kT = [ph.sb('nkT%d' % i, [64, TT], BF16, dma=True) for i in range(2)]
        qT = [ph.sb('nqT%d' % i, [64, TT], BF16, dma=True) for i in range(2)]
        vs = [ph.sb('nvs%d' % i, [128, 34, 64], BF16, dma=True) for i in range(2)]
        vt = [ph.sb('nvt%d' % i, [128, 14, 64], F32, dma=True) for i in range(2)]
        od = [ph.sb('nod%d' % i, [128, 5, 64], F32, dma=True) for i in range(2)]
        cbt = [ph.sb('ncb%d' % i, [128, 5, 128], F32, dma=True) for i in range(2)]
        yb = [ph.sb('nyb%d' % i, [64, TT], BF16, dma=True) for i in range(2)]
        RD = 2
        ps_L = [ph.ps('npl%d' % i) for i in range(RD)]
        ps_X = [ph.ps('npx%d' % i) for i in range(RD)]
        ps_od = [ph.ps('npo%d' % i) for i in range(RD)]
        tmp = [ph.sb('ntmp%d' % i, [128, 640]) for i in range(RD)]
        pT = [ph.sb('npT%d' % i, [128, 640], BF16) for i in range(RD)]
        pC = [ph.sb('npC%d' % i, [128, 256], BF16) for i in range(RD)]
        rd = [ph.sb('nrd%d' % i, [128, 256]) for i in range(RD)]
        vaugn = [ph.sb('nvaug%d' % i, [128, 34, 128], BF16) for i in range(2)]
        n = 0
        for h in range(8):
            k_ = kT[h % 2]; q_ = qT[h % 2]; v_ = vs[h % 2]; vt_ = vt[h % 2]; od_ = od[h % 2]; cb_ = cbt[h % 2]; y_ = yb[h % 2]
            tr.dma('sp', k_[:, 0:NCX], KNC[h], writes=[k_]); tr.dma('sp', k_[:, NCX:TT], KN[h], acc=[k_])
            tr.dma('sp', q_[:, 0:NCX], QNC[h], writes=[q_]); tr.dma('sp', q_[:, NCX:TT], QN[h], acc=[q_])
            tr.dma('sp', v_[:], VN[:, h * 64:(h + 1) * 64].rearrange('(c p) d -> p c d', p=128), writes=[v_])
            tr.dma('sp', vt_[:], navt_in[l, h], writes=[vt_]); tr.dma('sp', od_[:], naod_in[l, h], writes=[od_])
            tr.dma('sp', cb_[:], nacb_in[l, h], writes=[cb_])
            va_ = vaugn[h % 2]
            tr.op('dve', lambda: V.memset(va_[:, :, 64:128], 1.0), writes=[va_])
            tr.op('act', lambda: A.activation(out=va_[:, :, 0:64], in_=v_[:], func=AF.Copy), reads=[v_], acc=[va_])
            units = ([('c', 0), ('c', 1)] if ctx_out else []) + [('l', r) for r in range(4)] + [('p', r) for r in range(4, 60, 2)] + [('l', r) for r in range(60, 64)]
            for ui, (kind, r) in enumerate(units):
                psl = ps_L[n % RD]; psx = ps_X[n % RD]; pob = ps_od[n % RD]
                tm = tmp[n % RD]; pt = pT[n % RD]; pc = pC[n % RD]; rdn = rd[n % RD]; n += 1
                p0 = 0
                if kind == 'c':
                    Nq = 128; qap = q_[:, r * 128:(r + 1) * 128]; npair = 0; oc0 = r * 128
                elif kind == 'p':
                    Nq = 128; qap = q_[:, NCX + r * 64:NCX + (r + 2) * 64]; oc0 = NCX + r * 64
                    p0 = (r - 4) // 2; npair = 5
                else:
                    Nq = 64; qap = q_[:, NCX + r * 64:NCX + (r + 1) * 64]; oc0 = NCX + r * 64
                    rs = min(max(r - 4, 0), 56)
                    p0 = rs // 2; npair = 4; i0_ = rs - r + 7
                    bias = vt_[:, i0_:i0_ + 7:2, :]
                fns = [lambda c=c: PE.matmul(psx[:, 128 + c * Nq:128 + (c + 1) * Nq], lhsT=k_[:, c * 128:(c + 1) * 128], rhs=qap, start=True, stop=True) for c in range(2)]
                if npair == 5:
                    fns.append(lambda: PE.matmul(psx[:, 0:Nq], lhsT=k_[:, NCX + (p0 + 4) * 128:NCX + (p0 + 5) * 128], rhs=qap, start=True, stop=True))
                tr.group('pe', fns, reads=[k_, q_], writes=[psx])
                if npair:
                    tr.group('pe', [lambda k=k: PE.matmul(psl[:, k * Nq:(k + 1) * Nq], lhsT=k_[:, NCX + (p0 + k) * 128:NCX + (p0 + k + 1) * 128], rhs=qap,
                                                          start=True, stop=True) for k in range(4)], reads=[k_, q_], writes=[psl])
                tr.op('act', lambda: A.activation(out=pc[:, :2 * Nq], in_=psx[:, 128:128 + 2 * Nq], func=AF.Exp), reads=[psx], writes=[pc])
                if kind == 'l':
                    tr.op('dve', lambda: V.tensor_tensor(out=tm[:, :256].rearrange('p (a b) -> p a b', a=4),
                                                         in0=psl[:, :256].rearrange('p (a b) -> p a b', a=4), in1=bias, op=ALU.add),
                          reads=[psl, vt_], writes=[tm])
                    tr.op('act', lambda: A.activation(out=pt[:, :256], in_=tm[:, :256], func=AF.Exp), reads=[tm], writes=[pt])
                elif kind == 'p':
                    tr.op('dve', lambda: V.tensor_tensor(out=tm[:, 0:512], in0=psl[:, 0:512], in1=cb_[:, 0:4, :].rearrange('p a b -> p (a b)'), op=ALU.add),
                          reads=[psl, cb_], writes=[tm])
                    tr.op('dve', lambda: V.tensor_tensor(out=tm[:, 512:640], in0=psx[:, 0:128], in1=cb_[:, 4, :], op=ALU.add),
                          reads=[psx, cb_], acc=[tm])
                    tr.op('act', lambda: A.activation(out=pt[:, :640], in_=tm[:, :640], func=AF.Exp), reads=[tm], writes=[pt])
                mm = [(va_[:, c, :], pc[:, c * Nq:(c + 1) * Nq]) for c in range(2)]
                mm += [(va_[:, 2 + p0 + k, :], pt[:, k * Nq:(k + 1) * Nq]) for k in range(npair)]
                tr.group('pe', [lambda i=i, a=a, b=b: PE.matmul(pob[:, 0:Nq], lhsT=a, rhs=b, start=(i == 0), stop=(i == len(mm) - 1))
                                for i, (a, b) in enumerate(mm)], reads=[va_, pc, pt], writes=[pob])
                tr.op('dve', lambda: V.reciprocal(out=rdn[64:128, :Nq], in_=pob[64:128, 0:Nq]), reads=[pob], writes=[rdn])
                tr.op('dve', lambda: V.tensor_tensor(out=y_[:, oc0:oc0 + Nq], in0=pob[0:64, 0:Nq], in1=rdn[64:128, :Nq], op=ALU.mult),
                      reads=[pob, rdn], writes=[y_] if ui == 0 else (), acc=() if ui == 0 else [y_])
            if ctx_out: tr.dma('pool', YNC[h], y_[:, 0:NCX], reads=[y_])
            tr.dma('pool', YN[h], y_[:, NCX:TT], reads=[y_])
        ph.close()

    pcbig = None

    def phase_E(l, ctx_out, last):
        ph = Phase()
        R = ffn_bufs(ph)
        xTs = [ph.sb('xT%d' % i, [128, KC, 512], F32, dma=True) for i in range(2)]
        hT = ph.sb('hT', [128, KC, 512], BF16); aT = ph.sb('aT', [128, FC, 512], BF16)
        tg = [ph.sb('tg%d' % i, [128, 3, 512], F32, dma=True) for i in range(2)]
        tgb = ph.sb('tgb', [128, 3, 512], BF16); ys = ph.sb('ys', [128, 3, 512], BF16)
        yg = [ph.sb('yg%d' % i, [128, 4, 512], BF16, dma=True) for i in range(2)]
        yn = [ph.sb('yn%d' % i, [128, 4, 512], BF16, dma=True) for i in range(2)]
        sgr = [ph.sb('sg%d' % i, [128, 3, 512], BF16, dma=True) for i in range(2)]
        wpgr = [ph.sb('wpg%d' % i, [128, 4, 128], BF16, dma=True) for i in range(2)]
        wpnr = [ph.sb('wpn%d' % i, [128, 4, 128], BF16, dma=True) for i in range(2)]
        wglu = ph.sb('wglu', [128, 3, 3, 128], BF16, dma=True)
        tr.dma('sp', wglu[:], w_b[('wglu', l)].rearrange('m p k c -> p m k c'), reads=[w_buf[('wglu', l)]], writes=[wglu])
        wps = ph.sb('wps', [128, 8, 3, 128], BF16, dma=True)
        tr.dma('sp', wps[:], w_b[('wps', l)].rearrange('m p k c -> p m k c'), reads=[w_buf[('wps', l)]], writes=[wps])
        wo = [ph.sb('wo%d' % i, [128, KC, 128], BF16, dma=True) for i in range(2)]
        acc = ph.sb('acc', [128, 512]); t2 = ph.sb('t2e', [128, 512])
        ost = [ph.sb('ost%d' % i, [128, D], F32, dma=True) for i in range(1)] if last else None
        tl = tiles if ctx_out else tiles[1:]
        cnt = dict(wo=0, ost=0)

        def loads(idx):
            c0, N, col = tl[idx]; b = idx % 2
            tr.dma('sp', xTs[b][:, :, :N], XT[:, c0:c0 + N].rearrange('(k p) t -> p k t', p=128), writes=[xTs[b]])
            tr.dma('sp', tg[b][:, :, :N], TG[:, c0:c0 + N].rearrange('(k p) t -> p k t', p=128), writes=[tg[b]])
            if col == 1:
                tr.dma('sp', yg[b][:, :, :N], YGC.rearrange('h d t -> (h d) t').rearrange('(k p) t -> p k t', p=128), writes=[yg[b]])
                tr.dma('sp', yn[b][:, :, :N], YNC.rearrange('h d t -> (h d) t').rearrange('(k p) t -> p k t', p=128), writes=[yn[b]])
            else:
                tr.dma('sp', yg[b][:, :, :N], YG.rearrange('h d t -> (h d) t')[:, c0 - NCX:c0 - NCX + N].rearrange('(k p) t -> p k t', p=128), writes=[yg[b]])
                tr.dma('sp', yn[b][:, :, :N], YN.rearrange('h d t -> (h d) t')[:, c0 - NCX:c0 - NCX + N].rearrange('(k p) t -> p k t', p=128), writes=[yn[b]])

        mcount = 0
        loads(0)
        for idx in range(len(tl)):
            c0, N, col = tl[idx]; b = idx % 2
            xT = xTs[b]; tg_ = tg[b]; yg_ = yg[b]; yn_ = yn[b]
            if idx + 1 < len(tl): loads(idx + 1)
            tr.op('act', lambda: A.activation(out=tgb[:, :, :N], in_=tg_[:, :, :N], func=AF.Copy), reads=[tg_], writes=[tgb])
            for m in range(3):
                pg = R['ps_g'][m % 2]
                tr.group('pe', [lambda kc=kc: PE.matmul(pg[:, :N], lhsT=wglu[:, m, kc, :], rhs=tgb[:, kc, :N], start=(kc == 0), stop=(kc == 2))
                                for kc in range(3)], reads=[wglu, tgb], writes=[pg])
                tr.op('act', lambda: A.activation(out=acc[:, :N], in_=pg[:, :N], func=AF.Sigmoid), reads=[pg], writes=[acc])
                tr.op('dve', lambda: V.tensor_tensor(out=ys[:, m, :N], in0=acc[:, :N], in1=tg_[:, m, :N], op=ALU.mult), reads=[acc, tg_],
                      writes=[ys] if m == 0 else (), acc=() if m == 0 else [ys])
            for m in range(KC):
                p1 = R['ps_g'][m % 2]; p2 = R['ps_u'][m % 2]; p3 = R['ps_m'][m % 2]
                sg_ = sgr[mcount % 2]; wpg = wpgr[mcount % 2]; wpn = wpnr[mcount % 2]; mcount += 1
                tr.dma('sp', sg_[:, :, :N], SG[:, c0:c0 + N].rearrange('(b m p) t -> m p b t', b=3, p=128)[m], writes=[sg_])
                tr.dma('sp', wpg[:], w_b[('wpg', l)][m], reads=[w_buf[('wpg', l)]], writes=[wpg])
                tr.dma('sp', wpn[:], w_b[('wpn', l)][m], reads=[w_buf[('wpn', l)]], writes=[wpn])
                tr.group('pe', [lambda kc=kc: PE.matmul(p1[:, :N], lhsT=wps[:, m, kc, :], rhs=ys[:, kc, :N], start=(kc == 0), stop=(kc == 2))
                                for kc in range(3)], reads=[wps, ys], writes=[p1])
                tr.group('pe', [lambda h=h: PE.matmul(p2[:, :N], lhsT=wpg[:, h, :], rhs=yg_[:, h, :N], start=(h == 0), stop=(h == 3))
                                for h in range(4)], reads=[wpg, yg_], writes=[p2])
                tr.group('pe', [lambda h=h: PE.matmul(p3[:, :N], lhsT=wpn[:, h, :], rhs=yn_[:, h, :N], start=(h == 0), stop=(h == 3))
                                for h in range(4)], reads=[wpn, yn_], writes=[p3])
                tr.op('dve', lambda: V.tensor_tensor(out=acc[:, :N], in0=p1[:, :N], in1=sg_[:, 0, :N], op=ALU.mult), reads=[p1, sg_], writes=[acc])
                tr.op('dve', lambda: V.tensor_tensor(out=t2[:, :N], in0=p2[:, :N], in1=sg_[:, 1, :N], op=ALU.mult), reads=[p2, sg_], writes=[t2])
                tr.op('pool', lambda: G.tensor_tensor(out=acc[:, :N], in0=acc[:, :N], in1=t2[:, :N], op=ALU.add), reads=[acc, t2], writes=[acc])
                tr.op('dve', lambda: V.tensor_tensor(out=t2[:, :N], in0=p3[:, :N], in1=sg_[:, 2, :N], op=ALU.mult), reads=[p3, sg_], writes=[t2])
                tr.op('pool', lambda: G.tensor_tensor(out=hT[:, m, :N], in0=acc[:, :N], in1=t2[:, :N], op=ALU.add), reads=[acc, t2],
                      writes=[hT] if m == 0 else (), acc=() if m == 0 else [hT])
            for m in range(KC):
                wb = wo[cnt['wo'] % 2]; cnt['wo'] += 1
                tr.dma('sp', wb[:], w_b[('wout', l)][m], reads=[w_buf[('wout', l)]], writes=[wb])
                pd = R['ps_m'][m % 2]
                tr.group('pe', [lambda kc=kc: PE.matmul(pd[:, :N], lhsT=wb[:, kc, :], rhs=hT[:, kc, :N], start=(kc == 0), stop=(kc == KC - 1))
                                for kc in range(KC)], reads=[wb, hT], writes=[pd])
                tr.op('dve', lambda: V.scalar_tensor_tensor(out=xT[:, m, :N], in0=pd[:, :N], scalar=modG[:, 1, m, col:col + 1],
                                                            in1=xT[:, m, :N], op0=ALU.mult, op1=ALU.add), reads=[pd, modG, xT], acc=[xT])
            ffn(ph, l, 2, xT, hT, aT, N, col, 2, R)
            if not last:
                tr.dma('pool', XT[:, c0:c0 + N].rearrange('(k p) t -> p k t', p=128), xT[:, :, :N], reads=[xT])
            else:
                sq = R['sq']; rstd = R['rstd']; tb = R['tmpbig']; pss = R['ps_m'][0]
                tr.op('act', lambda: A.activation(out=sq[:, :, :N], in_=xT[:, :, :N], func=AF.Square), reads=[xT], writes=[sq])
                tr.group('pe', [lambda kc=kc: PE.matmul(pss[:, :N], lhsT=ones_b[:], rhs=sq[:, kc, :N], start=(kc == 0), stop=(kc == KC - 1))
                                for kc in range(KC)], reads=[sq, ones_b], writes=[pss])
                tr.op('act', lambda: A.activation(out=rstd[:, :N], in_=pss[:, :N], func=AF.Sqrt, scale=1.0 / D, bias=epsb[:, 0:1]), reads=[pss, epsb], writes=[rstd])
                tr.op('dve', lambda: V.reciprocal(out=rstd[:, :N], in_=rstd[:, :N]), reads=[rstd], writes=[rstd])
                tr.op('dve', lambda: V.tensor_tensor(out=tb[:, :, :N], in0=xT[:, :, :N], in1=rstd[:, :N].unsqueeze(1).to_broadcast([128, KC, N]), op=ALU.mult),
                      reads=[xT, rstd], writes=[tb])
                tr.group('act', [lambda kc=kc: A.activation(out=tb[:, kc, :N], in_=tb[:, kc, :N], func=AF.Identity, scale=fing[:, kc:kc + 1])
                                 for kc in range(KC)], reads=[tb, fing], writes=[tb])
                for ts in range(N // 128):
                    o_ = ost[0]; cnt['ost'] += 1
                    for hf in range(2):
                        pt = R['ps_g'][hf]
                        tr.group('pe', [lambda k=k: PE.transpose(pt[:, k * 128:(k + 1) * 128], tb[:, hf * 4 + k, ts * 128:(ts + 1) * 128], ident[:])
                                        for k in range(4)], reads=[tb, ident], writes=[pt])
                        tr.op('act' if hf == 0 else 'dve',
                              (lambda: A.activation(out=o_[:, 0:512], in_=pt[:], func=AF.Copy)) if hf == 0 else (lambda: V.tensor_copy(out=o_[:, 512:1024], in_=pt[:])),
                              reads=[pt], writes=[o_] if hf == 0 else (), acc=() if hf == 0 else [o_])
                    r0 = c0 - NCX + ts * 128
                    tr.dma('pool', out_d[r0:r0 + 128, :], o_[:], reads=[o_])
        ph.close()

    epsb = gp.sb('epsb', [128, 1]); tr.op('dve', lambda: V.memset(epsb[:], 1e-6), writes=[epsb])
    tr.barrier()

    def run():
        nonlocal stg_o, pcbig
        stg_o = [gp.sb('stgo%d' % i, [128, 512], F32, dma=True) for i in range(2)]
        pcbig = gp.sb('pcbig', [128, 256], BF16)
        if only is not None:
            {'B': phase_B}[only[0]](only[1]); return
        for l in range(DEPTH):
            ctx_out = l < DEPTH - 1
            compute_mod(l)
            if stop_after == ('mod', l): return
            phase_A(l)
            if stop_after == ('A', l): return
            emit_casts(l, G2)
            phase_B(l)
            if stop_after == ('B', l): return
            if l + 1 < DEPTH: emit_casts(l + 1, G1)
            phase_C(l, ctx_out)
            if stop_after == ('C', l): return
            phase_D(l, ctx_out)
            if stop_after == ('D', l): return
            phase_E(l, ctx_out, l == DEPTH - 1)
            if stop_after == ('E', l): return

    run()
    tr.barrier()
    gp.es.close()
    tr.es.close()
    return nc


def prep_shared(inp):
    sh = {}
    L = DEPTH
    sh['wgu1'] = np.stack([np.stack([tile_w(inp['ffn1_wg'][l]), tile_w(inp['ffn1_wu'][l])], 2) for l in range(L)])
    sh['wd1'] = np.stack([tile_w(inp['ffn1_wd'][l]) for l in range(L)])
    sh['wgu2'] = np.stack([np.stack([tile_w(inp['ffn2_wg'][l]), tile_w(inp['ffn2_wu'][l])], 2) for l in range(L)])
    sh['wd2'] = np.stack([tile_w(inp['ffn2_wd'][l]) for l in range(L)])
    cols = win_fm_cols()
    sh['winfm'] = np.stack([tile_w(inp['w_in'][l][:, cols]) for l in range(L)])
    tmc = np.concatenate([IN_OFF['gv'] + np.arange(128), IN_OFF['nv'] + np.arange(512)])
    sh['wintm'] = np.stack([np.ascontiguousarray(inp['w_in'][l][:, tmc].reshape(KC, 128, 640).transpose(1, 0, 2)) for l in range(L)])
    sh['wada'] = np.stack([tile_w(inp['w_ada'][l]) for l in range(L)])
    sh['wglu'] = np.stack([tile_w(inp['ssm_w_glu'][l]) for l in range(L)])
    sh['wps'] = np.stack([tile_w(inp['w_p_ssm'][l]) for l in range(L)])
    sh['wpg'] = np.stack([tile_w(inp['w_p_gqa'][l]) for l in range(L)])
    sh['wpn'] = np.stack([tile_w(inp['w_p_na'][l]) for l in range(L)])
    sh['wout'] = np.stack([tile_w(inp['w_out'][l]) for l in range(L)])
    sh['bada'] = np.ascontiguousarray(inp['b_ada'].reshape(L, 72, 128).transpose(0, 2, 1))
    sh['normg'] = np.ascontiguousarray(inp['norm_g'].reshape(L, 3, KC, 128).transpose(0, 3, 1, 2))
    sh['fing'] = np.ascontiguousarray(inp['final_g'].reshape(KC, 128).T)

    def st(a):
        return a.reshape(L, 2, 12, 2, 64).transpose(0, 1, 3, 4, 2).reshape(L, 2, 128, 12)
    ldt = np.broadcast_to(inp['ssm_log_dt'][:, :, :, None], (L, 2, 24, 64))
    sh['ssm_a'] = np.ascontiguousarray(np.stack([st(inp['ssm_a_re']), st(inp['ssm_a_im']), st(ldt)], 3))

    def sbt(a):
        return a.reshape(L, 2, 12, 2, 64, 16).transpose(0, 1, 3, 4, 2, 5).reshape(L, 2, 128, 12, 16)
    sh['ssm_b'] = np.ascontiguousarray(np.stack([sbt(inp['ssm_b_re']), sbt(inp['ssm_b_im'])], 2))

    def sct(a):
        return a.reshape(L, 2, 3, 8, 16, 64).transpose(0, 1, 3, 4, 2, 5).reshape(L, 2, 128, 3, 64)
    sh['ssm_c'] = np.ascontiguousarray(np.stack([sct(inp['ssm_c_re']), sct(inp['ssm_c_im'])], 2))
    sh['ssm_d'] = np.ascontiguousarray(inp['ssm_d'].reshape(L, 3, 128).transpose(0, 2, 1))
    sh['sink'] = np.ascontiguousarray(np.broadcast_to(inp['gqa_sink'][:, None, :], (L, 64, 8)))
    vt, od, cb = na_bias_tables(inp['na_rpb'])
    sh['navt'] = vt; sh['naod'] = od; sh['nacb'] = cb.reshape(L, 8, 128, 5, 128)
    sh.update(host_consts())
    return {k: np.ascontiguousarray(v, dtype=np.float32) for k, v in sh.items()}


def core_inputs(inp, b, sh):
    m = dict(sh)
    m['x'] = np.ascontiguousarray(inp['x'][b]); m['ctx'] = np.ascontiguousarray(inp['ctx'][b])
    sv = np.stack([inp['c'][b].reshape(KC, 128).T, inp['c_ctx'].reshape(KC, 128).T], 2)
    m['svec'] = np.ascontiguousarray(sv, dtype=np.float32)
    return m


def kernel(**inputs):
    inp = {k: np.asarray(v) for k, v in inputs.items()}
    sh = prep_shared(inp)
    nc = build()
    in_maps = [core_inputs(inp, b, sh) for b in range(8)]
    res = run_bass_kernel_spmd(nc, in_maps, core_ids=list(range(8)))
    return np.stack([np.asarray(r['out'], dtype=np.float32) for r in res.results], 0)
```
